# Optimizing a Trainium2 kernel written in Bass

```python
import math
import jax, jax.numpy as jnp
from jax import lax
import numpy as np

D_MODEL = 1024
BATCH = 16
SEQ = 256
DEPTH = 4
DEC_BATCH = 2
DEC_SEQ = 4096
PAST_LEN = 512

GRID_W = 64
N_MIXERS = 4
N_A = (DEPTH + 3) // N_MIXERS
N_B = (DEPTH + 2) // N_MIXERS
N_C = (DEPTH + 1) // N_MIXERS
N_D = DEPTH // N_MIXERS
D_FF = 2816
MACARON_W = 0.5
RMS_EPS = 1e-6
NEG_INF = -1e30
Q_BLOCK = 128
A_HEADS = 16
A_KV_HEADS = 4
A_HEAD_DIM = D_MODEL // A_HEADS
A_WINDOW = 128
A_BLOCK = 128
ROPE_BASE = 10000.0
S5_GROUP = 16
S5_GROUPS = D_MODEL // S5_GROUP
S5_STATE = 64
C_HEADS = 16
C_HEAD_DIM = D_MODEL // C_HEADS
NA_ROWS = 8
NA_COLS = 16
DN_QK_HEADS = 4
DN_V_HEADS = 8
DN_HEAD_DIM = 128
DN_CONV = 5
DN_CHUNK = 64

kernel_name = 'hybrid_diffusion_prefix_trunk_step'

F32 = jnp.float32


def rmsnorm(x, g):
    xf = x.astype(F32)
    y = xf * lax.rsqrt(jnp.mean(xf * xf, axis=-1, keepdims=True) + RMS_EPS)
    return (y * g.astype(F32)).astype(x.dtype)


def l2norm(x):
    xf = x.astype(F32)
    return (xf * lax.rsqrt(jnp.sum(xf * xf, axis=-1, keepdims=True) + RMS_EPS)).astype(x.dtype)


def swiglu(h, w_gu, w_d):
    g, u = jnp.split(h @ w_gu, 2, axis=-1)
    return (jax.nn.silu(g) * u) @ w_d


def rotate(x, ang):
    ang = ang.reshape((ang.shape[0],) + (1,) * (x.ndim - 3) + (ang.shape[1],))
    cos, sin = jnp.cos(ang).astype(x.dtype), jnp.sin(ang).astype(x.dtype)
    x1, x2 = jnp.split(x, 2, axis=-1)
    return jnp.concatenate([x1 * cos - x2 * sin, x2 * cos + x1 * sin], axis=-1)


def axial_rope(x):
    L, d = x.shape[1], x.shape[-1]
    n = d // 4
    inv = ROPE_BASE ** (-jnp.arange(n, dtype=F32) / n)
    t = jnp.arange(L)
    ang_r = (t // GRID_W).astype(F32)[:, None] * inv[None, :]
    ang_c = (t % GRID_W).astype(F32)[:, None] * inv[None, :]
    half = d // 2
    return jnp.concatenate([rotate(x[..., :half], ang_r), rotate(x[..., half:], ang_c)], axis=-1)


def blocked_attention(q, k, v, sink):
    B, Lq = q.shape[:2]
    nb = Lq // Q_BLOCK
    scale = q.shape[-1] ** -0.5
    qb = jnp.moveaxis(q.reshape((B, nb, Q_BLOCK) + q.shape[2:]), 1, 0)

    def one_block(qblk):
        s = jnp.einsum('bikgd,bjkd->bkgij', qblk, k).astype(F32) * scale
        if sink is not None:
            sk = jnp.broadcast_to(sink.astype(F32)[None, :, :, None, None], s.shape[:-1] + (1,))
            p = jax.nn.softmax(jnp.concatenate([s, sk], axis=-1), axis=-1)[..., :-1]
        else:
            p = jax.nn.softmax(s, axis=-1)
        return jnp.einsum('bkgij,bjkd->bikgd', p.astype(v.dtype), v)

    o = lax.map(one_block, qb)
    return jnp.moveaxis(o, 0, 1).reshape(q.shape)


def gqa_project(h, w_qkv):
    B, L, _ = h.shape
    G = A_HEADS // A_KV_HEADS
    q, k, v = jnp.split(h @ w_qkv, [A_HEADS * A_HEAD_DIM, (A_HEADS + A_KV_HEADS) * A_HEAD_DIM], axis=-1)
    return (q.reshape(B, L, A_KV_HEADS, G, A_HEAD_DIM), k.reshape(B, L, A_KV_HEADS, A_HEAD_DIM),
            v.reshape(B, L, A_KV_HEADS, A_HEAD_DIM))


def mixer_a_ctx(h, w_qkv, w_o, sink):
    B, L, _ = h.shape
    q, k, v = gqa_project(h, w_qkv)
    o = blocked_attention(q, k, v, sink.reshape(A_KV_HEADS, A_HEADS // A_KV_HEADS))
    return o.reshape(B, L, -1) @ w_o, (k, v)


def mixer_a_lat(h, w_qkv, w_o, sink, k_ctx, v_ctx):
    B, L, _ = h.shape
    G = A_HEADS // A_KV_HEADS
    q, k, v = gqa_project(h, w_qkv)
    q, k = axial_rope(q), axial_rope(k)
    nb = L // A_BLOCK
    side = A_WINDOW // A_BLOCK
    width = (2 * side + 1) * A_BLOCK

    def band(t):
        tp = jnp.pad(t, ((0, 0), (side * A_BLOCK, side * A_BLOCK), (0, 0), (0, 0)))
        tp = tp.reshape((B, nb + 2 * side, A_BLOCK) + t.shape[2:])
        return jnp.concatenate([tp[:, o:o + nb] for o in range(2 * side + 1)], axis=2)

    kb, vb = band(k), band(v)
    qb = q.reshape(B, nb, A_BLOCK, A_KV_HEADS, G, A_HEAD_DIM)
    blk = jnp.arange(nb)[:, None] * A_BLOCK
    q_pos = blk + jnp.arange(A_BLOCK)[None, :]
    k_pos = blk - side * A_BLOCK + jnp.arange(width)[None, :]
    ok = ((jnp.abs(k_pos[:, None, :] - q_pos[:, :, None]) <= A_WINDOW)
          & ((k_pos >= 0) & (k_pos < L))[:, None, :])
    scale = A_HEAD_DIM ** -0.5
    s_loc = jnp.einsum('bnikgd,bnjkd->bnkgij', qb, kb).astype(F32) * scale
    s_loc = jnp.where(ok[None, :, None, None], s_loc, NEG_INF)
    s_ctx = jnp.einsum('bnikgd,bmkd->bnkgim', qb, k_ctx).astype(F32) * scale
    sk = jnp.broadcast_to(sink.astype(F32).reshape(1, 1, A_KV_HEADS, G, 1, 1), s_loc.shape[:-1] + (1,))
    p = jax.nn.softmax(jnp.concatenate([s_loc, s_ctx, sk], axis=-1), axis=-1).astype(v.dtype)
    o = (jnp.einsum('bnkgij,bnjkd->bnikgd', p[..., :width], vb)
         + jnp.einsum('bnkgim,bmkd->bnikgd', p[..., width:-1], v_ctx))
    return o.reshape(B, L, -1) @ w_o, None


def complex_affine_combine(e1, e2):
    a1r, a1i, b1r, b1i = e1
    a2r, a2i, b2r, b2i = e2
    return (a2r * a1r - a2i * a1i, a2r * a1i + a2i * a1r,
            a2r * b1r - a2i * b1i + b2r, a2r * b1i + a2i * b1r + b2i)


def s5_scan(u, lam_re, lam_im, log_dt, b_re, b_im, h0):
    lam_re, lam_im = lam_re.astype(F32), lam_im.astype(F32)
    dt = jnp.exp(log_dt.astype(F32))[:, None]
    lr, li = lam_re * dt, lam_im * dt
    a_re, a_im = jnp.exp(lr) * jnp.cos(li), jnp.exp(lr) * jnp.sin(li)
    den = lam_re * lam_re + lam_im * lam_im
    fr = ((a_re - 1.0) * lam_re + a_im * lam_im) / den
    fi = (a_im * lam_re - (a_re - 1.0) * lam_im) / den
    b_re, b_im = b_re.astype(F32), b_im.astype(F32)
    bb_re = fr[..., None] * b_re - fi[..., None] * b_im
    bb_im = fr[..., None] * b_im + fi[..., None] * b_re
    bu_re = jnp.einsum('blgc,gpc->blgp', u, bb_re)
    bu_im = jnp.einsum('blgc,gpc->blgp', u, bb_im)
    if h0 is not None:
        bu_re = jnp.concatenate([h0[0][:, None], bu_re], axis=1)
        bu_im = jnp.concatenate([h0[1][:, None], bu_im], axis=1)
    A_re, A_im = jnp.broadcast_to(a_re, bu_re.shape), jnp.broadcast_to(a_im, bu_re.shape)
    _, _, xr, xi = lax.associative_scan(complex_affine_combine, (A_re, A_im, bu_re, bu_im), axis=1)
    if h0 is not None:
        xr, xi = xr[:, 1:], xi[:, 1:]
    return xr, xi


def mixer_s5(h, p, state_re=None, state_im=None):
    lam_re, lam_im, log_dt, b_re, b_im, c_re, c_im, d_skip, w_glu = p
    B, L, _ = h.shape
    u = h.astype(F32).reshape(B, L, S5_GROUPS, S5_GROUP)
    y = jnp.zeros_like(u)
    fin_re, fin_im = [], []
    for d in range(2):
        ud = u if d == 0 else jnp.flip(u, 1)
        h0 = None if state_re is None else (state_re[:, d].astype(F32), state_im[:, d].astype(F32))
        xr, xi = s5_scan(ud, lam_re[d], lam_im[d], log_dt[d], b_re[d], b_im[d], h0)
        yd = (jnp.einsum('gcp,blgp->blgc', c_re[d].astype(F32), xr)
              - jnp.einsum('gcp,blgp->blgc', c_im[d].astype(F32), xi))
        y = y + (yd if d == 0 else jnp.flip(yd, 1))
        if state_re is None:
            fin_re.append(xr[:, -1])
            fin_im.append(xi[:, -1])
    y = y.reshape(B, L, D_MODEL).astype(h.dtype) + d_skip * h
    a, gt = jnp.split(jax.nn.gelu(y) @ w_glu, 2, axis=-1)
    out = a * jax.nn.sigmoid(gt)
    aux = (jnp.stack(fin_re, 1), jnp.stack(fin_im, 1)) if state_re is None else None
    return out, aux


def na_project(h, w_qkv):
    B, L, _ = h.shape
    return [t.reshape(B, L, C_HEADS, C_HEAD_DIM) for t in jnp.split(h @ w_qkv, 3, axis=-1)]


def mixer_na_ctx(h, w_qkv, w_o):
    B, L, _ = h.shape
    q, k, v = na_project(h, w_qkv)
    o = blocked_attention(q[:, :, :, None], k, v, None)
    return o.reshape(B, L, -1) @ w_o, (k, v)


def mixer_na_lat(h, w_qkv, w_o, rpb, k_ctx, v_ctx):
    B, L, _ = h.shape
    rows = L // GRID_W
    kh = min(NA_ROWS, rows)
    q, k, v = [t.reshape(B, rows, GRID_W, C_HEADS, C_HEAD_DIM) for t in na_project(h, w_qkv)]
    r = jnp.arange(rows)
    row_idx = jnp.clip(r - kh // 2, 0, rows - kh)[:, None] + jnp.arange(kh)[None, :]
    kr, vr = k[:, row_idx], v[:, row_idx]
    col = jnp.arange(GRID_W)
    col_start = jnp.clip(col - NA_COLS // 2, 0, GRID_W - NA_COLS)
    col_ok = (col[None, :] >= col_start[:, None]) & (col[None, :] < col_start[:, None] + NA_COLS)
    d_row = row_idx - r[:, None]
    d_col = jnp.clip(col[None, :] - col[:, None], 1 - NA_COLS, NA_COLS - 1)
    bias = rpb[:, (d_row + NA_ROWS - 1)[:, None, :, None], (d_col + NA_COLS - 1)[None, :, None, :]]
    scale = C_HEAD_DIM ** -0.5
    s = jnp.einsum('brqhd,brjkhd->bhrqjk', q, kr).astype(F32) * scale + bias[None].astype(F32)
    s = jnp.where(col_ok[:, None, :], s, NEG_INF).reshape(B, C_HEADS, rows, GRID_W, kh * GRID_W)
    s_ctx = jnp.einsum('brqhd,bmhd->bhrqm', q, k_ctx).astype(F32) * scale
    p = jax.nn.softmax(jnp.concatenate([s, s_ctx], axis=-1), axis=-1).astype(v.dtype)
    n_loc = kh * GRID_W
    p_loc = p[..., :n_loc].reshape(B, C_HEADS, rows, GRID_W, kh, GRID_W)
    o = (jnp.einsum('bhrqjk,brjkhd->brqhd', p_loc, vr)
         + jnp.einsum('bhrqm,bmhd->brqhd', p[..., n_loc:], v_ctx))
    return o.reshape(B, L, -1) @ w_o, None


def centred_conv(x, w):
    K, C = w.shape
    return lax.conv_general_dilated(x, w[:, None, :], window_strides=(1,), padding=[(K // 2, K // 2)],
                                    dimension_numbers=('NWC', 'WIO', 'NWC'), feature_group_count=C)


def gated_delta_rule(q, k, v, g, beta, s0):
    B, L, H, dk = k.shape
    dv = v.shape[-1]
    n = L // DN_CHUNK

    def chunk(t):
        t = t.astype(F32).reshape((B, n, DN_CHUNK) + t.shape[2:])
        return jnp.swapaxes(t, 2, 3)

    q, k, v, g, beta = [chunk(t) for t in (q, k, v, g, beta)]
    gc = jnp.cumsum(g, axis=-1)
    idx = jnp.arange(DN_CHUNK)
    lower = idx[:, None] >= idx[None, :]
    strict = idx[:, None] > idx[None, :]
    decay = jnp.exp(jnp.where(lower, gc[..., :, None] - gc[..., None, :], NEG_INF))
    kb = k * beta[..., None]
    lmat = jnp.where(strict, jnp.einsum('bnhid,bnhjd->bnhij', kb, k) * decay, 0.0)
    rhs = jnp.concatenate([v * beta[..., None], kb * jnp.exp(gc)[..., None]], axis=-1)
    sol = lax.linalg.triangular_solve(lmat + jnp.eye(DN_CHUNK, dtype=F32), rhs, left_side=True, lower=True)
    u, w = sol[..., :dv], sol[..., dv:]
    a_qk = jnp.where(lower, jnp.einsum('bnhid,bnhjd->bnhij', q, k) * decay, 0.0)

    def step(S, xs):
        q_c, k_c, u_c, w_c, g_c, a_c = xs
        v_new = u_c - jnp.einsum('bhcd,bhde->bhce', w_c, S)
        o = (jnp.einsum('bhcd,bhde->bhce', q_c * jnp.exp(g_c)[..., None], S)
             + jnp.einsum('bhij,bhje->bhie', a_c, v_new))
        g_last = g_c[..., -1:]
        S = (S * jnp.exp(g_last)[..., None]
             + jnp.einsum('bhcd,bhce->bhde', k_c * jnp.exp(g_last - g_c)[..., None], v_new))
        return S, o

    xs = tuple(jnp.moveaxis(t, 1, 0) for t in (q, k, u, w, gc, a_qk))
    S, o = lax.scan(step, s0.astype(F32), xs)
    o = jnp.swapaxes(jnp.moveaxis(o, 0, 1), 2, 3).reshape(B, L, H, dv)
    return o, S


def mixer_dn(h, p, state=None):
    w_in, conv_w, w_ba, a_log, dt_bias, out_g, w_o = p
    B, L, _ = h.shape
    nqk = DN_QK_HEADS * DN_HEAD_DIM
    nv = DN_V_HEADS * DN_HEAD_DIM
    proj = h @ w_in
    qkv = jax.nn.silu(centred_conv(proj[..., :2 * nqk + nv], conv_w))
    z = proj[..., 2 * nqk + nv:].reshape(B, L, DN_V_HEADS, DN_HEAD_DIM)
    rep = DN_V_HEADS // DN_QK_HEADS
    q = jnp.repeat(l2norm(qkv[..., :nqk].reshape(B, L, DN_QK_HEADS, DN_HEAD_DIM)), rep, axis=2) * DN_HEAD_DIM ** -0.5
    k = jnp.repeat(l2norm(qkv[..., nqk:2 * nqk].reshape(B, L, DN_QK_HEADS, DN_HEAD_DIM)), rep, axis=2)
    v = qkv[..., 2 * nqk:].reshape(B, L, DN_V_HEADS, DN_HEAD_DIM)
    o_sum = jnp.zeros((B, L, DN_V_HEADS, DN_HEAD_DIM), F32)
    finals = []
    for d in range(2):
        b_raw, a_raw = jnp.split((h @ w_ba[d]).astype(F32), 2, axis=-1)
        beta = jax.nn.sigmoid(b_raw)
        g = -jnp.exp(a_log[d].astype(F32)) * jax.nn.softplus(a_raw + dt_bias[d].astype(F32))
        seqs = (q, k, v, g, beta) if d == 0 else tuple(jnp.flip(t, 1) for t in (q, k, v, g, beta))
        s0 = jnp.zeros((B, DN_V_HEADS, DN_HEAD_DIM, DN_HEAD_DIM), F32) if state is None else state[:, d]
        o, S = gated_delta_rule(*seqs, s0)
        o_sum = o_sum + (o if d == 0 else jnp.flip(o, 1))
        if state is None:
            finals.append(S)
    o = rmsnorm(o_sum, out_g) * jax.nn.silu(z.astype(F32))
    out = o.astype(h.dtype).reshape(B, L, -1) @ w_o
    return out, (jnp.stack(finals, 1) if state is None else None)


def trunk_layer(x, cond, i, mix, norm_g, w_ada, b_ada, ffn_w_gu, ffn_w_d):
    mods = jnp.split((jax.nn.silu(cond) @ w_ada[i] + b_ada[i])[:, None, :], 9, axis=-1)

    def sub(x, s, f, weight):
        shift, scale, gate = mods[3 * s], mods[3 * s + 1], mods[3 * s + 2]
        y, aux = f(rmsnorm(x, norm_g[i, 2 * s]) * (1.0 + scale) + shift)
        return x + weight * gate * rmsnorm(y, norm_g[i, 2 * s + 1]), aux

    x, _ = sub(x, 0, lambda t: (swiglu(t, ffn_w_gu[i, 0], ffn_w_d[i, 0]), None), MACARON_W)
    x, aux = sub(x, 1, mix, 1.0)
    x, _ = sub(x, 2, lambda t: (swiglu(t, ffn_w_gu[i, 1], ffn_w_d[i, 1]), None), MACARON_W)
    return x, aux


def setup_inputs(seed: int = 0) -> dict:
    key = jax.random.key(seed)
    ks = iter(jax.random.split(key, 64))

    def nrm(shape, scale=1.0):
        return scale * jax.random.normal(next(ks), shape, F32)

    def unif(shape, lo, hi):
        return jax.random.uniform(next(ks), shape, F32, lo, hi)

    D = D_MODEL
    nqkv = 2 * DN_QK_HEADS * DN_HEAD_DIM + DN_V_HEADS * DN_HEAD_DIM
    s5_shape = (N_B, 2, S5_GROUPS, S5_STATE)
    dn_dt = jnp.exp(unif((N_D, 2, DN_V_HEADS), math.log(1e-3), math.log(1e-1)))
    return {
        'x_prompt': nrm((BATCH, SEQ, D)),
        'x_sample': nrm((DEC_BATCH, DEC_SEQ, D)),
        'cache_attn_k': nrm((DEC_BATCH, N_A, PAST_LEN, A_KV_HEADS, A_HEAD_DIM)),
        'cache_attn_v': nrm((DEC_BATCH, N_A, PAST_LEN, A_KV_HEADS, A_HEAD_DIM)),
        'state_s5_re': nrm((DEC_BATCH, N_B, 2, S5_GROUPS, S5_STATE), 0.1),
        'state_s5_im': nrm((DEC_BATCH, N_B, 2, S5_GROUPS, S5_STATE), 0.1),
        'cache_na_k': nrm((DEC_BATCH, N_C, PAST_LEN, C_HEADS, C_HEAD_DIM)),
        'cache_na_v': nrm((DEC_BATCH, N_C, PAST_LEN, C_HEADS, C_HEAD_DIM)),
        'state_dn': nrm((DEC_BATCH, N_D, 2, DN_V_HEADS, DN_HEAD_DIM, DN_HEAD_DIM), 0.1),
        'c': nrm((DEC_BATCH, D)),
        'c_ctx': nrm((D,)),
        'norm_g': 1.0 + nrm((DEPTH, 6, D), 0.02),
        'w_ada': nrm((DEPTH, D, 9 * D), 0.5 * D ** -0.5),
        'b_ada': nrm((DEPTH, 9 * D), 0.01),
        'ffn_w_gu': nrm((DEPTH, 2, D, 2 * D_FF), D ** -0.5),
        'ffn_w_d': nrm((DEPTH, 2, D_FF, D), D_FF ** -0.5),
        'a_w_qkv': nrm((N_A, D, (A_HEADS + 2 * A_KV_HEADS) * A_HEAD_DIM), D ** -0.5),
        'a_w_o': nrm((N_A, A_HEADS * A_HEAD_DIM, D), (A_HEADS * A_HEAD_DIM) ** -0.5),
        'a_sink': nrm((N_A, A_HEADS)),
        's5_lam_re': -0.5 + nrm(s5_shape, 0.01),
        's5_lam_im': math.pi * jnp.arange(S5_STATE, dtype=F32) + nrm(s5_shape, 0.01),
        's5_log_dt': unif((N_B, 2, S5_GROUPS), math.log(1e-3), math.log(1e-1)),
        's5_b_re': nrm((N_B, 2, S5_GROUPS, S5_STATE, S5_GROUP), (2 * S5_GROUP) ** -0.5),
        's5_b_im': nrm((N_B, 2, S5_GROUPS, S5_STATE, S5_GROUP), (2 * S5_GROUP) ** -0.5),
        's5_c_re': nrm((N_B, 2, S5_GROUPS, S5_GROUP, S5_STATE), 0.5),
        's5_c_im': nrm((N_B, 2, S5_GROUPS, S5_GROUP, S5_STATE), 0.5),
        's5_d': nrm((N_B, D), 0.5),
        's5_w_glu': nrm((N_B, D, 2 * D), D ** -0.5),
        'na_w_qkv': nrm((N_C, D, 3 * C_HEADS * C_HEAD_DIM), D ** -0.5),
        'na_w_o': nrm((N_C, C_HEADS * C_HEAD_DIM, D), (C_HEADS * C_HEAD_DIM) ** -0.5),
        'na_rpb': nrm((N_C, C_HEADS, 2 * NA_ROWS - 1, 2 * NA_COLS - 1), 0.1),
        'dn_w_in': nrm((N_D, D, nqkv + DN_V_HEADS * DN_HEAD_DIM), D ** -0.5),
        'dn_conv_w': nrm((N_D, DN_CONV, nqkv), DN_CONV ** -0.5),
        'dn_w_ba': nrm((N_D, 2, D, 2 * DN_V_HEADS), D ** -0.5),
        'dn_a_log': jnp.log(unif((N_D, 2, DN_V_HEADS), 1.0, 16.0)),
        'dn_dt_bias': dn_dt + jnp.log(-jnp.expm1(-dn_dt)),
        'dn_out_g': 1.0 + nrm((N_D, DN_HEAD_DIM), 0.02),
        'dn_w_o': nrm((N_D, DN_V_HEADS * DN_HEAD_DIM, D), (DN_V_HEADS * DN_HEAD_DIM) ** -0.5),
    }


def reference(x_prompt, x_sample, cache_attn_k, cache_attn_v, state_s5_re, state_s5_im, cache_na_k, cache_na_v,
              state_dn, c, c_ctx, norm_g, w_ada, b_ada, ffn_w_gu, ffn_w_d, a_w_qkv, a_w_o, a_sink,
              s5_lam_re, s5_lam_im, s5_log_dt, s5_b_re, s5_b_im, s5_c_re, s5_c_im, s5_d, s5_w_glu,
              na_w_qkv, na_w_o, na_rpb, dn_w_in, dn_conv_w, dn_w_ba, dn_a_log, dn_dt_bias, dn_out_g, dn_w_o):
    common = (norm_g, w_ada, b_ada, ffn_w_gu, ffn_w_d)

    def s5_params(j):
        return (s5_lam_re[j], s5_lam_im[j], s5_log_dt[j], s5_b_re[j], s5_b_im[j], s5_c_re[j], s5_c_im[j],
                s5_d[j], s5_w_glu[j])

    def dn_params(j):
        return (dn_w_in[j], dn_conv_w[j], dn_w_ba[j], dn_a_log[j], dn_dt_bias[j], dn_out_g[j], dn_w_o[j])

    y = x_prompt
    cond_ctx = c_ctx[None, :]
    ctx_state = ([], [], [], [])
    for i in range(DEPTH):
        kind, j = i % N_MIXERS, i // N_MIXERS
        if kind == 0:
            mix = lambda t: mixer_a_ctx(t, a_w_qkv[j], a_w_o[j], a_sink[j])
        elif kind == 1:
            mix = lambda t: mixer_s5(t, s5_params(j))
        elif kind == 2:
            mix = lambda t: mixer_na_ctx(t, na_w_qkv[j], na_w_o[j])
        else:
            mix = lambda t: mixer_dn(t, dn_params(j))
        y, aux = trunk_layer(y, cond_ctx, i, mix, *common)
        ctx_state[kind].append(aux)
    new_attn_k = jnp.stack([s[0] for s in ctx_state[0]], 1)
    new_attn_v = jnp.stack([s[1] for s in ctx_state[0]], 1)
    new_s5_re = jnp.stack([s[0] for s in ctx_state[1]], 1)
    new_s5_im = jnp.stack([s[1] for s in ctx_state[1]], 1)
    new_na_k = jnp.stack([s[0] for s in ctx_state[2]], 1)
    new_na_v = jnp.stack([s[1] for s in ctx_state[2]], 1)
    new_dn = jnp.stack(ctx_state[3], 1)

    z = x_sample
    for i in range(DEPTH):
        kind, j = i % N_MIXERS, i // N_MIXERS
        if kind == 0:
            mix = lambda t: mixer_a_lat(t, a_w_qkv[j], a_w_o[j], a_sink[j], cache_attn_k[:, j], cache_attn_v[:, j])
        elif kind == 1:
            mix = lambda t: mixer_s5(t, s5_params(j), state_s5_re[:, j], state_s5_im[:, j])
        elif kind == 2:
            mix = lambda t: mixer_na_lat(t, na_w_qkv[j], na_w_o[j], na_rpb[j], cache_na_k[:, j], cache_na_v[:, j])
        else:
            mix = lambda t: mixer_dn(t, dn_params(j), state_dn[:, j])
        z, _ = trunk_layer(z, c, i, mix, *common)

    return (y, z, new_attn_k, new_attn_v, new_s5_re, new_s5_im, new_na_k, new_na_v, new_dn)
```

```python
import numpy as np
from contextlib import ExitStack
import concourse.bass as bass
import concourse.mybir as mybir
from concourse.bass_utils import run_bass_kernel_spmd

F32 = mybir.dt.float32
BF16 = mybir.dt.bfloat16
AF = mybir.ActivationFunctionType
ALU = mybir.AluOpType
AX = mybir.AxisListType

D = 1024
KC = 8
DFF = 2816
FC = 22
NCORES = 8
TCTX = 512
SEQ = 256
EPS = 1e-6
WSLOT = 6144
NWS = 2
STAGE = 99
LSEQ = 4096
CTX_SKIP = False
SUB = 99
DEBUG = False


class Eng:
    def __init__(self, name, h, sem):
        self.name, self.h, self.sem = name, h, sem
        self.count = 0
        self.waited = {}


class Builder:
    def __init__(self, nc, es):
        self.nc, self.es = nc, es
        self.es_sem = es
        self.sems = []
        self.engs = {}
        for nm, h in [("pe", nc.tensor), ("act", nc.scalar), ("dve", nc.vector),
                      ("pool", nc.gpsimd), ("sp", nc.sync)]:
            sem = es.enter_context(nc.semaphore("sem_" + nm))
            self.sems.append(sem)
            e = Eng(nm, h, sem)
            e.key = len(self.sems) - 1
            self.engs[nm] = e
        self.track = {}
        self.dsem = {}
        self.out_events = []

    def sb(self, name, shape, dt):
        return self.es.enter_context(self.nc.sbuf_tensor(name, list(shape), dt))

    def ps(self, name, shape, dt=F32):
        return self.es.enter_context(self.nc.psum_tensor(name, list(shape), dt))

    def _nm(self, a):
        return a if isinstance(a, str) else a.tensor.name

    def _deps(self, reads, writes, skip_own_waw=None, own=None):
        deps = set()
        for r in reads:
            nm = self._nm(r)
            st = self.track.get(nm)
            if st and st[0]:
                deps.add(st[0])
            if st and nm.startswith("ps"):
                for ev in st[1]:
                    if ev[0] != own:
                        deps.add(ev)
        for w in writes:
            st = self.track.get(self._nm(w))
            if st:
                if st[0] and not (skip_own_waw is not None and st[0][0] == skip_own_waw):
                    deps.add(st[0])
                for ev in st[1]:
                    deps.add(ev)
        return deps

    def _wait(self, e, deps):
        best = {}
        for (k, v) in deps:
            if best.get(k, 0) < v:
                best[k] = v
        for k, v in best.items():
            if e.waited.get(k, 0) < v:
                e.h.wait_ge(self.sems[k], v)
                e.waited[k] = v

    def _commit(self, ev, reads, writes, accum=False):
        for r in reads:
            st = self.track.setdefault(self._nm(r), [None, []])
            st[1].append(ev)
        for w in writes:
            nm = self._nm(w)
            if accum and nm in self.track:
                self.track[nm][0] = ev
            else:
                self.track[nm] = [ev, []]

    def op(self, eng, fname, accum=False, extra_reads=(), extra_writes=(), **kw):
        e = self.engs[eng]
        reads, writes = list(extra_reads), list(extra_writes)
        for k, v in kw.items():
            if isinstance(v, bass.AP):
                if k in ("out", "accum_out"):
                    writes.append(v)
                else:
                    reads.append(v)
        deps = self._deps(reads, writes, skip_own_waw=(e.key if accum else None), own=e.key)
        self._wait(e, deps)
        inst = getattr(e.h, fname)(**kw)
        e.count += 1
        inst.then_inc(e.sem, 1)
        ev = (e.key, e.count)
        self._commit(ev, reads, writes, accum=accum)
        return ev

    def raw(self, eng, fn, reads=(), writes=()):
        e = self.engs[eng]
        self._wait(e, self._deps(list(reads), list(writes), own=e.key))
        inst = fn()
        e.count += 1
        inst.then_inc(e.sem, 1)
        ev = (e.key, e.count)
        self._commit(ev, list(reads), list(writes))
        return ev

    def dma(self, q, out, in_, **kw):
        e = self.engs[q]
        deps = self._deps([in_], [out])
        self._wait(e, deps)
        nm = self._nm(out)
        if nm not in self.dsem:
            sem = self.es_sem.enter_context(self.nc.semaphore("dsem_" + nm))
            self.sems.append(sem)
            self.dsem[nm] = [len(self.sems) - 1, 0]
        ds = self.dsem[nm]
        e.h.dma_start(out=out, in_=in_, **kw).then_inc(self.sems[ds[0]], 16)
        ds[1] += 16
        ev = (ds[0], ds[1])
        self._commit(ev, [in_], [out])
        if out.tensor.name.startswith("out_"):
            self.out_events.append(ev)
        return ev

    def fence(self):
        evs = {(x.key, x.count) for x in self.engs.values() if x.count > 0}
        evs |= {(k, c) for (k, c) in self.dsem.values() if c > 0}
        for e in self.engs.values():
            self._wait(e, {ev for ev in evs if ev[0] != e.key})

    def finish(self):
        e = self.engs["sp"]
        self._wait(e, set(self.out_events))
        self._wait(e, {(x.key, x.count) for x in self.engs.values() if x.count > 0 and x.name != "sp"})


def build_program(nl=4):
    nc = bass.Bass("TRN2", target_bir_lowering=False)

    def din(name, shape):
        return nc.dram_tensor(name, list(shape), F32, kind="ExternalInput").ap()

    def dout(name, shape):
        return nc.dram_tensor("out_" + name, list(shape), F32, kind="ExternalOutput").ap()

    T = TCTX
    xT_in = din("xT_ctx", [D, T])
    ident_in = din("ident", [128, 128])
    c_ctx = din("c_ctx", [8, 128])
    norm_g = din("norm_g", [192, 128])
    w_ada = din("w_ada", [nl, D, 9 * D])
    b_ada = din("b_ada", [288, 128])
    ffn_w_gu = din("ffn_w_gu", [nl, 2, D, 2 * DFF])
    ffn_w_d = din("ffn_w_d", [nl, 2, DFF, D])
    a_w_qkv = din("a_w_qkv", [D, 1536])
    a_w_o = din("a_w_o", [D, D])
    a_sink = din("a_sink", [1, 16])
    s5_lam_re = din("s5_lam_re", [64, 128])
    s5_lam_im = din("s5_lam_im", [64, 128])
    s5_logdt = din("s5_logdt", [64, 128])
    s5_b_re = din("s5_b_re", [2, 64, 64, 16])
    s5_b_im = din("s5_b_im", [2, 64, 64, 16])
    s5_c_re = din("s5_c_re", [2, 64, 16, 64])
    s5_c_im = din("s5_c_im", [2, 64, 16, 64])
    s5_d = din("s5_d", [8, 128])
    s5_w_glu = din("s5_w_glu", [D, 2 * D])
    na_w_qkv = din("na_w_qkv", [D, 3 * D])
    na_w_o = din("na_w_o", [D, D])
    dn_w_in = din("dn_w_in", [D, 3 * D])
    dn_conv_w = din("dn_conv_w", [80, 128])
    dn_w_ba = din("dn_w_ba", [2, D, 16])
    dn_a_log = din("dn_a_log", [1, 16])
    dn_dt_bias = din("dn_dt_bias", [1, 16])
    dn_out_g = din("dn_out_g", [1, 128])
    dn_w_o = din("dn_w_o", [D, D])
    dn_mask = din("dn_mask", [64, 6, 64])
    xT_lat = din("xT_lat", [D, LSEQ])
    c_lat = din("c_lat", [8, 128])
    cache_ak = din("cache_ak", [512, 256])
    cache_av = din("cache_av", [512, 256])
    rope_cs = din("rope_cs", [LSEQ, 64])
    band_mask = din("band_mask", [128, 384])
    zres = nc.dram_tensor("zres", [D, LSEQ], F32, kind="Internal").ap()
    ysc = nc.dram_tensor("ysc", [D, LSEQ], F32, kind="Internal").ap()
    rot = nc.dram_tensor("rot", [64, 128, 1536], F32, kind="Internal").ap()
    cache_nk = din("cache_nk", [512, D])
    cache_nv = din("cache_nv", [512, D])
    na_bias = din("na_bias", [16, 64, 960])
    na_cmask = din("na_cmask", [64, 64])
    st_dn = din("st_dn", [16, 128, 128])
    pjsc = nc.dram_tensor("pjsc", [3, 128, LSEQ + 4], F32, kind="Internal").ap()
    osc = nc.dram_tensor("osc", [2, 64, LSEQ // 64, 128], F32, kind="Internal").ap()
    st5_re = din("st5_re", [64, 128])
    st5_im = din("st5_im", [64, 128])

    yT_out = dout("yT_ctx", [D, T])
    k_out = dout("attn_k", [T, 256])
    v_out = dout("attn_v", [T, 256])
    nak_out = dout("na_k", [T, D])
    nav_out = dout("na_v", [T, D])
    ysamp_out = dout("y_sample", [D, LSEQ])
    dn_out = dout("dn", [32, 128, 128])
    s5re_out = dout("s5_re", [128, 128])
    s5im_out = dout("s5_im", [128, 128])
    dbg_out = dout("dbg", [128, 4096]) if DEBUG else None

    with ExitStack() as es:
        b = Builder(nc, es)
        x = b.sb("x", [128, KC, T], F32)
        h = b.sb("h", [128, KC, T], BF16)
        y = b.sb("y", [128, KC, T], F32)
        act = b.sb("act", [128, FC, T], BF16)
        sq = act
        rstd = b.sb("rstd", [128, T], F32)
        tmpA = [b.sb(f"tmpA{i}", [128, T], F32) for i in range(2)]
        ident = b.sb("ident_sb", [128, 128], F32)
        ident_bf = b.sb("ident_bf", [128, 128], BF16)
        ones_bf = b.sb("ones_bf", [128, 128], BF16)
        wslots = [b.sb(f"wslot{i}", [128, WSLOT], BF16) for i in range(NWS)]
        ng = b.sb("ng", [128, 192], F32)
        bada = b.sb("bada", [128, 288], F32)
        scond = b.sb("scond", [128, KC, 1], BF16)
        scond_lat = b.sb("scond_lat", [128, KC, 1], BF16)
        cc = b.sb("cc", [128, KC], F32)
        mods = [b.sb(f"mods{i}", [128, 72], F32) for i in range(4)]
        Acoef = b.sb("Acoef", [128, 4 * 3 * KC], F32)
        Gcoef = b.sb("Gcoef", [128, 4 * 3 * KC], F32)
        rows_tmp = b.sb("rows_tmp", [96, 128], F32)
        oT = b.sb("oT", [128, KC, T], BF16)
        sink_bc = b.sb("sink_bc", [128, 16], F32)
        st_m = [b.sb(f"st_m{i}", [128, 8], F32) for i in range(2)]
        att_es = ExitStack()
        b.es = att_es
        qT = b.sb("qT", [128, 8, T], BF16)
        kT = b.sb("kT", [128, 8, T], BF16)
        vtok = b.sb("vtok", [128, 4, 1024], BF16)
        kv32 = [b.sb(f"kv32_{i}", [128, 512], F32) for i in range(2)]
        Pm = [b.sb(f"Pm{i}", [128, 256], BF16) for i in range(2)]
        PT = [b.sb(f"PT{i}", [128, 2, 128], BF16) for i in range(2)]
        otok = b.sb("otok", [128, 1024], BF16)
        b.es = es
        psG = [b.ps(f"psG{i}", [128, 512]) for i in range(2)]
        psU = [b.ps(f"psU{i}", [128, 512]) for i in range(2)]
        psY = [b.ps(f"psY{i}", [128, 512]) for i in range(2)]
        psS = b.ps("psS", [128, 512])
        psM = psS
        psT_all = b.ps("psT", [128, 1024], BF16)
        psTb = [psT_all[:, i * 256:(i + 1) * 256].rearrange("p (a n) -> p a n", a=2) for i in range(2)]

        wctr = [0]

        def wslot():
            s = wslots[wctr[0] % NWS]
            wctr[0] += 1
            return s

        b.dma("sp", ident[:], ident_in)
        b.op("dve", "tensor_copy", out=ident_bf[:], in_=ident[:])
        b.raw("dve", lambda: nc.vector.memset(ones_bf[:], 1.0), writes=[ones_bf[:]])

        def load_rows_T(dst_ap, src_rows_ap, nrows):
            b.dma("sp", rows_tmp[0:nrows, :], src_rows_ap)
            b.op("pe", "transpose", out=psM[:, 0:nrows], in_=rows_tmp[0:nrows, :], identity=ident[0:nrows, 0:nrows])
            b.op("dve", "tensor_copy", out=dst_ap, in_=psM[:, 0:nrows])

        load_rows_T(ng[:, 0:96], norm_g[0:96, :], 96)
        load_rows_T(ng[:, 96:192], norm_g[96:192, :], 96)
        for i in range(3):
            load_rows_T(bada[:, 96 * i:96 * (i + 1)], b_ada[96 * i:96 * (i + 1), :], 96)
        load_rows_T(cc[:, :], c_ctx, 8)
        b.op("act", "activation", out=scond[:, :, 0], in_=cc[:, :], func=AF.Silu)
        b.dma("sp", sink_bc[:], a_sink.broadcast_to([128, 16]))

        b.dma("sp", x[:], xT_in.rearrange("(c p) t -> p c t", p=128))

        def ada_layer(i, sc=None):
            sc = scond if sc is None else sc
            for pc in range(18):
                sl = wslot()
                v = sl[:, 0:KC * 512].rearrange("p (k n) -> p k n", k=KC)
                b.dma("pool", v, w_ada[i, :, pc * 512:(pc + 1) * 512].rearrange("(k p) n -> p k n", p=128))
                for cj in range(4):
                    j = pc * 4 + cj
                    for k in range(KC):
                        b.op("pe", "matmul", accum=(k > 0), out=psM[:, j:j + 1],
                             lhsT=v[:, k, cj * 128:(cj + 1) * 128], rhs=sc[:, k, :],
                             start=(k == 0), stop=(k == KC - 1))
            b.op("dve", "tensor_tensor", out=mods[i][:, :], in0=psM[:, 0:72], in1=bada[:, 72 * i:72 * (i + 1)],
                 op=ALU.add)
            for s in range(3):
                w = 1.0 if s == 1 else 0.5
                col = (i * 3 + s) * KC
                gpre = ng[:, (i * 6 + 2 * s) * KC:(i * 6 + 2 * s + 1) * KC]
                gpost = ng[:, (i * 6 + 2 * s + 1) * KC:(i * 6 + 2 * s + 2) * KC]
                scale_ = mods[i][:, (3 * s + 1) * KC:(3 * s + 2) * KC]
                gate_ = mods[i][:, (3 * s + 2) * KC:(3 * s + 3) * KC]
                b.op("dve", "scalar_tensor_tensor", out=Acoef[:, col:col + KC], in0=scale_, scalar=1.0, in1=gpre,
                     op0=ALU.add, op1=ALU.mult)
                b.op("dve", "scalar_tensor_tensor", out=Gcoef[:, col:col + KC], in0=gate_, scalar=w, in1=gpost,
                     op0=ALU.mult, op1=ALU.mult)

        def rms_stats(src):
            for c in range(KC):
                b.op("act", "activation", out=sq[:, c, :], in_=src[:, c, :], func=AF.Square)
            for c in range(KC):
                b.op("pe", "matmul", accum=(c > 0), out=psS[:, 0:T], lhsT=ones_bf[:, :], rhs=sq[:, c, :],
                     start=(c == 0), stop=(c == KC - 1))
            b.op("act", "activation", out=rstd[:, :], in_=psS[:, 0:T], func=AF.Sqrt, scale=1.0 / D, bias=EPS)
            b.op("dve", "reciprocal", out=rstd[:, :], in_=rstd[:, :])

        def pre_norm(i, s):
            rms_stats(x)
            col = (i * 3 + s) * KC
            for c in range(KC):
                t = tmpA[c % 2]
                b.op("dve", "tensor_tensor", out=t[:, :], in0=x[:, c, :], in1=rstd[:, :], op=ALU.mult)
                b.op("act", "activation", out=h[:, c, :], in_=t[:, :], func=AF.Identity,
                     scale=Acoef[:, col + c:col + c + 1], bias=mods[i][:, 3 * s * KC + c:3 * s * KC + c + 1])

        def post_norm_residual(i, s):
            rms_stats(y)
            col = (i * 3 + s) * KC
            for c in range(KC):
                t = tmpA[c % 2]
                b.op("dve", "tensor_tensor", out=t[:, :], in0=y[:, c, :], in1=rstd[:, :], op=ALU.mult)
                b.op("dve", "tensor_scalar", out=t[:, :], in0=t[:, :], scalar1=Gcoef[:, col + c:col + c + 1],
                     scalar2=None, op0=ALU.mult)
                b.op("dve", "tensor_tensor", out=x[:, c, :], in0=t[:, :], in1=x[:, c, :], op=ALU.add)

        def ffn(wgu, wd):
            for jp in range(FC // 2):
                sl = wslot()
                v = sl[:, 0:KC * 512].rearrange("p (k n) -> p k n", k=KC)
                b.dma("pool", v[:, :, 0:256], wgu[:, jp * 256:(jp + 1) * 256].rearrange("(k p) n -> p k n", p=128))
                b.dma("pool", v[:, :, 256:512],
                      wgu[:, DFF + jp * 256:DFF + (jp + 1) * 256].rearrange("(k p) n -> p k n", p=128))
                for jj in range(2):
                    j = jp * 2 + jj
                    pg, pu = psG[j % 2], psU[j % 2]
                    for k in range(KC):
                        b.op("pe", "matmul", accum=(k > 0), out=pg[:, 0:T], lhsT=v[:, k, jj * 128:(jj + 1) * 128],
                             rhs=h[:, k, :], start=(k == 0), stop=(k == KC - 1))
                    for k in range(KC):
                        b.op("pe", "matmul", accum=(k > 0), out=pu[:, 0:T],
                             lhsT=v[:, k, 256 + jj * 128:256 + (jj + 1) * 128],
                             rhs=h[:, k, :], start=(k == 0), stop=(k == KC - 1))
                    t = tmpA[j % 2]
                    b.op("act", "activation", out=t[:, :], in_=pg[:, 0:T], func=AF.Silu)
                    b.op("dve", "tensor_tensor", out=act[:, j, :], in0=t[:, :], in1=pu[:, 0:T], op=ALU.mult)
            for op_ in range(4):
                sl = wslot()
                v = sl[:, 0:FC * 256].rearrange("p (k n) -> p k n", k=FC)
                b.dma("pool", v, wd[:, op_ * 256:(op_ + 1) * 256].rearrange("(k p) n -> p k n", p=128))
                for oo in range(2):
                    o = op_ * 2 + oo
                    py = psY[o % 2]
                    for j in range(FC):
                        b.op("pe", "matmul", accum=(j > 0), out=py[:, 0:T], lhsT=v[:, j, oo * 128:(oo + 1) * 128],
                             rhs=act[:, j, :], start=(j == 0), stop=(j == FC - 1))
                    b.op("act", "activation", out=y[:, o, :], in_=py[:, 0:T], func=AF.Copy)

        def proj_featmajor(dst, dst_chunk0, wsrc, col0, nchunks, src_act, dup64=False):
            for g0 in range(0, nchunks, 4):
                ng_ = min(4, nchunks - g0)
                sl = wslot()
                v = sl[:, 0:KC * 512].rearrange("p (k n) -> p k n", k=KC)
                if dup64:
                    sl0 = wslot()
                    v0 = sl0[:, 0:KC * 256].rearrange("p (k n) -> p k n", k=KC)
                    b.dma("pool", v0[:, :, 0:ng_ * 64],
                          wsrc[:, col0 + g0 * 64:col0 + (g0 + ng_) * 64].rearrange("(k p) n -> p k n", p=128))
                    v4 = sl[:, 0:KC * 512].rearrange("p (k m a d) -> p k m a d", k=KC, m=4, a=2)
                    for a in range(2):
                        b.op("dve", "tensor_copy", out=v4[:, :, 0:ng_, a, :],
                             in_=v0[:, :, 0:ng_ * 64].rearrange("p k (m d) -> p k m d", d=64))
                else:
                    b.dma("pool", v[:, :, 0:ng_ * 128],
                          wsrc[:, col0 + g0 * 128:col0 + (g0 + ng_) * 128].rearrange("(k p) n -> p k n", p=128))
                for m in range(ng_):
                    pq = psG[m % 2]
                    for k in range(KC):
                        b.op("pe", "matmul", accum=(k > 0), out=pq[:, 0:T], lhsT=v[:, k, m * 128:(m + 1) * 128],
                             rhs=src_act[:, k, :], start=(k == 0), stop=(k == KC - 1))
                    b.op("act", "activation", out=dst[:, dst_chunk0 + g0 + m, :], in_=pq[:, 0:T], func=AF.Copy)

        def proj_out_featmajor(wsrc, src_act):
            for g0 in range(0, KC, 4):
                sl = wslot()
                v = sl[:, 0:KC * 512].rearrange("p (k n) -> p k n", k=KC)
                b.dma("pool", v, wsrc[:, g0 * 128:(g0 + 4) * 128].rearrange("(k p) n -> p k n", p=128))
                for m in range(4):
                    py = psY[m % 2]
                    for k in range(KC):
                        b.op("pe", "matmul", accum=(k > 0), out=py[:, 0:T], lhsT=v[:, k, m * 128:(m + 1) * 128],
                             rhs=src_act[:, k, :], start=(k == 0), stop=(k == KC - 1))
                    b.op("act", "activation", out=y[:, g0 + m, :], in_=py[:, 0:T], func=AF.Copy)

        def proj_tokmajor(wsrc, col0, ncols, out_dram, vdst_col0=None):
            for c0 in range(0, ncols, 512):
                n = min(512, ncols - c0)
                sl = wslot()
                v = sl[:, 0:KC * 512].rearrange("p (k n) -> p k n", k=KC)
                b.dma("pool", v[:, :, 0:n], wsrc[:, col0 + c0:col0 + c0 + n].rearrange("(k p) n -> p k n", p=128))
                for tb in range(T // 128):
                    pq = psU[tb % 2]
                    for k in range(KC):
                        b.op("pe", "matmul", accum=(k > 0), out=pq[:, 0:n], lhsT=h[:, k, tb * 128:(tb + 1) * 128],
                             rhs=v[:, k, 0:n], start=(k == 0), stop=(k == KC - 1))
                    t32 = kv32[tb % 2]
                    b.op("dve", "tensor_copy", out=t32[:, 0:n], in_=pq[:, 0:n])
                    if out_dram is not None:
                        b.dma("sp", out_dram[tb * 128:(tb + 1) * 128, c0:c0 + n], t32[:, 0:n])
                    if vdst_col0 is not None:
                        b.op("act", "activation", out=vtok[:, tb, vdst_col0 + c0:vdst_col0 + c0 + n], in_=t32[:, 0:n],
                             func=AF.Copy)

        def attn_ctx(nheads, head_map, sink_cols, scale):
            nseq = T // SEQ
            nqb = SEQ // 128
            it = 0
            for s in range(nseq):
                for qb in range(nqb):
                    q0 = s * SEQ + qb * 128
                    for hh in range(nheads):
                        qc, base, kc, vc0 = head_map[hh]
                        pS = psG[it % 2]
                        b.op("pe", "matmul", out=pS[:, 0:SEQ], lhsT=qT[base:base + 64, qc, q0:q0 + 128],
                             rhs=kT[base:base + 64, kc, s * SEQ:(s + 1) * SEQ], start=True, stop=True)
                        sm = st_m[it % 2]
                        b.op("dve", "reduce_max", out=sm[:, 0:1], in_=pS[:, 0:SEQ], axis=AX.X)
                        if sink_cols is not None:
                            b.op("dve", "tensor_scalar", out=sm[:, 1:2], in0=sm[:, 0:1], scalar1=scale,
                                 scalar2=sink_bc[:, hh:hh + 1], op0=ALU.mult, op1=ALU.max)
                            b.op("dve", "tensor_scalar", out=sm[:, 1:2], in0=sm[:, 1:2], scalar1=-1.0, scalar2=None,
                                 op0=ALU.mult)
                        else:
                            b.op("dve", "tensor_scalar", out=sm[:, 1:2], in0=sm[:, 0:1], scalar1=-scale, scalar2=None,
                                 op0=ALU.mult)
                        P = Pm[it % 2]
                        b.op("act", "activation", out=P[:, 0:SEQ], in_=pS[:, 0:SEQ], func=AF.Exp, scale=scale,
                             bias=sm[:, 1:2], accum_out=sm[:, 2:3])
                        if sink_cols is not None:
                            b.op("act", "activation", out=sm[:, 3:4], in_=sink_bc[:, hh:hh + 1], func=AF.Exp,
                                 bias=sm[:, 1:2], scale=1.0)
                            b.op("dve", "tensor_tensor", out=sm[:, 4:5], in0=sm[:, 2:3], in1=sm[:, 3:4], op=ALU.add)
                            b.op("dve", "reciprocal", out=sm[:, 5:6], in_=sm[:, 4:5])
                        else:
                            b.op("dve", "reciprocal", out=sm[:, 5:6], in_=sm[:, 2:3])
                        pt_sb = PT[it % 2]
                        if SUB < 3:
                            it += 1
                            continue
                        for kb in range(nqb):
                            b.op("pe", "transpose", out=psTb[it % 2][:, kb, :], in_=P[:, kb * 128:(kb + 1) * 128],
                                 identity=ident_bf[:, :])
                        b.op("dve", "tensor_copy", out=pt_sb[:, :, :], in_=psTb[it % 2][:, :, :])
                        if SUB < 4:
                            it += 1
                            continue
                        pO = psY[it % 2]
                        for kb in range(nqb):
                            b.op("pe", "matmul", accum=(kb > 0), out=pO[:, 0:64], lhsT=pt_sb[:, kb, :],
                                 rhs=vtok[:, s * nqb + kb, vc0:vc0 + 64], start=(kb == 0), stop=(kb == nqb - 1))
                        b.op("dve", "tensor_scalar", out=otok[:, hh * 64:(hh + 1) * 64], in0=pO[:, 0:64],
                             scalar1=sm[:, 5:6], scalar2=None, op0=ALU.mult)
                        it += 1
                    for c in range(KC):
                        b.op("pe", "transpose", out=psTb[c % 2][:, 0, :], in_=otok[:, c * 128:(c + 1) * 128],
                             identity=ident_bf[:, :])
                        b.op("act", "activation", out=oT[:, c, q0:q0 + 128], in_=psTb[c % 2][:, 0, :], func=AF.Copy)


        def s5_setup(npow=8, pfx="s5p_"):
            P = {}
            def t64(name):
                P[name] = b.sb(pfx + name, [128, 64], F32)
                return P[name]
            for nm, src in (("lamre", s5_lam_re), ("lamim", s5_lam_im), ("logdt", s5_logdt)):
                t64(nm)
                load_rows_T(P[nm][:, :], src, 64)
            for nm in ("dt", "lr", "li", "mag", "sn", "cs", "ar", "ai", "fr", "fi", "t1", "t2", "t3", "kk"):
                t64(nm)
            dsk = b.sb(pfx + "dskip", [128, KC], F32)
            load_rows_T(dsk[:, :], s5_d, 8)
            P["dskip"] = dsk
            b.op("act", "activation", out=P["dt"][:, :], in_=P["logdt"][:, :], func=AF.Exp)
            b.op("dve", "tensor_tensor", out=P["lr"][:, :], in0=P["lamre"][:, :], in1=P["dt"][:, :], op=ALU.mult)
            b.op("dve", "tensor_tensor", out=P["li"][:, :], in0=P["lamim"][:, :], in1=P["dt"][:, :], op=ALU.mult)
            b.op("act", "activation", out=P["mag"][:, :], in_=P["lr"][:, :], func=AF.Exp)

            def sin_of(dst, ang, shift):
                b.op("dve", "tensor_scalar", out=P["t1"][:, :], in0=ang, scalar1=float(shift), scalar2=None, op0=ALU.add)
                b.raw("dve", lambda: nc.vector.memset(P["kk"][:, :], 0.0), writes=[P["kk"][:, :]])
                for j in range(1, 5):
                    b.op("dve", "tensor_scalar", out=P["t2"][:, :], in0=P["t1"][:, :],
                         scalar1=float((2 * j - 1) * np.pi), scalar2=None, op0=ALU.is_gt)
                    b.op("dve", "tensor_tensor", out=P["kk"][:, :], in0=P["kk"][:, :], in1=P["t2"][:, :], op=ALU.add)
                b.op("dve", "tensor_scalar", out=P["kk"][:, :], in0=P["kk"][:, :], scalar1=float(-2 * np.pi),
                     scalar2=None, op0=ALU.mult)
                b.op("dve", "tensor_tensor", out=P["t1"][:, :], in0=P["t1"][:, :], in1=P["kk"][:, :], op=ALU.add)
                b.op("act", "activation", out=dst, in_=P["t1"][:, :], func=AF.Sin)

            sin_of(P["sn"][:, :], P["li"][:, :], 0.0)
            sin_of(P["cs"][:, :], P["li"][:, :], np.pi / 2)
            b.op("dve", "tensor_tensor", out=P["ar"][:, :], in0=P["mag"][:, :], in1=P["cs"][:, :], op=ALU.mult)
            b.op("dve", "tensor_tensor", out=P["ai"][:, :], in0=P["mag"][:, :], in1=P["sn"][:, :], op=ALU.mult)
            TT = lambda o, a_, b_, op: b.op("dve", "tensor_tensor", out=o, in0=a_, in1=b_, op=op)
            t1, t2, t3 = P["t1"][:, :], P["t2"][:, :], P["t3"][:, :]
            TT(t1, P["lamre"][:, :], P["lamre"][:, :], ALU.mult)
            TT(t2, P["lamim"][:, :], P["lamim"][:, :], ALU.mult)
            TT(t1, t1, t2, ALU.add)
            b.op("dve", "reciprocal", out=t3, in_=t1)
            b.op("dve", "tensor_scalar", out=P["kk"][:, :], in0=P["ar"][:, :], scalar1=-1.0, scalar2=None, op0=ALU.add)
            TT(t1, P["kk"][:, :], P["lamre"][:, :], ALU.mult)
            TT(t2, P["ai"][:, :], P["lamim"][:, :], ALU.mult)
            TT(t1, t1, t2, ALU.add)
            TT(P["fr"][:, :], t1, t3, ALU.mult)
            TT(t1, P["ai"][:, :], P["lamre"][:, :], ALU.mult)
            TT(t2, P["kk"][:, :], P["lamim"][:, :], ALU.mult)
            TT(t1, t1, t2, ALU.subtract)
            TT(P["fi"][:, :], t1, t3, ALU.mult)
            pw = [(P["ar"], P["ai"])]
            for k in range(1, npow):
                pr, pi_ = pw[-1]
                nr = b.sb(f"{pfx}pr{k}", [128, 64], F32)
                ni = b.sb(f"{pfx}pi{k}", [128, 64], F32)
                TT(t1, pr[:, :], pr[:, :], ALU.mult)
                TT(t2, pi_[:, :], pi_[:, :], ALU.mult)
                TT(nr[:, :], t1, t2, ALU.subtract)
                TT(t1, pr[:, :], pi_[:, :], ALU.mult)
                b.op("dve", "tensor_scalar", out=ni[:, :], in0=t1, scalar1=2.0, scalar2=None, op0=ALU.mult)
                pw.append((nr, ni))
            P["pw"] = pw
            return P

        def s5_mixer(P):
            nat = {}
            for t4 in range(4):
                for nm in ("bre", "bim", "cre", "cim"):
                    tl = b.sb(f"s5n_{nm}{t4}", [128, 128], F32)
                    b.raw("dve", lambda tl=tl: nc.vector.memset(tl[:, :], 0.0), writes=[tl[:, :]])
                    nat[(nm, t4)] = tl
            wB = [[b.sb(f"s5_wB{ri}{i}", [128, 128], BF16) for i in range(2)] for ri in range(2)]
            wC = [[b.sb(f"s5_wC{ri}{i}", [128, 128], BF16) for i in range(2)] for ri in range(2)]
            c32 = [b.sb(f"s5_c32{i}", [128, 128], F32) for i in range(2)]
            cm = [b.sb(f"s5_cm{i}", [128, 128], F32) for i in range(4)]
            xr = [b.sb(f"s5_xr{i}", [128, 2, SEQ], F32) for i in range(1)] * 2
            xi = [b.sb(f"s5_xi{i}", [128, 2, SEQ], F32) for i in range(1)] * 2
            xrb = [b.sb(f"s5_xrb{i}", [128, T], BF16) for i in range(1)] * 2
            xib = [b.sb(f"s5_xib{i}", [128, T], BF16) for i in range(1)] * 2
            tm = [b.sb(f"s5_tm{i}", [128, 2, SEQ], F32) for i in range(4)]
            fin_r = b.sb("s5_finr", [128, 64, 2], F32)
            fin_i = b.sb("s5_fini", [128, 64, 2], F32)
            fin_o = b.sb("s5_fino", [128, 128], F32)
            it = 0
            for d in range(2):
                for t in range(32):
                    tau = d * 32 + t
                    ch, t4 = t // 4, t % 4
                    pp = it % 2
                    for g2 in range(2):
                        g = 2 * t + g2
                        cb = (2 * t4 + g2) * 16
                        b.dma("sp", nat[("bre", t4)][g2 * 64:(g2 + 1) * 64, cb:cb + 16], s5_b_re[d, g, :, :])
                        b.dma("sp", nat[("bim", t4)][g2 * 64:(g2 + 1) * 64, cb:cb + 16], s5_b_im[d, g, :, :])
                        b.dma("sp", nat[("cre", t4)][cb:cb + 16, g2 * 64:(g2 + 1) * 64], s5_c_re[d, g, :, :])
                        b.dma("sp", nat[("cim", t4)][cb:cb + 16, g2 * 64:(g2 + 1) * 64], s5_c_im[d, g, :, :])
                    for ri, nm in enumerate(("bre", "bim")):
                        b.op("pe", "transpose", out=psS[:, 0:128], in_=nat[(nm, t4)][:, :], identity=ident[:, :])
                        b.op("act", "activation", out=wB[ri][pp][:, :], in_=psS[:, 0:128], func=AF.Copy)
                    for ri, nm in enumerate(("cre", "cim")):
                        b.op("pe", "transpose", out=psS[:, 0:128], in_=nat[(nm, t4)][:, :], identity=ident[:, :])
                        b.op("act", "activation", out=c32[ri][:, :], in_=psS[:, 0:128], func=AF.Copy)
                    fr_, fi_ = P["fr"][:, tau:tau + 1], P["fi"][:, tau:tau + 1]
                    TS = lambda o, i_, sc: b.op("dve", "tensor_scalar", out=o, in0=i_, scalar1=sc, scalar2=None, op0=ALU.mult)
                    TS(cm[0][:, :], c32[0][:, :], fr_)
                    TS(cm[1][:, :], c32[1][:, :], fi_)
                    b.op("dve", "tensor_tensor", out=wC[0][pp][:, :], in0=cm[0][:, :], in1=cm[1][:, :], op=ALU.subtract)
                    TS(cm[2][:, :], c32[0][:, :], fi_)
                    TS(cm[3][:, :], c32[1][:, :], fr_)
                    b.op("dve", "tensor_tensor", out=cm[2][:, :], in0=cm[2][:, :], in1=cm[3][:, :], op=ALU.add)
                    b.op("dve", "tensor_scalar", out=wC[1][pp][:, :], in0=cm[2][:, :], scalar1=-1.0, scalar2=None, op0=ALU.mult)
                    X = (xr[pp], xi[pp])
                    for ri in range(2):
                        pq = psG[ri]
                        b.op("pe", "matmul", out=pq[:, 0:T], lhsT=wB[ri][pp][:, :], rhs=h[:, ch, :], start=True, stop=True)
                        b.op("act", "activation", out=X[ri][:, :, :].rearrange("p a n -> p (a n)"), in_=pq[:, 0:T], func=AF.Copy)
                    for k in range(8):
                        sh = 1 << k
                        pr, pi_ = P["pw"][k]
                        sr_, si_ = pr[:, tau:tau + 1], pi_[:, tau:tau + 1]
                        if d == 0:
                            src = lambda A: A[:, :, 0:SEQ - sh]
                            dst = lambda A: A[:, :, sh:SEQ]
                        else:
                            src = lambda A: A[:, :, sh:SEQ]
                            dst = lambda A: A[:, :, 0:SEQ - sh]
                        tt = [tm[j] for j in range(4)]
                        n = SEQ - sh
                        MUL = lambda o, i_, sc: b.op("act", "activation", out=o, in_=i_, func=AF.Identity, scale=sc)
                        MUL(tt[0][:, :, 0:n], src(X[0]), sr_)
                        MUL(tt[1][:, :, 0:n], src(X[1]), si_)
                        MUL(tt[2][:, :, 0:n], src(X[1]), sr_)
                        MUL(tt[3][:, :, 0:n], src(X[0]), si_)
                        b.op("dve", "tensor_tensor", out=dst(X[0]), in0=dst(X[0]), in1=tt[0][:, :, 0:n], op=ALU.add)
                        b.op("dve", "tensor_tensor", out=dst(X[0]), in0=dst(X[0]), in1=tt[1][:, :, 0:n], op=ALU.subtract)
                        b.op("dve", "tensor_tensor", out=dst(X[1]), in0=dst(X[1]), in1=tt[2][:, :, 0:n], op=ALU.add)
                        b.op("dve", "tensor_tensor", out=dst(X[1]), in0=dst(X[1]), in1=tt[3][:, :, 0:n], op=ALU.add)
                    e_ = SEQ - 1 if d == 0 else 0
                    b.op("dve", "tensor_copy", out=fin_r[:, tau, :], in_=X[0][:, :, e_])
                    b.op("dve", "tensor_copy", out=fin_i[:, tau, :], in_=X[1][:, :, e_])
                    b.op("act", "activation", out=xrb[pp][:, :], in_=X[0][:, :, :].rearrange("p a n -> p (a n)"), func=AF.Copy)
                    b.op("act", "activation", out=xib[pp][:, :], in_=X[1][:, :, :].rearrange("p a n -> p (a n)"), func=AF.Copy)
                    py = psY[it % 2]
                    b.op("pe", "matmul", out=py[:, 0:T], lhsT=wC[0][pp][:, :], rhs=xrb[pp][:, :], start=True, stop=False)
                    b.op("pe", "matmul", accum=True, out=py[:, 0:T], lhsT=wC[1][pp][:, :], rhs=xib[pp][:, :], start=False, stop=True)
                    if d == 0 and t4 == 0:
                        b.op("dve", "tensor_copy", out=y[:, ch, :], in_=py[:, 0:T])
                    else:
                        b.op("dve", "tensor_tensor", out=y[:, ch, :], in0=y[:, ch, :], in1=py[:, 0:T], op=ALU.add)
                    it += 1
            FR = P["fr"][:, :].unsqueeze(2).to_broadcast([128, 64, 2]) if False else None
            for sq_ in range(2):
                a_, b_ = tm[0][:, 0, 0:64], tm[1][:, 0, 0:64]
                TT = lambda o, x_, y_, op: b.op("dve", "tensor_tensor", out=o, in0=x_, in1=y_, op=op)
                TT(a_, fin_r[:, :, sq_], P["fr"][:, :], ALU.mult)
                TT(b_, fin_i[:, :, sq_], P["fi"][:, :], ALU.mult)
                TT(tm[2][:, 0, sq_ * 64:(sq_ + 1) * 64], a_, b_, ALU.subtract)
                TT(a_, fin_i[:, :, sq_], P["fr"][:, :], ALU.mult)
                TT(b_, fin_r[:, :, sq_], P["fi"][:, :], ALU.mult)
                TT(tm[3][:, 0, sq_ * 64:(sq_ + 1) * 64], a_, b_, ALU.add)
            for src_t, dst_d in ((tm[2], s5re_out), (tm[3], s5im_out)):
                b.op("pe", "transpose", out=psS[:, 0:128], in_=src_t[:, 0, 0:128], identity=ident[:, :])
                b.op("dve", "tensor_copy", out=fin_o[:, :], in_=psS[:, 0:128])
                b.dma("sp", dst_d, fin_o[:, :])
            g_bf = act
            for c in range(KC):
                t_a, t_b = tmpA[0], tmpA[1]
                b.op("dve", "tensor_scalar", out=t_a[:, :], in0=h[:, c, :], scalar1=P["dskip"][:, c:c + 1], scalar2=None, op0=ALU.mult)
                b.op("dve", "tensor_tensor", out=y[:, c, :], in0=y[:, c, :], in1=t_a[:, :], op=ALU.add)
                b.op("act", "activation", out=t_a[:, :], in_=y[:, c, :], func=AF.Square)
                b.op("dve", "tensor_scalar", out=t_a[:, :], in0=t_a[:, :], scalar1=0.044715, scalar2=1.0, op0=ALU.mult, op1=ALU.add)
                b.op("dve", "tensor_tensor", out=t_a[:, :], in0=t_a[:, :], in1=y[:, c, :], op=ALU.mult)
                b.op("act", "activation", out=t_b[:, :], in_=t_a[:, :], func=AF.Tanh, scale=0.7978845608028654)
                b.op("dve", "tensor_scalar", out=t_b[:, :], in0=t_b[:, :], scalar1=1.0, scalar2=0.5, op0=ALU.add, op1=ALU.mult)
                b.op("dve", "tensor_tensor", out=g_bf[:, c, :], in0=t_b[:, :], in1=y[:, c, :], op=ALU.mult)
            for o4 in range(0, KC, 2):
                sl = wslot()
                v = sl[:, 0:KC * 512].rearrange("p (k n) -> p k n", k=KC)
                b.dma("pool", v[:, :, 0:256], s5_w_glu[:, o4 * 128:(o4 + 2) * 128].rearrange("(k p) n -> p k n", p=128))
                b.dma("pool", v[:, :, 256:512], s5_w_glu[:, D + o4 * 128:D + (o4 + 2) * 128].rearrange("(k p) n -> p k n", p=128))
                for oo in range(2):
                    o = o4 + oo
                    pa, pg = psG[o % 2], psU[o % 2]
                    for k in range(KC):
                        b.op("pe", "matmul", accum=(k > 0), out=pa[:, 0:T], lhsT=v[:, k, oo * 128:(oo + 1) * 128],
                             rhs=g_bf[:, k, :], start=(k == 0), stop=(k == KC - 1))
                    for k in range(KC):
                        b.op("pe", "matmul", accum=(k > 0), out=pg[:, 0:T], lhsT=v[:, k, 256 + oo * 128:256 + (oo + 1) * 128],
                             rhs=g_bf[:, k, :], start=(k == 0), stop=(k == KC - 1))
                    tq = tmpA[o % 2]
                    b.op("act", "activation", out=tq[:, :], in_=pg[:, 0:T], func=AF.Sigmoid)
                    b.op("dve", "tensor_tensor", out=y[:, o, :], in0=tq[:, :], in1=pa[:, 0:T], op=ALU.mult)


        def dn_mixer():
            sbf = lambda n_, sh, dt_=F32: b.sb("dn_" + n_, sh, dt_)
            msk = sbf("msk", [64, 6, 64]); b.dma("sp", msk[:], dn_mask)
            cw = sbf("cw", [128, 80]); load_rows_T(cw[:, :], dn_conv_w, 80)
            ones_f = sbf("ones", [64, 128]); b.raw("dve", lambda: nc.vector.memset(ones_f[:, :], 1.0), writes=[ones_f[:, :]])
            wba = sbf("wba", [128, 2, KC, 16], BF16)
            for d in range(2):
                b.dma("pool", wba[:, d, :, :], dn_w_ba[d].rearrange("(k p) n -> p k n", p=128))
            alog = sbf("alog", [64, 16]); b.dma("sp", alog[:], dn_a_log.broadcast_to([64, 16]))
            dtb = sbf("dtb", [64, 16]); b.dma("sp", dtb[:], dn_dt_bias.broadcast_to([64, 16]))
            outg = sbf("outg", [64, 128]); b.dma("sp", outg[:], dn_out_g.broadcast_to([64, 128]))
            nexpa = sbf("nexpa", [64, 16])
            b.op("act", "activation", out=nexpa[:, :], in_=alog[:, :], func=AF.Exp)
            b.op("dve", "tensor_scalar", out=nexpa[:, :], in0=nexpa[:, :], scalar1=-1.0, scalar2=None, op0=ALU.mult)
            cin = sbf("cin", [128, 2, SEQ + 4]); b.raw("dve", lambda: nc.vector.memset(cin[:, :, :], 0.0), writes=[cin[:, :, :]])
            qn = sbf("qn", [128, 4, T], BF16); kn = sbf("kn", [128, 4, T], BF16)
            kt_tok = sbf("kt", [64, 8, 4, 128], BF16); vt_tok = sbf("vt", [64, 8, 8, 128], BF16)
            zs = sbf("zs", [128, 8, T], BF16); vtmp = sbf("vtmp", [128, T], BF16)
            acc3 = tmpA[0][:, :].rearrange("p (a n) -> p a n", a=2)
            tm3 = tmpA[1][:, :].rearrange("p (a n) -> p a n", a=2)
            for m0 in range(0, 24, 4):
                sl = wslot()
                v = sl[:, 0:KC * 512].rearrange("p (k n) -> p k n", k=KC)
                b.dma("pool", v, dn_w_in[:, m0 * 128:(m0 + 4) * 128].rearrange("(k p) n -> p k n", p=128))
                for mm in range(4):
                    m = m0 + mm
                    pq = psG[mm % 2]
                    for k in range(KC):
                        b.op("pe", "matmul", accum=(k > 0), out=pq[:, 0:T], lhsT=v[:, k, mm * 128:(mm + 1) * 128],
                             rhs=h[:, k, :], start=(k == 0), stop=(k == KC - 1))
                    if m >= 16:
                        b.op("act", "activation", out=zs[:, m - 16, :], in_=pq[:, 0:T], func=AF.Silu)
                        continue
                    b.op("act", "activation", out=cin[:, :, 2:2 + SEQ], in_=pq[:, 0:T].rearrange("p (a n) -> p a n", a=2),
                         func=AF.Copy)
                    for j in range(5):
                        dst = acc3 if j == 0 else tm3
                        b.op("act", "activation", out=dst, in_=cin[:, :, j:j + SEQ], func=AF.Identity,
                             scale=cw[:, j * 16 + m:j * 16 + m + 1])
                        if j > 0:
                            b.op("dve", "tensor_tensor", out=acc3, in0=acc3, in1=tm3, op=ALU.add)
                    if m < 8:
                        b.op("act", "activation", out=tmpA[1][:, :], in_=tmpA[0][:, :], func=AF.Silu)
                        b.op("act", "activation", out=act[:, 0, :], in_=tmpA[1][:, :], func=AF.Square)
                        b.op("pe", "matmul", out=psS[:, 0:T], lhsT=ones_bf[:, :], rhs=act[:, 0, :], start=True, stop=True)
                        b.op("act", "activation", out=rstd[:, :], in_=psS[:, 0:T], func=AF.Sqrt, scale=1.0, bias=EPS)
                        b.op("dve", "reciprocal", out=rstd[:, :], in_=rstd[:, :])
                        b.op("dve", "tensor_tensor", out=tmpA[1][:, :], in0=tmpA[1][:, :], in1=rstd[:, :], op=ALU.mult)
                        if m < 4:
                            b.op("act", "activation", out=qn[:, m, :], in_=tmpA[1][:, :], func=AF.Identity, scale=128 ** -0.5)
                        else:
                            b.op("act", "activation", out=kn[:, m - 4, :], in_=tmpA[1][:, :], func=AF.Identity, scale=1.0)
                            for blk in range(8):
                                b.op("pe", "transpose", out=psTb[blk % 2][0:64, 0, :], in_=kn[:, m - 4, blk * 64:(blk + 1) * 64],
                                     identity=ident_bf[:, :])
                                b.op("dve", "tensor_copy", out=kt_tok[:, blk, m - 4, :], in_=psTb[blk % 2][0:64, 0, :])
                    else:
                        b.op("act", "activation", out=vtmp[:, :], in_=tmpA[0][:, :], func=AF.Silu)
                        for blk in range(8):
                            b.op("pe", "transpose", out=psTb[blk % 2][0:64, 0, :], in_=vtmp[:, blk * 64:(blk + 1) * 64],
                                 identity=ident_bf[:, :])
                            b.op("dve", "tensor_copy", out=vt_tok[:, blk, m - 8, :], in_=psTb[blk % 2][0:64, 0, :])
            gts = sbf("gts", [64, 2, 8, 16]); beta = sbf("beta", [64, 2, 8, 8]); xa = sbf("xa", [64, 2, 8, 8])
            xe = sbf("xe", [64, 2, 8, 8]); gg = sbf("gg", [64, 2, 8, 8]); gc = sbf("gc", [64, 2, 8, 8]); eg = sbf("eg", [64, 2, 8, 8])
            for d in range(2):
                for blk in range(8):
                    pq = psU[blk % 2]
                    for k in range(KC):
                        b.op("pe", "matmul", accum=(k > 0), out=pq[0:64, 0:16], lhsT=h[:, k, blk * 64:(blk + 1) * 64],
                             rhs=wba[:, d, k, :], start=(k == 0), stop=(k == KC - 1))
                    b.op("dve", "tensor_copy", out=gts[:, d, blk, :], in_=pq[0:64, 0:16])
                    b.op("act", "activation", out=beta[:, d, blk, :], in_=gts[:, d, blk, 0:8], func=AF.Sigmoid)
                    b.op("dve", "tensor_tensor", out=xa[:, d, blk, :], in0=gts[:, d, blk, 8:16], in1=dtb[:, d * 8:(d + 1) * 8], op=ALU.add)
            fl = lambda A: A[:, :, :, :].rearrange("p a b c -> p (a b c)")
            b.op("act", "activation", out=fl(xe), in_=fl(xa), func=AF.Abs)
            b.op("act", "activation", out=fl(xe), in_=fl(xe), func=AF.Exp, scale=-1.0)
            b.op("act", "activation", out=fl(xe), in_=fl(xe), func=AF.Ln, bias=1.0, scale=1.0)
            b.op("dve", "tensor_scalar", out=fl(xa), in0=fl(xa), scalar1=0.0, scalar2=None, op0=ALU.max)
            b.op("dve", "tensor_tensor", out=fl(xa), in0=fl(xa), in1=fl(xe), op=ALU.add)
            for d in range(2):
                for blk in range(8):
                    b.op("dve", "tensor_tensor", out=gg[:, d, blk, :], in0=xa[:, d, blk, :], in1=nexpa[:, d * 8:(d + 1) * 8], op=ALU.mult)
                b.op("pe", "matmul", out=psS[0:64, 0:64], lhsT=msk[:, d, :], rhs=gg[:, d, :, :].rearrange("p b c -> p (b c)"),
                     start=True, stop=True)
                b.op("dve", "tensor_copy", out=gc[:, d, :, :].rearrange("p b c -> p (b c)"), in_=psS[0:64, 0:64])
            b.op("act", "activation", out=fl(eg), in_=fl(gc), func=AF.Exp)
            f64 = lambda n_: sbf(n_, [64, 64])
            dg, decS, decT, PTt = f64("dg"), f64("decS"), f64("decT"), f64("PTt")
            Ms = [(f64("Ma"), f64("MTa")), (f64("Mb"), f64("MTb"))]
            gcr = sbf("gcr", [128, 64]); TTb = sbf("TTb", [64, 64], BF16); aT = sbf("aT", [64, 64], BF16)
            vb = sbf("vb", [64, 128], BF16); kbg = sbf("kbg", [64, 128], BF16); kg = sbf("kg", [64, 128], BF16)
            wT = sbf("wT", [128, 64], BF16); u_sb = sbf("u", [64, 128]); vnew = sbf("vnew", [64, 128]); vnew_b = sbf("vnewb", [64, 128], BF16)
            o1 = sbf("o1", [64, 128]); sc1 = sbf("sc1", [64, 4]); egl = sbf("egl", [128, 1])
            S = sbf("S", [128, 128]); Sb = sbf("Sb", [128, 128], BF16)
            o_acc = sbf("oacc", [64, 4, 128]); ss = sbf("ss", [64, 4]); onb = sbf("onb", [64, 128], BF16); junk = sbf("junk", [64, 128])
            id64 = ident[0:64, 0:64]
            for s_ in range(2):
                for hv in range(8):
                    hq = hv // 2
                    for d in range(2):
                        b.raw("dve", lambda: nc.vector.memset(S[:, :], 0.0), writes=[S[:, :]])
                        b.raw("dve", lambda: nc.vector.memset(Sb[:, :], 0.0), writes=[Sb[:, :]])
                        last = 63 if d == 0 else 0
                        for n in (range(4) if d == 0 else range(3, -1, -1)):
                            blk = s_ * 4 + n
                            tok0 = blk * 64
                            gc_, be_, eg_ = gc[:, d, blk, hv:hv + 1], beta[:, d, blk, hv:hv + 1], eg[:, d, blk, hv:hv + 1]
                            kf, qf = kn[:, hq, tok0:tok0 + 64], qn[:, hq, tok0:tok0 + 64]
                            kt, vt = kt_tok[:, blk, hq, :], vt_tok[:, blk, hv, :]
                            b.op("dve", "tensor_scalar", out=dg[:, :], in0=id64, scalar1=gc_, scalar2=None, op0=ALU.mult)
                            b.op("pe", "matmul", out=psS[:, 0:64], lhsT=ones_f[:, :], rhs=dg[:, :], start=True, stop=True)
                            b.op("act", "activation", out=gcr[:, :], in_=psS[:, 0:64], func=AF.Copy)
                            b.op("dve", "tensor_scalar", out=decS[:, :], in0=gcr[0:64, :], scalar1=-1.0, scalar2=gc_, op0=ALU.mult, op1=ALU.add)
                            b.op("dve", "tensor_scalar", out=decS[:, :], in0=decS[:, :], scalar1=0.0, scalar2=None, op0=ALU.min)
                            b.op("act", "activation", out=decS[:, :], in_=decS[:, :], func=AF.Exp)
                            b.op("dve", "tensor_tensor", out=decS[:, :], in0=decS[:, :], in1=msk[:, 4 + d, :], op=ALU.mult)
                            b.op("dve", "tensor_scalar", out=decT[:, :], in0=gcr[0:64, :], scalar1=gc_, scalar2=None, op0=ALU.subtract)
                            b.op("dve", "tensor_scalar", out=decT[:, :], in0=decT[:, :], scalar1=0.0, scalar2=None, op0=ALU.min)
                            b.op("act", "activation", out=decT[:, :], in_=decT[:, :], func=AF.Exp)
                            b.op("dve", "tensor_tensor", out=decT[:, :], in0=decT[:, :], in1=msk[:, d, :], op=ALU.mult)
                            M, MT = Ms[0]
                            b.op("pe", "matmul", out=psG[0][0:64, 0:64], lhsT=kf, rhs=kf, start=True, stop=True)
                            b.op("dve", "tensor_scalar", out=M[:, :], in0=psG[0][0:64, 0:64], scalar1=be_, scalar2=-1.0, op0=ALU.mult, op1=ALU.mult)
                            b.op("dve", "tensor_tensor", out=M[:, :], in0=M[:, :], in1=decS[:, :], op=ALU.mult)
                            b.op("pe", "transpose", out=psG[1][0:64, 0:64], in_=M[:, :], identity=id64)
                            b.op("act", "activation", out=MT[:, :], in_=psG[1][0:64, 0:64], func=AF.Copy)
                            b.op("dve", "tensor_tensor", out=PTt[:, :], in0=MT[:, :], in1=id64, op=ALU.add)
                            cur = 0
                            for k in range(1, 6):
                                M, MT = Ms[cur]
                                Mn, MTn = Ms[1 - cur]
                                b.op("pe", "matmul", out=psU[0][0:64, 0:64], lhsT=MT[:, :], rhs=M[:, :], start=True, stop=True)
                                b.op("act", "activation", out=Mn[:, :], in_=psU[0][0:64, 0:64], func=AF.Copy)
                                if k < 5:
                                    b.op("pe", "matmul", out=psU[1][0:64, 0:64], lhsT=M[:, :], rhs=MT[:, :], start=True, stop=True)
                                    b.op("dve", "tensor_copy", out=MTn[:, :], in_=psU[1][0:64, 0:64])
                                b.op("pe", "matmul", out=psY[0][0:64, 0:64], lhsT=Mn[:, :], rhs=PTt[:, :], start=True, stop=True)
                                b.op("dve", "tensor_tensor", out=PTt[:, :], in0=PTt[:, :], in1=psY[0][0:64, 0:64], op=ALU.add)
                                cur = 1 - cur
                            b.op("act", "activation", out=TTb[:, :], in_=PTt[:, :], func=AF.Copy)
                            b.op("dve", "tensor_tensor", out=sc1[:, 0:1], in0=be_, in1=eg_, op=ALU.mult)
                            b.op("dve", "tensor_scalar", out=vb[:, :], in0=vt, scalar1=be_, scalar2=None, op0=ALU.mult)
                            b.op("dve", "tensor_scalar", out=kbg[:, :], in0=kt, scalar1=sc1[:, 0:1], scalar2=None, op0=ALU.mult)
                            b.op("pe", "matmul", out=psG[0][0:64, 0:128], lhsT=TTb[:, :], rhs=vb[:, :], start=True, stop=True)
                            b.op("act", "activation", out=u_sb[:, :], in_=psG[0][0:64, 0:128], func=AF.Copy)
                            b.op("pe", "matmul", out=psG[1][:, 0:64], lhsT=kbg[:, :], rhs=TTb[:, :], start=True, stop=True)
                            b.op("dve", "tensor_copy", out=wT[:, :], in_=psG[1][:, 0:64])
                            b.op("pe", "matmul", out=psU[0][0:64, 0:128], lhsT=wT[:, :], rhs=Sb[:, :], start=True, stop=True)
                            b.op("dve", "tensor_tensor", out=vnew[:, :], in0=u_sb[:, :], in1=psU[0][0:64, 0:128], op=ALU.subtract)
                            b.op("act", "activation", out=vnew_b[:, :], in_=vnew[:, :], func=AF.Copy)
                            b.op("pe", "matmul", out=psU[1][0:64, 0:128], lhsT=qf, rhs=Sb[:, :], start=True, stop=True)
                            b.op("dve", "tensor_scalar", out=o1[:, :], in0=psU[1][0:64, 0:128], scalar1=eg_, scalar2=None, op0=ALU.mult)
                            b.op("pe", "matmul", out=psY[0][0:64, 0:64], lhsT=kf, rhs=qf, start=True, stop=True)
                            b.op("dve", "tensor_tensor", out=aT[:, :], in0=psY[0][0:64, 0:64], in1=decT[:, :], op=ALU.mult)
                            b.op("pe", "matmul", out=psY[1][0:64, 0:128], lhsT=aT[:, :], rhs=vnew_b[:, :], start=True, stop=True)
                            b.op("dve", "tensor_tensor", out=o1[:, :], in0=o1[:, :], in1=psY[1][0:64, 0:128], op=ALU.add)
                            if d == 0:
                                b.op("dve", "tensor_copy", out=o_acc[:, n, :], in_=o1[:, :])
                            else:
                                b.op("dve", "tensor_tensor", out=o_acc[:, n, :], in0=o_acc[:, n, :], in1=o1[:, :], op=ALU.add)
                            b.op("act", "activation", out=sc1[:, 1:2], in_=gc_, func=AF.Exp, scale=-1.0, bias=gcr[0:64, last:last + 1])
                            b.op("dve", "tensor_scalar", out=kg[:, :], in0=kt, scalar1=sc1[:, 1:2], scalar2=None, op0=ALU.mult)
                            b.op("act", "activation", out=egl[:, :], in_=gcr[:, last:last + 1], func=AF.Exp)
                            b.op("pe", "matmul", out=psG[0][:, 0:128], lhsT=kg[:, :], rhs=vnew_b[:, :], start=True, stop=True)
                            b.op("dve", "tensor_scalar", out=S[:, :], in0=S[:, :], scalar1=egl[:, 0:1], scalar2=None, op0=ALU.mult)
                            b.op("dve", "tensor_tensor", out=S[:, :], in0=S[:, :], in1=psG[0][:, 0:128], op=ALU.add)
                            b.op("act", "activation", out=Sb[:, :], in_=S[:, :], func=AF.Copy)
                        b.dma("sp", dn_out[(s_ * 2 + d) * 8 + hv, :, :], S[:, :])
                    for n in range(4):
                        b.op("act", "activation", out=junk[:, :], in_=o_acc[:, n, :], func=AF.Square, accum_out=ss[:, n:n + 1])
                    b.op("act", "activation", out=ss[:, :], in_=ss[:, :], func=AF.Sqrt, scale=1.0 / 128, bias=EPS)
                    b.op("dve", "reciprocal", out=ss[:, :], in_=ss[:, :])
                    for n in range(4):
                        tok0 = (s_ * 4 + n) * 64
                        b.op("dve", "tensor_scalar", out=junk[:, :], in0=o_acc[:, n, :], scalar1=ss[:, n:n + 1], scalar2=None, op0=ALU.mult)
                        b.op("dve", "tensor_tensor", out=onb[:, :], in0=junk[:, :], in1=outg[:, :], op=ALU.mult)
                        b.op("pe", "transpose", out=psTb[n % 2][:, 0, 0:64], in_=onb[:, :], identity=ident_bf[0:64, 0:64])
                        b.op("dve", "tensor_tensor", out=oT[:, hv, tok0:tok0 + 64], in0=psTb[n % 2][:, 0, 0:64],
                             in1=zs[:, hv, tok0:tok0 + 64], op=ALU.mult)
            proj_out_featmajor(dn_w_o, oT)


        NTL = LSEQ // 512
        NBL = LSEQ // 128

        def lat_load(src, tile):
            b.dma("sp", x[:], src[:, tile * 512:(tile + 1) * 512].rearrange("(c p) t -> p c t", p=128))

        def lat_store(dst, tile):
            b.dma("sp", dst[:, tile * 512:(tile + 1) * 512].rearrange("(c p) t -> p c t", p=128), x[:])

        def lat_ffn_sub(i, s):
            pre_norm(i, s)
            ffn(ffn_w_gu[i, 0 if s == 0 else 1], ffn_w_d[i, 0 if s == 0 else 1])
            post_norm_residual(i, s)

        def a_lat(src0):
            sbf = lambda n_, sh, dt_=F32: b.sb("la_" + n_, sh, dt_)
            kTf = sbf("kT", [128, 4, LSEQ], BF16); vtf = sbf("vt", [128, NBL, 256], BF16)
            kTc = sbf("kTc", [128, 4, 512], BF16); vtc = sbf("vtc", [128, 4, 256], BF16)
            bmask = sbf("bm", [128, 384]); b.dma("sp", bmask[:], band_mask)
            cs = [sbf(f"cs{i}", [128, 64]) for i in range(2)]
            kdup = sbf("kdup", [128, 4, 2, 64], BF16)
            qr = sbf("qr", [128, 1024], BF16); qTb = sbf("qTb", [128, 8, 128], BF16)
            slc = sbf("sl", [128, 384]); Pl = sbf("Pl", [128, 384], BF16); Pc = sbf("Pc", [128, 512], BF16)
            PTl = sbf("PTl", [128, 8, 128], BF16); otk = sbf("otk", [128, 1024], BF16)
            rt = [sbf(f"rt{i}", [128, 2, 16]) for i in range(4)]
            smm = sbf("sm", [128, 12]); c32t = sbf("c32", [128, 256])

            def rope(src, nh, dst_fn, cst):
                cosv = cst[:, 0:32].rearrange("p (a f) -> p a f", a=2)
                sinv = cst[:, 32:64].rearrange("p (a f) -> p a f", a=2)
                for hh in range(nh):
                    s4 = src[:, hh * 64:(hh + 1) * 64].rearrange("p (a b f) -> p a b f", a=2, b=2)
                    x1, x2 = s4[:, :, 0, :], s4[:, :, 1, :]
                    d4 = dst_fn(hh).rearrange("p (a b f) -> p a b f", a=2, b=2)
                    TT = lambda o, a_, b_, op: b.op("dve", "tensor_tensor", out=o, in0=a_, in1=b_, op=op)
                    TT(rt[0][:, :, :], x1, cosv, ALU.mult)
                    TT(rt[1][:, :, :], x2, sinv, ALU.mult)
                    TT(d4[:, :, 0, :], rt[0][:, :, :], rt[1][:, :, :], ALU.subtract)
                    TT(rt[2][:, :, :], x2, cosv, ALU.mult)
                    TT(rt[3][:, :, :], x1, sinv, ALU.mult)
                    TT(d4[:, :, 1, :], rt[2][:, :, :], rt[3][:, :, :], ALU.add)

            def k_to_featmajor(dstT, col0):
                b.op("dve", "tensor_copy", out=kdup[:, :, 1, :], in_=kdup[:, :, 0, :])
                for kv in range(4):
                    b.op("pe", "transpose", out=psTb[kv % 2][:, 0, :], in_=kdup[:, kv, :, :].rearrange("p a d -> p (a d)"),
                         identity=ident_bf[:, :])
                    b.op("act", "activation", out=dstT[:, kv, col0:col0 + 128], in_=psTb[kv % 2][:, 0, :], func=AF.Copy)

            for cb in range(4):
                b.dma("sp", c32t[:, :], cache_ak[cb * 128:(cb + 1) * 128, :])
                b.op("dve", "tensor_copy", out=kdup[:, :, 0, :], in_=c32t[:, :].rearrange("p (k d) -> p k d", k=4))
                k_to_featmajor(kTc, cb * 128)
                b.dma("pool", vtc[:, cb, :], cache_av[cb * 128:(cb + 1) * 128, :])
            for tl in range(NTL):
                lat_load(src0, tl)
                lat_ffn_sub(0, 0)
                lat_store(zres, tl)
                pre_norm(0, 1)
                sl_ = wslot()
                v = sl_[:, 0:KC * 512].rearrange("p (k n) -> p k n", k=KC)
                b.dma("pool", v, a_w_qkv[:, 1024:1536].rearrange("(k p) n -> p k n", p=128))
                for tb in range(4):
                    blk = tl * 4 + tb
                    pq = psU[tb % 2]
                    for k in range(KC):
                        b.op("pe", "matmul", accum=(k > 0), out=pq[:, 0:512], lhsT=h[:, k, tb * 128:(tb + 1) * 128],
                             rhs=v[:, k, :], start=(k == 0), stop=(k == KC - 1))
                    b.dma("sp", cs[tb % 2][:, :], rope_cs[blk * 128:(blk + 1) * 128, :])
                    rope(pq[:, 0:256], 4, lambda hh: kdup[:, hh, 0, :], cs[tb % 2])
                    k_to_featmajor(kTf, blk * 128)
                    b.op("act", "activation", out=vtf[:, blk, :], in_=pq[:, 256:512], func=AF.Copy)
            it = 0
            for tl in range(NTL):
                lat_load(zres, tl)
                pre_norm(0, 1)
                wq = []
                for half in range(2):
                    sl_ = wslot()
                    v = sl_[:, 0:KC * 512].rearrange("p (k n) -> p k n", k=KC)
                    b.dma("pool", v, a_w_qkv[:, half * 512:(half + 1) * 512].rearrange("(k p) n -> p k n", p=128))
                    wq.append(v)
                for tb in range(4):
                    blk = tl * 4 + tb
                    b.dma("sp", cs[tb % 2][:, :], rope_cs[blk * 128:(blk + 1) * 128, :])
                    for half in range(2):
                        pq = psG[half]
                        for k in range(KC):
                            b.op("pe", "matmul", accum=(k > 0), out=pq[:, 0:512], lhsT=h[:, k, tb * 128:(tb + 1) * 128],
                                 rhs=wq[half][:, k, :], start=(k == 0), stop=(k == KC - 1))
                        rope(pq[:, 0:512], 8, lambda hh, half=half: qr[:, half * 512 + hh * 64:half * 512 + (hh + 1) * 64], cs[tb % 2])
                    for c in range(KC):
                        b.op("pe", "transpose", out=psTb[c % 2][:, 0, :], in_=qr[:, c * 128:(c + 1) * 128], identity=ident_bf[:, :])
                        b.op("act", "activation", out=qTb[:, c, :], in_=psTb[c % 2][:, 0, :], func=AF.Copy)
                    lo, hi = max(blk - 1, 0), min(blk + 1, NBL - 1)
                    nlb = hi - lo + 1
                    nl = nlb * 128
                    bm = bmask[:, 128:128 + nl] if blk == 0 else bmask[:, 0:nl]
                    for hh in range(16):
                        qc, base, kv = hh // 2, 64 * (hh % 2), hh // 4
                        b.op("pe", "matmul", out=psG[0][:, 0:nl], lhsT=qTb[base:base + 64, qc, :],
                             rhs=kTf[base:base + 64, kv, lo * 128:(hi + 1) * 128], start=True, stop=True)
                        b.op("pe", "matmul", out=psG[1][:, 0:512], lhsT=qTb[base:base + 64, qc, :],
                             rhs=kTc[base:base + 64, kv, :], start=True, stop=True)
                        b.op("dve", "tensor_tensor", out=slc[:, 0:nl], in0=psG[0][:, 0:nl], in1=bm, op=ALU.add)
                        b.op("dve", "reduce_max", out=smm[:, 0:1], in_=slc[:, 0:nl], axis=AX.X)
                        b.op("dve", "reduce_max", out=smm[:, 1:2], in_=psG[1][:, 0:512], axis=AX.X)
                        b.op("dve", "tensor_tensor", out=smm[:, 0:1], in0=smm[:, 0:1], in1=smm[:, 1:2], op=ALU.max)
                        b.op("dve", "tensor_scalar", out=smm[:, 2:3], in0=smm[:, 0:1], scalar1=0.125,
                             scalar2=sink_bc[:, hh:hh + 1], op0=ALU.mult, op1=ALU.max)
                        b.op("dve", "tensor_scalar", out=smm[:, 2:3], in0=smm[:, 2:3], scalar1=-1.0, scalar2=None, op0=ALU.mult)
                        b.op("act", "activation", out=Pl[:, 0:nl], in_=slc[:, 0:nl], func=AF.Exp, scale=0.125,
                             bias=smm[:, 2:3], accum_out=smm[:, 3:4])
                        b.op("act", "activation", out=Pc[:, :], in_=psG[1][:, 0:512], func=AF.Exp, scale=0.125,
                             bias=smm[:, 2:3], accum_out=smm[:, 4:5])
                        b.op("act", "activation", out=smm[:, 5:6], in_=sink_bc[:, hh:hh + 1], func=AF.Exp, bias=smm[:, 2:3], scale=1.0)
                        b.op("dve", "tensor_tensor", out=smm[:, 6:7], in0=smm[:, 3:4], in1=smm[:, 4:5], op=ALU.add)
                        b.op("dve", "tensor_tensor", out=smm[:, 6:7], in0=smm[:, 6:7], in1=smm[:, 5:6], op=ALU.add)
                        b.op("dve", "reciprocal", out=smm[:, 7:8], in_=smm[:, 6:7])
                        srcs = [Pl[:, j * 128:(j + 1) * 128] for j in range(nlb)] + [Pc[:, j * 128:(j + 1) * 128] for j in range(4)]
                        for j, sp_ in enumerate(srcs):
                            b.op("pe", "transpose", out=psTb[j % 2][:, 0, :], in_=sp_, identity=ident_bf[:, :])
                            b.op("dve" if j % 2 == 0 else "act", "tensor_copy" if j % 2 == 0 else "activation",
                                 out=PTl[:, j, :], in_=psTb[j % 2][:, 0, :], **({} if j % 2 == 0 else {"func": AF.Copy}))
                        pO = psY[it % 2]
                        nsrc = len(srcs)
                        for j in range(nsrc):
                            rv = vtf[:, lo + j, kv * 64:(kv + 1) * 64] if j < nlb else vtc[:, j - nlb, kv * 64:(kv + 1) * 64]
                            b.op("pe", "matmul", accum=(j > 0), out=pO[:, 0:64], lhsT=PTl[:, j, :], rhs=rv,
                                 start=(j == 0), stop=(j == nsrc - 1))
                        b.op("dve", "tensor_scalar", out=otk[:, hh * 64:(hh + 1) * 64], in0=pO[:, 0:64], scalar1=smm[:, 7:8],
                             scalar2=None, op0=ALU.mult)
                        it += 1
                    for c in range(KC):
                        b.op("pe", "transpose", out=psTb[c % 2][:, 0, :], in_=otk[:, c * 128:(c + 1) * 128], identity=ident_bf[:, :])
                        b.op("act", "activation", out=oT[:, c, tb * 128:(tb + 1) * 128], in_=psTb[c % 2][:, 0, :], func=AF.Copy)
                proj_out_featmajor(a_w_o, oT)
                post_norm_residual(0, 1)
                lat_ffn_sub(0, 2)
                lat_store(zres, tl)


        def s5_lat():
            P = s5_setup(npow=9, pfx="l5p_")
            LT = 512
            sbf = lambda n_, sh, dt_=F32: b.sb("l5_" + n_, sh, dt_)
            TT = lambda o, a_, b_, op: b.op("dve", "tensor_tensor", out=o, in0=a_, in1=b_, op=op)
            h0r, h0i = sbf("h0r", [128, 64]), sbf("h0i", [128, 64])
            load_rows_T(h0r[:, :], st5_re, 64)
            load_rows_T(h0i[:, :], st5_im, 64)
            cr, ci = sbf("cr", [128, 64]), sbf("ci", [128, 64])
            t1, t2, t3 = P["t1"][:, :], P["t2"][:, :], P["t3"][:, :]
            TT(t1, P["fr"][:, :], P["fr"][:, :], ALU.mult)
            TT(t2, P["fi"][:, :], P["fi"][:, :], ALU.mult)
            TT(t1, t1, t2, ALU.add)
            b.op("dve", "reciprocal", out=t3, in_=t1)
            TT(t1, h0r[:, :], P["fr"][:, :], ALU.mult)
            TT(t2, h0i[:, :], P["fi"][:, :], ALU.mult)
            TT(t1, t1, t2, ALU.add)
            TT(cr[:, :], t1, t3, ALU.mult)
            TT(t1, h0i[:, :], P["fr"][:, :], ALU.mult)
            TT(t2, h0r[:, :], P["fi"][:, :], ALU.mult)
            TT(t1, t1, t2, ALU.subtract)
            TT(ci[:, :], t1, t3, ALU.mult)
            nat = {}
            for t4 in range(4):
                for nm in ("bre", "bim", "cre", "cim"):
                    tl_ = sbf(f"n_{nm}{t4}", [128, 128])
                    b.raw("dve", lambda tl_=tl_: nc.vector.memset(tl_[:, :], 0.0), writes=[tl_[:, :]])
                    nat[(nm, t4)] = tl_
            wB = [sbf(f"wB{ri}", [128, 128], BF16) for ri in range(2)]
            wC = [sbf(f"wC{ri}", [128, 128], BF16) for ri in range(2)]
            c32 = [sbf(f"c32{i}", [128, 128]) for i in range(2)]
            cm = [sbf(f"cm{i}", [128, 128]) for i in range(4)]
            X = (sbf("xr", [128, LT]), sbf("xi", [128, LT]))
            xrb, xib = sbf("xrb", [128, LT], BF16), sbf("xib", [128, LT], BF16)
            tm = [sbf(f"tm{i}", [128, LT]) for i in range(6)]
            sc = sbf("sc", [128, 8])
            rotb = [sbf(f"rotb{i}", [128, 1536]) for i in range(2)]
            ones512 = sbf("ones512", [128, LT])
            b.raw("dve", lambda: nc.vector.memset(ones512[:, :], 1.0), writes=[ones512[:, :]])
            pwu = [(P["cs"], P["sn"])]
            for k in range(1, 9):
                pc_, ps_ = pwu[-1]
                nc_ = sbf(f"uc{k}", [128, 64]); ns_ = sbf(f"us{k}", [128, 64])
                TT(t1, pc_[:, :], pc_[:, :], ALU.mult)
                TT(t2, ps_[:, :], ps_[:, :], ALU.mult)
                TT(nc_[:, :], t1, t2, ALU.subtract)
                TT(t1, pc_[:, :], ps_[:, :], ALU.mult)
                b.op("dve", "tensor_scalar", out=ns_[:, :], in0=t1, scalar1=2.0, scalar2=None, op0=ALU.mult)
                pwu.append((nc_, ns_))
            TSm = lambda o, i_, sc_: b.op("dve", "tensor_scalar", out=o, in0=i_, scalar1=sc_, scalar2=None, op0=ALU.mult)
            for tau in range(64):
                Er, Ei = X[0], X[1]
                b.op("dve", "tensor_copy", out=Er[:, 0:1], in_=P["cs"][:, tau:tau + 1])
                b.op("dve", "tensor_copy", out=Ei[:, 0:1], in_=P["sn"][:, tau:tau + 1])
                for k in range(9):
                    n = 1 << k
                    pc_, ps_ = pwu[k][0][:, tau:tau + 1], pwu[k][1][:, tau:tau + 1]
                    TSm(tm[0][:, 0:n], Er[:, 0:n], pc_)
                    TSm(tm[1][:, 0:n], Ei[:, 0:n], ps_)
                    TSm(tm[2][:, 0:n], Ei[:, 0:n], pc_)
                    TSm(tm[3][:, 0:n], Er[:, 0:n], ps_)
                    TT(Er[:, n:2 * n], tm[0][:, 0:n], tm[1][:, 0:n], ALU.subtract)
                    TT(Ei[:, n:2 * n], tm[2][:, 0:n], tm[3][:, 0:n], ALU.add)
                b.op("act", "activation", out=tm[4][:, :], in_=ones512[:, :], func=AF.Identity, scale=P["mag"][:, tau:tau + 1])
                b.dma("sp", rot[tau, :, 0:512], Er[:, :])
                b.dma("sp", rot[tau, :, 512:1024], Ei[:, :])
                b.dma("sp", rot[tau, :, 1024:1536], tm[4][:, :])
            rit = [0]

            def tile_pass(d, tl):
                for t in range(32):
                    tau = d * 32 + t
                    ch, t4 = t // 4, t % 4
                    for g2 in range(2):
                        g = 2 * t + g2
                        cb = (2 * t4 + g2) * 16
                        b.dma("sp", nat[("bre", t4)][g2 * 64:(g2 + 1) * 64, cb:cb + 16], s5_b_re[d, g, :, :])
                        b.dma("sp", nat[("bim", t4)][g2 * 64:(g2 + 1) * 64, cb:cb + 16], s5_b_im[d, g, :, :])
                        b.dma("sp", nat[("cre", t4)][cb:cb + 16, g2 * 64:(g2 + 1) * 64], s5_c_re[d, g, :, :])
                        b.dma("sp", nat[("cim", t4)][cb:cb + 16, g2 * 64:(g2 + 1) * 64], s5_c_im[d, g, :, :])
                    for ri, nm in enumerate(("bre", "bim")):
                        b.op("pe", "transpose", out=psS[:, 0:128], in_=nat[(nm, t4)][:, :], identity=ident[:, :])
                        b.op("act", "activation", out=wB[ri][:, :], in_=psS[:, 0:128], func=AF.Copy)
                    for ri, nm in enumerate(("cre", "cim")):
                        b.op("pe", "transpose", out=psS[:, 0:128], in_=nat[(nm, t4)][:, :], identity=ident[:, :])
                        b.op("act", "activation", out=c32[ri][:, :], in_=psS[:, 0:128], func=AF.Copy)
                    fr_, fi_ = P["fr"][:, tau:tau + 1], P["fi"][:, tau:tau + 1]
                    TS = lambda o, i_, sc_: b.op("dve", "tensor_scalar", out=o, in0=i_, scalar1=sc_, scalar2=None, op0=ALU.mult)
                    TS(cm[0][:, :], c32[0][:, :], fr_)
                    TS(cm[1][:, :], c32[1][:, :], fi_)
                    TT(wC[0][:, :], cm[0][:, :], cm[1][:, :], ALU.subtract)
                    TS(cm[2][:, :], c32[0][:, :], fi_)
                    TS(cm[3][:, :], c32[1][:, :], fr_)
                    TT(cm[2][:, :], cm[2][:, :], cm[3][:, :], ALU.add)
                    b.op("dve", "tensor_scalar", out=wC[1][:, :], in0=cm[2][:, :], scalar1=-1.0, scalar2=None, op0=ALU.mult)
                    tb = rotb[rit[0] % 2]
                    rit[0] += 1
                    b.dma("sp", tb[:, :], rot[tau, :, :])
                    cT, sT, rT = tb[:, 0:512], tb[:, 512:1024], tb[:, 1024:1536]
                    for ri in range(2):
                        b.op("pe", "matmul", out=psG[ri][:, 0:LT], lhsT=wB[ri][:, :], rhs=h[:, ch, :], start=True, stop=True)
                    e1 = LT - 1 if d == 0 else 0
                    cr_, ci_ = cr[:, tau:tau + 1], ci[:, tau:tau + 1]
                    rv = (lambda A: A[:, 0:LT]) if d == 0 else (lambda A: A[:, LT - 1::-1])
                    brv, biv = rv(psG[0]), rv(psG[1])
                    TT(tm[0][:, :], brv, cT, ALU.mult)
                    TT(tm[1][:, :], biv, sT, ALU.mult)
                    TT(tm[2][:, :], biv, cT, ALU.mult)
                    TT(tm[3][:, :], brv, sT, ALU.mult)
                    TT(X[0][:, :], tm[0][:, :], tm[1][:, :], ALU.add)
                    TT(X[1][:, :], tm[2][:, :], tm[3][:, :], ALU.subtract)
                    b.op("dve", "tensor_tensor_scan", out=tm[4][:, :], data0=rT, data1=X[0][:, :], initial=cr_, op0=ALU.mult, op1=ALU.add)
                    b.op("dve", "tensor_tensor_scan", out=tm[5][:, :], data0=rT, data1=X[1][:, :], initial=ci_, op0=ALU.mult, op1=ALU.add)
                    PT_ = lambda o, a_, b_, op: b.op("pool", "tensor_tensor", out=o, in0=a_, in1=b_, op=op)
                    PT_(tm[0][:, :], tm[4][:, :], cT, ALU.mult)
                    PT_(tm[1][:, :], tm[5][:, :], sT, ALU.mult)
                    PT_(tm[2][:, :], tm[5][:, :], cT, ALU.mult)
                    PT_(tm[3][:, :], tm[4][:, :], sT, ALU.mult)
                    TT(rv(X[0]), tm[0][:, :], tm[1][:, :], ALU.subtract)
                    TT(rv(X[1]), tm[2][:, :], tm[3][:, :], ALU.add)
                    b.op("dve", "tensor_copy", out=cr_, in_=X[0][:, e1:e1 + 1])
                    b.op("dve", "tensor_copy", out=ci_, in_=X[1][:, e1:e1 + 1])
                    b.op("act", "activation", out=xrb[:, :], in_=X[0][:, :], func=AF.Copy)
                    b.op("act", "activation", out=xib[:, :], in_=X[1][:, :], func=AF.Copy)
                    py = psY[t % 2]
                    b.op("pe", "matmul", out=py[:, 0:LT], lhsT=wC[0][:, :], rhs=xrb[:, :], start=True, stop=False)
                    b.op("pe", "matmul", accum=True, out=py[:, 0:LT], lhsT=wC[1][:, :], rhs=xib[:, :], start=False, stop=True)
                    if t4 == 0:
                        b.op("dve", "tensor_copy", out=y[:, ch, :], in_=py[:, 0:LT])
                    else:
                        TT(y[:, ch, :], y[:, ch, :], py[:, 0:LT], ALU.add)

            def tail():
                g_bf = act
                for c in range(KC):
                    t_a, t_b = tmpA[0], tmpA[1]
                    b.op("dve", "tensor_scalar", out=t_a[:, :], in0=h[:, c, :], scalar1=P["dskip"][:, c:c + 1], scalar2=None, op0=ALU.mult)
                    TT(y[:, c, :], y[:, c, :], t_a[:, :], ALU.add)
                    b.op("act", "activation", out=t_a[:, :], in_=y[:, c, :], func=AF.Square)
                    b.op("dve", "tensor_scalar", out=t_a[:, :], in0=t_a[:, :], scalar1=0.044715, scalar2=1.0, op0=ALU.mult, op1=ALU.add)
                    TT(t_a[:, :], t_a[:, :], y[:, c, :], ALU.mult)
                    b.op("act", "activation", out=t_b[:, :], in_=t_a[:, :], func=AF.Tanh, scale=0.7978845608028654)
                    b.op("dve", "tensor_scalar", out=t_b[:, :], in0=t_b[:, :], scalar1=1.0, scalar2=0.5, op0=ALU.add, op1=ALU.mult)
                    TT(g_bf[:, c, :], t_b[:, :], y[:, c, :], ALU.mult)
                for o4 in range(0, KC, 2):
                    sl = wslot()
                    v = sl[:, 0:KC * 512].rearrange("p (k n) -> p k n", k=KC)
                    b.dma("pool", v[:, :, 0:256], s5_w_glu[:, o4 * 128:(o4 + 2) * 128].rearrange("(k p) n -> p k n", p=128))
                    b.dma("pool", v[:, :, 256:512], s5_w_glu[:, D + o4 * 128:D + (o4 + 2) * 128].rearrange("(k p) n -> p k n", p=128))
                    for oo in range(2):
                        o = o4 + oo
                        pa, pg = psG[o % 2], psU[o % 2]
                        for k in range(KC):
                            b.op("pe", "matmul", accum=(k > 0), out=pa[:, 0:T], lhsT=v[:, k, oo * 128:(oo + 1) * 128],
                                 rhs=g_bf[:, k, :], start=(k == 0), stop=(k == KC - 1))
                        for k in range(KC):
                            b.op("pe", "matmul", accum=(k > 0), out=pg[:, 0:T], lhsT=v[:, k, 256 + oo * 128:256 + (oo + 1) * 128],
                                 rhs=g_bf[:, k, :], start=(k == 0), stop=(k == KC - 1))
                        tq = tmpA[o % 2]
                        b.op("act", "activation", out=tq[:, :], in_=pg[:, 0:T], func=AF.Sigmoid)
                        TT(y[:, o, :], tq[:, :], pa[:, 0:T], ALU.mult)

            for tl in range(NTL):
                lat_load(zres, tl)
                lat_ffn_sub(1, 0)
                lat_store(zres, tl)
                pre_norm(1, 1)
                tile_pass(0, tl)
                b.dma("sp", ysc[:, tl * 512:(tl + 1) * 512].rearrange("(c p) t -> p c t", p=128), y[:])
            for tl in range(NTL - 1, -1, -1):
                lat_load(zres, tl)
                pre_norm(1, 1)
                tile_pass(1, tl)
                for c in range(KC):
                    b.dma("sp", tmpA[c % 2][:, :], ysc[c * 128:(c + 1) * 128, tl * 512:(tl + 1) * 512])
                    TT(y[:, c, :], y[:, c, :], tmpA[c % 2][:, :], ALU.add)
                tail()
                post_norm_residual(1, 1)
                lat_ffn_sub(1, 2)
                lat_store(zres, tl)


        def na_lat():
            sbf = lambda n_, sh, dt_=F32: b.sb("ln_" + n_, sh, dt_)
            TT = lambda o, a_, b_, op: b.op("dve", "tensor_tensor", out=o, in0=a_, in1=b_, op=op)
            NR = LSEQ // 64
            kTg = sbf("kT", [128, LSEQ], BF16); vtg = sbf("vt", [64, NR, 128], BF16)
            kTc = sbf("kTc", [128, 512], BF16); vtc = sbf("vtc", [128, 4, 128], BF16)
            Bh = sbf("Bh", [64, 2, 15, 64]); cmask = sbf("cmask", [64, 64]); b.dma("sp", cmask[:], na_cmask)
            c32t = sbf("c32", [128, 128]); ckb = sbf("ckb", [128, 128], BF16)
            qTg = sbf("qT", [128, 512], BF16)
            slc = sbf("sl", [64, 512]); Pl = sbf("Pl", [64, 512], BF16); Pc = sbf("Pc", [64, 512], BF16)
            PTl = sbf("PTl", [128, 12, 64], BF16); otk = sbf("otk", [64, 128], BF16)
            smm = sbf("sm", [64, 8]); woT = sbf("woT", [128, 512], BF16)
            for gi in range(8):
                for j in range(2):
                    b.dma("sp", Bh[:, j, :, :].rearrange("p a k -> p (a k)"), na_bias[2 * gi + j, :, :])
                    for dr in range(15):
                        TT(Bh[:, j, dr, :], Bh[:, j, dr, :], cmask[:, :], ALU.add)
                    b.op("dve", "tensor_scalar", out=Bh[:, j, :, :].rearrange("p a k -> p (a k)"),
                         in0=Bh[:, j, :, :].rearrange("p a k -> p (a k)"), scalar1=8.0, scalar2=None, op0=ALU.mult)
                for cb in range(4):
                    b.dma("sp", c32t[:, :], cache_nk[cb * 128:(cb + 1) * 128, gi * 128:(gi + 1) * 128])
                    b.op("dve", "tensor_copy", out=ckb[:, :], in_=c32t[:, :])
                    b.op("pe", "transpose", out=psTb[cb % 2][:, 0, :], in_=ckb[:, :], identity=ident_bf[:, :])
                    b.op("act", "activation", out=kTc[:, cb * 128:(cb + 1) * 128], in_=psTb[cb % 2][:, 0, :], func=AF.Copy)
                    b.dma("pool", vtc[:, cb, :], cache_nv[cb * 128:(cb + 1) * 128, gi * 128:(gi + 1) * 128])
                for tl in range(NTL):
                    lat_load(zres, tl)
                    if gi == 0:
                        lat_ffn_sub(2, 0)
                        lat_store(zres, tl)
                    pre_norm(2, 1)
                    sl_ = wslot()
                    v = sl_[:, 0:KC * 256].rearrange("p (k n) -> p k n", k=KC)
                    b.dma("pool", v[:, :, 0:128], na_w_qkv[:, D + gi * 128:D + (gi + 1) * 128].rearrange("(k p) n -> p k n", p=128))
                    b.dma("pool", v[:, :, 128:256], na_w_qkv[:, 2 * D + gi * 128:2 * D + (gi + 1) * 128].rearrange("(k p) n -> p k n", p=128))
                    for k in range(KC):
                        b.op("pe", "matmul", accum=(k > 0), out=psG[0][:, 0:512], lhsT=v[:, k, 0:128], rhs=h[:, k, :],
                             start=(k == 0), stop=(k == KC - 1))
                    b.op("act", "activation", out=kTg[:, tl * 512:(tl + 1) * 512], in_=psG[0][:, 0:512], func=AF.Copy)
                    for rr in range(8):
                        pq = psU[rr % 2]
                        for k in range(KC):
                            b.op("pe", "matmul", accum=(k > 0), out=pq[0:64, 0:128], lhsT=h[:, k, rr * 64:(rr + 1) * 64],
                                 rhs=v[:, k, 128:256], start=(k == 0), stop=(k == KC - 1))
                        b.op("dve", "tensor_copy", out=vtg[:, tl * 8 + rr, :], in_=pq[0:64, 0:128])
                it = 0
                for tl in range(NTL):
                    lat_load(zres, tl)
                    pre_norm(2, 1)
                    sl_ = wslot()
                    v = sl_[:, 0:KC * 128].rearrange("p (k n) -> p k n", k=KC)
                    b.dma("pool", v, na_w_qkv[:, gi * 128:(gi + 1) * 128].rearrange("(k p) n -> p k n", p=128))
                    for k in range(KC):
                        b.op("pe", "matmul", accum=(k > 0), out=psG[0][:, 0:512], lhsT=v[:, k, :], rhs=h[:, k, :],
                             start=(k == 0), stop=(k == KC - 1))
                    b.op("act", "activation", out=qTg[:, :], in_=psG[0][:, 0:512], func=AF.Copy)
                    for rr in range(8):
                        r = tl * 8 + rr
                        kr0 = min(max(r - 4, 0), NR - 8)
                        dr0 = kr0 - r + 7
                        for j in range(2):
                            base = 64 * j
                            lq = qTg[base:base + 64, rr * 64:(rr + 1) * 64]
                            b.op("pe", "matmul", out=psG[0][0:64, 0:512], lhsT=lq, rhs=kTg[base:base + 64, kr0 * 64:kr0 * 64 + 512],
                                 start=True, stop=True)
                            b.op("pe", "matmul", out=psG[1][0:64, 0:512], lhsT=lq, rhs=kTc[base:base + 64, :], start=True, stop=True)
                            TT(slc[:, :], psG[0][0:64, 0:512], Bh[:, j, dr0:dr0 + 8, :].rearrange("p a k -> p (a k)"), ALU.add)
                            b.op("dve", "reduce_max", out=smm[:, 0:1], in_=slc[:, :], axis=AX.X)
                            b.op("dve", "reduce_max", out=smm[:, 1:2], in_=psG[1][0:64, 0:512], axis=AX.X)
                            TT(smm[:, 0:1], smm[:, 0:1], smm[:, 1:2], ALU.max)
                            b.op("dve", "tensor_scalar", out=smm[:, 2:3], in0=smm[:, 0:1], scalar1=-0.125, scalar2=None, op0=ALU.mult)
                            b.op("act", "activation", out=Pl[:, :], in_=slc[:, :], func=AF.Exp, scale=0.125, bias=smm[:, 2:3],
                                 accum_out=smm[:, 3:4])
                            b.op("act", "activation", out=Pc[:, :], in_=psG[1][0:64, 0:512], func=AF.Exp, scale=0.125,
                                 bias=smm[:, 2:3], accum_out=smm[:, 4:5])
                            TT(smm[:, 5:6], smm[:, 3:4], smm[:, 4:5], ALU.add)
                            b.op("dve", "reciprocal", out=smm[:, 6:7], in_=smm[:, 5:6])
                            for jj in range(8):
                                b.op("pe", "transpose", out=psTb[jj % 2][0:64, 0, 0:64], in_=Pl[:, jj * 64:(jj + 1) * 64],
                                     identity=ident_bf[0:64, 0:64])
                                b.op("dve" if jj % 2 == 0 else "act", "tensor_copy" if jj % 2 == 0 else "activation",
                                     out=PTl[0:64, jj, :], in_=psTb[jj % 2][0:64, 0, 0:64], **({} if jj % 2 == 0 else {"func": AF.Copy}))
                            for jj in range(4):
                                b.op("pe", "transpose", out=psTb[jj % 2][:, 0, 0:64], in_=Pc[:, jj * 128:(jj + 1) * 128],
                                     identity=ident_bf[0:64, 0:64])
                                b.op("dve" if jj % 2 == 0 else "act", "tensor_copy" if jj % 2 == 0 else "activation",
                                     out=PTl[:, 8 + jj, :], in_=psTb[jj % 2][:, 0, 0:64], **({} if jj % 2 == 0 else {"func": AF.Copy}))
                            pO = psY[it % 2]
                            for jj in range(8):
                                b.op("pe", "matmul", accum=(jj > 0), out=pO[0:64, 0:64], lhsT=PTl[0:64, jj, :],
                                     rhs=vtg[:, kr0 + jj, base:base + 64], start=(jj == 0), stop=False)
                            for jj in range(4):
                                b.op("pe", "matmul", accum=True, out=pO[0:64, 0:64], lhsT=PTl[:, 8 + jj, :],
                                     rhs=vtc[:, jj, base:base + 64], start=False, stop=(jj == 3))
                            b.op("dve", "tensor_scalar", out=otk[:, base:base + 64], in0=pO[0:64, 0:64], scalar1=smm[:, 6:7],
                                 scalar2=None, op0=ALU.mult)
                            it += 1
                        b.op("pe", "transpose", out=psTb[rr % 2][:, 0, 0:64], in_=otk[:, :], identity=ident_bf[0:64, 0:64])
                        b.op("act", "activation", out=oT[:, 0, rr * 64:(rr + 1) * 64], in_=psTb[rr % 2][:, 0, 0:64], func=AF.Copy)
                    for o4 in range(2):
                        b.dma("pool", woT[:, :], na_w_o[gi * 128:(gi + 1) * 128, o4 * 512:(o4 + 1) * 512])
                        for oo in range(4):
                            o = o4 * 4 + oo
                            py = psY[oo % 2]
                            b.op("pe", "matmul", out=py[:, 0:512], lhsT=woT[:, oo * 128:(oo + 1) * 128], rhs=oT[:, 0, :],
                                 start=True, stop=True)
                            if gi == 0:
                                b.op("act", "activation", out=y[:, o, :], in_=py[:, 0:512], func=AF.Copy)
                            else:
                                b.dma("sp", tmpA[oo % 2][:, :], ysc[o * 128:(o + 1) * 128, tl * 512:(tl + 1) * 512])
                                TT(y[:, o, :], tmpA[oo % 2][:, :], py[:, 0:512], ALU.add)
                    if gi < 7:
                        b.dma("sp", ysc[:, tl * 512:(tl + 1) * 512].rearrange("(c p) t -> p c t", p=128), y[:])
                    else:
                        post_norm_residual(2, 1)
                        lat_ffn_sub(2, 2)
                        lat_store(zres, tl)


        def dn_lat(final_dst):
            sbf = lambda n_, sh, dt_=F32: b.sb("ld_" + n_, sh, dt_)
            TT = lambda o, a_, b_, op: b.op("dve", "tensor_tensor", out=o, in0=a_, in1=b_, op=op)
            NCH = LSEQ // 64
            msk = sbf("msk", [64, 6, 64]); b.dma("sp", msk[:], dn_mask)
            cw = sbf("cw", [128, 80]); load_rows_T(cw[:, :], dn_conv_w, 80)
            ones_f = sbf("ones", [64, 128]); b.raw("dve", lambda: nc.vector.memset(ones_f[:, :], 1.0), writes=[ones_f[:, :]])
            wba = sbf("wba", [128, 2, KC, 16], BF16)
            for d in range(2):
                b.dma("pool", wba[:, d, :, :], dn_w_ba[d].rearrange("(k p) n -> p k n", p=128))
            alog = sbf("alog", [64, 16]); b.dma("sp", alog[:], dn_a_log.broadcast_to([64, 16]))
            dtb = sbf("dtb", [64, 16]); b.dma("sp", dtb[:], dn_dt_bias.broadcast_to([64, 16]))
            outg = sbf("outg", [64, 128]); b.dma("sp", outg[:], dn_out_g.broadcast_to([64, 128]))
            nexpa = sbf("nexpa", [64, 16])
            b.op("act", "activation", out=nexpa[:, :], in_=alog[:, :], func=AF.Exp)
            b.op("dve", "tensor_scalar", out=nexpa[:, :], in0=nexpa[:, :], scalar1=-1.0, scalar2=None, op0=ALU.mult)
            zt = sbf("zt", [128, 2]); b.raw("dve", lambda: nc.vector.memset(zt[:, :], 0.0), writes=[zt[:, :]])
            for mm in range(3):
                b.dma("sp", pjsc[mm, :, 0:2], zt[:, :])
                b.dma("sp", pjsc[mm, :, LSEQ + 2:LSEQ + 4], zt[:, :])
            cin = sbf("cin", [128, 516])
            qn = sbf("qn", [128, LSEQ], BF16); kn = sbf("kn", [128, LSEQ], BF16)
            kt_tok = sbf("kt", [64, NCH, 128], BF16); vt_tok = sbf("vt", [64, NCH, 128], BF16)
            zs = sbf("zs", [128, LSEQ], BF16); vtmp = sbf("vtmp", [128, 512], BF16)
            oTf = oT[:, :, :].rearrange("p c t -> p (c t)")
            graw = sbf("graw", [64, 2, NCH, 16]); beta = sbf("beta", [64, 2, NCH]); xa = sbf("xa", [64, 2, NCH])
            xe = sbf("xe", [64, 2, NCH]); gg = sbf("gg", [64, 2, NCH]); gc = sbf("gc", [64, 2, NCH]); eg = sbf("eg", [64, 2, NCH])
            def mkbufs(sx):
                f64 = lambda n_: sbf(n_ + sx, [64, 64])
                U = {}
                U["dg"], U["decS"], U["decT"], U["PTt"] = f64("dg"), f64("decS"), f64("decT"), f64("PTt")
                U["Ms"] = [(f64("Ma"), f64("MTa")), (f64("Mb"), f64("MTb"))]
                U["gcr"] = sbf("gcr" + sx, [128, 64]); U["TTb"] = sbf("TTb" + sx, [64, 64], BF16); U["aT"] = sbf("aT" + sx, [64, 64], BF16)
                U["vb"] = sbf("vb" + sx, [64, 128], BF16); U["kbg"] = sbf("kbg" + sx, [64, 128], BF16); U["kg"] = sbf("kg" + sx, [64, 128], BF16)
                U["wT"] = sbf("wT" + sx, [128, 64], BF16); U["u_sb"] = sbf("u" + sx, [64, 128]); U["vnew"] = sbf("vnew" + sx, [64, 128])
                U["vnew_b"] = sbf("vnewb" + sx, [64, 128], BF16)
                U["o1"] = sbf("o1" + sx, [64, 128]); U["sc1"] = sbf("sc1" + sx, [64, 4]); U["egl"] = sbf("egl" + sx, [128, 1])
                U["S"] = sbf("S" + sx, [128, 128]); U["Sb"] = sbf("Sb" + sx, [128, 128], BF16)
                return U
            UB = [mkbufs("_a"), mkbufs("_b")]
            og = [sbf("og0", [64, 8, 128]), sbf("og1", [64, 8, 128])]
            ss = sbf("ss", [64, 2]); onb = sbf("onb", [64, 128], BF16); junk = sbf("junk", [64, 128])
            woT = sbf("woT", [128, 1024], BF16)
            id64 = ident[0:64, 0:64]
            fl = lambda A: A[:, :, :].rearrange("p a b -> p (a b)")
            for hv in range(8):
                hq = hv // 2
                wcols = [hq * 128, 512 + hq * 128, 1024 + hv * 128, 2048 + hv * 128]
                cwc = [hq, 4 + hq, 8 + hv]
                for tl in range(NTL):
                    lat_load(zres, tl)
                    if hv == 0:
                        lat_ffn_sub(3, 0)
                        lat_store(zres, tl)
                    pre_norm(3, 1)
                    sl_ = wslot()
                    v = sl_[:, 0:KC * 512].rearrange("p (k n) -> p k n", k=KC)
                    for mm in range(4):
                        b.dma("pool", v[:, :, mm * 128:(mm + 1) * 128],
                              dn_w_in[:, wcols[mm]:wcols[mm] + 128].rearrange("(k p) n -> p k n", p=128))
                    for mm in range(4):
                        pq = psG[mm % 2]
                        for k in range(KC):
                            b.op("pe", "matmul", accum=(k > 0), out=pq[:, 0:512], lhsT=v[:, k, mm * 128:(mm + 1) * 128],
                                 rhs=h[:, k, :], start=(k == 0), stop=(k == KC - 1))
                        if mm == 3:
                            b.op("act", "activation", out=zs[:, tl * 512:(tl + 1) * 512], in_=pq[:, 0:512], func=AF.Silu)
                        else:
                            b.op("act", "activation", out=tmpA[mm % 2][:, :], in_=pq[:, 0:512], func=AF.Copy)
                            b.dma("sp", pjsc[mm, :, 2 + tl * 512:2 + (tl + 1) * 512], tmpA[mm % 2][:, :])
                    for blk in range(8 if hv == 0 else 0):
                        n = tl * 8 + blk
                        for d in range(2):
                            pq = psU[d]
                            for k in range(KC):
                                b.op("pe", "matmul", accum=(k > 0), out=pq[0:64, 0:16], lhsT=h[:, k, blk * 64:(blk + 1) * 64],
                                     rhs=wba[:, d, k, :], start=(k == 0), stop=(k == KC - 1))
                            b.op("dve", "tensor_copy", out=graw[:, d, n, :], in_=pq[0:64, 0:16])
                for d in range(2):
                    b.op("act", "activation", out=beta[:, d, :], in_=graw[:, d, :, hv], func=AF.Sigmoid)
                    b.op("dve", "tensor_scalar", out=xa[:, d, :], in0=graw[:, d, :, 8 + hv], scalar1=dtb[:, d * 8 + hv:d * 8 + hv + 1],
                         scalar2=None, op0=ALU.add)
                b.op("act", "activation", out=fl(xe), in_=fl(xa), func=AF.Abs)
                b.op("act", "activation", out=fl(xe), in_=fl(xe), func=AF.Exp, scale=-1.0)
                b.op("act", "activation", out=fl(xe), in_=fl(xe), func=AF.Ln, bias=1.0, scale=1.0)
                b.op("dve", "tensor_scalar", out=fl(xa), in0=fl(xa), scalar1=0.0, scalar2=None, op0=ALU.max)
                TT(fl(xa), fl(xa), fl(xe), ALU.add)
                for d in range(2):
                    b.op("dve", "tensor_scalar", out=gg[:, d, :], in0=xa[:, d, :], scalar1=nexpa[:, d * 8 + hv:d * 8 + hv + 1],
                         scalar2=None, op0=ALU.mult)
                    b.op("pe", "matmul", out=psS[0:64, 0:NCH], lhsT=msk[:, d, :], rhs=gg[:, d, :], start=True, stop=True)
                    b.op("dve", "tensor_copy", out=gc[:, d, :], in_=psS[0:64, 0:NCH])
                b.op("act", "activation", out=fl(eg), in_=fl(gc), func=AF.Exp)
                for tl in range(NTL):
                    for mm in range(3):
                        b.dma("sp", cin[:, :], pjsc[mm, :, tl * 512:tl * 512 + 516])
                        for j in range(5):
                            dst = tmpA[0] if j == 0 else tmpA[1]
                            b.op("act", "activation", out=dst[:, :], in_=cin[:, j:j + 512], func=AF.Identity,
                                 scale=cw[:, j * 16 + cwc[mm]:j * 16 + cwc[mm] + 1])
                            if j > 0:
                                TT(tmpA[0][:, :], tmpA[0][:, :], tmpA[1][:, :], ALU.add)
                        if mm < 2:
                            b.op("act", "activation", out=tmpA[1][:, :], in_=tmpA[0][:, :], func=AF.Silu)
                            b.op("act", "activation", out=act[:, 0, :], in_=tmpA[1][:, :], func=AF.Square)
                            b.op("pe", "matmul", out=psS[:, 0:512], lhsT=ones_bf[:, :], rhs=act[:, 0, :], start=True, stop=True)
                            b.op("act", "activation", out=rstd[:, :], in_=psS[:, 0:512], func=AF.Sqrt, scale=1.0, bias=EPS)
                            b.op("dve", "reciprocal", out=rstd[:, :], in_=rstd[:, :])
                            TT(tmpA[1][:, :], tmpA[1][:, :], rstd[:, :], ALU.mult)
                            if mm == 0:
                                b.op("act", "activation", out=qn[:, tl * 512:(tl + 1) * 512], in_=tmpA[1][:, :], func=AF.Identity,
                                     scale=128 ** -0.5)
                            else:
                                b.op("act", "activation", out=kn[:, tl * 512:(tl + 1) * 512], in_=tmpA[1][:, :], func=AF.Identity, scale=1.0)
                                for blk in range(8):
                                    n = tl * 8 + blk
                                    b.op("pe", "transpose", out=psTb[blk % 2][0:64, 0, :], in_=kn[:, n * 64:(n + 1) * 64], identity=ident_bf[:, :])
                                    b.op("dve", "tensor_copy", out=kt_tok[:, n, :], in_=psTb[blk % 2][0:64, 0, :])
                        else:
                            b.op("act", "activation", out=vtmp[:, :], in_=tmpA[0][:, :], func=AF.Silu)
                            for blk in range(8):
                                n = tl * 8 + blk
                                b.op("pe", "transpose", out=psTb[blk % 2][0:64, 0, :], in_=vtmp[:, blk * 64:(blk + 1) * 64], identity=ident_bf[:, :])
                                b.op("dve", "tensor_copy", out=vt_tok[:, n, :], in_=psTb[blk % 2][0:64, 0, :])
                def chain(d):
                    U = UB[d]
                    dg, decS, decT, PTt, Ms, gcr, TTb, aT = U["dg"], U["decS"], U["decT"], U["PTt"], U["Ms"], U["gcr"], U["TTb"], U["aT"]
                    vb, kbg, kg, wT, u_sb, vnew, vnew_b = U["vb"], U["kbg"], U["kg"], U["wT"], U["u_sb"], U["vnew"], U["vnew_b"]
                    o1, sc1, egl, S, Sb = U["o1"], U["sc1"], U["egl"], U["S"], U["Sb"]
                    pG, pU, pY = psG[d], psU[d], psY[d]
                    b.dma("sp", S[:, :], st_dn[d * 8 + hv, :, :])
                    b.op("act", "activation", out=Sb[:, :], in_=S[:, :], func=AF.Copy)
                    yield
                    last = 63 if d == 0 else 0
                    for n in (range(NCH) if d == 0 else range(NCH - 1, -1, -1)):
                        tok0 = n * 64
                        gc_, be_, eg_ = gc[:, d, n:n + 1], beta[:, d, n:n + 1], eg[:, d, n:n + 1]
                        kf, qf = kn[:, tok0:tok0 + 64], qn[:, tok0:tok0 + 64]
                        kt, vt = kt_tok[:, n, :], vt_tok[:, n, :]
                        b.op("dve", "tensor_scalar", out=dg[:, :], in0=id64, scalar1=gc_, scalar2=None, op0=ALU.mult)
                        b.op("pe", "matmul", out=psS[:, 0:64], lhsT=ones_f[:, :], rhs=dg[:, :], start=True, stop=True)
                        b.op("act", "activation", out=gcr[:, :], in_=psS[:, 0:64], func=AF.Copy)
                        yield
                        b.op("dve", "tensor_scalar", out=decS[:, :], in0=gcr[0:64, :], scalar1=-1.0, scalar2=gc_, op0=ALU.mult, op1=ALU.add)
                        b.op("dve", "tensor_scalar", out=decS[:, :], in0=decS[:, :], scalar1=0.0, scalar2=None, op0=ALU.min)
                        b.op("act", "activation", out=decS[:, :], in_=decS[:, :], func=AF.Exp)
                        TT(decS[:, :], decS[:, :], msk[:, 4 + d, :], ALU.mult)
                        yield
                        b.op("dve", "tensor_scalar", out=decT[:, :], in0=gcr[0:64, :], scalar1=gc_, scalar2=None, op0=ALU.subtract)
                        b.op("dve", "tensor_scalar", out=decT[:, :], in0=decT[:, :], scalar1=0.0, scalar2=None, op0=ALU.min)
                        b.op("act", "activation", out=decT[:, :], in_=decT[:, :], func=AF.Exp)
                        TT(decT[:, :], decT[:, :], msk[:, d, :], ALU.mult)
                        yield
                        M, MT = Ms[0]
                        b.op("pe", "matmul", out=pG[0:64, 0:64], lhsT=kf, rhs=kf, start=True, stop=True)
                        b.op("dve", "tensor_scalar", out=M[:, :], in0=pG[0:64, 0:64], scalar1=be_, scalar2=-1.0, op0=ALU.mult, op1=ALU.mult)
                        TT(M[:, :], M[:, :], decS[:, :], ALU.mult)
                        yield
                        b.op("pe", "transpose", out=pU[0:64, 0:64], in_=M[:, :], identity=id64)
                        b.op("act", "activation", out=MT[:, :], in_=pU[0:64, 0:64], func=AF.Copy)
                        TT(PTt[:, :], MT[:, :], id64, ALU.add)
                        yield
                        cur = 0
                        for k in range(1, 6):
                            M, MT = Ms[cur]
                            Mn, MTn = Ms[1 - cur]
                            b.op("pe", "matmul", out=pG[0:64, 0:64], lhsT=MT[:, :], rhs=M[:, :], start=True, stop=True)
                            b.op("act", "activation", out=Mn[:, :], in_=pG[0:64, 0:64], func=AF.Copy)
                            if k < 5:
                                b.op("pe", "matmul", out=pU[0:64, 0:64], lhsT=M[:, :], rhs=MT[:, :], start=True, stop=True)
                                b.op("dve", "tensor_copy", out=MTn[:, :], in_=pU[0:64, 0:64])
                            yield
                            b.op("pe", "matmul", out=pY[0:64, 0:64], lhsT=Mn[:, :], rhs=PTt[:, :], start=True, stop=True)
                            TT(PTt[:, :], PTt[:, :], pY[0:64, 0:64], ALU.add)
                            yield
                            cur = 1 - cur
                        b.op("act", "activation", out=TTb[:, :], in_=PTt[:, :], func=AF.Copy)
                        TT(sc1[:, 0:1], be_, eg_, ALU.mult)
                        b.op("dve", "tensor_scalar", out=vb[:, :], in0=vt, scalar1=be_, scalar2=None, op0=ALU.mult)
                        b.op("dve", "tensor_scalar", out=kbg[:, :], in0=kt, scalar1=sc1[:, 0:1], scalar2=None, op0=ALU.mult)
                        yield
                        b.op("pe", "matmul", out=pG[0:64, 0:128], lhsT=TTb[:, :], rhs=vb[:, :], start=True, stop=True)
                        b.op("act", "activation", out=u_sb[:, :], in_=pG[0:64, 0:128], func=AF.Copy)
                        b.op("pe", "matmul", out=pU[:, 0:64], lhsT=kbg[:, :], rhs=TTb[:, :], start=True, stop=True)
                        b.op("dve", "tensor_copy", out=wT[:, :], in_=pU[:, 0:64])
                        yield
                        b.op("pe", "matmul", out=pY[0:64, 0:128], lhsT=wT[:, :], rhs=Sb[:, :], start=True, stop=True)
                        TT(vnew[:, :], u_sb[:, :], pY[0:64, 0:128], ALU.subtract)
                        b.op("act", "activation", out=vnew_b[:, :], in_=vnew[:, :], func=AF.Copy)
                        yield
                        b.op("pe", "matmul", out=pG[0:64, 0:128], lhsT=qf, rhs=Sb[:, :], start=True, stop=True)
                        b.op("dve", "tensor_scalar", out=o1[:, :], in0=pG[0:64, 0:128], scalar1=eg_, scalar2=None, op0=ALU.mult)
                        b.op("pe", "matmul", out=pU[0:64, 0:64], lhsT=kf, rhs=qf, start=True, stop=True)
                        TT(aT[:, :], pU[0:64, 0:64], decT[:, :], ALU.mult)
                        yield
                        b.op("pe", "matmul", out=pY[0:64, 0:128], lhsT=aT[:, :], rhs=vnew_b[:, :], start=True, stop=True)
                        TT(o1[:, :], o1[:, :], pY[0:64, 0:128], ALU.add)
                        b.dma("sp", osc[d, :, n, :], o1[:, :])
                        yield
                        b.op("act", "activation", out=sc1[:, 1:2], in_=gc_, func=AF.Exp, scale=-1.0, bias=gcr[0:64, last:last + 1])
                        b.op("dve", "tensor_scalar", out=kg[:, :], in0=kt, scalar1=sc1[:, 1:2], scalar2=None, op0=ALU.mult)
                        b.op("act", "activation", out=egl[:, :], in_=gcr[:, last:last + 1], func=AF.Exp)
                        yield
                        b.op("pe", "matmul", out=pG[:, 0:128], lhsT=kg[:, :], rhs=vnew_b[:, :], start=True, stop=True)
                        b.op("dve", "tensor_scalar", out=S[:, :], in0=S[:, :], scalar1=egl[:, 0:1], scalar2=None, op0=ALU.mult)
                        TT(S[:, :], S[:, :], pG[:, 0:128], ALU.add)
                        b.op("act", "activation", out=Sb[:, :], in_=S[:, :], func=AF.Copy)
                        yield

                gens = [chain(0), chain(1)]
                alive = [True, True]
                while any(alive):
                    for gi_ in range(2):
                        if alive[gi_]:
                            try:
                                next(gens[gi_])
                            except StopIteration:
                                alive[gi_] = False
                for n0 in range(0, NCH, 8):
                    for d in range(2):
                        b.dma("sp", og[d][:, :, :], osc[d, :, n0:n0 + 8, :])
                    for nn in range(8):
                        n = n0 + nn
                        tok0 = n * 64
                        o1 = UB[0]["o1"]
                        TT(o1[:, :], og[0][:, nn, :], og[1][:, nn, :], ALU.add)
                        b.op("act", "activation", out=junk[:, :], in_=o1[:, :], func=AF.Square, accum_out=ss[:, 0:1])
                        b.op("act", "activation", out=ss[:, 1:2], in_=ss[:, 0:1], func=AF.Sqrt, scale=1.0 / 128, bias=EPS)
                        b.op("dve", "reciprocal", out=ss[:, 1:2], in_=ss[:, 1:2])
                        b.op("dve", "tensor_scalar", out=junk[:, :], in0=o1[:, :], scalar1=ss[:, 1:2], scalar2=None, op0=ALU.mult)
                        TT(onb[:, :], junk[:, :], outg[:, :], ALU.mult)
                        b.op("pe", "transpose", out=psTb[n % 2][:, 0, 0:64], in_=onb[:, :], identity=ident_bf[0:64, 0:64])
                        TT(oTf[:, tok0:tok0 + 64], psTb[n % 2][:, 0, 0:64], zs[:, tok0:tok0 + 64], ALU.mult)
                b.dma("pool", woT[:, :], dn_w_o[hv * 128:(hv + 1) * 128, :])
                for tl in range(NTL):
                    if hv == 7:
                        lat_load(zres, tl)
                    for o in range(KC):
                        py = psY[o % 2]
                        b.op("pe", "matmul", out=py[:, 0:512], lhsT=woT[:, o * 128:(o + 1) * 128], rhs=oTf[:, tl * 512:(tl + 1) * 512],
                             start=True, stop=True)
                        if hv == 0:
                            b.op("act", "activation", out=y[:, o, :], in_=py[:, 0:512], func=AF.Copy)
                        else:
                            b.dma("sp", tmpA[o % 2][:, :], ysc[o * 128:(o + 1) * 128, tl * 512:(tl + 1) * 512])
                            TT(y[:, o, :], tmpA[o % 2][:, :], py[:, 0:512], ALU.add)
                    if hv < 7:
                        b.dma("sp", ysc[:, tl * 512:(tl + 1) * 512].rearrange("(c p) t -> p c t", p=128), y[:])
                    else:
                        post_norm_residual(3, 1)
                        lat_ffn_sub(3, 2)
                        lat_store(final_dst, tl)

        CS = 0 if CTX_SKIP else STAGE
        def dbg(ap, n, col0=0):
            b.dma("sp", dbg_out[:, col0:col0 + n], ap)

        if CS >= 1:
            ada_layer(0)
            if DEBUG:
                dbg(mods[0][:, :], 72)
                dbg(Acoef[:, 0:24], 24, 72)
                dbg(Gcoef[:, 0:24], 24, 96)
        if CS >= 2:
            pre_norm(0, 0)
            if DEBUG:
                dbg(rstd[:, :], 512, 512)
        if CS >= 3:
            ffn(ffn_w_gu[0, 0], ffn_w_d[0, 0])
            if DEBUG:
                dbg(y[:, 0, :], 512, 1024)
        if CS >= 4:
            post_norm_residual(0, 0)
        if CS >= 5:
            pre_norm(0, 1)
            proj_tokmajor(a_w_qkv, 1024, 256, k_out)
            proj_tokmajor(a_w_qkv, 1280, 256, v_out, vdst_col0=0)
        if CS >= 6:
            proj_featmajor(qT, 0, a_w_qkv, 0, 8, h)
            proj_featmajor(kT, 0, a_w_qkv, 1024, 4, h, dup64=True)
            hm = {hh: (hh // 2, 64 * (hh % 2), hh // 4, (hh // 4) * 64) for hh in range(16)}
            if SUB >= 2:
                attn_ctx(16, hm, True, 0.125)
            if DEBUG and SUB >= 2:
                b.op("dve", "tensor_copy", out=tmpA[0][:, :], in_=oT[:, 0, :])
                dbg(tmpA[0][:, :], 512, 1536)
            if SUB >= 1:
                proj_out_featmajor(a_w_o, oT)
                post_norm_residual(0, 1)
        if CS >= 7:
            pre_norm(0, 2)
            ffn(ffn_w_gu[0, 1], ffn_w_d[0, 1])
            post_norm_residual(0, 2)
        if CS >= 8:
            ada_layer(1)
            pre_norm(1, 0)
            ffn(ffn_w_gu[1, 0], ffn_w_d[1, 0])
            post_norm_residual(1, 0)
        if CS >= 9:
            pre_norm(1, 1)
            s5_es = ExitStack()
            b.es = s5_es
            S5P = s5_setup()
            s5_mixer(S5P)
            b.es = es
            if DEBUG:
                dbg(y[:, 0, :], 512, 2048)
            post_norm_residual(1, 1)
        if CS >= 10:
            pre_norm(1, 2)
            ffn(ffn_w_gu[1, 1], ffn_w_d[1, 1])
            post_norm_residual(1, 2)
        if CS >= 11:
            ada_layer(2)
            pre_norm(2, 0)
            ffn(ffn_w_gu[2, 0], ffn_w_d[2, 0])
            post_norm_residual(2, 0)
            pre_norm(2, 1)
            proj_tokmajor(na_w_qkv, 1024, 1024, nak_out)
            proj_tokmajor(na_w_qkv, 2048, 1024, nav_out, vdst_col0=0)
            proj_featmajor(qT, 0, na_w_qkv, 0, 8, h)
            proj_featmajor(kT, 0, na_w_qkv, 1024, 8, h)
            hm2 = {hh: (hh // 2, 64 * (hh % 2), hh // 2, hh * 64) for hh in range(16)}
            attn_ctx(16, hm2, None, 0.125)
            proj_out_featmajor(na_w_o, oT)
            post_norm_residual(2, 1)
            pre_norm(2, 2)
            ffn(ffn_w_gu[2, 1], ffn_w_d[2, 1])
            post_norm_residual(2, 2)
        if CS >= 12:
            ada_layer(3)
            pre_norm(3, 0)
            ffn(ffn_w_gu[3, 0], ffn_w_d[3, 0])
            post_norm_residual(3, 0)
        if CS >= 13:
            pre_norm(3, 1)
            b.fence()
            s5_es.close()
            att_es.close()
            dn_es = ExitStack()
            b.es = dn_es
            dn_mixer()
            b.es = es
            if DEBUG:
                dbg(y[:, 0, :], 512, 2560)
            post_norm_residual(3, 1)
            pre_norm(3, 2)
            ffn(ffn_w_gu[3, 1], ffn_w_d[3, 1])
            post_norm_residual(3, 2)
        b.dma("sp", yT_out.rearrange("(c p) t -> p c t", p=128), x[:])
        if STAGE >= 20:
            b.fence()
            if CTX_SKIP:
                att_es.close()
            else:
                dn_es.close()
            load_rows_T(cc[:, :], c_lat, 8)
            b.op("act", "activation", out=scond_lat[:, :, 0], in_=cc[:, :], func=AF.Silu)
            lat_es = ExitStack()
            b.es = lat_es
            ada_layer(0, scond_lat)
            a_lat(xT_lat)
            b.es = es
            b.fence()
            lat_es.close()
            if STAGE >= 21:
                lat_es = ExitStack()
                b.es = lat_es
                ada_layer(1, scond_lat)
                s5_lat()
                b.es = es
                b.fence()
                lat_es.close()
            if STAGE >= 22:
                lat_es = ExitStack()
                b.es = lat_es
                ada_layer(2, scond_lat)
                na_lat()
                b.es = es
                b.fence()
                lat_es.close()
            if STAGE >= 23:
                lat_es = ExitStack()
                b.es = lat_es
                ada_layer(3, scond_lat)
                dn_lat(zres)
                b.es = es
                b.fence()
                lat_es.close()
        if STAGE >= 20:
            for tl in range(NTL):
                lat_load(zres, tl)
                lat_store(ysamp_out, tl)
        else:
            b.raw("dve", lambda: nc.vector.memset(tmpA[0][:, :], 0.0), writes=[tmpA[0][:, :]])
            for tl in range(NTL):
                for c in range(KC):
                    b.dma("sp", ysamp_out[c * 128:(c + 1) * 128, tl * 512:(tl + 1) * 512], tmpA[0][:, :])
        if STAGE < 13:
            for j in range(32):
                b.dma("sp", dn_out[j, :, :], tmpA[0][:, 0:128])

        b.finish()
        if STAGE >= 20:
            pass
        elif STAGE >= 13:
            dn_es.close()
        else:
            if CS >= 9:
                s5_es.close()
            att_es.close()
    return nc


_PROG = None


def _dn_masks():
    i = np.arange(64)
    bef0 = (i[:, None] <= i[None, :]).astype(np.float32)
    bef1 = (i[:, None] >= i[None, :]).astype(np.float32)
    eye = np.eye(64, dtype=np.float32)
    m = np.stack([bef0, bef1, bef0.T, bef1.T, bef0.T - eye, bef1.T - eye], 1)
    return np.ascontiguousarray(m, dtype=np.float32)


def _rope_tables(L):
    n = 16
    inv = (np.float32(10000.0) ** (-np.arange(n, dtype=np.float32) / np.float32(n))).astype(np.float32)
    t = np.arange(L)
    ang_r = (t // 64).astype(np.float32)[:, None] * inv[None, :]
    ang_c = (t % 64).astype(np.float32)[:, None] * inv[None, :]
    return np.ascontiguousarray(np.concatenate([np.cos(ang_r), np.cos(ang_c), np.sin(ang_r), np.sin(ang_c)], 1), dtype=np.float32)


def _na_bias_gather(rpb):
    q = np.arange(64)[:, None]
    k = np.arange(64)[None, :]
    dc = np.clip(k - q, -15, 15) + 15
    g = rpb[:, :, dc]
    return np.ascontiguousarray(g.transpose(0, 2, 1, 3).reshape(16, 64, 960), dtype=np.float32)


def _na_colmask():
    col = np.arange(64)
    cs = np.clip(col - 8, 0, 48)
    ok = (col[None, :] >= cs[:, None]) & (col[None, :] < cs[:, None] + 16)
    return np.where(ok, 0.0, -30000.0).astype(np.float32)


def _band_mask():
    q = np.arange(128)[:, None]
    j = np.arange(384)[None, :] - 128
    return np.where(np.abs(j - q) <= 128, 0.0, -30000.0).astype(np.float32)


def prep_core(inputs, r, lseq=None):
    f = lambda a: np.ascontiguousarray(np.asarray(a, dtype=np.float32))
    L = LSEQ if lseq is None else lseq
    bsel = r % 2
    return {
        "xT_ctx": np.ascontiguousarray(f(inputs["x_prompt"])[2 * r:2 * r + 2].reshape(TCTX, D).T),
        "xT_lat": np.ascontiguousarray(f(inputs["x_sample"])[bsel, :L].T),
        "c_lat": f(inputs["c"])[bsel].reshape(8, 128),
        "cache_ak": f(inputs["cache_attn_k"])[bsel, 0].reshape(512, 256),
        "cache_av": f(inputs["cache_attn_v"])[bsel, 0].reshape(512, 256),
        "rope_cs": _rope_tables(L),
        "st5_re": f(inputs["state_s5_re"])[bsel, 0].reshape(64, 128),
        "st_dn": f(inputs["state_dn"])[bsel, 0].reshape(16, 128, 128),
        "cache_nk": f(inputs["cache_na_k"])[bsel, 0].reshape(512, D),
        "cache_nv": f(inputs["cache_na_v"])[bsel, 0].reshape(512, D),
        "na_bias": _na_bias_gather(f(inputs["na_rpb"])[0]),
        "na_cmask": _na_colmask(),
        "st5_im": f(inputs["state_s5_im"])[bsel, 0].reshape(64, 128),
        "band_mask": _band_mask(),
    }


def prep_shared(inputs, nl=4):
    f = lambda a: np.ascontiguousarray(np.asarray(a, dtype=np.float32))
    return {
        "ident": np.eye(128, dtype=np.float32),
        "c_ctx": f(inputs["c_ctx"]).reshape(8, 128),
        "norm_g": f(inputs["norm_g"]).reshape(192, 128),
        "w_ada": f(inputs["w_ada"][0:nl]),
        "b_ada": f(inputs["b_ada"]).reshape(288, 128),
        "ffn_w_gu": f(inputs["ffn_w_gu"][0:nl]),
        "ffn_w_d": f(inputs["ffn_w_d"][0:nl]),
        "a_w_qkv": f(inputs["a_w_qkv"])[0],
        "a_w_o": f(inputs["a_w_o"])[0],
        "a_sink": f(inputs["a_sink"]).reshape(1, 16),
        "s5_lam_re": f(inputs["s5_lam_re"]).reshape(64, 128),
        "s5_lam_im": f(inputs["s5_lam_im"]).reshape(64, 128),
        "s5_logdt": f(np.repeat(np.asarray(inputs["s5_log_dt"], np.float32).reshape(2, 32, 2), 64, axis=-1)).reshape(64, 128),
        "s5_b_re": f(inputs["s5_b_re"])[0],
        "s5_b_im": f(inputs["s5_b_im"])[0],
        "s5_c_re": f(inputs["s5_c_re"])[0],
        "s5_c_im": f(inputs["s5_c_im"])[0],
        "s5_d": f(inputs["s5_d"]).reshape(8, 128),
        "s5_w_glu": f(inputs["s5_w_glu"])[0],
        "na_w_qkv": f(inputs["na_w_qkv"])[0],
        "na_w_o": f(inputs["na_w_o"])[0],
        "dn_w_in": f(inputs["dn_w_in"])[0],
        "dn_conv_w": f(inputs["dn_conv_w"]).reshape(5, 16, 128).reshape(80, 128),
        "dn_w_ba": f(inputs["dn_w_ba"])[0],
        "dn_a_log": f(inputs["dn_a_log"]).reshape(1, 16),
        "dn_dt_bias": f(inputs["dn_dt_bias"]).reshape(1, 16),
        "dn_out_g": f(inputs["dn_out_g"]).reshape(1, 128),
        "dn_w_o": f(inputs["dn_w_o"])[0],
        "dn_mask": _dn_masks(),
    }


def kernel(**inputs):
    global _PROG
    f = lambda a: np.ascontiguousarray(np.asarray(a, dtype=np.float32))
    x_prompt = f(inputs["x_prompt"])
    if _PROG is None:
        _PROG = build_program()
    nc = _PROG
    shared = prep_shared(inputs)
    in_maps = []
    for r in range(NCORES):
        m = dict(shared)
        m.update(prep_core(inputs, r))
        in_maps.append(m)
    res = run_bass_kernel_spmd(nc, in_maps, core_ids=list(range(NCORES)))
    R = res.results
    y_prompt = np.stack([R[r]["out_yT_ctx"].T.reshape(2, SEQ, D) for r in range(NCORES)]).reshape(16, SEQ, D)
    new_k = np.concatenate([R[r]["out_attn_k"].reshape(2, 1, SEQ, 4, 64) for r in range(NCORES)], 0)
    new_v = np.concatenate([R[r]["out_attn_v"].reshape(2, 1, SEQ, 4, 64) for r in range(NCORES)], 0)
    s5re = np.concatenate([R[r]["out_s5_re"].reshape(2, 1, 2, 64, 64) for r in range(NCORES)], 0)
    s5im = np.concatenate([R[r]["out_s5_im"].reshape(2, 1, 2, 64, 64) for r in range(NCORES)], 0)
    nak = np.concatenate([R[r]["out_na_k"].reshape(2, 1, SEQ, 16, 64) for r in range(NCORES)], 0)
    nav = np.concatenate([R[r]["out_na_v"].reshape(2, 1, SEQ, 16, 64) for r in range(NCORES)], 0)
    ysamp = np.stack([R[r]["out_y_sample"].T for r in range(2)], 0)
    dn = np.concatenate([R[r]["out_dn"].reshape(2, 1, 2, 8, 128, 128) for r in range(NCORES)], 0)
    c32 = lambda a: np.ascontiguousarray(a, dtype=np.float32)
    return (c32(y_prompt), c32(ysamp), c32(new_k), c32(new_v), c32(s5re), c32(s5im), c32(nak), c32(nav), c32(dn))
```

```python
import numpy as np
from contextlib import ExitStack
import concourse.bass as bass
import concourse.mybir as mybir
from concourse.bass_utils import run_bass_kernel_spmd

F32 = mybir.dt.float32
BF16 = mybir.dt.bfloat16
AF = mybir.ActivationFunctionType
ALU = mybir.AluOpType
AX = mybir.AxisListType

D = 1024
KC = 8
DFF = 2816
FC = 22
NCORES = 8
TCTX = 512
SEQ = 256
EPS = 1e-6
WSLOT = 6144
NWS = 2
STAGE = 99
LSEQ = 4096
CTX_SKIP = False
SUB = 99
DEBUG = False


class Eng:
    def __init__(self, name, h, sem):
        self.name, self.h, self.sem = name, h, sem
        self.count = 0
        self.waited = {}


class Builder:
    def __init__(self, nc, es):
        self.nc, self.es = nc, es
        self.es_sem = es
        self.sems = []
        self.engs = {}
        for nm, h in [("pe", nc.tensor), ("act", nc.scalar), ("dve", nc.vector),
                      ("pool", nc.gpsimd), ("sp", nc.sync)]:
            sem = es.enter_context(nc.semaphore("sem_" + nm))
            self.sems.append(sem)
            e = Eng(nm, h, sem)
            e.key = len(self.sems) - 1
            self.engs[nm] = e
        self.track = {}
        self.dsem = {}
        self.out_events = []

    def sb(self, name, shape, dt):
        return self.es.enter_context(self.nc.sbuf_tensor(name, list(shape), dt))

    def ps(self, name, shape, dt=F32):
        return self.es.enter_context(self.nc.psum_tensor(name, list(shape), dt))

    def _nm(self, a):
        return a if isinstance(a, str) else a.tensor.name

    def _deps(self, reads, writes, skip_own_waw=None, own=None):
        deps = set()
        for r in reads:
            nm = self._nm(r)
            st = self.track.get(nm)
            if st and st[0]:
                deps.add(st[0])
            if st and nm.startswith("ps"):
                for ev in st[1]:
                    if ev[0] != own:
                        deps.add(ev)
        for w in writes:
            st = self.track.get(self._nm(w))
            if st:
                if st[0] and not (skip_own_waw is not None and st[0][0] == skip_own_waw):
                    deps.add(st[0])
                for ev in st[1]:
                    deps.add(ev)
        return deps

    def _wait(self, e, deps):
        best = {}
        for (k, v) in deps:
            if best.get(k, 0) < v:
                best[k] = v
        for k, v in best.items():
            if e.waited.get(k, 0) < v:
                e.h.wait_ge(self.sems[k], v)
                e.waited[k] = v

    def _commit(self, ev, reads, writes, accum=False):
        for r in reads:
            st = self.track.setdefault(self._nm(r), [None, []])
            st[1].append(ev)
        for w in writes:
            nm = self._nm(w)
            if accum and nm in self.track:
                self.track[nm][0] = ev
            else:
                self.track[nm] = [ev, []]

    def op(self, eng, fname, accum=False, extra_reads=(), extra_writes=(), **kw):
        e = self.engs[eng]
        reads, writes = list(extra_reads), list(extra_writes)
        for k, v in kw.items():
            if isinstance(v, bass.AP):
                if k in ("out", "accum_out"):
                    writes.append(v)
                else:
                    reads.append(v)
        deps = self._deps(reads, writes, skip_own_waw=(e.key if accum else None), own=e.key)
        self._wait(e, deps)
        inst = getattr(e.h, fname)(**kw)
        e.count += 1
        inst.then_inc(e.sem, 1)
        ev = (e.key, e.count)
        self._commit(ev, reads, writes, accum=accum)
        return ev

    def raw(self, eng, fn, reads=(), writes=()):
        e = self.engs[eng]
        self._wait(e, self._deps(list(reads), list(writes), own=e.key))
        inst = fn()
        e.count += 1
        inst.then_inc(e.sem, 1)
        ev = (e.key, e.count)
        self._commit(ev, list(reads), list(writes))
        return ev

    def dma(self, q, out, in_, **kw):
        e = self.engs[q]
        deps = self._deps([in_], [out])
        self._wait(e, deps)
        nm = self._nm(out)
        if nm not in self.dsem:
            sem = self.es_sem.enter_context(self.nc.semaphore("dsem_" + nm))
            self.sems.append(sem)
            self.dsem[nm] = [len(self.sems) - 1, 0]
        ds = self.dsem[nm]
        e.h.dma_start(out=out, in_=in_, **kw).then_inc(self.sems[ds[0]], 16)
        ds[1] += 16
        ev = (ds[0], ds[1])
        self._commit(ev, [in_], [out])
        if out.tensor.name.startswith("out_"):
            self.out_events.append(ev)
        return ev

    def fence(self):
        evs = {(x.key, x.count) for x in self.engs.values() if x.count > 0}
        evs |= {(k, c) for (k, c) in self.dsem.values() if c > 0}
        for e in self.engs.values():
            self._wait(e, {ev for ev in evs if ev[0] != e.key})

    def finish(self):
        e = self.engs["sp"]
        self._wait(e, set(self.out_events))
        self._wait(e, {(x.key, x.count) for x in self.engs.values() if x.count > 0 and x.name != "sp"})


def build_program(nl=4):
    nc = bass.Bass("TRN2", target_bir_lowering=False)

    def din(name, shape):
        return nc.dram_tensor(name, list(shape), F32, kind="ExternalInput").ap()

    def dout(name, shape):
        return nc.dram_tensor("out_" + name, list(shape), F32, kind="ExternalOutput").ap()

    T = TCTX
    xT_in = din("xT_ctx", [D, T])
    ident_in = din("ident", [128, 128])
    c_ctx = din("c_ctx", [8, 128])
    norm_g = din("norm_g", [192, 128])
    w_ada = din("w_ada", [nl, D, 9 * D])
    b_ada = din("b_ada", [288, 128])
    ffn_w_gu = din("ffn_w_gu", [nl, 2, D, 2 * DFF])
    ffn_w_d = din("ffn_w_d", [nl, 2, DFF, D])
    a_w_qkv = din("a_w_qkv", [D, 1536])
    a_w_o = din("a_w_o", [D, D])
    a_sink = din("a_sink", [1, 16])
    s5_lam_re = din("s5_lam_re", [64, 128])
    s5_lam_im = din("s5_lam_im", [64, 128])
    s5_logdt = din("s5_logdt", [64, 128])
    s5_b_re = din("s5_b_re", [2, 64, 64, 16])
    s5_b_im = din("s5_b_im", [2, 64, 64, 16])
    s5_c_re = din("s5_c_re", [2, 64, 16, 64])
    s5_c_im = din("s5_c_im", [2, 64, 16, 64])
    s5_d = din("s5_d", [8, 128])
    s5_w_glu = din("s5_w_glu", [D, 2 * D])
    na_w_qkv = din("na_w_qkv", [D, 3 * D])
    na_w_o = din("na_w_o", [D, D])
    dn_w_in = din("dn_w_in", [D, 3 * D])
    dn_conv_w = din("dn_conv_w", [80, 128])
    dn_w_ba = din("dn_w_ba", [2, D, 16])
    dn_a_log = din("dn_a_log", [1, 16])
    dn_dt_bias = din("dn_dt_bias", [1, 16])
    dn_out_g = din("dn_out_g", [1, 128])
    dn_w_o = din("dn_w_o", [D, D])
    dn_mask = din("dn_mask", [64, 6, 64])
    xT_lat = din("xT_lat", [D, LSEQ])
    c_lat = din("c_lat", [8, 128])
    cache_ak = din("cache_ak", [512, 256])
    cache_av = din("cache_av", [512, 256])
    rope_cs = din("rope_cs", [LSEQ, 64])
    band_mask = din("band_mask", [128, 384])
    zres = nc.dram_tensor("zres", [D, LSEQ], F32, kind="Internal").ap()
    wsc = nc.dram_tensor("wsc", [8, 15, 128, WSLOT], BF16, kind="Internal").ap()
    ysc = nc.dram_tensor("ysc", [D, LSEQ], F32, kind="Internal").ap()
    rot = nc.dram_tensor("rot", [64, 128, 1536], F32, kind="Internal").ap()
    cache_nk = din("cache_nk", [512, D])
    cache_nv = din("cache_nv", [512, D])
    na_bias = din("na_bias", [16, 64, 960])
    na_cmask = din("na_cmask", [64, 64])
    st_dn = din("st_dn", [16, 128, 128])
    pjsc = nc.dram_tensor("pjsc", [3, 128, LSEQ + 4], F32, kind="Internal").ap()
    osc = nc.dram_tensor("osc", [2, 64, LSEQ // 64, 128], F32, kind="Internal").ap()
    st5_re = din("st5_re", [64, 128])
    st5_im = din("st5_im", [64, 128])

    yT_out = dout("yT_ctx", [D, T])
    k_out = dout("attn_k", [T, 256])
    v_out = dout("attn_v", [T, 256])
    nak_out = dout("na_k", [T, D])
    nav_out = dout("na_v", [T, D])
    ysamp_out = dout("y_sample", [D, LSEQ])
    dn_out = dout("dn", [32, 128, 128])
    s5re_out = dout("s5_re", [128, 128])
    s5im_out = dout("s5_im", [128, 128])
    dbg_out = dout("dbg", [128, 4096]) if DEBUG else None

    with ExitStack() as es:
        b = Builder(nc, es)
        x = b.sb("x", [128, KC, T], F32)
        h = b.sb("h", [128, KC, T], BF16)
        y = b.sb("y", [128, KC, T], F32)
        act = b.sb("act", [128, FC, T], BF16)
        sq = act
        rstd = b.sb("rstd", [128, T], F32)
        tmpA = [b.sb(f"tmpA{i}", [128, T], F32) for i in range(2)]
        ident = b.sb("ident_sb", [128, 128], F32)
        ident_bf = b.sb("ident_bf", [128, 128], BF16)
        ones_bf = b.sb("ones_bf", [128, 128], BF16)
        wslots = [b.sb(f"wslot{i}", [128, WSLOT], BF16) for i in range(NWS)]
        ng = b.sb("ng", [128, 192], F32)
        bada = b.sb("bada", [128, 288], F32)
        scond = b.sb("scond", [128, KC, 1], BF16)
        scond_lat = b.sb("scond_lat", [128, KC, 1], BF16)
        cc = b.sb("cc", [128, KC], F32)
        mods = [b.sb(f"mods{i}", [128, 72], F32) for i in range(4)]
        Acoef = b.sb("Acoef", [128, 4 * 3 * KC], F32)
        Gcoef = b.sb("Gcoef", [128, 4 * 3 * KC], F32)
        rows_tmp = b.sb("rows_tmp", [96, 128], F32)
        oT = b.sb("oT", [128, KC, T], BF16)
        sink_bc = b.sb("sink_bc", [128, 16], F32)
        st_m = [b.sb(f"st_m{i}", [128, 8], F32) for i in range(2)]
        att_es = ExitStack()
        b.es = att_es
        qT = b.sb("qT", [128, 8, T], BF16)
        kT = b.sb("kT", [128, 8, T], BF16)
        vtok = b.sb("vtok", [128, 4, 1024], BF16)
        kv32 = [b.sb(f"kv32_{i}", [128, 512], F32) for i in range(2)]
        Pm = [b.sb(f"Pm{i}", [128, 256], BF16) for i in range(2)]
        PT = [b.sb(f"PT{i}", [128, 2, 128], BF16) for i in range(2)]
        otok = b.sb("otok", [128, 1024], BF16)
        b.es = es
        psG = [b.ps(f"psG{i}", [128, 512]) for i in range(2)]
        psU = [b.ps(f"psU{i}", [128, 512]) for i in range(2)]
        psY = [b.ps(f"psY{i}", [128, 512]) for i in range(2)]
        psS = b.ps("psS", [128, 512])
        psM = psS
        psT_all = b.ps("psT", [128, 1024], BF16)
        psTb = [psT_all[:, i * 256:(i + 1) * 256].rearrange("p (a n) -> p a n", a=2) for i in range(2)]
        psTc = [psTb[0], psS[:, :].bitcast(BF16)[:, 0:256].rearrange("p (a n) -> p a n", a=2)]

        def run_interleaved(gens):
            alive = [True] * len(gens)
            while any(alive):
                for gi_ in range(len(gens)):
                    if alive[gi_]:
                        try:
                            next(gens[gi_])
                        except StopIteration:
                            alive[gi_] = False

        wctr = [0]

        def wslot():
            s = wslots[wctr[0] % NWS]
            wctr[0] += 1
            return s

        b.dma("sp", ident[:], ident_in)
        b.op("dve", "tensor_copy", out=ident_bf[:], in_=ident[:])
        b.raw("dve", lambda: nc.vector.memset(ones_bf[:], 1.0), writes=[ones_bf[:]])

        def load_rows_T(dst_ap, src_rows_ap, nrows):
            b.dma("sp", rows_tmp[0:nrows, :], src_rows_ap)
            b.op("pe", "transpose", out=psM[:, 0:nrows], in_=rows_tmp[0:nrows, :], identity=ident[0:nrows, 0:nrows])
            b.op("dve", "tensor_copy", out=dst_ap, in_=psM[:, 0:nrows])

        load_rows_T(ng[:, 0:96], norm_g[0:96, :], 96)
        load_rows_T(ng[:, 96:192], norm_g[96:192, :], 96)
        for i in range(3):
            load_rows_T(bada[:, 96 * i:96 * (i + 1)], b_ada[96 * i:96 * (i + 1), :], 96)
        load_rows_T(cc[:, :], c_ctx, 8)
        b.op("act", "activation", out=scond[:, :, 0], in_=cc[:, :], func=AF.Silu)
        b.dma("sp", sink_bc[:], a_sink.broadcast_to([128, 16]))

        b.dma("sp", x[:], xT_in.rearrange("(c p) t -> p c t", p=128))

        def ada_layer(i, sc=None):
            sc = scond if sc is None else sc
            for pc in range(18):
                sl = wslot()
                v = sl[:, 0:KC * 512].rearrange("p (k n) -> p k n", k=KC)
                b.dma("pool", v, w_ada[i, :, pc * 512:(pc + 1) * 512].rearrange("(k p) n -> p k n", p=128))
                for cj in range(4):
                    j = pc * 4 + cj
                    for k in range(KC):
                        b.op("pe", "matmul", accum=(k > 0), out=psM[:, j:j + 1],
                             lhsT=v[:, k, cj * 128:(cj + 1) * 128], rhs=sc[:, k, :],
                             start=(k == 0), stop=(k == KC - 1))
            b.op("dve", "tensor_tensor", out=mods[i][:, :], in0=psM[:, 0:72], in1=bada[:, 72 * i:72 * (i + 1)],
                 op=ALU.add)
            for s in range(3):
                w = 1.0 if s == 1 else 0.5
                col = (i * 3 + s) * KC
                gpre = ng[:, (i * 6 + 2 * s) * KC:(i * 6 + 2 * s + 1) * KC]
                gpost = ng[:, (i * 6 + 2 * s + 1) * KC:(i * 6 + 2 * s + 2) * KC]
                scale_ = mods[i][:, (3 * s + 1) * KC:(3 * s + 2) * KC]
                gate_ = mods[i][:, (3 * s + 2) * KC:(3 * s + 3) * KC]
                b.op("dve", "scalar_tensor_tensor", out=Acoef[:, col:col + KC], in0=scale_, scalar=1.0, in1=gpre,
                     op0=ALU.add, op1=ALU.mult)
                b.op("dve", "scalar_tensor_tensor", out=Gcoef[:, col:col + KC], in0=gate_, scalar=w, in1=gpost,
                     op0=ALU.mult, op1=ALU.mult)

        def rms_stats(src):
            for c in range(KC):
                b.op("act", "activation", out=sq[:, c, :], in_=src[:, c, :], func=AF.Square)
            for c in range(KC):
                b.op("pe", "matmul", accum=(c > 0), out=psS[:, 0:T], lhsT=ones_bf[:, :], rhs=sq[:, c, :],
                     start=(c == 0), stop=(c == KC - 1))
            b.op("act", "activation", out=rstd[:, :], in_=psS[:, 0:T], func=AF.Sqrt, scale=1.0 / D, bias=EPS)
            b.op("dve", "reciprocal", out=rstd[:, :], in_=rstd[:, :])

        def pre_norm(i, s):
            rms_stats(x)
            col = (i * 3 + s) * KC
            for c in range(KC):
                t = tmpA[c % 2]
                b.op("dve", "tensor_tensor", out=t[:, :], in0=x[:, c, :], in1=rstd[:, :], op=ALU.mult)
                b.op("act", "activation", out=h[:, c, :], in_=t[:, :], func=AF.Identity,
                     scale=Acoef[:, col + c:col + c + 1], bias=mods[i][:, 3 * s * KC + c:3 * s * KC + c + 1])

        def post_norm_residual(i, s):
            rms_stats(y)
            col = (i * 3 + s) * KC
            for c in range(KC):
                t = tmpA[c % 2]
                b.op("dve", "tensor_tensor", out=t[:, :], in0=y[:, c, :], in1=rstd[:, :], op=ALU.mult)
                b.op("dve", "tensor_scalar", out=t[:, :], in0=t[:, :], scalar1=Gcoef[:, col + c:col + c + 1],
                     scalar2=None, op0=ALU.mult)
                b.op("dve", "tensor_tensor", out=x[:, c, :], in0=t[:, :], in1=x[:, c, :], op=ALU.add)

        def ffn(wgu, wd, key=None, consume=False):
            for jp in range(FC // 2):
                sl = wslot()
                v = sl[:, 0:KC * 512].rearrange("p (k n) -> p k n", k=KC)
                if consume:
                    b.dma("pool", sl[:, 0:KC * 512], wsc[key, jp, :, 0:KC * 512])
                else:
                    b.dma("pool", v[:, :, 0:256], wgu[:, jp * 256:(jp + 1) * 256].rearrange("(k p) n -> p k n", p=128))
                    b.dma("pool", v[:, :, 256:512],
                          wgu[:, DFF + jp * 256:DFF + (jp + 1) * 256].rearrange("(k p) n -> p k n", p=128))
                    if key is not None:
                        b.dma("sp", wsc[key, jp, :, 0:KC * 512], sl[:, 0:KC * 512])
                for jj in range(2):
                    j = jp * 2 + jj
                    pg, pu = psG[j % 2], psU[j % 2]
                    for k in range(KC):
                        b.op("pe", "matmul", accum=(k > 0), out=pg[:, 0:T], lhsT=v[:, k, jj * 128:(jj + 1) * 128],
                             rhs=h[:, k, :], start=(k == 0), stop=(k == KC - 1))
                    for k in range(KC):
                        b.op("pe", "matmul", accum=(k > 0), out=pu[:, 0:T],
                             lhsT=v[:, k, 256 + jj * 128:256 + (jj + 1) * 128],
                             rhs=h[:, k, :], start=(k == 0), stop=(k == KC - 1))
                    t = tmpA[j % 2]
                    b.op("act", "activation", out=t[:, :], in_=pg[:, 0:T], func=AF.Silu)
                    b.op("dve", "tensor_tensor", out=act[:, j, :], in0=t[:, :], in1=pu[:, 0:T], op=ALU.mult)
            for op_ in range(4):
                sl = wslot()
                v = sl[:, 0:FC * 256].rearrange("p (k n) -> p k n", k=FC)
                if consume:
                    b.dma("pool", sl[:, 0:FC * 256], wsc[key, 11 + op_, :, 0:FC * 256])
                else:
                    b.dma("pool", v, wd[:, op_ * 256:(op_ + 1) * 256].rearrange("(k p) n -> p k n", p=128))
                    if key is not None:
                        b.dma("sp", wsc[key, 11 + op_, :, 0:FC * 256], sl[:, 0:FC * 256])
                for oo in range(2):
                    o = op_ * 2 + oo
                    py = psY[o % 2]
                    for j in range(FC):
                        b.op("pe", "matmul", accum=(j > 0), out=py[:, 0:T], lhsT=v[:, j, oo * 128:(oo + 1) * 128],
                             rhs=act[:, j, :], start=(j == 0), stop=(j == FC - 1))
                    b.op("act", "activation", out=y[:, o, :], in_=py[:, 0:T], func=AF.Copy)

        def proj_featmajor(dst, dst_chunk0, wsrc, col0, nchunks, src_act, dup64=False):
            for g0 in range(0, nchunks, 4):
                ng_ = min(4, nchunks - g0)
                sl = wslot()
                v = sl[:, 0:KC * 512].rearrange("p (k n) -> p k n", k=KC)
                if dup64:
                    sl0 = wslot()
                    v0 = sl0[:, 0:KC * 256].rearrange("p (k n) -> p k n", k=KC)
                    b.dma("pool", v0[:, :, 0:ng_ * 64],
                          wsrc[:, col0 + g0 * 64:col0 + (g0 + ng_) * 64].rearrange("(k p) n -> p k n", p=128))
                    v4 = sl[:, 0:KC * 512].rearrange("p (k m a d) -> p k m a d", k=KC, m=4, a=2)
                    for a in range(2):
                        b.op("dve", "tensor_copy", out=v4[:, :, 0:ng_, a, :],
                             in_=v0[:, :, 0:ng_ * 64].rearrange("p k (m d) -> p k m d", d=64))
                else:
                    b.dma("pool", v[:, :, 0:ng_ * 128],
                          wsrc[:, col0 + g0 * 128:col0 + (g0 + ng_) * 128].rearrange("(k p) n -> p k n", p=128))
                for m in range(ng_):
                    pq = psG[m % 2]
                    for k in range(KC):
                        b.op("pe", "matmul", accum=(k > 0), out=pq[:, 0:T], lhsT=v[:, k, m * 128:(m + 1) * 128],
                             rhs=src_act[:, k, :], start=(k == 0), stop=(k == KC - 1))
                    b.op("act", "activation", out=dst[:, dst_chunk0 + g0 + m, :], in_=pq[:, 0:T], func=AF.Copy)

        def proj_out_featmajor(wsrc, src_act):
            for g0 in range(0, KC, 4):
                sl = wslot()
                v = sl[:, 0:KC * 512].rearrange("p (k n) -> p k n", k=KC)
                b.dma("pool", v, wsrc[:, g0 * 128:(g0 + 4) * 128].rearrange("(k p) n -> p k n", p=128))
                for m in range(4):
                    py = psY[m % 2]
                    for k in range(KC):
                        b.op("pe", "matmul", accum=(k > 0), out=py[:, 0:T], lhsT=v[:, k, m * 128:(m + 1) * 128],
                             rhs=src_act[:, k, :], start=(k == 0), stop=(k == KC - 1))
                    b.op("act", "activation", out=y[:, g0 + m, :], in_=py[:, 0:T], func=AF.Copy)

        def proj_tokmajor(wsrc, col0, ncols, out_dram, vdst_col0=None):
            for c0 in range(0, ncols, 512):
                n = min(512, ncols - c0)
                sl = wslot()
                v = sl[:, 0:KC * 512].rearrange("p (k n) -> p k n", k=KC)
                b.dma("pool", v[:, :, 0:n], wsrc[:, col0 + c0:col0 + c0 + n].rearrange("(k p) n -> p k n", p=128))
                for tb in range(T // 128):
                    pq = psU[tb % 2]
                    for k in range(KC):
                        b.op("pe", "matmul", accum=(k > 0), out=pq[:, 0:n], lhsT=h[:, k, tb * 128:(tb + 1) * 128],
                             rhs=v[:, k, 0:n], start=(k == 0), stop=(k == KC - 1))
                    t32 = kv32[tb % 2]
                    b.op("dve", "tensor_copy", out=t32[:, 0:n], in_=pq[:, 0:n])
                    if out_dram is not None:
                        b.dma("sp", out_dram[tb * 128:(tb + 1) * 128, c0:c0 + n], t32[:, 0:n])
                    if vdst_col0 is not None:
                        b.op("act", "activation", out=vtok[:, tb, vdst_col0 + c0:vdst_col0 + c0 + n], in_=t32[:, 0:n],
                             func=AF.Copy)

        def attn_ctx(nheads, head_map, sink_cols, scale):
            nseq = T // SEQ
            nqb = SEQ // 128
            it = 0
            for s in range(nseq):
                for qb in range(nqb):
                    q0 = s * SEQ + qb * 128
                    for hh in range(nheads):
                        qc, base, kc, vc0 = head_map[hh]
                        pS = psG[it % 2]
                        b.op("pe", "matmul", out=pS[:, 0:SEQ], lhsT=qT[base:base + 64, qc, q0:q0 + 128],
                             rhs=kT[base:base + 64, kc, s * SEQ:(s + 1) * SEQ], start=True, stop=True)
                        sm = st_m[it % 2]
                        b.op("dve", "reduce_max", out=sm[:, 0:1], in_=pS[:, 0:SEQ], axis=AX.X)
                        if sink_cols is not None:
                            b.op("dve", "tensor_scalar", out=sm[:, 1:2], in0=sm[:, 0:1], scalar1=scale,
                                 scalar2=sink_bc[:, hh:hh + 1], op0=ALU.mult, op1=ALU.max)
                            b.op("dve", "tensor_scalar", out=sm[:, 1:2], in0=sm[:, 1:2], scalar1=-1.0, scalar2=None,
                                 op0=ALU.mult)
                        else:
                            b.op("dve", "tensor_scalar", out=sm[:, 1:2], in0=sm[:, 0:1], scalar1=-scale, scalar2=None,
                                 op0=ALU.mult)
                        P = Pm[it % 2]
                        b.op("act", "activation", out=P[:, 0:SEQ], in_=pS[:, 0:SEQ], func=AF.Exp, scale=scale,
                             bias=sm[:, 1:2], accum_out=sm[:, 2:3])
                        if sink_cols is not None:
                            b.op("act", "activation", out=sm[:, 3:4], in_=sink_bc[:, hh:hh + 1], func=AF.Exp,
                                 bias=sm[:, 1:2], scale=1.0)
                            b.op("dve", "tensor_tensor", out=sm[:, 4:5], in0=sm[:, 2:3], in1=sm[:, 3:4], op=ALU.add)
                            b.op("dve", "reciprocal", out=sm[:, 5:6], in_=sm[:, 4:5])
                        else:
                            b.op("dve", "reciprocal", out=sm[:, 5:6], in_=sm[:, 2:3])
                        pt_sb = PT[it % 2]
                        if SUB < 3:
                            it += 1
                            continue
                        for kb in range(nqb):
                            b.op("pe", "transpose", out=psTb[it % 2][:, kb, :], in_=P[:, kb * 128:(kb + 1) * 128],
                                 identity=ident_bf[:, :])
                        b.op("dve", "tensor_copy", out=pt_sb[:, :, :], in_=psTb[it % 2][:, :, :])
                        if SUB < 4:
                            it += 1
                            continue
                        pO = psY[it % 2]
                        for kb in range(nqb):
                            b.op("pe", "matmul", accum=(kb > 0), out=pO[:, 0:64], lhsT=pt_sb[:, kb, :],
                                 rhs=vtok[:, s * nqb + kb, vc0:vc0 + 64], start=(kb == 0), stop=(kb == nqb - 1))
                        b.op("dve", "tensor_scalar", out=otok[:, hh * 64:(hh + 1) * 64], in0=pO[:, 0:64],
                             scalar1=sm[:, 5:6], scalar2=None, op0=ALU.mult)
                        it += 1
                    for c in range(KC):
                        b.op("pe", "transpose", out=psTb[c % 2][:, 0, :], in_=otok[:, c * 128:(c + 1) * 128],
                             identity=ident_bf[:, :])
                        b.op("act", "activation", out=oT[:, c, q0:q0 + 128], in_=psTb[c % 2][:, 0, :], func=AF.Copy)


        def s5_setup(npow=8, pfx="s5p_"):
            P = {}
            def t64(name):
                P[name] = b.sb(pfx + name, [128, 64], F32)
                return P[name]
            for nm, src in (("lamre", s5_lam_re), ("lamim", s5_lam_im), ("logdt", s5_logdt)):
                t64(nm)
                load_rows_T(P[nm][:, :], src, 64)
            for nm in ("dt", "lr", "li", "mag", "sn", "cs", "ar", "ai", "fr", "fi", "t1", "t2", "t3", "kk"):
                t64(nm)
            dsk = b.sb(pfx + "dskip", [128, KC], F32)
            load_rows_T(dsk[:, :], s5_d, 8)
            P["dskip"] = dsk
            b.op("act", "activation", out=P["dt"][:, :], in_=P["logdt"][:, :], func=AF.Exp)
            b.op("dve", "tensor_tensor", out=P["lr"][:, :], in0=P["lamre"][:, :], in1=P["dt"][:, :], op=ALU.mult)
            b.op("dve", "tensor_tensor", out=P["li"][:, :], in0=P["lamim"][:, :], in1=P["dt"][:, :], op=ALU.mult)
            b.op("act", "activation", out=P["mag"][:, :], in_=P["lr"][:, :], func=AF.Exp)

            def sin_of(dst, ang, shift):
                b.op("dve", "tensor_scalar", out=P["t1"][:, :], in0=ang, scalar1=float(shift), scalar2=None, op0=ALU.add)
                b.raw("dve", lambda: nc.vector.memset(P["kk"][:, :], 0.0), writes=[P["kk"][:, :]])
                for j in range(1, 5):
                    b.op("dve", "tensor_scalar", out=P["t2"][:, :], in0=P["t1"][:, :],
                         scalar1=float((2 * j - 1) * np.pi), scalar2=None, op0=ALU.is_gt)
                    b.op("dve", "tensor_tensor", out=P["kk"][:, :], in0=P["kk"][:, :], in1=P["t2"][:, :], op=ALU.add)
                b.op("dve", "tensor_scalar", out=P["kk"][:, :], in0=P["kk"][:, :], scalar1=float(-2 * np.pi),
                     scalar2=None, op0=ALU.mult)
                b.op("dve", "tensor_tensor", out=P["t1"][:, :], in0=P["t1"][:, :], in1=P["kk"][:, :], op=ALU.add)
                b.op("act", "activation", out=dst, in_=P["t1"][:, :], func=AF.Sin)

            sin_of(P["sn"][:, :], P["li"][:, :], 0.0)
            sin_of(P["cs"][:, :], P["li"][:, :], np.pi / 2)
            b.op("dve", "tensor_tensor", out=P["ar"][:, :], in0=P["mag"][:, :], in1=P["cs"][:, :], op=ALU.mult)
            b.op("dve", "tensor_tensor", out=P["ai"][:, :], in0=P["mag"][:, :], in1=P["sn"][:, :], op=ALU.mult)
            TT = lambda o, a_, b_, op: b.op("dve", "tensor_tensor", out=o, in0=a_, in1=b_, op=op)
            t1, t2, t3 = P["t1"][:, :], P["t2"][:, :], P["t3"][:, :]
            TT(t1, P["lamre"][:, :], P["lamre"][:, :], ALU.mult)
            TT(t2, P["lamim"][:, :], P["lamim"][:, :], ALU.mult)
            TT(t1, t1, t2, ALU.add)
            b.op("dve", "reciprocal", out=t3, in_=t1)
            b.op("dve", "tensor_scalar", out=P["kk"][:, :], in0=P["ar"][:, :], scalar1=-1.0, scalar2=None, op0=ALU.add)
            TT(t1, P["kk"][:, :], P["lamre"][:, :], ALU.mult)
            TT(t2, P["ai"][:, :], P["lamim"][:, :], ALU.mult)
            TT(t1, t1, t2, ALU.add)
            TT(P["fr"][:, :], t1, t3, ALU.mult)
            TT(t1, P["ai"][:, :], P["lamre"][:, :], ALU.mult)
            TT(t2, P["kk"][:, :], P["lamim"][:, :], ALU.mult)
            TT(t1, t1, t2, ALU.subtract)
            TT(P["fi"][:, :], t1, t3, ALU.mult)
            pw = [(P["ar"], P["ai"])]
            for k in range(1, npow):
                pr, pi_ = pw[-1]
                nr = b.sb(f"{pfx}pr{k}", [128, 64], F32)
                ni = b.sb(f"{pfx}pi{k}", [128, 64], F32)
                TT(t1, pr[:, :], pr[:, :], ALU.mult)
                TT(t2, pi_[:, :], pi_[:, :], ALU.mult)
                TT(nr[:, :], t1, t2, ALU.subtract)
                TT(t1, pr[:, :], pi_[:, :], ALU.mult)
                b.op("dve", "tensor_scalar", out=ni[:, :], in0=t1, scalar1=2.0, scalar2=None, op0=ALU.mult)
                pw.append((nr, ni))
            P["pw"] = pw
            return P

        def s5_mixer(P):
            nat = {}
            for t4 in range(4):
                for nm in ("bre", "bim", "cre", "cim"):
                    tl = b.sb(f"s5n_{nm}{t4}", [128, 128], F32)
                    b.raw("dve", lambda tl=tl: nc.vector.memset(tl[:, :], 0.0), writes=[tl[:, :]])
                    nat[(nm, t4)] = tl
            wB = [[b.sb(f"s5_wB{ri}{i}", [128, 128], BF16) for i in range(2)] for ri in range(2)]
            wC = [[b.sb(f"s5_wC{ri}{i}", [128, 128], BF16) for i in range(2)] for ri in range(2)]
            c32 = [b.sb(f"s5_c32{i}", [128, 128], F32) for i in range(2)]
            cm = [b.sb(f"s5_cm{i}", [128, 128], F32) for i in range(4)]
            xr = [b.sb(f"s5_xr{i}", [128, 2, SEQ], F32) for i in range(1)] * 2
            xi = [b.sb(f"s5_xi{i}", [128, 2, SEQ], F32) for i in range(1)] * 2
            xrb = [b.sb(f"s5_xrb{i}", [128, T], BF16) for i in range(1)] * 2
            xib = [b.sb(f"s5_xib{i}", [128, T], BF16) for i in range(1)] * 2
            tm = [b.sb(f"s5_tm{i}", [128, 2, SEQ], F32) for i in range(4)]
            fin_r = b.sb("s5_finr", [128, 64, 2], F32)
            fin_i = b.sb("s5_fini", [128, 64, 2], F32)
            fin_o = b.sb("s5_fino", [128, 128], F32)
            it = 0
            for d in range(2):
                for t in range(32):
                    tau = d * 32 + t
                    ch, t4 = t // 4, t % 4
                    pp = it % 2
                    for g2 in range(2):
                        g = 2 * t + g2
                        cb = (2 * t4 + g2) * 16
                        b.dma("sp", nat[("bre", t4)][g2 * 64:(g2 + 1) * 64, cb:cb + 16], s5_b_re[d, g, :, :])
                        b.dma("sp", nat[("bim", t4)][g2 * 64:(g2 + 1) * 64, cb:cb + 16], s5_b_im[d, g, :, :])
                        b.dma("sp", nat[("cre", t4)][cb:cb + 16, g2 * 64:(g2 + 1) * 64], s5_c_re[d, g, :, :])
                        b.dma("sp", nat[("cim", t4)][cb:cb + 16, g2 * 64:(g2 + 1) * 64], s5_c_im[d, g, :, :])
                    for ri, nm in enumerate(("bre", "bim")):
                        b.op("pe", "transpose", out=psS[:, 0:128], in_=nat[(nm, t4)][:, :], identity=ident[:, :])
                        b.op("act", "activation", out=wB[ri][pp][:, :], in_=psS[:, 0:128], func=AF.Copy)
                    for ri, nm in enumerate(("cre", "cim")):
                        b.op("pe", "transpose", out=psS[:, 0:128], in_=nat[(nm, t4)][:, :], identity=ident[:, :])
                        b.op("act", "activation", out=c32[ri][:, :], in_=psS[:, 0:128], func=AF.Copy)
                    fr_, fi_ = P["fr"][:, tau:tau + 1], P["fi"][:, tau:tau + 1]
                    TS = lambda o, i_, sc: b.op("dve", "tensor_scalar", out=o, in0=i_, scalar1=sc, scalar2=None, op0=ALU.mult)
                    TS(cm[0][:, :], c32[0][:, :], fr_)
                    TS(cm[1][:, :], c32[1][:, :], fi_)
                    b.op("dve", "tensor_tensor", out=wC[0][pp][:, :], in0=cm[0][:, :], in1=cm[1][:, :], op=ALU.subtract)
                    TS(cm[2][:, :], c32[0][:, :], fi_)
                    TS(cm[3][:, :], c32[1][:, :], fr_)
                    b.op("dve", "tensor_tensor", out=cm[2][:, :], in0=cm[2][:, :], in1=cm[3][:, :], op=ALU.add)
                    b.op("dve", "tensor_scalar", out=wC[1][pp][:, :], in0=cm[2][:, :], scalar1=-1.0, scalar2=None, op0=ALU.mult)
                    X = (xr[pp], xi[pp])
                    for ri in range(2):
                        pq = psG[ri]
                        b.op("pe", "matmul", out=pq[:, 0:T], lhsT=wB[ri][pp][:, :], rhs=h[:, ch, :], start=True, stop=True)
                        b.op("act", "activation", out=X[ri][:, :, :].rearrange("p a n -> p (a n)"), in_=pq[:, 0:T], func=AF.Copy)
                    for k in range(8):
                        sh = 1 << k
                        pr, pi_ = P["pw"][k]
                        sr_, si_ = pr[:, tau:tau + 1], pi_[:, tau:tau + 1]
                        if d == 0:
                            src = lambda A: A[:, :, 0:SEQ - sh]
                            dst = lambda A: A[:, :, sh:SEQ]
                        else:
                            src = lambda A: A[:, :, sh:SEQ]
                            dst = lambda A: A[:, :, 0:SEQ - sh]
                        tt = [tm[j] for j in range(4)]
                        n = SEQ - sh
                        MUL = lambda o, i_, sc: b.op("act", "activation", out=o, in_=i_, func=AF.Identity, scale=sc)
                        MUL(tt[0][:, :, 0:n], src(X[0]), sr_)
                        MUL(tt[1][:, :, 0:n], src(X[1]), si_)
                        MUL(tt[2][:, :, 0:n], src(X[1]), sr_)
                        MUL(tt[3][:, :, 0:n], src(X[0]), si_)
                        b.op("dve", "tensor_tensor", out=dst(X[0]), in0=dst(X[0]), in1=tt[0][:, :, 0:n], op=ALU.add)
                        b.op("dve", "tensor_tensor", out=dst(X[0]), in0=dst(X[0]), in1=tt[1][:, :, 0:n], op=ALU.subtract)
                        b.op("dve", "tensor_tensor", out=dst(X[1]), in0=dst(X[1]), in1=tt[2][:, :, 0:n], op=ALU.add)
                        b.op("dve", "tensor_tensor", out=dst(X[1]), in0=dst(X[1]), in1=tt[3][:, :, 0:n], op=ALU.add)
                    e_ = SEQ - 1 if d == 0 else 0
                    b.op("dve", "tensor_copy", out=fin_r[:, tau, :], in_=X[0][:, :, e_])
                    b.op("dve", "tensor_copy", out=fin_i[:, tau, :], in_=X[1][:, :, e_])
                    b.op("act", "activation", out=xrb[pp][:, :], in_=X[0][:, :, :].rearrange("p a n -> p (a n)"), func=AF.Copy)
                    b.op("act", "activation", out=xib[pp][:, :], in_=X[1][:, :, :].rearrange("p a n -> p (a n)"), func=AF.Copy)
                    py = psY[it % 2]
                    b.op("pe", "matmul", out=py[:, 0:T], lhsT=wC[0][pp][:, :], rhs=xrb[pp][:, :], start=True, stop=False)
                    b.op("pe", "matmul", accum=True, out=py[:, 0:T], lhsT=wC[1][pp][:, :], rhs=xib[pp][:, :], start=False, stop=True)
                    if d == 0 and t4 == 0:
                        b.op("dve", "tensor_copy", out=y[:, ch, :], in_=py[:, 0:T])
                    else:
                        b.op("dve", "tensor_tensor", out=y[:, ch, :], in0=y[:, ch, :], in1=py[:, 0:T], op=ALU.add)
                    it += 1
            FR = P["fr"][:, :].unsqueeze(2).to_broadcast([128, 64, 2]) if False else None
            for sq_ in range(2):
                a_, b_ = tm[0][:, 0, 0:64], tm[1][:, 0, 0:64]
                TT = lambda o, x_, y_, op: b.op("dve", "tensor_tensor", out=o, in0=x_, in1=y_, op=op)
                TT(a_, fin_r[:, :, sq_], P["fr"][:, :], ALU.mult)
                TT(b_, fin_i[:, :, sq_], P["fi"][:, :], ALU.mult)
                TT(tm[2][:, 0, sq_ * 64:(sq_ + 1) * 64], a_, b_, ALU.subtract)
                TT(a_, fin_i[:, :, sq_], P["fr"][:, :], ALU.mult)
                TT(b_, fin_r[:, :, sq_], P["fi"][:, :], ALU.mult)
                TT(tm[3][:, 0, sq_ * 64:(sq_ + 1) * 64], a_, b_, ALU.add)
            for src_t, dst_d in ((tm[2], s5re_out), (tm[3], s5im_out)):
                b.op("pe", "transpose", out=psS[:, 0:128], in_=src_t[:, 0, 0:128], identity=ident[:, :])
                b.op("dve", "tensor_copy", out=fin_o[:, :], in_=psS[:, 0:128])
                b.dma("sp", dst_d, fin_o[:, :])
            g_bf = act
            for c in range(KC):
                t_a, t_b = tmpA[0], tmpA[1]
                b.op("dve", "tensor_scalar", out=t_a[:, :], in0=h[:, c, :], scalar1=P["dskip"][:, c:c + 1], scalar2=None, op0=ALU.mult)
                b.op("dve", "tensor_tensor", out=y[:, c, :], in0=y[:, c, :], in1=t_a[:, :], op=ALU.add)
                b.op("act", "activation", out=t_a[:, :], in_=y[:, c, :], func=AF.Square)
                b.op("dve", "tensor_scalar", out=t_a[:, :], in0=t_a[:, :], scalar1=0.044715, scalar2=1.0, op0=ALU.mult, op1=ALU.add)
                b.op("dve", "tensor_tensor", out=t_a[:, :], in0=t_a[:, :], in1=y[:, c, :], op=ALU.mult)
                b.op("act", "activation", out=t_b[:, :], in_=t_a[:, :], func=AF.Tanh, scale=0.7978845608028654)
                b.op("dve", "tensor_scalar", out=t_b[:, :], in0=t_b[:, :], scalar1=1.0, scalar2=0.5, op0=ALU.add, op1=ALU.mult)
                b.op("dve", "tensor_tensor", out=g_bf[:, c, :], in0=t_b[:, :], in1=y[:, c, :], op=ALU.mult)
            for o4 in range(0, KC, 2):
                sl = wslot()
                v = sl[:, 0:KC * 512].rearrange("p (k n) -> p k n", k=KC)
                b.dma("pool", v[:, :, 0:256], s5_w_glu[:, o4 * 128:(o4 + 2) * 128].rearrange("(k p) n -> p k n", p=128))
                b.dma("pool", v[:, :, 256:512], s5_w_glu[:, D + o4 * 128:D + (o4 + 2) * 128].rearrange("(k p) n -> p k n", p=128))
                for oo in range(2):
                    o = o4 + oo
                    pa, pg = psG[o % 2], psU[o % 2]
                    for k in range(KC):
                        b.op("pe", "matmul", accum=(k > 0), out=pa[:, 0:T], lhsT=v[:, k, oo * 128:(oo + 1) * 128],
                             rhs=g_bf[:, k, :], start=(k == 0), stop=(k == KC - 1))
                    for k in range(KC):
                        b.op("pe", "matmul", accum=(k > 0), out=pg[:, 0:T], lhsT=v[:, k, 256 + oo * 128:256 + (oo + 1) * 128],
                             rhs=g_bf[:, k, :], start=(k == 0), stop=(k == KC - 1))
                    tq = tmpA[o % 2]
                    b.op("act", "activation", out=tq[:, :], in_=pg[:, 0:T], func=AF.Sigmoid)
                    b.op("dve", "tensor_tensor", out=y[:, o, :], in0=tq[:, :], in1=pa[:, 0:T], op=ALU.mult)


        def dn_mixer():
            sbf = lambda n_, sh, dt_=F32: b.sb("dn_" + n_, sh, dt_)
            msk = sbf("msk", [64, 6, 64]); b.dma("sp", msk[:], dn_mask)
            cw = sbf("cw", [128, 80]); load_rows_T(cw[:, :], dn_conv_w, 80)
            ones_f = sbf("ones", [64, 128]); b.raw("dve", lambda: nc.vector.memset(ones_f[:, :], 1.0), writes=[ones_f[:, :]])
            wba = sbf("wba", [128, 2, KC, 16], BF16)
            for d in range(2):
                b.dma("pool", wba[:, d, :, :], dn_w_ba[d].rearrange("(k p) n -> p k n", p=128))
            alog = sbf("alog", [64, 16]); b.dma("sp", alog[:], dn_a_log.broadcast_to([64, 16]))
            dtb = sbf("dtb", [64, 16]); b.dma("sp", dtb[:], dn_dt_bias.broadcast_to([64, 16]))
            outg = sbf("outg", [64, 128]); b.dma("sp", outg[:], dn_out_g.broadcast_to([64, 128]))
            nexpa = sbf("nexpa", [64, 16])
            b.op("act", "activation", out=nexpa[:, :], in_=alog[:, :], func=AF.Exp)
            b.op("dve", "tensor_scalar", out=nexpa[:, :], in0=nexpa[:, :], scalar1=-1.0, scalar2=None, op0=ALU.mult)
            cin = sbf("cin", [128, 2, SEQ + 4]); b.raw("dve", lambda: nc.vector.memset(cin[:, :, :], 0.0), writes=[cin[:, :, :]])
            qn = sbf("qn", [128, 4, T], BF16); kn = sbf("kn", [128, 4, T], BF16)
            kt_tok = sbf("kt", [64, 8, 4, 128], BF16); vt_tok = sbf("vt", [64, 8, 8, 128], BF16)
            zs = sbf("zs", [128, 8, T], BF16); vtmp = sbf("vtmp", [128, T], BF16)
            acc3 = tmpA[0][:, :].rearrange("p (a n) -> p a n", a=2)
            tm3 = tmpA[1][:, :].rearrange("p (a n) -> p a n", a=2)
            for m0 in range(0, 24, 4):
                sl = wslot()
                v = sl[:, 0:KC * 512].rearrange("p (k n) -> p k n", k=KC)
                b.dma("pool", v, dn_w_in[:, m0 * 128:(m0 + 4) * 128].rearrange("(k p) n -> p k n", p=128))
                for mm in range(4):
                    m = m0 + mm
                    pq = psG[mm % 2]
                    for k in range(KC):
                        b.op("pe", "matmul", accum=(k > 0), out=pq[:, 0:T], lhsT=v[:, k, mm * 128:(mm + 1) * 128],
                             rhs=h[:, k, :], start=(k == 0), stop=(k == KC - 1))
                    if m >= 16:
                        b.op("act", "activation", out=zs[:, m - 16, :], in_=pq[:, 0:T], func=AF.Silu)
                        continue
                    b.op("act", "activation", out=cin[:, :, 2:2 + SEQ], in_=pq[:, 0:T].rearrange("p (a n) -> p a n", a=2),
                         func=AF.Copy)
                    for j in range(5):
                        dst = acc3 if j == 0 else tm3
                        b.op("act", "activation", out=dst, in_=cin[:, :, j:j + SEQ], func=AF.Identity,
                             scale=cw[:, j * 16 + m:j * 16 + m + 1])
                        if j > 0:
                            b.op("dve", "tensor_tensor", out=acc3, in0=acc3, in1=tm3, op=ALU.add)
                    if m < 8:
                        b.op("act", "activation", out=tmpA[1][:, :], in_=tmpA[0][:, :], func=AF.Silu)
                        b.op("act", "activation", out=act[:, 0, :], in_=tmpA[1][:, :], func=AF.Square)
                        b.op("pe", "matmul", out=psS[:, 0:T], lhsT=ones_bf[:, :], rhs=act[:, 0, :], start=True, stop=True)
                        b.op("act", "activation", out=rstd[:, :], in_=psS[:, 0:T], func=AF.Sqrt, scale=1.0, bias=EPS)
                        b.op("dve", "reciprocal", out=rstd[:, :], in_=rstd[:, :])
                        b.op("dve", "tensor_tensor", out=tmpA[1][:, :], in0=tmpA[1][:, :], in1=rstd[:, :], op=ALU.mult)
                        if m < 4:
                            b.op("act", "activation", out=qn[:, m, :], in_=tmpA[1][:, :], func=AF.Identity, scale=128 ** -0.5)
                        else:
                            b.op("act", "activation", out=kn[:, m - 4, :], in_=tmpA[1][:, :], func=AF.Identity, scale=1.0)
                            for blk in range(8):
                                b.op("pe", "transpose", out=psTb[blk % 2][0:64, 0, :], in_=kn[:, m - 4, blk * 64:(blk + 1) * 64],
                                     identity=ident_bf[:, :])
                                b.op("dve", "tensor_copy", out=kt_tok[:, blk, m - 4, :], in_=psTb[blk % 2][0:64, 0, :])
                    else:
                        b.op("act", "activation", out=vtmp[:, :], in_=tmpA[0][:, :], func=AF.Silu)
                        for blk in range(8):
                            b.op("pe", "transpose", out=psTb[blk % 2][0:64, 0, :], in_=vtmp[:, blk * 64:(blk + 1) * 64],
                                 identity=ident_bf[:, :])
                            b.op("dve", "tensor_copy", out=vt_tok[:, blk, m - 8, :], in_=psTb[blk % 2][0:64, 0, :])
            gts = sbf("gts", [64, 2, 8, 16]); beta = sbf("beta", [64, 2, 8, 8]); xa = sbf("xa", [64, 2, 8, 8])
            xe = sbf("xe", [64, 2, 8, 8]); gg = sbf("gg", [64, 2, 8, 8]); gc = sbf("gc", [64, 2, 8, 8]); eg = sbf("eg", [64, 2, 8, 8])
            for d in range(2):
                for blk in range(8):
                    pq = psU[blk % 2]
                    for k in range(KC):
                        b.op("pe", "matmul", accum=(k > 0), out=pq[0:64, 0:16], lhsT=h[:, k, blk * 64:(blk + 1) * 64],
                             rhs=wba[:, d, k, :], start=(k == 0), stop=(k == KC - 1))
                    b.op("dve", "tensor_copy", out=gts[:, d, blk, :], in_=pq[0:64, 0:16])
                    b.op("act", "activation", out=beta[:, d, blk, :], in_=gts[:, d, blk, 0:8], func=AF.Sigmoid)
                    b.op("dve", "tensor_tensor", out=xa[:, d, blk, :], in0=gts[:, d, blk, 8:16], in1=dtb[:, d * 8:(d + 1) * 8], op=ALU.add)
            fl = lambda A: A[:, :, :, :].rearrange("p a b c -> p (a b c)")
            b.op("act", "activation", out=fl(xe), in_=fl(xa), func=AF.Abs)
            b.op("act", "activation", out=fl(xe), in_=fl(xe), func=AF.Exp, scale=-1.0)
            b.op("act", "activation", out=fl(xe), in_=fl(xe), func=AF.Ln, bias=1.0, scale=1.0)
            b.op("dve", "tensor_scalar", out=fl(xa), in0=fl(xa), scalar1=0.0, scalar2=None, op0=ALU.max)
            b.op("dve", "tensor_tensor", out=fl(xa), in0=fl(xa), in1=fl(xe), op=ALU.add)
            for d in range(2):
                for blk in range(8):
                    b.op("dve", "tensor_tensor", out=gg[:, d, blk, :], in0=xa[:, d, blk, :], in1=nexpa[:, d * 8:(d + 1) * 8], op=ALU.mult)
                b.op("pe", "matmul", out=psS[0:64, 0:64], lhsT=msk[:, d, :], rhs=gg[:, d, :, :].rearrange("p b c -> p (b c)"),
                     start=True, stop=True)
                b.op("dve", "tensor_copy", out=gc[:, d, :, :].rearrange("p b c -> p (b c)"), in_=psS[0:64, 0:64])
            b.op("act", "activation", out=fl(eg), in_=fl(gc), func=AF.Exp)
            f64 = lambda n_: sbf(n_, [64, 64])
            dg, decS, decT, PTt = f64("dg"), f64("decS"), f64("decT"), f64("PTt")
            Ms = [(f64("Ma"), f64("MTa")), (f64("Mb"), f64("MTb"))]
            gcr = sbf("gcr", [128, 64]); TTb = sbf("TTb", [64, 64], BF16); aT = sbf("aT", [64, 64], BF16)
            vb = sbf("vb", [64, 128], BF16); kbg = sbf("kbg", [64, 128], BF16); kg = sbf("kg", [64, 128], BF16)
            wT = sbf("wT", [128, 64], BF16); u_sb = sbf("u", [64, 128]); vnew = sbf("vnew", [64, 128]); vnew_b = sbf("vnewb", [64, 128], BF16)
            o1 = sbf("o1", [64, 128]); sc1 = sbf("sc1", [64, 4]); egl = sbf("egl", [128, 1])
            S = sbf("S", [128, 128]); Sb = sbf("Sb", [128, 128], BF16)
            o_acc = sbf("oacc", [64, 4, 128]); ss = sbf("ss", [64, 4]); onb = sbf("onb", [64, 128], BF16); junk = sbf("junk", [64, 128])
            id64 = ident[0:64, 0:64]
            for s_ in range(2):
                for hv in range(8):
                    hq = hv // 2
                    for d in range(2):
                        b.raw("dve", lambda: nc.vector.memset(S[:, :], 0.0), writes=[S[:, :]])
                        b.raw("dve", lambda: nc.vector.memset(Sb[:, :], 0.0), writes=[Sb[:, :]])
                        last = 63 if d == 0 else 0
                        for n in (range(4) if d == 0 else range(3, -1, -1)):
                            blk = s_ * 4 + n
                            tok0 = blk * 64
                            gc_, be_, eg_ = gc[:, d, blk, hv:hv + 1], beta[:, d, blk, hv:hv + 1], eg[:, d, blk, hv:hv + 1]
                            kf, qf = kn[:, hq, tok0:tok0 + 64], qn[:, hq, tok0:tok0 + 64]
                            kt, vt = kt_tok[:, blk, hq, :], vt_tok[:, blk, hv, :]
                            b.op("dve", "tensor_scalar", out=dg[:, :], in0=id64, scalar1=gc_, scalar2=None, op0=ALU.mult)
                            b.op("pe", "matmul", out=psS[:, 0:64], lhsT=ones_f[:, :], rhs=dg[:, :], start=True, stop=True)
                            b.op("act", "activation", out=gcr[:, :], in_=psS[:, 0:64], func=AF.Copy)
                            b.op("dve", "tensor_scalar", out=decS[:, :], in0=gcr[0:64, :], scalar1=-1.0, scalar2=gc_, op0=ALU.mult, op1=ALU.add)
                            b.op("dve", "tensor_scalar", out=decS[:, :], in0=decS[:, :], scalar1=0.0, scalar2=None, op0=ALU.min)
                            b.op("act", "activation", out=decS[:, :], in_=decS[:, :], func=AF.Exp)
                            b.op("dve", "tensor_tensor", out=decS[:, :], in0=decS[:, :], in1=msk[:, 4 + d, :], op=ALU.mult)
                            b.op("dve", "tensor_scalar", out=decT[:, :], in0=gcr[0:64, :], scalar1=gc_, scalar2=None, op0=ALU.subtract)
                            b.op("dve", "tensor_scalar", out=decT[:, :], in0=decT[:, :], scalar1=0.0, scalar2=None, op0=ALU.min)
                            b.op("act", "activation", out=decT[:, :], in_=decT[:, :], func=AF.Exp)
                            b.op("dve", "tensor_tensor", out=decT[:, :], in0=decT[:, :], in1=msk[:, d, :], op=ALU.mult)
                            M, MT = Ms[0]
                            b.op("pe", "matmul", out=psG[0][0:64, 0:64], lhsT=kf, rhs=kf, start=True, stop=True)
                            b.op("dve", "tensor_scalar", out=M[:, :], in0=psG[0][0:64, 0:64], scalar1=be_, scalar2=-1.0, op0=ALU.mult, op1=ALU.mult)
                            b.op("dve", "tensor_tensor", out=M[:, :], in0=M[:, :], in1=decS[:, :], op=ALU.mult)
                            b.op("pe", "transpose", out=psG[1][0:64, 0:64], in_=M[:, :], identity=id64)
                            b.op("act", "activation", out=MT[:, :], in_=psG[1][0:64, 0:64], func=AF.Copy)
                            b.op("dve", "tensor_tensor", out=PTt[:, :], in0=MT[:, :], in1=id64, op=ALU.add)
                            cur = 0
                            for k in range(1, 6):
                                M, MT = Ms[cur]
                                Mn, MTn = Ms[1 - cur]
                                b.op("pe", "matmul", out=psU[0][0:64, 0:64], lhsT=MT[:, :], rhs=M[:, :], start=True, stop=True)
                                b.op("act", "activation", out=Mn[:, :], in_=psU[0][0:64, 0:64], func=AF.Copy)
                                if k < 5:
                                    b.op("pe", "matmul", out=psU[1][0:64, 0:64], lhsT=M[:, :], rhs=MT[:, :], start=True, stop=True)
                                    b.op("dve", "tensor_copy", out=MTn[:, :], in_=psU[1][0:64, 0:64])
                                b.op("pe", "matmul", out=psY[0][0:64, 0:64], lhsT=Mn[:, :], rhs=PTt[:, :], start=True, stop=True)
                                b.op("dve", "tensor_tensor", out=PTt[:, :], in0=PTt[:, :], in1=psY[0][0:64, 0:64], op=ALU.add)
                                cur = 1 - cur
                            b.op("act", "activation", out=TTb[:, :], in_=PTt[:, :], func=AF.Copy)
                            b.op("dve", "tensor_tensor", out=sc1[:, 0:1], in0=be_, in1=eg_, op=ALU.mult)
                            b.op("dve", "tensor_scalar", out=vb[:, :], in0=vt, scalar1=be_, scalar2=None, op0=ALU.mult)
                            b.op("dve", "tensor_scalar", out=kbg[:, :], in0=kt, scalar1=sc1[:, 0:1], scalar2=None, op0=ALU.mult)
                            b.op("pe", "matmul", out=psG[0][0:64, 0:128], lhsT=TTb[:, :], rhs=vb[:, :], start=True, stop=True)
                            b.op("act", "activation", out=u_sb[:, :], in_=psG[0][0:64, 0:128], func=AF.Copy)
                            b.op("pe", "matmul", out=psG[1][:, 0:64], lhsT=kbg[:, :], rhs=TTb[:, :], start=True, stop=True)
                            b.op("dve", "tensor_copy", out=wT[:, :], in_=psG[1][:, 0:64])
                            b.op("pe", "matmul", out=psU[0][0:64, 0:128], lhsT=wT[:, :], rhs=Sb[:, :], start=True, stop=True)
                            b.op("dve", "tensor_tensor", out=vnew[:, :], in0=u_sb[:, :], in1=psU[0][0:64, 0:128], op=ALU.subtract)
                            b.op("act", "activation", out=vnew_b[:, :], in_=vnew[:, :], func=AF.Copy)
                            b.op("pe", "matmul", out=psU[1][0:64, 0:128], lhsT=qf, rhs=Sb[:, :], start=True, stop=True)
                            b.op("dve", "tensor_scalar", out=o1[:, :], in0=psU[1][0:64, 0:128], scalar1=eg_, scalar2=None, op0=ALU.mult)
                            b.op("pe", "matmul", out=psY[0][0:64, 0:64], lhsT=kf, rhs=qf, start=True, stop=True)
                            b.op("dve", "tensor_tensor", out=aT[:, :], in0=psY[0][0:64, 0:64], in1=decT[:, :], op=ALU.mult)
                            b.op("pe", "matmul", out=psY[1][0:64, 0:128], lhsT=aT[:, :], rhs=vnew_b[:, :], start=True, stop=True)
                            b.op("dve", "tensor_tensor", out=o1[:, :], in0=o1[:, :], in1=psY[1][0:64, 0:128], op=ALU.add)
                            if d == 0:
                                b.op("dve", "tensor_copy", out=o_acc[:, n, :], in_=o1[:, :])
                            else:
                                b.op("dve", "tensor_tensor", out=o_acc[:, n, :], in0=o_acc[:, n, :], in1=o1[:, :], op=ALU.add)
                            b.op("act", "activation", out=sc1[:, 1:2], in_=gc_, func=AF.Exp, scale=-1.0, bias=gcr[0:64, last:last + 1])
                            b.op("dve", "tensor_scalar", out=kg[:, :], in0=kt, scalar1=sc1[:, 1:2], scalar2=None, op0=ALU.mult)
                            b.op("act", "activation", out=egl[:, :], in_=gcr[:, last:last + 1], func=AF.Exp)
                            b.op("pe", "matmul", out=psG[0][:, 0:128], lhsT=kg[:, :], rhs=vnew_b[:, :], start=True, stop=True)
                            b.op("dve", "tensor_scalar", out=S[:, :], in0=S[:, :], scalar1=egl[:, 0:1], scalar2=None, op0=ALU.mult)
                            b.op("dve", "tensor_tensor", out=S[:, :], in0=S[:, :], in1=psG[0][:, 0:128], op=ALU.add)
                            b.op("act", "activation", out=Sb[:, :], in_=S[:, :], func=AF.Copy)
                        b.dma("sp", dn_out[(s_ * 2 + d) * 8 + hv, :, :], S[:, :])
                    for n in range(4):
                        b.op("act", "activation", out=junk[:, :], in_=o_acc[:, n, :], func=AF.Square, accum_out=ss[:, n:n + 1])
                    b.op("act", "activation", out=ss[:, :], in_=ss[:, :], func=AF.Sqrt, scale=1.0 / 128, bias=EPS)
                    b.op("dve", "reciprocal", out=ss[:, :], in_=ss[:, :])
                    for n in range(4):
                        tok0 = (s_ * 4 + n) * 64
                        b.op("dve", "tensor_scalar", out=junk[:, :], in0=o_acc[:, n, :], scalar1=ss[:, n:n + 1], scalar2=None, op0=ALU.mult)
                        b.op("dve", "tensor_tensor", out=onb[:, :], in0=junk[:, :], in1=outg[:, :], op=ALU.mult)
                        b.op("pe", "transpose", out=psTb[n % 2][:, 0, 0:64], in_=onb[:, :], identity=ident_bf[0:64, 0:64])
                        b.op("dve", "tensor_tensor", out=oT[:, hv, tok0:tok0 + 64], in0=psTb[n % 2][:, 0, 0:64],
                             in1=zs[:, hv, tok0:tok0 + 64], op=ALU.mult)
            proj_out_featmajor(dn_w_o, oT)


        NTL = LSEQ // 512
        NBL = LSEQ // 128

        def lat_load(src, tile):
            b.dma("sp", x[:], src[:, tile * 512:(tile + 1) * 512].rearrange("(c p) t -> p c t", p=128))

        def lat_store(dst, tile):
            b.dma("sp", dst[:, tile * 512:(tile + 1) * 512].rearrange("(c p) t -> p c t", p=128), x[:])

        def lat_ffn_sub(i, s):
            pre_norm(i, s)
            ffn(ffn_w_gu[i, 0 if s == 0 else 1], ffn_w_d[i, 0 if s == 0 else 1], key=i * 2 + (0 if s == 0 else 1),
                consume=not CTX_SKIP)
            post_norm_residual(i, s)

        def a_lat(src0):
            sbf = lambda n_, sh, dt_=F32: b.sb("la_" + n_, sh, dt_)
            kTf = sbf("kT", [128, 4, LSEQ], BF16); vtf = sbf("vt", [128, NBL, 256], BF16)
            kTc = sbf("kTc", [128, 4, 512], BF16); vtc = sbf("vtc", [128, 4, 256], BF16)
            bmask = sbf("bm", [128, 384]); b.dma("sp", bmask[:], band_mask)
            cs = [sbf(f"cs{i}", [128, 64]) for i in range(2)]
            kdup = sbf("kdup", [128, 4, 2, 64], BF16)
            qr = sbf("qr", [128, 1024], BF16); qTb = sbf("qTb", [128, 8, 128], BF16)
            slcs = [sbf(f"sl{j}", [128, 384]) for j in range(2)]; Pls = [sbf(f"Pl{j}", [128, 384], BF16) for j in range(2)]
            Pcs = [sbf(f"Pc{j}", [128, 512], BF16) for j in range(2)]
            PTls = [sbf(f"PTl{j}", [128, 8, 128], BF16) for j in range(2)]; otk = sbf("otk", [128, 1024], BF16)
            smms = [sbf(f"smx{j}", [128, 12]) for j in range(2)]
            rt = [sbf(f"rt{i}", [128, 2, 16]) for i in range(4)]
            c32t = sbf("c32", [128, 256])

            def rope(src, nh, dst_fn, cst):
                cosv = cst[:, 0:32].rearrange("p (a f) -> p a f", a=2)
                sinv = cst[:, 32:64].rearrange("p (a f) -> p a f", a=2)
                for hh in range(nh):
                    s4 = src[:, hh * 64:(hh + 1) * 64].rearrange("p (a b f) -> p a b f", a=2, b=2)
                    x1, x2 = s4[:, :, 0, :], s4[:, :, 1, :]
                    d4 = dst_fn(hh).rearrange("p (a b f) -> p a b f", a=2, b=2)
                    TT = lambda o, a_, b_, op: b.op("dve", "tensor_tensor", out=o, in0=a_, in1=b_, op=op)
                    TT(rt[0][:, :, :], x1, cosv, ALU.mult)
                    TT(rt[1][:, :, :], x2, sinv, ALU.mult)
                    TT(d4[:, :, 0, :], rt[0][:, :, :], rt[1][:, :, :], ALU.subtract)
                    TT(rt[2][:, :, :], x2, cosv, ALU.mult)
                    TT(rt[3][:, :, :], x1, sinv, ALU.mult)
                    TT(d4[:, :, 1, :], rt[2][:, :, :], rt[3][:, :, :], ALU.add)

            def k_to_featmajor(dstT, col0):
                b.op("dve", "tensor_copy", out=kdup[:, :, 1, :], in_=kdup[:, :, 0, :])
                for kv in range(4):
                    b.op("pe", "transpose", out=psTb[kv % 2][:, 0, :], in_=kdup[:, kv, :, :].rearrange("p a d -> p (a d)"),
                         identity=ident_bf[:, :])
                    b.op("act", "activation", out=dstT[:, kv, col0:col0 + 128], in_=psTb[kv % 2][:, 0, :], func=AF.Copy)

            for cb in range(4):
                b.dma("sp", c32t[:, :], cache_ak[cb * 128:(cb + 1) * 128, :])
                b.op("dve", "tensor_copy", out=kdup[:, :, 0, :], in_=c32t[:, :].rearrange("p (k d) -> p k d", k=4))
                k_to_featmajor(kTc, cb * 128)
                b.dma("pool", vtc[:, cb, :], cache_av[cb * 128:(cb + 1) * 128, :])
            for tl in range(NTL):
                lat_load(src0, tl)
                lat_ffn_sub(0, 0)
                lat_store(zres, tl)
                pre_norm(0, 1)
                sl_ = wslot()
                v = sl_[:, 0:KC * 512].rearrange("p (k n) -> p k n", k=KC)
                b.dma("pool", v, a_w_qkv[:, 1024:1536].rearrange("(k p) n -> p k n", p=128))
                for tb in range(4):
                    blk = tl * 4 + tb
                    pq = psU[tb % 2]
                    for k in range(KC):
                        b.op("pe", "matmul", accum=(k > 0), out=pq[:, 0:512], lhsT=h[:, k, tb * 128:(tb + 1) * 128],
                             rhs=v[:, k, :], start=(k == 0), stop=(k == KC - 1))
                    b.dma("sp", cs[tb % 2][:, :], rope_cs[blk * 128:(blk + 1) * 128, :])
                    rope(pq[:, 0:256], 4, lambda hh: kdup[:, hh, 0, :], cs[tb % 2])
                    k_to_featmajor(kTf, blk * 128)
                    b.op("act", "activation", out=vtf[:, blk, :], in_=pq[:, 256:512], func=AF.Copy)
            it = 0
            for tl in range(NTL):
                lat_load(zres, tl)
                pre_norm(0, 1)
                wq = []
                for half in range(2):
                    sl_ = wslot()
                    v = sl_[:, 0:KC * 512].rearrange("p (k n) -> p k n", k=KC)
                    b.dma("pool", v, a_w_qkv[:, half * 512:(half + 1) * 512].rearrange("(k p) n -> p k n", p=128))
                    wq.append(v)
                for tb in range(4):
                    blk = tl * 4 + tb
                    b.dma("sp", cs[tb % 2][:, :], rope_cs[blk * 128:(blk + 1) * 128, :])
                    for half in range(2):
                        pq = psG[half]
                        for k in range(KC):
                            b.op("pe", "matmul", accum=(k > 0), out=pq[:, 0:512], lhsT=h[:, k, tb * 128:(tb + 1) * 128],
                                 rhs=wq[half][:, k, :], start=(k == 0), stop=(k == KC - 1))
                        rope(pq[:, 0:512], 8, lambda hh, half=half: qr[:, half * 512 + hh * 64:half * 512 + (hh + 1) * 64], cs[tb % 2])
                    for c in range(KC):
                        b.op("pe", "transpose", out=psTb[c % 2][:, 0, :], in_=qr[:, c * 128:(c + 1) * 128], identity=ident_bf[:, :])
                        b.op("act", "activation", out=qTb[:, c, :], in_=psTb[c % 2][:, 0, :], func=AF.Copy)
                    lo, hi = max(blk - 1, 0), min(blk + 1, NBL - 1)
                    nlb = hi - lo + 1
                    nl = nlb * 128
                    bm = bmask[:, 128:128 + nl] if blk == 0 else bmask[:, 0:nl]
                    def heads(j, lo=lo, hi=hi, nlb=nlb, nl=nl, bm=bm):
                        slc, Pl, Pc, PTl, smm = slcs[j], Pls[j], Pcs[j], PTls[j], smms[j]
                        pL, pC, pO, pT = psG[j], psU[j], psY[j], psTc[j]
                        base = 64 * j
                        for hh in range(j, 16, 2):
                            qc, kv = hh // 2, hh // 4
                            b.op("pe", "matmul", out=pL[:, 0:nl], lhsT=qTb[base:base + 64, qc, :],
                                 rhs=kTf[base:base + 64, kv, lo * 128:(hi + 1) * 128], start=True, stop=True)
                            b.op("pe", "matmul", out=pC[:, 0:512], lhsT=qTb[base:base + 64, qc, :],
                                 rhs=kTc[base:base + 64, kv, :], start=True, stop=True)
                            yield
                            b.op("dve", "tensor_tensor", out=slc[:, 0:nl], in0=pL[:, 0:nl], in1=bm, op=ALU.add)
                            b.op("dve", "reduce_max", out=smm[:, 0:1], in_=slc[:, 0:nl], axis=AX.X)
                            b.op("dve", "reduce_max", out=smm[:, 1:2], in_=pC[:, 0:512], axis=AX.X)
                            yield
                            b.op("dve", "tensor_tensor", out=smm[:, 0:1], in0=smm[:, 0:1], in1=smm[:, 1:2], op=ALU.max)
                            b.op("dve", "tensor_scalar", out=smm[:, 2:3], in0=smm[:, 0:1], scalar1=0.125,
                                 scalar2=sink_bc[:, hh:hh + 1], op0=ALU.mult, op1=ALU.max)
                            b.op("dve", "tensor_scalar", out=smm[:, 2:3], in0=smm[:, 2:3], scalar1=-1.0, scalar2=None, op0=ALU.mult)
                            yield
                            b.op("act", "activation", out=Pl[:, 0:nl], in_=slc[:, 0:nl], func=AF.Exp, scale=0.125,
                                 bias=smm[:, 2:3], accum_out=smm[:, 3:4])
                            b.op("act", "activation", out=Pc[:, :], in_=pC[:, 0:512], func=AF.Exp, scale=0.125,
                                 bias=smm[:, 2:3], accum_out=smm[:, 4:5])
                            b.op("act", "activation", out=smm[:, 5:6], in_=sink_bc[:, hh:hh + 1], func=AF.Exp, bias=smm[:, 2:3], scale=1.0)
                            yield
                            b.op("dve", "tensor_tensor", out=smm[:, 6:7], in0=smm[:, 3:4], in1=smm[:, 4:5], op=ALU.add)
                            b.op("dve", "tensor_tensor", out=smm[:, 6:7], in0=smm[:, 6:7], in1=smm[:, 5:6], op=ALU.add)
                            b.op("dve", "reciprocal", out=smm[:, 7:8], in_=smm[:, 6:7])
                            srcs = [Pl[:, jq * 128:(jq + 1) * 128] for jq in range(nlb)] + [Pc[:, jq * 128:(jq + 1) * 128] for jq in range(4)]
                            nsrc = len(srcs)
                            for j0 in range(0, nsrc, 2):
                                nn_ = min(2, nsrc - j0)
                                for q_ in range(nn_):
                                    b.op("pe", "transpose", out=pT[:, q_, :], in_=srcs[j0 + q_], identity=ident_bf[:, :])
                                if (j0 // 2) % 2 == 0:
                                    b.op("dve", "tensor_copy", out=PTl[:, j0:j0 + nn_, :], in_=pT[:, 0:nn_, :])
                                else:
                                    b.op("act", "activation", out=PTl[:, j0:j0 + nn_, :], in_=pT[:, 0:nn_, :], func=AF.Copy)
                                yield
                            for jq in range(nsrc):
                                rv = vtf[:, lo + jq, kv * 64:(kv + 1) * 64] if jq < nlb else vtc[:, jq - nlb, kv * 64:(kv + 1) * 64]
                                b.op("pe", "matmul", accum=(jq > 0), out=pO[:, 0:64], lhsT=PTl[:, jq, :], rhs=rv,
                                     start=(jq == 0), stop=(jq == nsrc - 1))
                            yield
                            b.op("dve", "tensor_scalar", out=otk[:, hh * 64:(hh + 1) * 64], in0=pO[:, 0:64], scalar1=smm[:, 7:8],
                                 scalar2=None, op0=ALU.mult)
                            yield
                    run_interleaved([heads(0), heads(1)])
                    for c in range(KC):
                        b.op("pe", "transpose", out=psTb[c % 2][:, 0, :], in_=otk[:, c * 128:(c + 1) * 128], identity=ident_bf[:, :])
                        b.op("act", "activation", out=oT[:, c, tb * 128:(tb + 1) * 128], in_=psTb[c % 2][:, 0, :], func=AF.Copy)
                proj_out_featmajor(a_w_o, oT)
                post_norm_residual(0, 1)
                lat_ffn_sub(0, 2)
                lat_store(zres, tl)


        def s5_lat():
            P = s5_setup(npow=9, pfx="l5p_")
            LT = 512
            sbf = lambda n_, sh, dt_=F32: b.sb("l5_" + n_, sh, dt_)
            TT = lambda o, a_, b_, op: b.op("dve", "tensor_tensor", out=o, in0=a_, in1=b_, op=op)
            h0r, h0i = sbf("h0r", [128, 64]), sbf("h0i", [128, 64])
            load_rows_T(h0r[:, :], st5_re, 64)
            load_rows_T(h0i[:, :], st5_im, 64)
            cr, ci = sbf("cr", [128, 64]), sbf("ci", [128, 64])
            t1, t2, t3 = P["t1"][:, :], P["t2"][:, :], P["t3"][:, :]
            TT(t1, P["fr"][:, :], P["fr"][:, :], ALU.mult)
            TT(t2, P["fi"][:, :], P["fi"][:, :], ALU.mult)
            TT(t1, t1, t2, ALU.add)
            b.op("dve", "reciprocal", out=t3, in_=t1)
            TT(t1, h0r[:, :], P["fr"][:, :], ALU.mult)
            TT(t2, h0i[:, :], P["fi"][:, :], ALU.mult)
            TT(t1, t1, t2, ALU.add)
            TT(cr[:, :], t1, t3, ALU.mult)
            TT(t1, h0i[:, :], P["fr"][:, :], ALU.mult)
            TT(t2, h0r[:, :], P["fi"][:, :], ALU.mult)
            TT(t1, t1, t2, ALU.subtract)
            TT(ci[:, :], t1, t3, ALU.mult)
            nat = {}
            for t4 in range(4):
                for nm in ("bre", "bim", "cre", "cim"):
                    tl_ = sbf(f"n_{nm}{t4}", [128, 128])
                    b.raw("dve", lambda tl_=tl_: nc.vector.memset(tl_[:, :], 0.0), writes=[tl_[:, :]])
                    nat[(nm, t4)] = tl_
            wB = [sbf(f"wB{ri}", [128, 128], BF16) for ri in range(2)]
            wC = [sbf(f"wC{ri}", [128, 128], BF16) for ri in range(2)]
            c32 = [sbf(f"c32{i}", [128, 128]) for i in range(2)]
            cm = [sbf(f"cm{i}", [128, 128]) for i in range(4)]
            X = (sbf("xr", [128, LT]), sbf("xi", [128, LT]))
            xrb, xib = sbf("xrb", [128, LT], BF16), sbf("xib", [128, LT], BF16)
            tm = [sbf(f"tm{i}", [128, LT]) for i in range(6)]
            sc = sbf("sc", [128, 8])
            rotb = [sbf(f"rotb{i}", [128, 1536]) for i in range(2)]
            ones512 = sbf("ones512", [128, LT])
            b.raw("dve", lambda: nc.vector.memset(ones512[:, :], 1.0), writes=[ones512[:, :]])
            pwu = [(P["cs"], P["sn"])]
            for k in range(1, 9):
                pc_, ps_ = pwu[-1]
                nc_ = sbf(f"uc{k}", [128, 64]); ns_ = sbf(f"us{k}", [128, 64])
                TT(t1, pc_[:, :], pc_[:, :], ALU.mult)
                TT(t2, ps_[:, :], ps_[:, :], ALU.mult)
                TT(nc_[:, :], t1, t2, ALU.subtract)
                TT(t1, pc_[:, :], ps_[:, :], ALU.mult)
                b.op("dve", "tensor_scalar", out=ns_[:, :], in0=t1, scalar1=2.0, scalar2=None, op0=ALU.mult)
                pwu.append((nc_, ns_))
            TSm = lambda o, i_, sc_: b.op("dve", "tensor_scalar", out=o, in0=i_, scalar1=sc_, scalar2=None, op0=ALU.mult)
            for tau in range(64):
                Er, Ei = X[0], X[1]
                b.op("dve", "tensor_copy", out=Er[:, 0:1], in_=P["cs"][:, tau:tau + 1])
                b.op("dve", "tensor_copy", out=Ei[:, 0:1], in_=P["sn"][:, tau:tau + 1])
                for k in range(9):
                    n = 1 << k
                    pc_, ps_ = pwu[k][0][:, tau:tau + 1], pwu[k][1][:, tau:tau + 1]
                    TSm(tm[0][:, 0:n], Er[:, 0:n], pc_)
                    TSm(tm[1][:, 0:n], Ei[:, 0:n], ps_)
                    TSm(tm[2][:, 0:n], Ei[:, 0:n], pc_)
                    TSm(tm[3][:, 0:n], Er[:, 0:n], ps_)
                    TT(Er[:, n:2 * n], tm[0][:, 0:n], tm[1][:, 0:n], ALU.subtract)
                    TT(Ei[:, n:2 * n], tm[2][:, 0:n], tm[3][:, 0:n], ALU.add)
                b.op("act", "activation", out=tm[4][:, :], in_=ones512[:, :], func=AF.Identity, scale=P["mag"][:, tau:tau + 1])
                b.dma("sp", rot[tau, :, 0:512], Er[:, :])
                b.dma("sp", rot[tau, :, 512:1024], Ei[:, :])
                b.dma("sp", rot[tau, :, 1024:1536], tm[4][:, :])
            rit = [0]

            def tile_pass(d, tl):
                for t in range(32):
                    tau = d * 32 + t
                    ch, t4 = t // 4, t % 4
                    for g2 in range(2):
                        g = 2 * t + g2
                        cb = (2 * t4 + g2) * 16
                        b.dma("sp", nat[("bre", t4)][g2 * 64:(g2 + 1) * 64, cb:cb + 16], s5_b_re[d, g, :, :])
                        b.dma("sp", nat[("bim", t4)][g2 * 64:(g2 + 1) * 64, cb:cb + 16], s5_b_im[d, g, :, :])
                        b.dma("sp", nat[("cre", t4)][cb:cb + 16, g2 * 64:(g2 + 1) * 64], s5_c_re[d, g, :, :])
                        b.dma("sp", nat[("cim", t4)][cb:cb + 16, g2 * 64:(g2 + 1) * 64], s5_c_im[d, g, :, :])
                    for ri, nm in enumerate(("bre", "bim")):
                        b.op("pe", "transpose", out=psS[:, 0:128], in_=nat[(nm, t4)][:, :], identity=ident[:, :])
                        b.op("act", "activation", out=wB[ri][:, :], in_=psS[:, 0:128], func=AF.Copy)
                    for ri, nm in enumerate(("cre", "cim")):
                        b.op("pe", "transpose", out=psS[:, 0:128], in_=nat[(nm, t4)][:, :], identity=ident[:, :])
                        b.op("act", "activation", out=c32[ri][:, :], in_=psS[:, 0:128], func=AF.Copy)
                    fr_, fi_ = P["fr"][:, tau:tau + 1], P["fi"][:, tau:tau + 1]
                    TS = lambda o, i_, sc_: b.op("dve", "tensor_scalar", out=o, in0=i_, scalar1=sc_, scalar2=None, op0=ALU.mult)
                    TS(cm[0][:, :], c32[0][:, :], fr_)
                    TS(cm[1][:, :], c32[1][:, :], fi_)
                    TT(wC[0][:, :], cm[0][:, :], cm[1][:, :], ALU.subtract)
                    TS(cm[2][:, :], c32[0][:, :], fi_)
                    TS(cm[3][:, :], c32[1][:, :], fr_)
                    TT(cm[2][:, :], cm[2][:, :], cm[3][:, :], ALU.add)
                    b.op("dve", "tensor_scalar", out=wC[1][:, :], in0=cm[2][:, :], scalar1=-1.0, scalar2=None, op0=ALU.mult)
                    tb = rotb[rit[0] % 2]
                    rit[0] += 1
                    b.dma("sp", tb[:, :], rot[tau, :, :])
                    cT, sT, rT = tb[:, 0:512], tb[:, 512:1024], tb[:, 1024:1536]
                    for ri in range(2):
                        b.op("pe", "matmul", out=psG[ri][:, 0:LT], lhsT=wB[ri][:, :], rhs=h[:, ch, :], start=True, stop=True)
                    e1 = LT - 1 if d == 0 else 0
                    cr_, ci_ = cr[:, tau:tau + 1], ci[:, tau:tau + 1]
                    rv = (lambda A: A[:, 0:LT]) if d == 0 else (lambda A: A[:, LT - 1::-1])
                    brv, biv = rv(psG[0]), rv(psG[1])
                    TT(tm[0][:, :], brv, cT, ALU.mult)
                    TT(tm[1][:, :], biv, sT, ALU.mult)
                    TT(tm[2][:, :], biv, cT, ALU.mult)
                    TT(tm[3][:, :], brv, sT, ALU.mult)
                    TT(X[0][:, :], tm[0][:, :], tm[1][:, :], ALU.add)
                    TT(X[1][:, :], tm[2][:, :], tm[3][:, :], ALU.subtract)
                    b.op("dve", "tensor_tensor_scan", out=tm[4][:, :], data0=rT, data1=X[0][:, :], initial=cr_, op0=ALU.mult, op1=ALU.add)
                    b.op("dve", "tensor_tensor_scan", out=tm[5][:, :], data0=rT, data1=X[1][:, :], initial=ci_, op0=ALU.mult, op1=ALU.add)
                    PT_ = lambda o, a_, b_, op: b.op("pool", "tensor_tensor", out=o, in0=a_, in1=b_, op=op)
                    PT_(tm[0][:, :], tm[4][:, :], cT, ALU.mult)
                    PT_(tm[1][:, :], tm[5][:, :], sT, ALU.mult)
                    PT_(tm[2][:, :], tm[5][:, :], cT, ALU.mult)
                    PT_(tm[3][:, :], tm[4][:, :], sT, ALU.mult)
                    TT(rv(X[0]), tm[0][:, :], tm[1][:, :], ALU.subtract)
                    TT(rv(X[1]), tm[2][:, :], tm[3][:, :], ALU.add)
                    b.op("dve", "tensor_copy", out=cr_, in_=X[0][:, e1:e1 + 1])
                    b.op("dve", "tensor_copy", out=ci_, in_=X[1][:, e1:e1 + 1])
                    b.op("act", "activation", out=xrb[:, :], in_=X[0][:, :], func=AF.Copy)
                    b.op("act", "activation", out=xib[:, :], in_=X[1][:, :], func=AF.Copy)
                    py = psY[t % 2]
                    b.op("pe", "matmul", out=py[:, 0:LT], lhsT=wC[0][:, :], rhs=xrb[:, :], start=True, stop=False)
                    b.op("pe", "matmul", accum=True, out=py[:, 0:LT], lhsT=wC[1][:, :], rhs=xib[:, :], start=False, stop=True)
                    if t4 == 0:
                        b.op("dve", "tensor_copy", out=y[:, ch, :], in_=py[:, 0:LT])
                    else:
                        TT(y[:, ch, :], y[:, ch, :], py[:, 0:LT], ALU.add)

            def tail():
                g_bf = act
                for c in range(KC):
                    t_a, t_b = tmpA[0], tmpA[1]
                    b.op("dve", "tensor_scalar", out=t_a[:, :], in0=h[:, c, :], scalar1=P["dskip"][:, c:c + 1], scalar2=None, op0=ALU.mult)
                    TT(y[:, c, :], y[:, c, :], t_a[:, :], ALU.add)
                    b.op("act", "activation", out=t_a[:, :], in_=y[:, c, :], func=AF.Square)
                    b.op("dve", "tensor_scalar", out=t_a[:, :], in0=t_a[:, :], scalar1=0.044715, scalar2=1.0, op0=ALU.mult, op1=ALU.add)
                    TT(t_a[:, :], t_a[:, :], y[:, c, :], ALU.mult)
                    b.op("act", "activation", out=t_b[:, :], in_=t_a[:, :], func=AF.Tanh, scale=0.7978845608028654)
                    b.op("dve", "tensor_scalar", out=t_b[:, :], in0=t_b[:, :], scalar1=1.0, scalar2=0.5, op0=ALU.add, op1=ALU.mult)
                    TT(g_bf[:, c, :], t_b[:, :], y[:, c, :], ALU.mult)
                for o4 in range(0, KC, 2):
                    sl = wslot()
                    v = sl[:, 0:KC * 512].rearrange("p (k n) -> p k n", k=KC)
                    b.dma("pool", v[:, :, 0:256], s5_w_glu[:, o4 * 128:(o4 + 2) * 128].rearrange("(k p) n -> p k n", p=128))
                    b.dma("pool", v[:, :, 256:512], s5_w_glu[:, D + o4 * 128:D + (o4 + 2) * 128].rearrange("(k p) n -> p k n", p=128))
                    for oo in range(2):
                        o = o4 + oo
                        pa, pg = psG[o % 2], psU[o % 2]
                        for k in range(KC):
                            b.op("pe", "matmul", accum=(k > 0), out=pa[:, 0:T], lhsT=v[:, k, oo * 128:(oo + 1) * 128],
                                 rhs=g_bf[:, k, :], start=(k == 0), stop=(k == KC - 1))
                        for k in range(KC):
                            b.op("pe", "matmul", accum=(k > 0), out=pg[:, 0:T], lhsT=v[:, k, 256 + oo * 128:256 + (oo + 1) * 128],
                                 rhs=g_bf[:, k, :], start=(k == 0), stop=(k == KC - 1))
                        tq = tmpA[o % 2]
                        b.op("act", "activation", out=tq[:, :], in_=pg[:, 0:T], func=AF.Sigmoid)
                        TT(y[:, o, :], tq[:, :], pa[:, 0:T], ALU.mult)

            for tl in range(NTL):
                lat_load(zres, tl)
                lat_ffn_sub(1, 0)
                lat_store(zres, tl)
                pre_norm(1, 1)
                tile_pass(0, tl)
                b.dma("sp", ysc[:, tl * 512:(tl + 1) * 512].rearrange("(c p) t -> p c t", p=128), y[:])
            for tl in range(NTL - 1, -1, -1):
                lat_load(zres, tl)
                pre_norm(1, 1)
                tile_pass(1, tl)
                for c in range(KC):
                    b.dma("sp", tmpA[c % 2][:, :], ysc[c * 128:(c + 1) * 128, tl * 512:(tl + 1) * 512])
                    TT(y[:, c, :], y[:, c, :], tmpA[c % 2][:, :], ALU.add)
                tail()
                post_norm_residual(1, 1)
                lat_ffn_sub(1, 2)
                lat_store(zres, tl)


        def na_lat():
            sbf = lambda n_, sh, dt_=F32: b.sb("ln_" + n_, sh, dt_)
            TT = lambda o, a_, b_, op: b.op("dve", "tensor_tensor", out=o, in0=a_, in1=b_, op=op)
            NR = LSEQ // 64
            kTg = sbf("kT", [128, LSEQ], BF16); vtg = sbf("vt", [64, NR, 128], BF16)
            kTc = sbf("kTc", [128, 512], BF16); vtc = sbf("vtc", [128, 4, 128], BF16)
            Bh = sbf("Bh", [64, 2, 15, 64]); cmask = sbf("cmask", [64, 64]); b.dma("sp", cmask[:], na_cmask)
            c32t = sbf("c32", [128, 128]); ckb = sbf("ckb", [128, 128], BF16)
            qTg = sbf("qT", [128, 512], BF16)
            slcs = [sbf(f"sl{j}", [64, 512]) for j in range(2)]
            Pls = [sbf(f"Pl{j}", [64, 512], BF16) for j in range(2)]; Pcs = [sbf(f"Pc{j}", [64, 512], BF16) for j in range(2)]
            PTls = [sbf(f"PTl{j}", [128, 12, 64], BF16) for j in range(2)]; otk = sbf("otk", [64, 128], BF16)
            smms = [sbf(f"sm{j}", [64, 8]) for j in range(2)]; woT = sbf("woT", [128, 512], BF16)
            for gi in range(8):
                for j in range(2):
                    b.dma("sp", Bh[:, j, :, :].rearrange("p a k -> p (a k)"), na_bias[2 * gi + j, :, :])
                    for dr in range(15):
                        TT(Bh[:, j, dr, :], Bh[:, j, dr, :], cmask[:, :], ALU.add)
                    b.op("dve", "tensor_scalar", out=Bh[:, j, :, :].rearrange("p a k -> p (a k)"),
                         in0=Bh[:, j, :, :].rearrange("p a k -> p (a k)"), scalar1=8.0, scalar2=None, op0=ALU.mult)
                for cb in range(4):
                    b.dma("sp", c32t[:, :], cache_nk[cb * 128:(cb + 1) * 128, gi * 128:(gi + 1) * 128])
                    b.op("dve", "tensor_copy", out=ckb[:, :], in_=c32t[:, :])
                    b.op("pe", "transpose", out=psTb[cb % 2][:, 0, :], in_=ckb[:, :], identity=ident_bf[:, :])
                    b.op("act", "activation", out=kTc[:, cb * 128:(cb + 1) * 128], in_=psTb[cb % 2][:, 0, :], func=AF.Copy)
                    b.dma("pool", vtc[:, cb, :], cache_nv[cb * 128:(cb + 1) * 128, gi * 128:(gi + 1) * 128])
                for tl in range(NTL):
                    lat_load(zres, tl)
                    if gi == 0:
                        lat_ffn_sub(2, 0)
                        lat_store(zres, tl)
                    pre_norm(2, 1)
                    sl_ = wslot()
                    v = sl_[:, 0:KC * 256].rearrange("p (k n) -> p k n", k=KC)
                    b.dma("pool", v[:, :, 0:128], na_w_qkv[:, D + gi * 128:D + (gi + 1) * 128].rearrange("(k p) n -> p k n", p=128))
                    b.dma("pool", v[:, :, 128:256], na_w_qkv[:, 2 * D + gi * 128:2 * D + (gi + 1) * 128].rearrange("(k p) n -> p k n", p=128))
                    for k in range(KC):
                        b.op("pe", "matmul", accum=(k > 0), out=psG[0][:, 0:512], lhsT=v[:, k, 0:128], rhs=h[:, k, :],
                             start=(k == 0), stop=(k == KC - 1))
                    b.op("act", "activation", out=kTg[:, tl * 512:(tl + 1) * 512], in_=psG[0][:, 0:512], func=AF.Copy)
                    for rr in range(8):
                        pq = psU[rr % 2]
                        for k in range(KC):
                            b.op("pe", "matmul", accum=(k > 0), out=pq[0:64, 0:128], lhsT=h[:, k, rr * 64:(rr + 1) * 64],
                                 rhs=v[:, k, 128:256], start=(k == 0), stop=(k == KC - 1))
                        b.op("dve", "tensor_copy", out=vtg[:, tl * 8 + rr, :], in_=pq[0:64, 0:128])
                it = 0
                for tl in range(NTL):
                    lat_load(zres, tl)
                    pre_norm(2, 1)
                    sl_ = wslot()
                    v = sl_[:, 0:KC * 128].rearrange("p (k n) -> p k n", k=KC)
                    b.dma("pool", v, na_w_qkv[:, gi * 128:(gi + 1) * 128].rearrange("(k p) n -> p k n", p=128))
                    for k in range(KC):
                        b.op("pe", "matmul", accum=(k > 0), out=psG[0][:, 0:512], lhsT=v[:, k, :], rhs=h[:, k, :],
                             start=(k == 0), stop=(k == KC - 1))
                    b.op("act", "activation", out=qTg[:, :], in_=psG[0][:, 0:512], func=AF.Copy)
                    for rr in range(8):
                        r = tl * 8 + rr
                        kr0 = min(max(r - 4, 0), NR - 8)
                        dr0 = kr0 - r + 7
                        def head(j, rr=rr, kr0=kr0, dr0=dr0):
                            base = 64 * j
                            slc, Pl, Pc, PTl, smm = slcs[j], Pls[j], Pcs[j], PTls[j], smms[j]
                            pL, pC, pO, pT = psG[j], psU[j], psY[j], psTc[j]
                            lq = qTg[base:base + 64, rr * 64:(rr + 1) * 64]
                            b.op("pe", "matmul", out=pL[0:64, 0:512], lhsT=lq, rhs=kTg[base:base + 64, kr0 * 64:kr0 * 64 + 512],
                                 start=True, stop=True)
                            b.op("pe", "matmul", out=pC[0:64, 0:512], lhsT=lq, rhs=kTc[base:base + 64, :], start=True, stop=True)
                            yield
                            TT(slc[:, :], pL[0:64, 0:512], Bh[:, j, dr0:dr0 + 8, :].rearrange("p a k -> p (a k)"), ALU.add)
                            b.op("dve", "reduce_max", out=smm[:, 0:1], in_=slc[:, :], axis=AX.X)
                            b.op("dve", "reduce_max", out=smm[:, 1:2], in_=pC[0:64, 0:512], axis=AX.X)
                            yield
                            TT(smm[:, 0:1], smm[:, 0:1], smm[:, 1:2], ALU.max)
                            b.op("dve", "tensor_scalar", out=smm[:, 2:3], in0=smm[:, 0:1], scalar1=-0.125, scalar2=None, op0=ALU.mult)
                            yield
                            b.op("act", "activation", out=Pl[:, :], in_=slc[:, :], func=AF.Exp, scale=0.125, bias=smm[:, 2:3],
                                 accum_out=smm[:, 3:4])
                            b.op("act", "activation", out=Pc[:, :], in_=pC[0:64, 0:512], func=AF.Exp, scale=0.125,
                                 bias=smm[:, 2:3], accum_out=smm[:, 4:5])
                            yield
                            TT(smm[:, 5:6], smm[:, 3:4], smm[:, 4:5], ALU.add)
                            b.op("dve", "reciprocal", out=smm[:, 6:7], in_=smm[:, 5:6])
                            for jj in range(0, 8, 2):
                                for q_ in range(2):
                                    b.op("pe", "transpose", out=pT[0:64, q_, 0:64], in_=Pl[:, (jj + q_) * 64:(jj + q_ + 1) * 64],
                                         identity=ident_bf[0:64, 0:64])
                                b.op("dve" if (jj // 2) % 2 == 0 else "act", "tensor_copy" if (jj // 2) % 2 == 0 else "activation",
                                     out=PTl[0:64, jj:jj + 2, :], in_=pT[0:64, :, 0:64], **({} if (jj // 2) % 2 == 0 else {"func": AF.Copy}))
                                yield
                            for jj in range(0, 4, 2):
                                for q_ in range(2):
                                    b.op("pe", "transpose", out=pT[:, q_, 0:64], in_=Pc[:, (jj + q_) * 128:(jj + q_ + 1) * 128],
                                         identity=ident_bf[0:64, 0:64])
                                b.op("dve" if (jj // 2) % 2 == 0 else "act", "tensor_copy" if (jj // 2) % 2 == 0 else "activation",
                                     out=PTl[:, 8 + jj:10 + jj, :], in_=pT[:, :, 0:64], **({} if (jj // 2) % 2 == 0 else {"func": AF.Copy}))
                                yield
                            for jj in range(8):
                                b.op("pe", "matmul", accum=(jj > 0), out=pO[0:64, 0:64], lhsT=PTl[0:64, jj, :],
                                     rhs=vtg[:, kr0 + jj, base:base + 64], start=(jj == 0), stop=False)
                            for jj in range(4):
                                b.op("pe", "matmul", accum=True, out=pO[0:64, 0:64], lhsT=PTl[:, 8 + jj, :],
                                     rhs=vtc[:, jj, base:base + 64], start=False, stop=(jj == 3))
                            yield
                            b.op("dve", "tensor_scalar", out=otk[:, base:base + 64], in0=pO[0:64, 0:64], scalar1=smm[:, 6:7],
                                 scalar2=None, op0=ALU.mult)
                        run_interleaved([head(0), head(1)])
                        b.op("pe", "transpose", out=psTb[rr % 2][:, 0, 0:64], in_=otk[:, :], identity=ident_bf[0:64, 0:64])
                        b.op("act", "activation", out=oT[:, 0, rr * 64:(rr + 1) * 64], in_=psTb[rr % 2][:, 0, 0:64], func=AF.Copy)
                    for o4 in range(2):
                        b.dma("pool", woT[:, :], na_w_o[gi * 128:(gi + 1) * 128, o4 * 512:(o4 + 1) * 512])
                        for oo in range(4):
                            o = o4 * 4 + oo
                            py = psY[oo % 2]
                            b.op("pe", "matmul", out=py[:, 0:512], lhsT=woT[:, oo * 128:(oo + 1) * 128], rhs=oT[:, 0, :],
                                 start=True, stop=True)
                            if gi == 0:
                                b.op("act", "activation", out=y[:, o, :], in_=py[:, 0:512], func=AF.Copy)
                            else:
                                b.dma("sp", tmpA[oo % 2][:, :], ysc[o * 128:(o + 1) * 128, tl * 512:(tl + 1) * 512])
                                TT(y[:, o, :], tmpA[oo % 2][:, :], py[:, 0:512], ALU.add)
                    if gi < 7:
                        b.dma("sp", ysc[:, tl * 512:(tl + 1) * 512].rearrange("(c p) t -> p c t", p=128), y[:])
                    else:
                        post_norm_residual(2, 1)
                        lat_ffn_sub(2, 2)
                        lat_store(zres, tl)


        def dn_lat(final_dst):
            sbf = lambda n_, sh, dt_=F32: b.sb("ld_" + n_, sh, dt_)
            TT = lambda o, a_, b_, op: b.op("dve", "tensor_tensor", out=o, in0=a_, in1=b_, op=op)
            NCH = LSEQ // 64
            msk = sbf("msk", [64, 6, 64]); b.dma("sp", msk[:], dn_mask)
            cw = sbf("cw", [128, 80]); load_rows_T(cw[:, :], dn_conv_w, 80)
            ones_f = sbf("ones", [64, 128]); b.raw("dve", lambda: nc.vector.memset(ones_f[:, :], 1.0), writes=[ones_f[:, :]])
            wba = sbf("wba", [128, 2, KC, 16], BF16)
            for d in range(2):
                b.dma("pool", wba[:, d, :, :], dn_w_ba[d].rearrange("(k p) n -> p k n", p=128))
            alog = sbf("alog", [64, 16]); b.dma("sp", alog[:], dn_a_log.broadcast_to([64, 16]))
            dtb = sbf("dtb", [64, 16]); b.dma("sp", dtb[:], dn_dt_bias.broadcast_to([64, 16]))
            outg = sbf("outg", [64, 128]); b.dma("sp", outg[:], dn_out_g.broadcast_to([64, 128]))
            nexpa = sbf("nexpa", [64, 16])
            b.op("act", "activation", out=nexpa[:, :], in_=alog[:, :], func=AF.Exp)
            b.op("dve", "tensor_scalar", out=nexpa[:, :], in0=nexpa[:, :], scalar1=-1.0, scalar2=None, op0=ALU.mult)
            zt = sbf("zt", [128, 2]); b.raw("dve", lambda: nc.vector.memset(zt[:, :], 0.0), writes=[zt[:, :]])
            for mm in range(3):
                b.dma("sp", pjsc[mm, :, 0:2], zt[:, :])
                b.dma("sp", pjsc[mm, :, LSEQ + 2:LSEQ + 4], zt[:, :])
            cin = sbf("cin", [128, 516])
            qn = sbf("qn", [128, LSEQ], BF16); kn = sbf("kn", [128, LSEQ], BF16)
            kt_tok = sbf("kt", [64, NCH, 128], BF16); vt_tok = sbf("vt", [64, NCH, 128], BF16)
            zs = sbf("zs", [128, LSEQ], BF16); vtmp = sbf("vtmp", [128, 512], BF16)
            oTf = oT[:, :, :].rearrange("p c t -> p (c t)")
            graw = sbf("graw", [64, 2, NCH, 16]); beta = sbf("beta", [64, 2, NCH]); xa = sbf("xa", [64, 2, NCH])
            xe = sbf("xe", [64, 2, NCH]); gg = sbf("gg", [64, 2, NCH]); gc = sbf("gc", [64, 2, NCH]); eg = sbf("eg", [64, 2, NCH])
            def mkbufs(sx):
                f64 = lambda n_: sbf(n_ + sx, [64, 64])
                U = {}
                U["dg"], U["decS"], U["decT"], U["PTt"] = f64("dg"), f64("decS"), f64("decT"), f64("PTt")
                U["Ms"] = [(f64("Ma"), f64("MTa")), (f64("Mb"), f64("MTb"))]
                U["gcr"] = sbf("gcr" + sx, [128, 64]); U["TTb"] = sbf("TTb" + sx, [64, 64], BF16); U["aT"] = sbf("aT" + sx, [64, 64], BF16)
                U["vb"] = sbf("vb" + sx, [64, 128], BF16); U["kbg"] = sbf("kbg" + sx, [64, 128], BF16); U["kg"] = sbf("kg" + sx, [64, 128], BF16)
                U["wT"] = sbf("wT" + sx, [128, 64], BF16); U["u_sb"] = sbf("u" + sx, [64, 128]); U["vnew"] = sbf("vnew" + sx, [64, 128])
                U["vnew_b"] = sbf("vnewb" + sx, [64, 128], BF16)
                U["o1"] = sbf("o1" + sx, [64, 128]); U["sc1"] = sbf("sc1" + sx, [64, 4]); U["egl"] = sbf("egl" + sx, [128, 1])
                U["S"] = sbf("S" + sx, [128, 128]); U["Sb"] = sbf("Sb" + sx, [128, 128], BF16)
                return U
            UB = [mkbufs("_a"), mkbufs("_b")]
            og = [sbf("og0", [64, 8, 128]), sbf("og1", [64, 8, 128])]
            ss = sbf("ss", [64, 2]); onb = sbf("onb", [64, 128], BF16); junk = sbf("junk", [64, 128])
            woT = sbf("woT", [128, 1024], BF16)
            id64 = ident[0:64, 0:64]
            fl = lambda A: A[:, :, :].rearrange("p a b -> p (a b)")
            for hv in range(8):
                hq = hv // 2
                wcols = [hq * 128, 512 + hq * 128, 1024 + hv * 128, 2048 + hv * 128]
                cwc = [hq, 4 + hq, 8 + hv]
                for tl in range(NTL):
                    lat_load(zres, tl)
                    if hv == 0:
                        lat_ffn_sub(3, 0)
                        lat_store(zres, tl)
                    pre_norm(3, 1)
                    sl_ = wslot()
                    v = sl_[:, 0:KC * 512].rearrange("p (k n) -> p k n", k=KC)
                    for mm in range(4):
                        b.dma("pool", v[:, :, mm * 128:(mm + 1) * 128],
                              dn_w_in[:, wcols[mm]:wcols[mm] + 128].rearrange("(k p) n -> p k n", p=128))
                    for mm in range(4):
                        pq = psG[mm % 2]
                        for k in range(KC):
                            b.op("pe", "matmul", accum=(k > 0), out=pq[:, 0:512], lhsT=v[:, k, mm * 128:(mm + 1) * 128],
                                 rhs=h[:, k, :], start=(k == 0), stop=(k == KC - 1))
                        if mm == 3:
                            b.op("act", "activation", out=zs[:, tl * 512:(tl + 1) * 512], in_=pq[:, 0:512], func=AF.Silu)
                        else:
                            b.op("act", "activation", out=tmpA[mm % 2][:, :], in_=pq[:, 0:512], func=AF.Copy)
                            b.dma("sp", pjsc[mm, :, 2 + tl * 512:2 + (tl + 1) * 512], tmpA[mm % 2][:, :])
                    for blk in range(8 if hv == 0 else 0):
                        n = tl * 8 + blk
                        for d in range(2):
                            pq = psU[d]
                            for k in range(KC):
                                b.op("pe", "matmul", accum=(k > 0), out=pq[0:64, 0:16], lhsT=h[:, k, blk * 64:(blk + 1) * 64],
                                     rhs=wba[:, d, k, :], start=(k == 0), stop=(k == KC - 1))
                            b.op("dve", "tensor_copy", out=graw[:, d, n, :], in_=pq[0:64, 0:16])
                for d in range(2):
                    b.op("act", "activation", out=beta[:, d, :], in_=graw[:, d, :, hv], func=AF.Sigmoid)
                    b.op("dve", "tensor_scalar", out=xa[:, d, :], in0=graw[:, d, :, 8 + hv], scalar1=dtb[:, d * 8 + hv:d * 8 + hv + 1],
                         scalar2=None, op0=ALU.add)
                b.op("act", "activation", out=fl(xe), in_=fl(xa), func=AF.Abs)
                b.op("act", "activation", out=fl(xe), in_=fl(xe), func=AF.Exp, scale=-1.0)
                b.op("act", "activation", out=fl(xe), in_=fl(xe), func=AF.Ln, bias=1.0, scale=1.0)
                b.op("dve", "tensor_scalar", out=fl(xa), in0=fl(xa), scalar1=0.0, scalar2=None, op0=ALU.max)
                TT(fl(xa), fl(xa), fl(xe), ALU.add)
                for d in range(2):
                    b.op("dve", "tensor_scalar", out=gg[:, d, :], in0=xa[:, d, :], scalar1=nexpa[:, d * 8 + hv:d * 8 + hv + 1],
                         scalar2=None, op0=ALU.mult)
                    b.op("pe", "matmul", out=psS[0:64, 0:NCH], lhsT=msk[:, d, :], rhs=gg[:, d, :], start=True, stop=True)
                    b.op("dve", "tensor_copy", out=gc[:, d, :], in_=psS[0:64, 0:NCH])
                b.op("act", "activation", out=fl(eg), in_=fl(gc), func=AF.Exp)
                for tl in range(NTL):
                    for mm in range(3):
                        b.dma("sp", cin[:, :], pjsc[mm, :, tl * 512:tl * 512 + 516])
                        for j in range(5):
                            dst = tmpA[0] if j == 0 else tmpA[1]
                            b.op("act", "activation", out=dst[:, :], in_=cin[:, j:j + 512], func=AF.Identity,
                                 scale=cw[:, j * 16 + cwc[mm]:j * 16 + cwc[mm] + 1])
                            if j > 0:
                                TT(tmpA[0][:, :], tmpA[0][:, :], tmpA[1][:, :], ALU.add)
                        if mm < 2:
                            b.op("act", "activation", out=tmpA[1][:, :], in_=tmpA[0][:, :], func=AF.Silu)
                            b.op("act", "activation", out=act[:, 0, :], in_=tmpA[1][:, :], func=AF.Square)
                            b.op("pe", "matmul", out=psS[:, 0:512], lhsT=ones_bf[:, :], rhs=act[:, 0, :], start=True, stop=True)
                            b.op("act", "activation", out=rstd[:, :], in_=psS[:, 0:512], func=AF.Sqrt, scale=1.0, bias=EPS)
                            b.op("dve", "reciprocal", out=rstd[:, :], in_=rstd[:, :])
                            TT(tmpA[1][:, :], tmpA[1][:, :], rstd[:, :], ALU.mult)
                            if mm == 0:
                                b.op("act", "activation", out=qn[:, tl * 512:(tl + 1) * 512], in_=tmpA[1][:, :], func=AF.Identity,
                                     scale=128 ** -0.5)
                            else:
                                b.op("act", "activation", out=kn[:, tl * 512:(tl + 1) * 512], in_=tmpA[1][:, :], func=AF.Identity, scale=1.0)
                                for blk in range(8):
                                    n = tl * 8 + blk
                                    b.op("pe", "transpose", out=psTb[blk % 2][0:64, 0, :], in_=kn[:, n * 64:(n + 1) * 64], identity=ident_bf[:, :])
                                    b.op("dve", "tensor_copy", out=kt_tok[:, n, :], in_=psTb[blk % 2][0:64, 0, :])
                        else:
                            b.op("act", "activation", out=vtmp[:, :], in_=tmpA[0][:, :], func=AF.Silu)
                            for blk in range(8):
                                n = tl * 8 + blk
                                b.op("pe", "transpose", out=psTb[blk % 2][0:64, 0, :], in_=vtmp[:, blk * 64:(blk + 1) * 64], identity=ident_bf[:, :])
                                b.op("dve", "tensor_copy", out=vt_tok[:, n, :], in_=psTb[blk % 2][0:64, 0, :])
                def chain(d):
                    U = UB[d]
                    dg, decS, decT, PTt, Ms, gcr, TTb, aT = U["dg"], U["decS"], U["decT"], U["PTt"], U["Ms"], U["gcr"], U["TTb"], U["aT"]
                    vb, kbg, kg, wT, u_sb, vnew, vnew_b = U["vb"], U["kbg"], U["kg"], U["wT"], U["u_sb"], U["vnew"], U["vnew_b"]
                    o1, sc1, egl, S, Sb = U["o1"], U["sc1"], U["egl"], U["S"], U["Sb"]
                    pG, pU, pY = psG[d], psU[d], psY[d]
                    b.dma("sp", S[:, :], st_dn[d * 8 + hv, :, :])
                    b.op("act", "activation", out=Sb[:, :], in_=S[:, :], func=AF.Copy)
                    yield
                    last = 63 if d == 0 else 0
                    for n in (range(NCH) if d == 0 else range(NCH - 1, -1, -1)):
                        tok0 = n * 64
                        gc_, be_, eg_ = gc[:, d, n:n + 1], beta[:, d, n:n + 1], eg[:, d, n:n + 1]
                        kf, qf = kn[:, tok0:tok0 + 64], qn[:, tok0:tok0 + 64]
                        kt, vt = kt_tok[:, n, :], vt_tok[:, n, :]
                        b.op("dve", "tensor_scalar", out=dg[:, :], in0=id64, scalar1=gc_, scalar2=None, op0=ALU.mult)
                        b.op("pe", "matmul", out=psS[:, 0:64], lhsT=ones_f[:, :], rhs=dg[:, :], start=True, stop=True)
                        b.op("act", "activation", out=gcr[:, :], in_=psS[:, 0:64], func=AF.Copy)
                        yield
                        b.op("dve", "tensor_scalar", out=decS[:, :], in0=gcr[0:64, :], scalar1=-1.0, scalar2=gc_, op0=ALU.mult, op1=ALU.add)
                        b.op("dve", "tensor_scalar", out=decS[:, :], in0=decS[:, :], scalar1=0.0, scalar2=None, op0=ALU.min)
                        b.op("act", "activation", out=decS[:, :], in_=decS[:, :], func=AF.Exp)
                        TT(decS[:, :], decS[:, :], msk[:, 4 + d, :], ALU.mult)
                        yield
                        b.op("dve", "tensor_scalar", out=decT[:, :], in0=gcr[0:64, :], scalar1=gc_, scalar2=None, op0=ALU.subtract)
                        b.op("dve", "tensor_scalar", out=decT[:, :], in0=decT[:, :], scalar1=0.0, scalar2=None, op0=ALU.min)
                        b.op("act", "activation", out=decT[:, :], in_=decT[:, :], func=AF.Exp)
                        TT(decT[:, :], decT[:, :], msk[:, d, :], ALU.mult)
                        yield
                        M, MT = Ms[0]
                        b.op("pe", "matmul", out=pG[0:64, 0:64], lhsT=kf, rhs=kf, start=True, stop=True)
                        b.op("dve", "tensor_scalar", out=M[:, :], in0=pG[0:64, 0:64], scalar1=be_, scalar2=-1.0, op0=ALU.mult, op1=ALU.mult)
                        TT(M[:, :], M[:, :], decS[:, :], ALU.mult)
                        yield
                        b.op("pe", "transpose", out=pU[0:64, 0:64], in_=M[:, :], identity=id64)
                        b.op("act", "activation", out=MT[:, :], in_=pU[0:64, 0:64], func=AF.Copy)
                        TT(PTt[:, :], MT[:, :], id64, ALU.add)
                        yield
                        cur = 0
                        for k in range(1, 6):
                            M, MT = Ms[cur]
                            Mn, MTn = Ms[1 - cur]
                            b.op("pe", "matmul", out=pG[0:64, 0:64], lhsT=MT[:, :], rhs=M[:, :], start=True, stop=True)
                            b.op("act", "activation", out=Mn[:, :], in_=pG[0:64, 0:64], func=AF.Copy)
                            if k < 5:
                                b.op("pe", "matmul", out=pU[0:64, 0:64], lhsT=M[:, :], rhs=MT[:, :], start=True, stop=True)
                                b.op("dve", "tensor_copy", out=MTn[:, :], in_=pU[0:64, 0:64])
                            yield
                            b.op("pe", "matmul", out=pY[0:64, 0:64], lhsT=Mn[:, :], rhs=PTt[:, :], start=True, stop=True)
                            TT(PTt[:, :], PTt[:, :], pY[0:64, 0:64], ALU.add)
                            yield
                            cur = 1 - cur
                        b.op("act", "activation", out=TTb[:, :], in_=PTt[:, :], func=AF.Copy)
                        TT(sc1[:, 0:1], be_, eg_, ALU.mult)
                        b.op("dve", "tensor_scalar", out=vb[:, :], in0=vt, scalar1=be_, scalar2=None, op0=ALU.mult)
                        b.op("dve", "tensor_scalar", out=kbg[:, :], in0=kt, scalar1=sc1[:, 0:1], scalar2=None, op0=ALU.mult)
                        yield
                        b.op("pe", "matmul", out=pG[0:64, 0:128], lhsT=TTb[:, :], rhs=vb[:, :], start=True, stop=True)
                        b.op("act", "activation", out=u_sb[:, :], in_=pG[0:64, 0:128], func=AF.Copy)
                        b.op("pe", "matmul", out=pU[:, 0:64], lhsT=kbg[:, :], rhs=TTb[:, :], start=True, stop=True)
                        b.op("dve", "tensor_copy", out=wT[:, :], in_=pU[:, 0:64])
                        yield
                        b.op("pe", "matmul", out=pY[0:64, 0:128], lhsT=wT[:, :], rhs=Sb[:, :], start=True, stop=True)
                        TT(vnew[:, :], u_sb[:, :], pY[0:64, 0:128], ALU.subtract)
                        b.op("act", "activation", out=vnew_b[:, :], in_=vnew[:, :], func=AF.Copy)
                        yield
                        b.op("pe", "matmul", out=pG[0:64, 0:128], lhsT=qf, rhs=Sb[:, :], start=True, stop=True)
                        b.op("dve", "tensor_scalar", out=o1[:, :], in0=pG[0:64, 0:128], scalar1=eg_, scalar2=None, op0=ALU.mult)
                        b.op("pe", "matmul", out=pU[0:64, 0:64], lhsT=kf, rhs=qf, start=True, stop=True)
                        TT(aT[:, :], pU[0:64, 0:64], decT[:, :], ALU.mult)
                        yield
                        b.op("pe", "matmul", out=pY[0:64, 0:128], lhsT=aT[:, :], rhs=vnew_b[:, :], start=True, stop=True)
                        TT(o1[:, :], o1[:, :], pY[0:64, 0:128], ALU.add)
                        b.dma("sp", osc[d, :, n, :], o1[:, :])
                        yield
                        b.op("act", "activation", out=sc1[:, 1:2], in_=gc_, func=AF.Exp, scale=-1.0, bias=gcr[0:64, last:last + 1])
                        b.op("dve", "tensor_scalar", out=kg[:, :], in0=kt, scalar1=sc1[:, 1:2], scalar2=None, op0=ALU.mult)
                        b.op("act", "activation", out=egl[:, :], in_=gcr[:, last:last + 1], func=AF.Exp)
                        yield
                        b.op("pe", "matmul", out=pG[:, 0:128], lhsT=kg[:, :], rhs=vnew_b[:, :], start=True, stop=True)
                        b.op("dve", "tensor_scalar", out=S[:, :], in0=S[:, :], scalar1=egl[:, 0:1], scalar2=None, op0=ALU.mult)
                        TT(S[:, :], S[:, :], pG[:, 0:128], ALU.add)
                        b.op("act", "activation", out=Sb[:, :], in_=S[:, :], func=AF.Copy)
                        yield

                gens = [chain(0), chain(1)]
                alive = [True, True]
                while any(alive):
                    for gi_ in range(2):
                        if alive[gi_]:
                            try:
                                next(gens[gi_])
                            except StopIteration:
                                alive[gi_] = False
                for n0 in range(0, NCH, 8):
                    for d in range(2):
                        b.dma("sp", og[d][:, :, :], osc[d, :, n0:n0 + 8, :])
                    for nn in range(8):
                        n = n0 + nn
                        tok0 = n * 64
                        o1 = UB[0]["o1"]
                        TT(o1[:, :], og[0][:, nn, :], og[1][:, nn, :], ALU.add)
                        b.op("act", "activation", out=junk[:, :], in_=o1[:, :], func=AF.Square, accum_out=ss[:, 0:1])
                        b.op("act", "activation", out=ss[:, 1:2], in_=ss[:, 0:1], func=AF.Sqrt, scale=1.0 / 128, bias=EPS)
                        b.op("dve", "reciprocal", out=ss[:, 1:2], in_=ss[:, 1:2])
                        b.op("dve", "tensor_scalar", out=junk[:, :], in0=o1[:, :], scalar1=ss[:, 1:2], scalar2=None, op0=ALU.mult)
                        TT(onb[:, :], junk[:, :], outg[:, :], ALU.mult)
                        b.op("pe", "transpose", out=psTb[n % 2][:, 0, 0:64], in_=onb[:, :], identity=ident_bf[0:64, 0:64])
                        TT(oTf[:, tok0:tok0 + 64], psTb[n % 2][:, 0, 0:64], zs[:, tok0:tok0 + 64], ALU.mult)
                b.dma("pool", woT[:, :], dn_w_o[hv * 128:(hv + 1) * 128, :])
                for tl in range(NTL):
                    if hv == 7:
                        lat_load(zres, tl)
                    for o in range(KC):
                        py = psY[o % 2]
                        b.op("pe", "matmul", out=py[:, 0:512], lhsT=woT[:, o * 128:(o + 1) * 128], rhs=oTf[:, tl * 512:(tl + 1) * 512],
                             start=True, stop=True)
                        if hv == 0:
                            b.op("act", "activation", out=y[:, o, :], in_=py[:, 0:512], func=AF.Copy)
                        else:
                            b.dma("sp", tmpA[o % 2][:, :], ysc[o * 128:(o + 1) * 128, tl * 512:(tl + 1) * 512])
                            TT(y[:, o, :], tmpA[o % 2][:, :], py[:, 0:512], ALU.add)
                    if hv < 7:
                        b.dma("sp", ysc[:, tl * 512:(tl + 1) * 512].rearrange("(c p) t -> p c t", p=128), y[:])
                    else:
                        post_norm_residual(3, 1)
                        lat_ffn_sub(3, 2)
                        lat_store(final_dst, tl)

        CS = 0 if CTX_SKIP else STAGE
        def dbg(ap, n, col0=0):
            b.dma("sp", dbg_out[:, col0:col0 + n], ap)

        if CS >= 1:
            ada_layer(0)
            if DEBUG:
                dbg(mods[0][:, :], 72)
                dbg(Acoef[:, 0:24], 24, 72)
                dbg(Gcoef[:, 0:24], 24, 96)
        if CS >= 2:
            pre_norm(0, 0)
            if DEBUG:
                dbg(rstd[:, :], 512, 512)
        if CS >= 3:
            ffn(ffn_w_gu[0, 0], ffn_w_d[0, 0], key=0)
            if DEBUG:
                dbg(y[:, 0, :], 512, 1024)
        if CS >= 4:
            post_norm_residual(0, 0)
        if CS >= 5:
            pre_norm(0, 1)
            proj_tokmajor(a_w_qkv, 1024, 256, k_out)
            proj_tokmajor(a_w_qkv, 1280, 256, v_out, vdst_col0=0)
        if CS >= 6:
            proj_featmajor(qT, 0, a_w_qkv, 0, 8, h)
            proj_featmajor(kT, 0, a_w_qkv, 1024, 4, h, dup64=True)
            hm = {hh: (hh // 2, 64 * (hh % 2), hh // 4, (hh // 4) * 64) for hh in range(16)}
            if SUB >= 2:
                attn_ctx(16, hm, True, 0.125)
            if DEBUG and SUB >= 2:
                b.op("dve", "tensor_copy", out=tmpA[0][:, :], in_=oT[:, 0, :])
                dbg(tmpA[0][:, :], 512, 1536)
            if SUB >= 1:
                proj_out_featmajor(a_w_o, oT)
                post_norm_residual(0, 1)
        if CS >= 7:
            pre_norm(0, 2)
            ffn(ffn_w_gu[0, 1], ffn_w_d[0, 1], key=1)
            post_norm_residual(0, 2)
        if CS >= 8:
            ada_layer(1)
            pre_norm(1, 0)
            ffn(ffn_w_gu[1, 0], ffn_w_d[1, 0], key=2)
            post_norm_residual(1, 0)
        if CS >= 9:
            pre_norm(1, 1)
            s5_es = ExitStack()
            b.es = s5_es
            S5P = s5_setup()
            s5_mixer(S5P)
            b.es = es
            if DEBUG:
                dbg(y[:, 0, :], 512, 2048)
            post_norm_residual(1, 1)
        if CS >= 10:
            pre_norm(1, 2)
            ffn(ffn_w_gu[1, 1], ffn_w_d[1, 1], key=3)
            post_norm_residual(1, 2)
        if CS >= 11:
            ada_layer(2)
            pre_norm(2, 0)
            ffn(ffn_w_gu[2, 0], ffn_w_d[2, 0], key=4)
            post_norm_residual(2, 0)
            pre_norm(2, 1)
            proj_tokmajor(na_w_qkv, 1024, 1024, nak_out)
            proj_tokmajor(na_w_qkv, 2048, 1024, nav_out, vdst_col0=0)
            proj_featmajor(qT, 0, na_w_qkv, 0, 8, h)
            proj_featmajor(kT, 0, na_w_qkv, 1024, 8, h)
            hm2 = {hh: (hh // 2, 64 * (hh % 2), hh // 2, hh * 64) for hh in range(16)}
            attn_ctx(16, hm2, None, 0.125)
            proj_out_featmajor(na_w_o, oT)
            post_norm_residual(2, 1)
            pre_norm(2, 2)
            ffn(ffn_w_gu[2, 1], ffn_w_d[2, 1], key=5)
            post_norm_residual(2, 2)
        if CS >= 12:
            ada_layer(3)
            pre_norm(3, 0)
            ffn(ffn_w_gu[3, 0], ffn_w_d[3, 0], key=6)
            post_norm_residual(3, 0)
        if CS >= 13:
            pre_norm(3, 1)
            b.fence()
            s5_es.close()
            att_es.close()
            dn_es = ExitStack()
            b.es = dn_es
            dn_mixer()
            b.es = es
            if DEBUG:
                dbg(y[:, 0, :], 512, 2560)
            post_norm_residual(3, 1)
            pre_norm(3, 2)
            ffn(ffn_w_gu[3, 1], ffn_w_d[3, 1], key=7)
            post_norm_residual(3, 2)
        b.dma("sp", yT_out.rearrange("(c p) t -> p c t", p=128), x[:])
        if STAGE >= 20:
            b.fence()
            if CTX_SKIP:
                att_es.close()
            else:
                dn_es.close()
            load_rows_T(cc[:, :], c_lat, 8)
            b.op("act", "activation", out=scond_lat[:, :, 0], in_=cc[:, :], func=AF.Silu)
            lat_es = ExitStack()
            b.es = lat_es
            ada_layer(0, scond_lat)
            a_lat(xT_lat)
            b.es = es
            b.fence()
            lat_es.close()
            if STAGE >= 21:
                lat_es = ExitStack()
                b.es = lat_es
                ada_layer(1, scond_lat)
                s5_lat()
                b.es = es
                b.fence()
                lat_es.close()
            if STAGE >= 22:
                lat_es = ExitStack()
                b.es = lat_es
                ada_layer(2, scond_lat)
                na_lat()
                b.es = es
                b.fence()
                lat_es.close()
            if STAGE >= 23:
                lat_es = ExitStack()
                b.es = lat_es
                ada_layer(3, scond_lat)
                dn_lat(zres)
                b.es = es
                b.fence()
                lat_es.close()
        if STAGE >= 20:
            for tl in range(NTL):
                lat_load(zres, tl)
                lat_store(ysamp_out, tl)
        else:
            b.raw("dve", lambda: nc.vector.memset(tmpA[0][:, :], 0.0), writes=[tmpA[0][:, :]])
            for tl in range(NTL):
                for c in range(KC):
                    b.dma("sp", ysamp_out[c * 128:(c + 1) * 128, tl * 512:(tl + 1) * 512], tmpA[0][:, :])
        if STAGE < 13:
            for j in range(32):
                b.dma("sp", dn_out[j, :, :], tmpA[0][:, 0:128])

        b.finish()
        if STAGE >= 20:
            pass
        elif STAGE >= 13:
            dn_es.close()
        else:
            if CS >= 9:
                s5_es.close()
            att_es.close()
    return nc


_PROG = None


def _dn_masks():
    i = np.arange(64)
    bef0 = (i[:, None] <= i[None, :]).astype(np.float32)
    bef1 = (i[:, None] >= i[None, :]).astype(np.float32)
    eye = np.eye(64, dtype=np.float32)
    m = np.stack([bef0, bef1, bef0.T, bef1.T, bef0.T - eye, bef1.T - eye], 1)
    return np.ascontiguousarray(m, dtype=np.float32)


def _rope_tables(L):
    n = 16
    inv = (np.float32(10000.0) ** (-np.arange(n, dtype=np.float32) / np.float32(n))).astype(np.float32)
    t = np.arange(L)
    ang_r = (t // 64).astype(np.float32)[:, None] * inv[None, :]
    ang_c = (t % 64).astype(np.float32)[:, None] * inv[None, :]
    return np.ascontiguousarray(np.concatenate([np.cos(ang_r), np.cos(ang_c), np.sin(ang_r), np.sin(ang_c)], 1), dtype=np.float32)


def _na_bias_gather(rpb):
    q = np.arange(64)[:, None]
    k = np.arange(64)[None, :]
    dc = np.clip(k - q, -15, 15) + 15
    g = rpb[:, :, dc]
    return np.ascontiguousarray(g.transpose(0, 2, 1, 3).reshape(16, 64, 960), dtype=np.float32)


def _na_colmask():
    col = np.arange(64)
    cs = np.clip(col - 8, 0, 48)
    ok = (col[None, :] >= cs[:, None]) & (col[None, :] < cs[:, None] + 16)
    return np.where(ok, 0.0, -30000.0).astype(np.float32)


def _band_mask():
    q = np.arange(128)[:, None]
    j = np.arange(384)[None, :] - 128
    return np.where(np.abs(j - q) <= 128, 0.0, -30000.0).astype(np.float32)


def prep_core(inputs, r, lseq=None):
    f = lambda a: np.ascontiguousarray(np.asarray(a, dtype=np.float32))
    L = LSEQ if lseq is None else lseq
    bsel = r % 2
    return {
        "xT_ctx": np.ascontiguousarray(f(inputs["x_prompt"])[2 * r:2 * r + 2].reshape(TCTX, D).T),
        "xT_lat": np.ascontiguousarray(f(inputs["x_sample"])[bsel, :L].T),
        "c_lat": f(inputs["c"])[bsel].reshape(8, 128),
        "cache_ak": f(inputs["cache_attn_k"])[bsel, 0].reshape(512, 256),
        "cache_av": f(inputs["cache_attn_v"])[bsel, 0].reshape(512, 256),
        "rope_cs": _rope_tables(L),
        "st5_re": f(inputs["state_s5_re"])[bsel, 0].reshape(64, 128),
        "st_dn": f(inputs["state_dn"])[bsel, 0].reshape(16, 128, 128),
        "cache_nk": f(inputs["cache_na_k"])[bsel, 0].reshape(512, D),
        "cache_nv": f(inputs["cache_na_v"])[bsel, 0].reshape(512, D),
        "na_bias": _na_bias_gather(f(inputs["na_rpb"])[0]),
        "na_cmask": _na_colmask(),
        "st5_im": f(inputs["state_s5_im"])[bsel, 0].reshape(64, 128),
        "band_mask": _band_mask(),
    }


def prep_shared(inputs, nl=4):
    f = lambda a: np.ascontiguousarray(np.asarray(a, dtype=np.float32))
    return {
        "ident": np.eye(128, dtype=np.float32),
        "c_ctx": f(inputs["c_ctx"]).reshape(8, 128),
        "norm_g": f(inputs["norm_g"]).reshape(192, 128),
        "w_ada": f(inputs["w_ada"][0:nl]),
        "b_ada": f(inputs["b_ada"]).reshape(288, 128),
        "ffn_w_gu": f(inputs["ffn_w_gu"][0:nl]),
        "ffn_w_d": f(inputs["ffn_w_d"][0:nl]),
        "a_w_qkv": f(inputs["a_w_qkv"])[0],
        "a_w_o": f(inputs["a_w_o"])[0],
        "a_sink": f(inputs["a_sink"]).reshape(1, 16),
        "s5_lam_re": f(inputs["s5_lam_re"]).reshape(64, 128),
        "s5_lam_im": f(inputs["s5_lam_im"]).reshape(64, 128),
        "s5_logdt": f(np.repeat(np.asarray(inputs["s5_log_dt"], np.float32).reshape(2, 32, 2), 64, axis=-1)).reshape(64, 128),
        "s5_b_re": f(inputs["s5_b_re"])[0],
        "s5_b_im": f(inputs["s5_b_im"])[0],
        "s5_c_re": f(inputs["s5_c_re"])[0],
        "s5_c_im": f(inputs["s5_c_im"])[0],
        "s5_d": f(inputs["s5_d"]).reshape(8, 128),
        "s5_w_glu": f(inputs["s5_w_glu"])[0],
        "na_w_qkv": f(inputs["na_w_qkv"])[0],
        "na_w_o": f(inputs["na_w_o"])[0],
        "dn_w_in": f(inputs["dn_w_in"])[0],
        "dn_conv_w": f(inputs["dn_conv_w"]).reshape(5, 16, 128).reshape(80, 128),
        "dn_w_ba": f(inputs["dn_w_ba"])[0],
        "dn_a_log": f(inputs["dn_a_log"]).reshape(1, 16),
        "dn_dt_bias": f(inputs["dn_dt_bias"]).reshape(1, 16),
        "dn_out_g": f(inputs["dn_out_g"]).reshape(1, 128),
        "dn_w_o": f(inputs["dn_w_o"])[0],
        "dn_mask": _dn_masks(),
    }


def kernel(**inputs):
    global _PROG
    f = lambda a: np.ascontiguousarray(np.asarray(a, dtype=np.float32))
    x_prompt = f(inputs["x_prompt"])
    if _PROG is None:
        _PROG = build_program()
    nc = _PROG
    shared = prep_shared(inputs)
    in_maps = []
    for r in range(NCORES):
        m = dict(shared)
        m.update(prep_core(inputs, r))
        in_maps.append(m)
    res = run_bass_kernel_spmd(nc, in_maps, core_ids=list(range(NCORES)))
    R = res.results
    y_prompt = np.stack([R[r]["out_yT_ctx"].T.reshape(2, SEQ, D) for r in range(NCORES)]).reshape(16, SEQ, D)
    new_k = np.concatenate([R[r]["out_attn_k"].reshape(2, 1, SEQ, 4, 64) for r in range(NCORES)], 0)
    new_v = np.concatenate([R[r]["out_attn_v"].reshape(2, 1, SEQ, 4, 64) for r in range(NCORES)], 0)
    s5re = np.concatenate([R[r]["out_s5_re"].reshape(2, 1, 2, 64, 64) for r in range(NCORES)], 0)
    s5im = np.concatenate([R[r]["out_s5_im"].reshape(2, 1, 2, 64, 64) for r in range(NCORES)], 0)
    nak = np.concatenate([R[r]["out_na_k"].reshape(2, 1, SEQ, 16, 64) for r in range(NCORES)], 0)
    nav = np.concatenate([R[r]["out_na_v"].reshape(2, 1, SEQ, 16, 64) for r in range(NCORES)], 0)
    ysamp = np.stack([R[r]["out_y_sample"].T for r in range(2)], 0)
    dn = np.concatenate([R[r]["out_dn"].reshape(2, 1, 2, 8, 128, 128) for r in range(NCORES)], 0)
    c32 = lambda a: np.ascontiguousarray(a, dtype=np.float32)
    return (c32(y_prompt), c32(ysamp), c32(new_k), c32(new_v), c32(s5re), c32(s5im), c32(nak), c32(nav), c32(dn))
```

```python
import numpy as np
from contextlib import ExitStack
import concourse.bass as bass
import concourse.mybir as mybir
from concourse.bass_utils import run_bass_kernel_spmd

F32 = mybir.dt.float32
BF16 = mybir.dt.bfloat16
AF = mybir.ActivationFunctionType
ALU = mybir.AluOpType
AX = mybir.AxisListType

D = 1024
KC = 8
DFF = 2816
FC = 22
NCORES = 8
TCTX = 512
SEQ = 256
EPS = 1e-6
WSLOT = 6144
NWS = 3
STAGE = 99
LSEQ = 4096
CTX_SKIP = False
SUB = 99
DEBUG = False


class Eng:
    def __init__(self, name, h, sem):
        self.name, self.h, self.sem = name, h, sem
        self.count = 0
        self.waited = {}


class Builder:
    def __init__(self, nc, es):
        self.nc, self.es = nc, es
        self.es_sem = es
        self.sems = []
        self.engs = {}
        for nm, h in [("pe", nc.tensor), ("act", nc.scalar), ("dve", nc.vector),
                      ("pool", nc.gpsimd), ("sp", nc.sync)]:
            sem = es.enter_context(nc.semaphore("sem_" + nm))
            self.sems.append(sem)
            e = Eng(nm, h, sem)
            e.key = len(self.sems) - 1
            self.engs[nm] = e
        self.track = {}
        self.dsem = {}
        self.out_events = []

    def sb(self, name, shape, dt):
        return self.es.enter_context(self.nc.sbuf_tensor(name, list(shape), dt))

    def ps(self, name, shape, dt=F32):
        return self.es.enter_context(self.nc.psum_tensor(name, list(shape), dt))

    def _nm(self, a):
        return a if isinstance(a, str) else a.tensor.name

    def _deps(self, reads, writes, skip_own_waw=None, own=None):
        deps = set()
        for r in reads:
            nm = self._nm(r)
            st = self.track.get(nm)
            if st and st[0]:
                deps.add(st[0])
            if st and nm.startswith("ps"):
                for ev in st[1]:
                    if ev[0] != own:
                        deps.add(ev)
        for w in writes:
            st = self.track.get(self._nm(w))
            if st:
                if st[0] and not (skip_own_waw is not None and st[0][0] == skip_own_waw):
                    deps.add(st[0])
                for ev in st[1]:
                    deps.add(ev)
        return deps

    def _wait(self, e, deps):
        best = {}
        for (k, v) in deps:
            if best.get(k, 0) < v:
                best[k] = v
        for k, v in best.items():
            if e.waited.get(k, 0) < v:
                e.h.wait_ge(self.sems[k], v)
                e.waited[k] = v

    def _commit(self, ev, reads, writes, accum=False):
        for r in reads:
            st = self.track.setdefault(self._nm(r), [None, []])
            st[1].append(ev)
        for w in writes:
            nm = self._nm(w)
            if accum and nm in self.track:
                self.track[nm][0] = ev
            else:
                self.track[nm] = [ev, []]

    def op(self, eng, fname, accum=False, extra_reads=(), extra_writes=(), **kw):
        e = self.engs[eng]
        reads, writes = list(extra_reads), list(extra_writes)
        for k, v in kw.items():
            if isinstance(v, bass.AP):
                if k in ("out", "accum_out"):
                    writes.append(v)
                else:
                    reads.append(v)
        deps = self._deps(reads, writes, skip_own_waw=(e.key if accum else None), own=e.key)
        self._wait(e, deps)
        inst = getattr(e.h, fname)(**kw)
        e.count += 1
        inst.then_inc(e.sem, 1)
        ev = (e.key, e.count)
        self._commit(ev, reads, writes, accum=accum)
        return ev

    def raw(self, eng, fn, reads=(), writes=()):
        e = self.engs[eng]
        self._wait(e, self._deps(list(reads), list(writes), own=e.key))
        inst = fn()
        e.count += 1
        inst.then_inc(e.sem, 1)
        ev = (e.key, e.count)
        self._commit(ev, list(reads), list(writes))
        return ev

    def dma(self, q, out, in_, **kw):
        e = self.engs[q]
        deps = self._deps([in_], [out])
        self._wait(e, deps)
        nm = self._nm(out)
        if nm not in self.dsem:
            sem = self.es_sem.enter_context(self.nc.semaphore("dsem_" + nm))
            self.sems.append(sem)
            self.dsem[nm] = [len(self.sems) - 1, 0]
        ds = self.dsem[nm]
        e.h.dma_start(out=out, in_=in_, **kw).then_inc(self.sems[ds[0]], 16)
        ds[1] += 16
        ev = (ds[0], ds[1])
        self._commit(ev, [in_], [out])
        if out.tensor.name.startswith("out_"):
            self.out_events.append(ev)
        return ev

    def fence(self):
        evs = {(x.key, x.count) for x in self.engs.values() if x.count > 0}
        evs |= {(k, c) for (k, c) in self.dsem.values() if c > 0}
        for e in self.engs.values():
            self._wait(e, {ev for ev in evs if ev[0] != e.key})

    def finish(self):
        e = self.engs["sp"]
        self._wait(e, set(self.out_events))
        self._wait(e, {(x.key, x.count) for x in self.engs.values() if x.count > 0 and x.name != "sp"})


def build_program(nl=4):
    nc = bass.Bass("TRN2", target_bir_lowering=False)

    def din(name, shape):
        return nc.dram_tensor(name, list(shape), F32, kind="ExternalInput").ap()

    def dout(name, shape):
        return nc.dram_tensor("out_" + name, list(shape), F32, kind="ExternalOutput").ap()

    T = TCTX
    xT_in = din("xT_ctx", [D, T])
    ident_in = din("ident", [128, 128])
    c_ctx = din("c_ctx", [8, 128])
    norm_g = din("norm_g", [192, 128])
    w_ada = din("w_ada", [nl, D, 9 * D])
    b_ada = din("b_ada", [288, 128])
    ffn_w_gu = din("ffn_w_gu", [nl, 2, D, 2 * DFF])
    ffn_w_d = din("ffn_w_d", [nl, 2, DFF, D])
    a_w_qkv = din("a_w_qkv", [D, 1536])
    a_w_o = din("a_w_o", [D, D])
    a_sink = din("a_sink", [1, 16])
    s5_lam_re = din("s5_lam_re", [64, 128])
    s5_lam_im = din("s5_lam_im", [64, 128])
    s5_logdt = din("s5_logdt", [64, 128])
    s5_b_re = din("s5_b_re", [2, 64, 64, 16])
    s5_b_im = din("s5_b_im", [2, 64, 64, 16])
    s5_c_re = din("s5_c_re", [2, 64, 16, 64])
    s5_c_im = din("s5_c_im", [2, 64, 16, 64])
    s5_d = din("s5_d", [8, 128])
    s5_w_glu = din("s5_w_glu", [D, 2 * D])
    na_w_qkv = din("na_w_qkv", [D, 3 * D])
    na_w_o = din("na_w_o", [D, D])
    dn_w_in = din("dn_w_in", [D, 3 * D])
    dn_conv_w = din("dn_conv_w", [80, 128])
    dn_w_ba = din("dn_w_ba", [2, D, 16])
    dn_a_log = din("dn_a_log", [1, 16])
    dn_dt_bias = din("dn_dt_bias", [1, 16])
    dn_out_g = din("dn_out_g", [1, 128])
    dn_w_o = din("dn_w_o", [D, D])
    dn_mask = din("dn_mask", [64, 6, 64])
    xT_lat = din("xT_lat", [D, LSEQ])
    c_lat = din("c_lat", [8, 128])
    cache_ak = din("cache_ak", [512, 256])
    cache_av = din("cache_av", [512, 256])
    rope_cs = din("rope_cs", [LSEQ, 64])
    band_mask = din("band_mask", [128, 384])
    zres = nc.dram_tensor("zres", [D, LSEQ], F32, kind="Internal").ap()
    wsc = nc.dram_tensor("wsc", [8, 15, 128, WSLOT], BF16, kind="Internal").ap()
    ysc = nc.dram_tensor("ysc", [D, LSEQ], F32, kind="Internal").ap()
    rot = nc.dram_tensor("rot", [64, 128, 1536], F32, kind="Internal").ap()
    cache_nk = din("cache_nk", [512, D])
    cache_nv = din("cache_nv", [512, D])
    na_bias = din("na_bias", [16, 64, 960])
    na_cmask = din("na_cmask", [64, 64])
    st_dn = din("st_dn", [16, 128, 128])
    pjsc = nc.dram_tensor("pjsc", [16, 128, LSEQ + 4], F32, kind="Internal").ap()
    zsc = nc.dram_tensor("zsc", [8, 128, LSEQ], BF16, kind="Internal").ap()
    qsc = nc.dram_tensor("qsc", [8, 128, LSEQ], BF16, kind="Internal").ap()
    ksc = nc.dram_tensor("ksc", [8, 128, LSEQ], BF16, kind="Internal").ap()
    vsc = nc.dram_tensor("vsc", [8, 64, LSEQ // 64, 128], BF16, kind="Internal").ap()
    osc = nc.dram_tensor("osc", [2, 64, LSEQ // 64, 128], F32, kind="Internal").ap()
    st5_re = din("st5_re", [64, 128])
    st5_im = din("st5_im", [64, 128])

    yT_out = dout("yT_ctx", [D, T])
    k_out = dout("attn_k", [T, 256])
    v_out = dout("attn_v", [T, 256])
    nak_out = dout("na_k", [T, D])
    nav_out = dout("na_v", [T, D])
    ysamp_out = dout("y_sample", [D, LSEQ])
    dn_out = dout("dn", [32, 128, 128])
    s5re_out = dout("s5_re", [128, 128])
    s5im_out = dout("s5_im", [128, 128])
    dbg_out = dout("dbg", [128, 4096]) if DEBUG else None

    with ExitStack() as es:
        b = Builder(nc, es)
        x = b.sb("x", [128, KC, T], F32)
        h = b.sb("h", [128, KC, T], BF16)
        y = b.sb("y", [128, KC, T], F32)
        act = b.sb("act", [128, FC, T], BF16)
        sq = act
        rstd = b.sb("rstd", [128, T], F32)
        tmpA = [b.sb(f"tmpA{i}", [128, T], F32) for i in range(2)]
        ident = b.sb("ident_sb", [128, 128], F32)
        ident_bf = b.sb("ident_bf", [128, 128], BF16)
        ones_bf = b.sb("ones_bf", [128, 128], BF16)
        wslots = [b.sb(f"wslot{i}", [128, WSLOT], BF16) for i in range(NWS)]
        ng = b.sb("ng", [128, 192], F32)
        bada = b.sb("bada", [128, 288], F32)
        scond = b.sb("scond", [128, KC, 1], BF16)
        scond_lat = b.sb("scond_lat", [128, KC, 1], BF16)
        cc = b.sb("cc", [128, KC], F32)
        mods = [b.sb(f"mods{i}", [128, 72], F32) for i in range(4)]
        Acoef = b.sb("Acoef", [128, 4 * 3 * KC], F32)
        Gcoef = b.sb("Gcoef", [128, 4 * 3 * KC], F32)
        rows_tmp = b.sb("rows_tmp", [96, 128], F32)
        oT = b.sb("oT", [128, KC, T], BF16)
        sink_bc = b.sb("sink_bc", [128, 16], F32)
        st_m = [b.sb(f"st_m{i}", [128, 8], F32) for i in range(2)]
        att_es = ExitStack()
        b.es = att_es
        qT = b.sb("qT", [128, 8, T], BF16)
        kT = b.sb("kT", [128, 8, T], BF16)
        vtok = b.sb("vtok", [128, 4, 1024], BF16)
        kv32 = [b.sb(f"kv32_{i}", [128, 512], F32) for i in range(2)]
        Pm = [b.sb(f"Pm{i}", [128, 256], BF16) for i in range(2)]
        PT = [b.sb(f"PT{i}", [128, 2, 128], BF16) for i in range(2)]
        otok = b.sb("otok", [128, 1024], BF16)
        b.es = es
        psG = [b.ps(f"psG{i}", [128, 512]) for i in range(2)]
        psU = [b.ps(f"psU{i}", [128, 512]) for i in range(2)]
        psY = [b.ps(f"psY{i}", [128, 512]) for i in range(2)]
        psS = b.ps("psS", [128, 512])
        psM = psS
        psT_all = b.ps("psT", [128, 1024], BF16)
        psTb = [psT_all[:, i * 256:(i + 1) * 256].rearrange("p (a n) -> p a n", a=2) for i in range(2)]
        psTc = [psTb[0], psS[:, :].bitcast(BF16)[:, 0:256].rearrange("p (a n) -> p a n", a=2)]

        def run_interleaved(gens):
            alive = [True] * len(gens)
            while any(alive):
                for gi_ in range(len(gens)):
                    if alive[gi_]:
                        try:
                            next(gens[gi_])
                        except StopIteration:
                            alive[gi_] = False

        wctr = [0]

        def wslot():
            s = wslots[wctr[0] % NWS]
            wctr[0] += 1
            return s

        b.dma("sp", ident[:], ident_in)
        b.op("dve", "tensor_copy", out=ident_bf[:], in_=ident[:])
        b.raw("dve", lambda: nc.vector.memset(ones_bf[:], 1.0), writes=[ones_bf[:]])

        def load_rows_T(dst_ap, src_rows_ap, nrows):
            b.dma("sp", rows_tmp[0:nrows, :], src_rows_ap)
            b.op("pe", "transpose", out=psM[:, 0:nrows], in_=rows_tmp[0:nrows, :], identity=ident[0:nrows, 0:nrows])
            b.op("dve", "tensor_copy", out=dst_ap, in_=psM[:, 0:nrows])

        load_rows_T(ng[:, 0:96], norm_g[0:96, :], 96)
        load_rows_T(ng[:, 96:192], norm_g[96:192, :], 96)
        for i in range(3):
            load_rows_T(bada[:, 96 * i:96 * (i + 1)], b_ada[96 * i:96 * (i + 1), :], 96)
        load_rows_T(cc[:, :], c_ctx, 8)
        b.op("act", "activation", out=scond[:, :, 0], in_=cc[:, :], func=AF.Silu)
        b.dma("sp", sink_bc[:], a_sink.broadcast_to([128, 16]))

        b.dma("sp", x[:], xT_in.rearrange("(c p) t -> p c t", p=128))

        def ada_layer(i, sc=None):
            sc = scond if sc is None else sc
            for pc in range(18):
                sl = wslot()
                v = sl[:, 0:KC * 512].rearrange("p (k n) -> p k n", k=KC)
                b.dma("pool", v, w_ada[i, :, pc * 512:(pc + 1) * 512].rearrange("(k p) n -> p k n", p=128))
                for cj in range(4):
                    j = pc * 4 + cj
                    for k in range(KC):
                        b.op("pe", "matmul", accum=(k > 0), out=psM[:, j:j + 1],
                             lhsT=v[:, k, cj * 128:(cj + 1) * 128], rhs=sc[:, k, :],
                             start=(k == 0), stop=(k == KC - 1))
            b.op("dve", "tensor_tensor", out=mods[i][:, :], in0=psM[:, 0:72], in1=bada[:, 72 * i:72 * (i + 1)],
                 op=ALU.add)
            for s in range(3):
                w = 1.0 if s == 1 else 0.5
                col = (i * 3 + s) * KC
                gpre = ng[:, (i * 6 + 2 * s) * KC:(i * 6 + 2 * s + 1) * KC]
                gpost = ng[:, (i * 6 + 2 * s + 1) * KC:(i * 6 + 2 * s + 2) * KC]
                scale_ = mods[i][:, (3 * s + 1) * KC:(3 * s + 2) * KC]
                gate_ = mods[i][:, (3 * s + 2) * KC:(3 * s + 3) * KC]
                b.op("dve", "scalar_tensor_tensor", out=Acoef[:, col:col + KC], in0=scale_, scalar=1.0, in1=gpre,
                     op0=ALU.add, op1=ALU.mult)
                b.op("dve", "scalar_tensor_tensor", out=Gcoef[:, col:col + KC], in0=gate_, scalar=w, in1=gpost,
                     op0=ALU.mult, op1=ALU.mult)

        def rms_stats(src):
            for c in range(KC):
                b.op("act", "activation", out=sq[:, c, :], in_=src[:, c, :], func=AF.Square)
            for c in range(KC):
                b.op("pe", "matmul", accum=(c > 0), out=psS[:, 0:T], lhsT=ones_bf[:, :], rhs=sq[:, c, :],
                     start=(c == 0), stop=(c == KC - 1))
            b.op("act", "activation", out=rstd[:, :], in_=psS[:, 0:T], func=AF.Sqrt, scale=1.0 / D, bias=EPS)
            b.op("dve", "reciprocal", out=rstd[:, :], in_=rstd[:, :])

        def pre_norm(i, s):
            rms_stats(x)
            col = (i * 3 + s) * KC
            for c in range(KC):
                t = tmpA[c % 2]
                b.op("dve", "tensor_tensor", out=t[:, :], in0=x[:, c, :], in1=rstd[:, :], op=ALU.mult)
                b.op("act", "activation", out=h[:, c, :], in_=t[:, :], func=AF.Identity,
                     scale=Acoef[:, col + c:col + c + 1], bias=mods[i][:, 3 * s * KC + c:3 * s * KC + c + 1])

        def post_norm_residual(i, s):
            rms_stats(y)
            col = (i * 3 + s) * KC
            for c in range(KC):
                t = tmpA[c % 2]
                b.op("dve", "tensor_tensor", out=t[:, :], in0=y[:, c, :], in1=rstd[:, :], op=ALU.mult)
                b.op("dve", "tensor_scalar", out=t[:, :], in0=t[:, :], scalar1=Gcoef[:, col + c:col + c + 1],
                     scalar2=None, op0=ALU.mult)
                b.op("dve", "tensor_tensor", out=x[:, c, :], in0=t[:, :], in1=x[:, c, :], op=ALU.add)

        def ffn(wgu, wd, key=None, consume=False):
            for jp in range(FC // 2):
                sl = wslot()
                v = sl[:, 0:KC * 512].rearrange("p (k n) -> p k n", k=KC)
                if consume:
                    b.dma("pool", sl[:, 0:KC * 512], wsc[key, jp, :, 0:KC * 512])
                else:
                    b.dma("pool", v[:, :, 0:256], wgu[:, jp * 256:(jp + 1) * 256].rearrange("(k p) n -> p k n", p=128))
                    b.dma("pool", v[:, :, 256:512],
                          wgu[:, DFF + jp * 256:DFF + (jp + 1) * 256].rearrange("(k p) n -> p k n", p=128))
                    if key is not None:
                        b.dma("sp", wsc[key, jp, :, 0:KC * 512], sl[:, 0:KC * 512])
                for jj in range(2):
                    j = jp * 2 + jj
                    pg, pu = psG[j % 2], psU[j % 2]
                    for k in range(KC):
                        b.op("pe", "matmul", accum=(k > 0), out=pg[:, 0:T], lhsT=v[:, k, jj * 128:(jj + 1) * 128],
                             rhs=h[:, k, :], start=(k == 0), stop=(k == KC - 1))
                    for k in range(KC):
                        b.op("pe", "matmul", accum=(k > 0), out=pu[:, 0:T],
                             lhsT=v[:, k, 256 + jj * 128:256 + (jj + 1) * 128],
                             rhs=h[:, k, :], start=(k == 0), stop=(k == KC - 1))
                    t = tmpA[j % 2]
                    b.op("act", "activation", out=t[:, :], in_=pg[:, 0:T], func=AF.Silu)
                    b.op("dve", "tensor_tensor", out=act[:, j, :], in0=t[:, :], in1=pu[:, 0:T], op=ALU.mult)
            for op_ in range(4):
                sl = wslot()
                v = sl[:, 0:FC * 256].rearrange("p (k n) -> p k n", k=FC)
                if consume:
                    b.dma("pool", sl[:, 0:FC * 256], wsc[key, 11 + op_, :, 0:FC * 256])
                else:
                    b.dma("pool", v, wd[:, op_ * 256:(op_ + 1) * 256].rearrange("(k p) n -> p k n", p=128))
                    if key is not None:
                        b.dma("sp", wsc[key, 11 + op_, :, 0:FC * 256], sl[:, 0:FC * 256])
                for oo in range(2):
                    o = op_ * 2 + oo
                    py = psY[o % 2]
                    for j in range(FC):
                        b.op("pe", "matmul", accum=(j > 0), out=py[:, 0:T], lhsT=v[:, j, oo * 128:(oo + 1) * 128],
                             rhs=act[:, j, :], start=(j == 0), stop=(j == FC - 1))
                    b.op("act", "activation", out=y[:, o, :], in_=py[:, 0:T], func=AF.Copy)

        def proj_featmajor(dst, dst_chunk0, wsrc, col0, nchunks, src_act, dup64=False):
            for g0 in range(0, nchunks, 4):
                ng_ = min(4, nchunks - g0)
                sl = wslot()
                v = sl[:, 0:KC * 512].rearrange("p (k n) -> p k n", k=KC)
                if dup64:
                    sl0 = wslot()
                    v0 = sl0[:, 0:KC * 256].rearrange("p (k n) -> p k n", k=KC)
                    b.dma("pool", v0[:, :, 0:ng_ * 64],
                          wsrc[:, col0 + g0 * 64:col0 + (g0 + ng_) * 64].rearrange("(k p) n -> p k n", p=128))
                    v4 = sl[:, 0:KC * 512].rearrange("p (k m a d) -> p k m a d", k=KC, m=4, a=2)
                    for a in range(2):
                        b.op("dve", "tensor_copy", out=v4[:, :, 0:ng_, a, :],
                             in_=v0[:, :, 0:ng_ * 64].rearrange("p k (m d) -> p k m d", d=64))
                else:
                    b.dma("pool", v[:, :, 0:ng_ * 128],
                          wsrc[:, col0 + g0 * 128:col0 + (g0 + ng_) * 128].rearrange("(k p) n -> p k n", p=128))
                for m in range(ng_):
                    pq = psG[m % 2]
                    for k in range(KC):
                        b.op("pe", "matmul", accum=(k > 0), out=pq[:, 0:T], lhsT=v[:, k, m * 128:(m + 1) * 128],
                             rhs=src_act[:, k, :], start=(k == 0), stop=(k == KC - 1))
                    b.op("act", "activation", out=dst[:, dst_chunk0 + g0 + m, :], in_=pq[:, 0:T], func=AF.Copy)

        def proj_out_featmajor(wsrc, src_act):
            for g0 in range(0, KC, 4):
                sl = wslot()
                v = sl[:, 0:KC * 512].rearrange("p (k n) -> p k n", k=KC)
                b.dma("pool", v, wsrc[:, g0 * 128:(g0 + 4) * 128].rearrange("(k p) n -> p k n", p=128))
                for m in range(4):
                    py = psY[m % 2]
                    for k in range(KC):
                        b.op("pe", "matmul", accum=(k > 0), out=py[:, 0:T], lhsT=v[:, k, m * 128:(m + 1) * 128],
                             rhs=src_act[:, k, :], start=(k == 0), stop=(k == KC - 1))
                    b.op("act", "activation", out=y[:, g0 + m, :], in_=py[:, 0:T], func=AF.Copy)

        def proj_tokmajor(wsrc, col0, ncols, out_dram, vdst_col0=None):
            for c0 in range(0, ncols, 512):
                n = min(512, ncols - c0)
                sl = wslot()
                v = sl[:, 0:KC * 512].rearrange("p (k n) -> p k n", k=KC)
                b.dma("pool", v[:, :, 0:n], wsrc[:, col0 + c0:col0 + c0 + n].rearrange("(k p) n -> p k n", p=128))
                for tb in range(T // 128):
                    pq = psU[tb % 2]
                    for k in range(KC):
                        b.op("pe", "matmul", accum=(k > 0), out=pq[:, 0:n], lhsT=h[:, k, tb * 128:(tb + 1) * 128],
                             rhs=v[:, k, 0:n], start=(k == 0), stop=(k == KC - 1))
                    t32 = kv32[tb % 2]
                    b.op("dve", "tensor_copy", out=t32[:, 0:n], in_=pq[:, 0:n])
                    if out_dram is not None:
                        b.dma("sp", out_dram[tb * 128:(tb + 1) * 128, c0:c0 + n], t32[:, 0:n])
                    if vdst_col0 is not None:
                        b.op("act", "activation", out=vtok[:, tb, vdst_col0 + c0:vdst_col0 + c0 + n], in_=t32[:, 0:n],
                             func=AF.Copy)

        def attn_ctx(nheads, head_map, sink_cols, scale):
            nseq = T // SEQ
            nqb = SEQ // 128
            it = 0
            for s in range(nseq):
                for qb in range(nqb):
                    q0 = s * SEQ + qb * 128
                    for hh in range(nheads):
                        qc, base, kc, vc0 = head_map[hh]
                        pS = psG[it % 2]
                        b.op("pe", "matmul", out=pS[:, 0:SEQ], lhsT=qT[base:base + 64, qc, q0:q0 + 128],
                             rhs=kT[base:base + 64, kc, s * SEQ:(s + 1) * SEQ], start=True, stop=True)
                        sm = st_m[it % 2]
                        b.op("dve", "reduce_max", out=sm[:, 0:1], in_=pS[:, 0:SEQ], axis=AX.X)
                        if sink_cols is not None:
                            b.op("dve", "tensor_scalar", out=sm[:, 1:2], in0=sm[:, 0:1], scalar1=scale,
                                 scalar2=sink_bc[:, hh:hh + 1], op0=ALU.mult, op1=ALU.max)
                            b.op("dve", "tensor_scalar", out=sm[:, 1:2], in0=sm[:, 1:2], scalar1=-1.0, scalar2=None,
                                 op0=ALU.mult)
                        else:
                            b.op("dve", "tensor_scalar", out=sm[:, 1:2], in0=sm[:, 0:1], scalar1=-scale, scalar2=None,
                                 op0=ALU.mult)
                        P = Pm[it % 2]
                        b.op("act", "activation", out=P[:, 0:SEQ], in_=pS[:, 0:SEQ], func=AF.Exp, scale=scale,
                             bias=sm[:, 1:2], accum_out=sm[:, 2:3])
                        if sink_cols is not None:
                            b.op("act", "activation", out=sm[:, 3:4], in_=sink_bc[:, hh:hh + 1], func=AF.Exp,
                                 bias=sm[:, 1:2], scale=1.0)
                            b.op("dve", "tensor_tensor", out=sm[:, 4:5], in0=sm[:, 2:3], in1=sm[:, 3:4], op=ALU.add)
                            b.op("dve", "reciprocal", out=sm[:, 5:6], in_=sm[:, 4:5])
                        else:
                            b.op("dve", "reciprocal", out=sm[:, 5:6], in_=sm[:, 2:3])
                        pt_sb = PT[it % 2]
                        if SUB < 3:
                            it += 1
                            continue
                        for kb in range(nqb):
                            b.op("pe", "transpose", out=psTb[it % 2][:, kb, :], in_=P[:, kb * 128:(kb + 1) * 128],
                                 identity=ident_bf[:, :])
                        b.op("dve", "tensor_copy", out=pt_sb[:, :, :], in_=psTb[it % 2][:, :, :])
                        if SUB < 4:
                            it += 1
                            continue
                        pO = psY[it % 2]
                        for kb in range(nqb):
                            b.op("pe", "matmul", accum=(kb > 0), out=pO[:, 0:64], lhsT=pt_sb[:, kb, :],
                                 rhs=vtok[:, s * nqb + kb, vc0:vc0 + 64], start=(kb == 0), stop=(kb == nqb - 1))
                        b.op("dve", "tensor_scalar", out=otok[:, hh * 64:(hh + 1) * 64], in0=pO[:, 0:64],
                             scalar1=sm[:, 5:6], scalar2=None, op0=ALU.mult)
                        it += 1
                    for c in range(KC):
                        b.op("pe", "transpose", out=psTb[c % 2][:, 0, :], in_=otok[:, c * 128:(c + 1) * 128],
                             identity=ident_bf[:, :])
                        b.op("act", "activation", out=oT[:, c, q0:q0 + 128], in_=psTb[c % 2][:, 0, :], func=AF.Copy)


        def s5_setup(npow=8, pfx="s5p_"):
            P = {}
            def t64(name):
                P[name] = b.sb(pfx + name, [128, 64], F32)
                return P[name]
            for nm, src in (("lamre", s5_lam_re), ("lamim", s5_lam_im), ("logdt", s5_logdt)):
                t64(nm)
                load_rows_T(P[nm][:, :], src, 64)
            for nm in ("dt", "lr", "li", "mag", "sn", "cs", "ar", "ai", "fr", "fi", "t1", "t2", "t3", "kk"):
                t64(nm)
            dsk = b.sb(pfx + "dskip", [128, KC], F32)
            load_rows_T(dsk[:, :], s5_d, 8)
            P["dskip"] = dsk
            b.op("act", "activation", out=P["dt"][:, :], in_=P["logdt"][:, :], func=AF.Exp)
            b.op("dve", "tensor_tensor", out=P["lr"][:, :], in0=P["lamre"][:, :], in1=P["dt"][:, :], op=ALU.mult)
            b.op("dve", "tensor_tensor", out=P["li"][:, :], in0=P["lamim"][:, :], in1=P["dt"][:, :], op=ALU.mult)
            b.op("act", "activation", out=P["mag"][:, :], in_=P["lr"][:, :], func=AF.Exp)

            def sin_of(dst, ang, shift):
                b.op("dve", "tensor_scalar", out=P["t1"][:, :], in0=ang, scalar1=float(shift), scalar2=None, op0=ALU.add)
                b.raw("dve", lambda: nc.vector.memset(P["kk"][:, :], 0.0), writes=[P["kk"][:, :]])
                for j in range(1, 5):
                    b.op("dve", "tensor_scalar", out=P["t2"][:, :], in0=P["t1"][:, :],
                         scalar1=float((2 * j - 1) * np.pi), scalar2=None, op0=ALU.is_gt)
                    b.op("dve", "tensor_tensor", out=P["kk"][:, :], in0=P["kk"][:, :], in1=P["t2"][:, :], op=ALU.add)
                b.op("dve", "tensor_scalar", out=P["kk"][:, :], in0=P["kk"][:, :], scalar1=float(-2 * np.pi),
                     scalar2=None, op0=ALU.mult)
                b.op("dve", "tensor_tensor", out=P["t1"][:, :], in0=P["t1"][:, :], in1=P["kk"][:, :], op=ALU.add)
                b.op("act", "activation", out=dst, in_=P["t1"][:, :], func=AF.Sin)

            sin_of(P["sn"][:, :], P["li"][:, :], 0.0)
            sin_of(P["cs"][:, :], P["li"][:, :], np.pi / 2)
            b.op("dve", "tensor_tensor", out=P["ar"][:, :], in0=P["mag"][:, :], in1=P["cs"][:, :], op=ALU.mult)
            b.op("dve", "tensor_tensor", out=P["ai"][:, :], in0=P["mag"][:, :], in1=P["sn"][:, :], op=ALU.mult)
            TT = lambda o, a_, b_, op: b.op("dve", "tensor_tensor", out=o, in0=a_, in1=b_, op=op)
            t1, t2, t3 = P["t1"][:, :], P["t2"][:, :], P["t3"][:, :]
            TT(t1, P["lamre"][:, :], P["lamre"][:, :], ALU.mult)
            TT(t2, P["lamim"][:, :], P["lamim"][:, :], ALU.mult)
            TT(t1, t1, t2, ALU.add)
            b.op("dve", "reciprocal", out=t3, in_=t1)
            b.op("dve", "tensor_scalar", out=P["kk"][:, :], in0=P["ar"][:, :], scalar1=-1.0, scalar2=None, op0=ALU.add)
            TT(t1, P["kk"][:, :], P["lamre"][:, :], ALU.mult)
            TT(t2, P["ai"][:, :], P["lamim"][:, :], ALU.mult)
            TT(t1, t1, t2, ALU.add)
            TT(P["fr"][:, :], t1, t3, ALU.mult)
            TT(t1, P["ai"][:, :], P["lamre"][:, :], ALU.mult)
            TT(t2, P["kk"][:, :], P["lamim"][:, :], ALU.mult)
            TT(t1, t1, t2, ALU.subtract)
            TT(P["fi"][:, :], t1, t3, ALU.mult)
            pw = [(P["ar"], P["ai"])]
            for k in range(1, npow):
                pr, pi_ = pw[-1]
                nr = b.sb(f"{pfx}pr{k}", [128, 64], F32)
                ni = b.sb(f"{pfx}pi{k}", [128, 64], F32)
                TT(t1, pr[:, :], pr[:, :], ALU.mult)
                TT(t2, pi_[:, :], pi_[:, :], ALU.mult)
                TT(nr[:, :], t1, t2, ALU.subtract)
                TT(t1, pr[:, :], pi_[:, :], ALU.mult)
                b.op("dve", "tensor_scalar", out=ni[:, :], in0=t1, scalar1=2.0, scalar2=None, op0=ALU.mult)
                pw.append((nr, ni))
            P["pw"] = pw
            return P

        def s5_mixer(P):
            nat = {}
            for t4 in range(4):
                for nm in ("bre", "bim", "cre", "cim"):
                    tl = b.sb(f"s5n_{nm}{t4}", [128, 128], F32)
                    b.raw("dve", lambda tl=tl: nc.vector.memset(tl[:, :], 0.0), writes=[tl[:, :]])
                    nat[(nm, t4)] = tl
            wB = [[b.sb(f"s5_wB{ri}{i}", [128, 128], BF16) for i in range(2)] for ri in range(2)]
            wC = [[b.sb(f"s5_wC{ri}{i}", [128, 128], BF16) for i in range(2)] for ri in range(2)]
            c32 = [b.sb(f"s5_c32{i}", [128, 128], F32) for i in range(2)]
            cm = [b.sb(f"s5_cm{i}", [128, 128], F32) for i in range(4)]
            xr = [b.sb(f"s5_xr{i}", [128, 2, SEQ], F32) for i in range(1)] * 2
            xi = [b.sb(f"s5_xi{i}", [128, 2, SEQ], F32) for i in range(1)] * 2
            xrb = [b.sb(f"s5_xrb{i}", [128, T], BF16) for i in range(1)] * 2
            xib = [b.sb(f"s5_xib{i}", [128, T], BF16) for i in range(1)] * 2
            tm = [b.sb(f"s5_tm{i}", [128, 2, SEQ], F32) for i in range(4)]
            fin_r = b.sb("s5_finr", [128, 64, 2], F32)
            fin_i = b.sb("s5_fini", [128, 64, 2], F32)
            fin_o = b.sb("s5_fino", [128, 128], F32)
            it = 0
            for d in range(2):
                for t in range(32):
                    tau = d * 32 + t
                    ch, t4 = t // 4, t % 4
                    pp = it % 2
                    for g2 in range(2):
                        g = 2 * t + g2
                        cb = (2 * t4 + g2) * 16
                        b.dma("sp", nat[("bre", t4)][g2 * 64:(g2 + 1) * 64, cb:cb + 16], s5_b_re[d, g, :, :])
                        b.dma("sp", nat[("bim", t4)][g2 * 64:(g2 + 1) * 64, cb:cb + 16], s5_b_im[d, g, :, :])
                        b.dma("sp", nat[("cre", t4)][cb:cb + 16, g2 * 64:(g2 + 1) * 64], s5_c_re[d, g, :, :])
                        b.dma("sp", nat[("cim", t4)][cb:cb + 16, g2 * 64:(g2 + 1) * 64], s5_c_im[d, g, :, :])
                    for ri, nm in enumerate(("bre", "bim")):
                        b.op("pe", "transpose", out=psS[:, 0:128], in_=nat[(nm, t4)][:, :], identity=ident[:, :])
                        b.op("act", "activation", out=wB[ri][pp][:, :], in_=psS[:, 0:128], func=AF.Copy)
                    for ri, nm in enumerate(("cre", "cim")):
                        b.op("pe", "transpose", out=psS[:, 0:128], in_=nat[(nm, t4)][:, :], identity=ident[:, :])
                        b.op("act", "activation", out=c32[ri][:, :], in_=psS[:, 0:128], func=AF.Copy)
                    fr_, fi_ = P["fr"][:, tau:tau + 1], P["fi"][:, tau:tau + 1]
                    TS = lambda o, i_, sc: b.op("dve", "tensor_scalar", out=o, in0=i_, scalar1=sc, scalar2=None, op0=ALU.mult)
                    TS(cm[0][:, :], c32[0][:, :], fr_)
                    TS(cm[1][:, :], c32[1][:, :], fi_)
                    b.op("dve", "tensor_tensor", out=wC[0][pp][:, :], in0=cm[0][:, :], in1=cm[1][:, :], op=ALU.subtract)
                    TS(cm[2][:, :], c32[0][:, :], fi_)
                    TS(cm[3][:, :], c32[1][:, :], fr_)
                    b.op("dve", "tensor_tensor", out=cm[2][:, :], in0=cm[2][:, :], in1=cm[3][:, :], op=ALU.add)
                    b.op("dve", "tensor_scalar", out=wC[1][pp][:, :], in0=cm[2][:, :], scalar1=-1.0, scalar2=None, op0=ALU.mult)
                    X = (xr[pp], xi[pp])
                    for ri in range(2):
                        pq = psG[ri]
                        b.op("pe", "matmul", out=pq[:, 0:T], lhsT=wB[ri][pp][:, :], rhs=h[:, ch, :], start=True, stop=True)
                        b.op("act", "activation", out=X[ri][:, :, :].rearrange("p a n -> p (a n)"), in_=pq[:, 0:T], func=AF.Copy)
                    for k in range(8):
                        sh = 1 << k
                        pr, pi_ = P["pw"][k]
                        sr_, si_ = pr[:, tau:tau + 1], pi_[:, tau:tau + 1]
                        if d == 0:
                            src = lambda A: A[:, :, 0:SEQ - sh]
                            dst = lambda A: A[:, :, sh:SEQ]
                        else:
                            src = lambda A: A[:, :, sh:SEQ]
                            dst = lambda A: A[:, :, 0:SEQ - sh]
                        tt = [tm[j] for j in range(4)]
                        n = SEQ - sh
                        MUL = lambda o, i_, sc: b.op("act", "activation", out=o, in_=i_, func=AF.Identity, scale=sc)
                        MUL(tt[0][:, :, 0:n], src(X[0]), sr_)
                        MUL(tt[1][:, :, 0:n], src(X[1]), si_)
                        MUL(tt[2][:, :, 0:n], src(X[1]), sr_)
                        MUL(tt[3][:, :, 0:n], src(X[0]), si_)
                        b.op("dve", "tensor_tensor", out=dst(X[0]), in0=dst(X[0]), in1=tt[0][:, :, 0:n], op=ALU.add)
                        b.op("dve", "tensor_tensor", out=dst(X[0]), in0=dst(X[0]), in1=tt[1][:, :, 0:n], op=ALU.subtract)
                        b.op("dve", "tensor_tensor", out=dst(X[1]), in0=dst(X[1]), in1=tt[2][:, :, 0:n], op=ALU.add)
                        b.op("dve", "tensor_tensor", out=dst(X[1]), in0=dst(X[1]), in1=tt[3][:, :, 0:n], op=ALU.add)
                    e_ = SEQ - 1 if d == 0 else 0
                    b.op("dve", "tensor_copy", out=fin_r[:, tau, :], in_=X[0][:, :, e_])
                    b.op("dve", "tensor_copy", out=fin_i[:, tau, :], in_=X[1][:, :, e_])
                    b.op("act", "activation", out=xrb[pp][:, :], in_=X[0][:, :, :].rearrange("p a n -> p (a n)"), func=AF.Copy)
                    b.op("act", "activation", out=xib[pp][:, :], in_=X[1][:, :, :].rearrange("p a n -> p (a n)"), func=AF.Copy)
                    py = psY[it % 2]
                    b.op("pe", "matmul", out=py[:, 0:T], lhsT=wC[0][pp][:, :], rhs=xrb[pp][:, :], start=True, stop=False)
                    b.op("pe", "matmul", accum=True, out=py[:, 0:T], lhsT=wC[1][pp][:, :], rhs=xib[pp][:, :], start=False, stop=True)
                    if d == 0 and t4 == 0:
                        b.op("dve", "tensor_copy", out=y[:, ch, :], in_=py[:, 0:T])
                    else:
                        b.op("dve", "tensor_tensor", out=y[:, ch, :], in0=y[:, ch, :], in1=py[:, 0:T], op=ALU.add)
                    it += 1
            FR = P["fr"][:, :].unsqueeze(2).to_broadcast([128, 64, 2]) if False else None
            for sq_ in range(2):
                a_, b_ = tm[0][:, 0, 0:64], tm[1][:, 0, 0:64]
                TT = lambda o, x_, y_, op: b.op("dve", "tensor_tensor", out=o, in0=x_, in1=y_, op=op)
                TT(a_, fin_r[:, :, sq_], P["fr"][:, :], ALU.mult)
                TT(b_, fin_i[:, :, sq_], P["fi"][:, :], ALU.mult)
                TT(tm[2][:, 0, sq_ * 64:(sq_ + 1) * 64], a_, b_, ALU.subtract)
                TT(a_, fin_i[:, :, sq_], P["fr"][:, :], ALU.mult)
                TT(b_, fin_r[:, :, sq_], P["fi"][:, :], ALU.mult)
                TT(tm[3][:, 0, sq_ * 64:(sq_ + 1) * 64], a_, b_, ALU.add)
            for src_t, dst_d in ((tm[2], s5re_out), (tm[3], s5im_out)):
                b.op("pe", "transpose", out=psS[:, 0:128], in_=src_t[:, 0, 0:128], identity=ident[:, :])
                b.op("dve", "tensor_copy", out=fin_o[:, :], in_=psS[:, 0:128])
                b.dma("sp", dst_d, fin_o[:, :])
            g_bf = act
            for c in range(KC):
                t_a, t_b = tmpA[0], tmpA[1]
                b.op("dve", "tensor_scalar", out=t_a[:, :], in0=h[:, c, :], scalar1=P["dskip"][:, c:c + 1], scalar2=None, op0=ALU.mult)
                b.op("dve", "tensor_tensor", out=y[:, c, :], in0=y[:, c, :], in1=t_a[:, :], op=ALU.add)
                b.op("act", "activation", out=t_a[:, :], in_=y[:, c, :], func=AF.Square)
                b.op("dve", "tensor_scalar", out=t_a[:, :], in0=t_a[:, :], scalar1=0.044715, scalar2=1.0, op0=ALU.mult, op1=ALU.add)
                b.op("dve", "tensor_tensor", out=t_a[:, :], in0=t_a[:, :], in1=y[:, c, :], op=ALU.mult)
                b.op("act", "activation", out=t_b[:, :], in_=t_a[:, :], func=AF.Tanh, scale=0.7978845608028654)
                b.op("dve", "tensor_scalar", out=t_b[:, :], in0=t_b[:, :], scalar1=1.0, scalar2=0.5, op0=ALU.add, op1=ALU.mult)
                b.op("dve", "tensor_tensor", out=g_bf[:, c, :], in0=t_b[:, :], in1=y[:, c, :], op=ALU.mult)
            for o4 in range(0, KC, 2):
                sl = wslot()
                v = sl[:, 0:KC * 512].rearrange("p (k n) -> p k n", k=KC)
                b.dma("pool", v[:, :, 0:256], s5_w_glu[:, o4 * 128:(o4 + 2) * 128].rearrange("(k p) n -> p k n", p=128))
                b.dma("pool", v[:, :, 256:512], s5_w_glu[:, D + o4 * 128:D + (o4 + 2) * 128].rearrange("(k p) n -> p k n", p=128))
                for oo in range(2):
                    o = o4 + oo
                    pa, pg = psG[o % 2], psU[o % 2]
                    for k in range(KC):
                        b.op("pe", "matmul", accum=(k > 0), out=pa[:, 0:T], lhsT=v[:, k, oo * 128:(oo + 1) * 128],
                             rhs=g_bf[:, k, :], start=(k == 0), stop=(k == KC - 1))
                    for k in range(KC):
                        b.op("pe", "matmul", accum=(k > 0), out=pg[:, 0:T], lhsT=v[:, k, 256 + oo * 128:256 + (oo + 1) * 128],
                             rhs=g_bf[:, k, :], start=(k == 0), stop=(k == KC - 1))
                    tq = tmpA[o % 2]
                    b.op("act", "activation", out=tq[:, :], in_=pg[:, 0:T], func=AF.Sigmoid)
                    b.op("dve", "tensor_tensor", out=y[:, o, :], in0=tq[:, :], in1=pa[:, 0:T], op=ALU.mult)


        def dn_mixer():
            sbf = lambda n_, sh, dt_=F32: b.sb("dn_" + n_, sh, dt_)
            msk = sbf("msk", [64, 6, 64]); b.dma("sp", msk[:], dn_mask)
            cw = sbf("cw", [128, 80]); load_rows_T(cw[:, :], dn_conv_w, 80)
            ones_f = sbf("ones", [64, 128]); b.raw("dve", lambda: nc.vector.memset(ones_f[:, :], 1.0), writes=[ones_f[:, :]])
            wba = sbf("wba", [128, 2, KC, 16], BF16)
            for d in range(2):
                b.dma("pool", wba[:, d, :, :], dn_w_ba[d].rearrange("(k p) n -> p k n", p=128))
            alog = sbf("alog", [64, 16]); b.dma("sp", alog[:], dn_a_log.broadcast_to([64, 16]))
            dtb = sbf("dtb", [64, 16]); b.dma("sp", dtb[:], dn_dt_bias.broadcast_to([64, 16]))
            outg = sbf("outg", [64, 128]); b.dma("sp", outg[:], dn_out_g.broadcast_to([64, 128]))
            nexpa = sbf("nexpa", [64, 16])
            b.op("act", "activation", out=nexpa[:, :], in_=alog[:, :], func=AF.Exp)
            b.op("dve", "tensor_scalar", out=nexpa[:, :], in0=nexpa[:, :], scalar1=-1.0, scalar2=None, op0=ALU.mult)
            cin = sbf("cin", [128, 2, SEQ + 4]); b.raw("dve", lambda: nc.vector.memset(cin[:, :, :], 0.0), writes=[cin[:, :, :]])
            qn = sbf("qn", [128, 4, T], BF16); kn = sbf("kn", [128, 4, T], BF16)
            kt_tok = sbf("kt", [64, 8, 4, 128], BF16); vt_tok = sbf("vt", [64, 8, 8, 128], BF16)
            zs = sbf("zs", [128, 8, T], BF16); vtmp = sbf("vtmp", [128, T], BF16)
            acc3 = tmpA[0][:, :].rearrange("p (a n) -> p a n", a=2)
            tm3 = tmpA[1][:, :].rearrange("p (a n) -> p a n", a=2)
            for m0 in range(0, 24, 4):
                sl = wslot()
                v = sl[:, 0:KC * 512].rearrange("p (k n) -> p k n", k=KC)
                b.dma("pool", v, dn_w_in[:, m0 * 128:(m0 + 4) * 128].rearrange("(k p) n -> p k n", p=128))
                for mm in range(4):
                    m = m0 + mm
                    pq = psG[mm % 2]
                    for k in range(KC):
                        b.op("pe", "matmul", accum=(k > 0), out=pq[:, 0:T], lhsT=v[:, k, mm * 128:(mm + 1) * 128],
                             rhs=h[:, k, :], start=(k == 0), stop=(k == KC - 1))
                    if m >= 16:
                        b.op("act", "activation", out=zs[:, m - 16, :], in_=pq[:, 0:T], func=AF.Silu)
                        continue
                    b.op("act", "activation", out=cin[:, :, 2:2 + SEQ], in_=pq[:, 0:T].rearrange("p (a n) -> p a n", a=2),
                         func=AF.Copy)
                    for j in range(5):
                        dst = acc3 if j == 0 else tm3
                        b.op("act", "activation", out=dst, in_=cin[:, :, j:j + SEQ], func=AF.Identity,
                             scale=cw[:, j * 16 + m:j * 16 + m + 1])
                        if j > 0:
                            b.op("dve", "tensor_tensor", out=acc3, in0=acc3, in1=tm3, op=ALU.add)
                    if m < 8:
                        b.op("act", "activation", out=tmpA[1][:, :], in_=tmpA[0][:, :], func=AF.Silu)
                        b.op("act", "activation", out=act[:, 0, :], in_=tmpA[1][:, :], func=AF.Square)
                        b.op("pe", "matmul", out=psS[:, 0:T], lhsT=ones_bf[:, :], rhs=act[:, 0, :], start=True, stop=True)
                        b.op("act", "activation", out=rstd[:, :], in_=psS[:, 0:T], func=AF.Sqrt, scale=1.0, bias=EPS)
                        b.op("dve", "reciprocal", out=rstd[:, :], in_=rstd[:, :])
                        b.op("dve", "tensor_tensor", out=tmpA[1][:, :], in0=tmpA[1][:, :], in1=rstd[:, :], op=ALU.mult)
                        if m < 4:
                            b.op("act", "activation", out=qn[:, m, :], in_=tmpA[1][:, :], func=AF.Identity, scale=128 ** -0.5)
                        else:
                            b.op("act", "activation", out=kn[:, m - 4, :], in_=tmpA[1][:, :], func=AF.Identity, scale=1.0)
                            for blk in range(8):
                                b.op("pe", "transpose", out=psTb[blk % 2][0:64, 0, :], in_=kn[:, m - 4, blk * 64:(blk + 1) * 64],
                                     identity=ident_bf[:, :])
                                b.op("dve", "tensor_copy", out=kt_tok[:, blk, m - 4, :], in_=psTb[blk % 2][0:64, 0, :])
                    else:
                        b.op("act", "activation", out=vtmp[:, :], in_=tmpA[0][:, :], func=AF.Silu)
                        for blk in range(8):
                            b.op("pe", "transpose", out=psTb[blk % 2][0:64, 0, :], in_=vtmp[:, blk * 64:(blk + 1) * 64],
                                 identity=ident_bf[:, :])
                            b.op("dve", "tensor_copy", out=vt_tok[:, blk, m - 8, :], in_=psTb[blk % 2][0:64, 0, :])
            gts = sbf("gts", [64, 2, 8, 16]); beta = sbf("beta", [64, 2, 8, 8]); xa = sbf("xa", [64, 2, 8, 8])
            xe = sbf("xe", [64, 2, 8, 8]); gg = sbf("gg", [64, 2, 8, 8]); gc = sbf("gc", [64, 2, 8, 8]); eg = sbf("eg", [64, 2, 8, 8])
            for d in range(2):
                for blk in range(8):
                    pq = psU[blk % 2]
                    for k in range(KC):
                        b.op("pe", "matmul", accum=(k > 0), out=pq[0:64, 0:16], lhsT=h[:, k, blk * 64:(blk + 1) * 64],
                             rhs=wba[:, d, k, :], start=(k == 0), stop=(k == KC - 1))
                    b.op("dve", "tensor_copy", out=gts[:, d, blk, :], in_=pq[0:64, 0:16])
                    b.op("act", "activation", out=beta[:, d, blk, :], in_=gts[:, d, blk, 0:8], func=AF.Sigmoid)
                    b.op("dve", "tensor_tensor", out=xa[:, d, blk, :], in0=gts[:, d, blk, 8:16], in1=dtb[:, d * 8:(d + 1) * 8], op=ALU.add)
            fl = lambda A: A[:, :, :, :].rearrange("p a b c -> p (a b c)")
            b.op("act", "activation", out=fl(xe), in_=fl(xa), func=AF.Abs)
            b.op("act", "activation", out=fl(xe), in_=fl(xe), func=AF.Exp, scale=-1.0)
            b.op("act", "activation", out=fl(xe), in_=fl(xe), func=AF.Ln, bias=1.0, scale=1.0)
            b.op("dve", "tensor_scalar", out=fl(xa), in0=fl(xa), scalar1=0.0, scalar2=None, op0=ALU.max)
            b.op("dve", "tensor_tensor", out=fl(xa), in0=fl(xa), in1=fl(xe), op=ALU.add)
            for d in range(2):
                for blk in range(8):
                    b.op("dve", "tensor_tensor", out=gg[:, d, blk, :], in0=xa[:, d, blk, :], in1=nexpa[:, d * 8:(d + 1) * 8], op=ALU.mult)
                b.op("pe", "matmul", out=psS[0:64, 0:64], lhsT=msk[:, d, :], rhs=gg[:, d, :, :].rearrange("p b c -> p (b c)"),
                     start=True, stop=True)
                b.op("dve", "tensor_copy", out=gc[:, d, :, :].rearrange("p b c -> p (b c)"), in_=psS[0:64, 0:64])
            b.op("act", "activation", out=fl(eg), in_=fl(gc), func=AF.Exp)
            f64 = lambda n_: sbf(n_, [64, 64])
            dg, decS, decT, PTt = f64("dg"), f64("decS"), f64("decT"), f64("PTt")
            Ms = [(f64("Ma"), f64("MTa")), (f64("Mb"), f64("MTb"))]
            gcr = sbf("gcr", [128, 64]); TTb = sbf("TTb", [64, 64], BF16); aT = sbf("aT", [64, 64], BF16)
            vb = sbf("vb", [64, 128], BF16); kbg = sbf("kbg", [64, 128], BF16); kg = sbf("kg", [64, 128], BF16)
            wT = sbf("wT", [128, 64], BF16); u_sb = sbf("u", [64, 128]); vnew = sbf("vnew", [64, 128]); vnew_b = sbf("vnewb", [64, 128], BF16)
            o1 = sbf("o1", [64, 128]); sc1 = sbf("sc1", [64, 4]); egl = sbf("egl", [128, 1])
            S = sbf("S", [128, 128]); Sb = sbf("Sb", [128, 128], BF16)
            o_acc = sbf("oacc", [64, 4, 128]); ss = sbf("ss", [64, 4]); onb = sbf("onb", [64, 128], BF16); junk = sbf("junk", [64, 128])
            id64 = ident[0:64, 0:64]
            for s_ in range(2):
                for hv in range(8):
                    hq = hv // 2
                    for d in range(2):
                        b.raw("dve", lambda: nc.vector.memset(S[:, :], 0.0), writes=[S[:, :]])
                        b.raw("dve", lambda: nc.vector.memset(Sb[:, :], 0.0), writes=[Sb[:, :]])
                        last = 63 if d == 0 else 0
                        for n in (range(4) if d == 0 else range(3, -1, -1)):
                            blk = s_ * 4 + n
                            tok0 = blk * 64
                            gc_, be_, eg_ = gc[:, d, blk, hv:hv + 1], beta[:, d, blk, hv:hv + 1], eg[:, d, blk, hv:hv + 1]
                            kf, qf = kn[:, hq, tok0:tok0 + 64], qn[:, hq, tok0:tok0 + 64]
                            kt, vt = kt_tok[:, blk, hq, :], vt_tok[:, blk, hv, :]
                            b.op("dve", "tensor_scalar", out=dg[:, :], in0=id64, scalar1=gc_, scalar2=None, op0=ALU.mult)
                            b.op("pe", "matmul", out=psS[:, 0:64], lhsT=ones_f[:, :], rhs=dg[:, :], start=True, stop=True)
                            b.op("act", "activation", out=gcr[:, :], in_=psS[:, 0:64], func=AF.Copy)
                            b.op("dve", "tensor_scalar", out=decS[:, :], in0=gcr[0:64, :], scalar1=-1.0, scalar2=gc_, op0=ALU.mult, op1=ALU.add)
                            b.op("dve", "tensor_scalar", out=decS[:, :], in0=decS[:, :], scalar1=0.0, scalar2=None, op0=ALU.min)
                            b.op("act", "activation", out=decS[:, :], in_=decS[:, :], func=AF.Exp)
                            b.op("dve", "tensor_tensor", out=decS[:, :], in0=decS[:, :], in1=msk[:, 4 + d, :], op=ALU.mult)
                            b.op("dve", "tensor_scalar", out=decT[:, :], in0=gcr[0:64, :], scalar1=gc_, scalar2=None, op0=ALU.subtract)
                            b.op("dve", "tensor_scalar", out=decT[:, :], in0=decT[:, :], scalar1=0.0, scalar2=None, op0=ALU.min)
                            b.op("act", "activation", out=decT[:, :], in_=decT[:, :], func=AF.Exp)
                            b.op("dve", "tensor_tensor", out=decT[:, :], in0=decT[:, :], in1=msk[:, d, :], op=ALU.mult)
                            M, MT = Ms[0]
                            b.op("pe", "matmul", out=psG[0][0:64, 0:64], lhsT=kf, rhs=kf, start=True, stop=True)
                            b.op("dve", "tensor_scalar", out=M[:, :], in0=psG[0][0:64, 0:64], scalar1=be_, scalar2=-1.0, op0=ALU.mult, op1=ALU.mult)
                            b.op("dve", "tensor_tensor", out=M[:, :], in0=M[:, :], in1=decS[:, :], op=ALU.mult)
                            b.op("pe", "transpose", out=psG[1][0:64, 0:64], in_=M[:, :], identity=id64)
                            b.op("act", "activation", out=MT[:, :], in_=psG[1][0:64, 0:64], func=AF.Copy)
                            b.op("dve", "tensor_tensor", out=PTt[:, :], in0=MT[:, :], in1=id64, op=ALU.add)
                            cur = 0
                            for k in range(1, 6):
                                M, MT = Ms[cur]
                                Mn, MTn = Ms[1 - cur]
                                b.op("pe", "matmul", out=psU[0][0:64, 0:64], lhsT=MT[:, :], rhs=M[:, :], start=True, stop=True)
                                b.op("act", "activation", out=Mn[:, :], in_=psU[0][0:64, 0:64], func=AF.Copy)
                                if k < 5:
                                    b.op("pe", "matmul", out=psU[1][0:64, 0:64], lhsT=M[:, :], rhs=MT[:, :], start=True, stop=True)
                                    b.op("dve", "tensor_copy", out=MTn[:, :], in_=psU[1][0:64, 0:64])
                                b.op("pe", "matmul", out=psY[0][0:64, 0:64], lhsT=Mn[:, :], rhs=PTt[:, :], start=True, stop=True)
                                b.op("dve", "tensor_tensor", out=PTt[:, :], in0=PTt[:, :], in1=psY[0][0:64, 0:64], op=ALU.add)
                                cur = 1 - cur
                            b.op("act", "activation", out=TTb[:, :], in_=PTt[:, :], func=AF.Copy)
                            b.op("dve", "tensor_tensor", out=sc1[:, 0:1], in0=be_, in1=eg_, op=ALU.mult)
                            b.op("dve", "tensor_scalar", out=vb[:, :], in0=vt, scalar1=be_, scalar2=None, op0=ALU.mult)
                            b.op("dve", "tensor_scalar", out=kbg[:, :], in0=kt, scalar1=sc1[:, 0:1], scalar2=None, op0=ALU.mult)
                            b.op("pe", "matmul", out=psG[0][0:64, 0:128], lhsT=TTb[:, :], rhs=vb[:, :], start=True, stop=True)
                            b.op("act", "activation", out=u_sb[:, :], in_=psG[0][0:64, 0:128], func=AF.Copy)
                            b.op("pe", "matmul", out=psG[1][:, 0:64], lhsT=kbg[:, :], rhs=TTb[:, :], start=True, stop=True)
                            b.op("dve", "tensor_copy", out=wT[:, :], in_=psG[1][:, 0:64])
                            b.op("pe", "matmul", out=psU[0][0:64, 0:128], lhsT=wT[:, :], rhs=Sb[:, :], start=True, stop=True)
                            b.op("dve", "tensor_tensor", out=vnew[:, :], in0=u_sb[:, :], in1=psU[0][0:64, 0:128], op=ALU.subtract)
                            b.op("act", "activation", out=vnew_b[:, :], in_=vnew[:, :], func=AF.Copy)
                            b.op("pe", "matmul", out=psU[1][0:64, 0:128], lhsT=qf, rhs=Sb[:, :], start=True, stop=True)
                            b.op("dve", "tensor_scalar", out=o1[:, :], in0=psU[1][0:64, 0:128], scalar1=eg_, scalar2=None, op0=ALU.mult)
                            b.op("pe", "matmul", out=psY[0][0:64, 0:64], lhsT=kf, rhs=qf, start=True, stop=True)
                            b.op("dve", "tensor_tensor", out=aT[:, :], in0=psY[0][0:64, 0:64], in1=decT[:, :], op=ALU.mult)
                            b.op("pe", "matmul", out=psY[1][0:64, 0:128], lhsT=aT[:, :], rhs=vnew_b[:, :], start=True, stop=True)
                            b.op("dve", "tensor_tensor", out=o1[:, :], in0=o1[:, :], in1=psY[1][0:64, 0:128], op=ALU.add)
                            if d == 0:
                                b.op("dve", "tensor_copy", out=o_acc[:, n, :], in_=o1[:, :])
                            else:
                                b.op("dve", "tensor_tensor", out=o_acc[:, n, :], in0=o_acc[:, n, :], in1=o1[:, :], op=ALU.add)
                            b.op("act", "activation", out=sc1[:, 1:2], in_=gc_, func=AF.Exp, scale=-1.0, bias=gcr[0:64, last:last + 1])
                            b.op("dve", "tensor_scalar", out=kg[:, :], in0=kt, scalar1=sc1[:, 1:2], scalar2=None, op0=ALU.mult)
                            b.op("act", "activation", out=egl[:, :], in_=gcr[:, last:last + 1], func=AF.Exp)
                            b.op("pe", "matmul", out=psG[0][:, 0:128], lhsT=kg[:, :], rhs=vnew_b[:, :], start=True, stop=True)
                            b.op("dve", "tensor_scalar", out=S[:, :], in0=S[:, :], scalar1=egl[:, 0:1], scalar2=None, op0=ALU.mult)
                            b.op("dve", "tensor_tensor", out=S[:, :], in0=S[:, :], in1=psG[0][:, 0:128], op=ALU.add)
                            b.op("act", "activation", out=Sb[:, :], in_=S[:, :], func=AF.Copy)
                        b.dma("sp", dn_out[(s_ * 2 + d) * 8 + hv, :, :], S[:, :])
                    for n in range(4):
                        b.op("act", "activation", out=junk[:, :], in_=o_acc[:, n, :], func=AF.Square, accum_out=ss[:, n:n + 1])
                    b.op("act", "activation", out=ss[:, :], in_=ss[:, :], func=AF.Sqrt, scale=1.0 / 128, bias=EPS)
                    b.op("dve", "reciprocal", out=ss[:, :], in_=ss[:, :])
                    for n in range(4):
                        tok0 = (s_ * 4 + n) * 64
                        b.op("dve", "tensor_scalar", out=junk[:, :], in0=o_acc[:, n, :], scalar1=ss[:, n:n + 1], scalar2=None, op0=ALU.mult)
                        b.op("dve", "tensor_tensor", out=onb[:, :], in0=junk[:, :], in1=outg[:, :], op=ALU.mult)
                        b.op("pe", "transpose", out=psTb[n % 2][:, 0, 0:64], in_=onb[:, :], identity=ident_bf[0:64, 0:64])
                        b.op("dve", "tensor_tensor", out=oT[:, hv, tok0:tok0 + 64], in0=psTb[n % 2][:, 0, 0:64],
                             in1=zs[:, hv, tok0:tok0 + 64], op=ALU.mult)
            proj_out_featmajor(dn_w_o, oT)


        NTL = LSEQ // 512
        NBL = LSEQ // 128

        def lat_load(src, tile):
            b.dma("sp", x[:], src[:, tile * 512:(tile + 1) * 512].rearrange("(c p) t -> p c t", p=128))

        def lat_store(dst, tile):
            b.dma("sp", dst[:, tile * 512:(tile + 1) * 512].rearrange("(c p) t -> p c t", p=128), x[:])

        def lat_ffn_sub(i, s):
            pre_norm(i, s)
            ffn(ffn_w_gu[i, 0 if s == 0 else 1], ffn_w_d[i, 0 if s == 0 else 1], key=i * 2 + (0 if s == 0 else 1),
                consume=not CTX_SKIP)
            post_norm_residual(i, s)

        def a_lat(src0):
            sbf = lambda n_, sh, dt_=F32: b.sb("la_" + n_, sh, dt_)
            kTf = sbf("kT", [128, 4, LSEQ], BF16); vtf = sbf("vt", [128, NBL, 256], BF16)
            kTc = sbf("kTc", [128, 4, 512], BF16); vtc = sbf("vtc", [128, 4, 256], BF16)
            bmask = sbf("bm", [128, 384]); b.dma("sp", bmask[:], band_mask)
            cs = [sbf(f"cs{i}", [128, 64]) for i in range(2)]
            kdup = sbf("kdup", [128, 4, 2, 64], BF16)
            qr = sbf("qr", [128, 1024], BF16); qTb = sbf("qTb", [128, 8, 128], BF16)
            slcs = [sbf(f"sl{j}", [128, 384]) for j in range(2)]; Pls = [sbf(f"Pl{j}", [128, 384], BF16) for j in range(2)]
            Pcs = [sbf(f"Pc{j}", [128, 512], BF16) for j in range(2)]
            PTls = [sbf(f"PTl{j}", [128, 8, 128], BF16) for j in range(2)]; otk = sbf("otk", [128, 1024], BF16)
            smms = [sbf(f"smx{j}", [128, 12]) for j in range(2)]
            rt = [sbf(f"rt{i}", [128, 2, 16]) for i in range(4)]
            c32t = sbf("c32", [128, 256])

            def rope(src, nh, dst_fn, cst):
                cosv = cst[:, 0:32].rearrange("p (a f) -> p a f", a=2)
                sinv = cst[:, 32:64].rearrange("p (a f) -> p a f", a=2)
                for hh in range(nh):
                    s4 = src[:, hh * 64:(hh + 1) * 64].rearrange("p (a b f) -> p a b f", a=2, b=2)
                    x1, x2 = s4[:, :, 0, :], s4[:, :, 1, :]
                    d4 = dst_fn(hh).rearrange("p (a b f) -> p a b f", a=2, b=2)
                    TT = lambda o, a_, b_, op: b.op("dve", "tensor_tensor", out=o, in0=a_, in1=b_, op=op)
                    TT(rt[0][:, :, :], x1, cosv, ALU.mult)
                    TT(rt[1][:, :, :], x2, sinv, ALU.mult)
                    TT(d4[:, :, 0, :], rt[0][:, :, :], rt[1][:, :, :], ALU.subtract)
                    TT(rt[2][:, :, :], x2, cosv, ALU.mult)
                    TT(rt[3][:, :, :], x1, sinv, ALU.mult)
                    TT(d4[:, :, 1, :], rt[2][:, :, :], rt[3][:, :, :], ALU.add)

            def k_to_featmajor(dstT, col0):
                b.op("dve", "tensor_copy", out=kdup[:, :, 1, :], in_=kdup[:, :, 0, :])
                for kv in range(4):
                    b.op("pe", "transpose", out=psTb[kv % 2][:, 0, :], in_=kdup[:, kv, :, :].rearrange("p a d -> p (a d)"),
                         identity=ident_bf[:, :])
                    b.op("act", "activation", out=dstT[:, kv, col0:col0 + 128], in_=psTb[kv % 2][:, 0, :], func=AF.Copy)

            for cb in range(4):
                b.dma("sp", c32t[:, :], cache_ak[cb * 128:(cb + 1) * 128, :])
                b.op("dve", "tensor_copy", out=kdup[:, :, 0, :], in_=c32t[:, :].rearrange("p (k d) -> p k d", k=4))
                k_to_featmajor(kTc, cb * 128)
                b.dma("pool", vtc[:, cb, :], cache_av[cb * 128:(cb + 1) * 128, :])
            for tl in range(NTL):
                lat_load(src0, tl)
                lat_ffn_sub(0, 0)
                lat_store(zres, tl)
                pre_norm(0, 1)
                sl_ = wslot()
                v = sl_[:, 0:KC * 512].rearrange("p (k n) -> p k n", k=KC)
                b.dma("pool", v, a_w_qkv[:, 1024:1536].rearrange("(k p) n -> p k n", p=128))
                for tb in range(4):
                    blk = tl * 4 + tb
                    pq = psU[tb % 2]
                    for k in range(KC):
                        b.op("pe", "matmul", accum=(k > 0), out=pq[:, 0:512], lhsT=h[:, k, tb * 128:(tb + 1) * 128],
                             rhs=v[:, k, :], start=(k == 0), stop=(k == KC - 1))
                    b.dma("sp", cs[tb % 2][:, :], rope_cs[blk * 128:(blk + 1) * 128, :])
                    rope(pq[:, 0:256], 4, lambda hh: kdup[:, hh, 0, :], cs[tb % 2])
                    k_to_featmajor(kTf, blk * 128)
                    b.op("act", "activation", out=vtf[:, blk, :], in_=pq[:, 256:512], func=AF.Copy)
            it = 0
            for tl in range(NTL):
                lat_load(zres, tl)
                pre_norm(0, 1)
                wq = []
                for half in range(2):
                    sl_ = wslot()
                    v = sl_[:, 0:KC * 512].rearrange("p (k n) -> p k n", k=KC)
                    b.dma("pool", v, a_w_qkv[:, half * 512:(half + 1) * 512].rearrange("(k p) n -> p k n", p=128))
                    wq.append(v)
                for tb in range(4):
                    blk = tl * 4 + tb
                    b.dma("sp", cs[tb % 2][:, :], rope_cs[blk * 128:(blk + 1) * 128, :])
                    for half in range(2):
                        pq = psG[half]
                        for k in range(KC):
                            b.op("pe", "matmul", accum=(k > 0), out=pq[:, 0:512], lhsT=h[:, k, tb * 128:(tb + 1) * 128],
                                 rhs=wq[half][:, k, :], start=(k == 0), stop=(k == KC - 1))
                        rope(pq[:, 0:512], 8, lambda hh, half=half: qr[:, half * 512 + hh * 64:half * 512 + (hh + 1) * 64], cs[tb % 2])
                    for c in range(KC):
                        b.op("pe", "transpose", out=psTb[c % 2][:, 0, :], in_=qr[:, c * 128:(c + 1) * 128], identity=ident_bf[:, :])
                        b.op("act", "activation", out=qTb[:, c, :], in_=psTb[c % 2][:, 0, :], func=AF.Copy)
                    lo, hi = max(blk - 1, 0), min(blk + 1, NBL - 1)
                    nlb = hi - lo + 1
                    nl = nlb * 128
                    bm = bmask[:, 128:128 + nl] if blk == 0 else bmask[:, 0:nl]
                    def heads(j, lo=lo, hi=hi, nlb=nlb, nl=nl, bm=bm):
                        slc, Pl, Pc, PTl, smm = slcs[j], Pls[j], Pcs[j], PTls[j], smms[j]
                        pL, pC, pO, pT = psG[j], psU[j], psY[j], psTc[j]
                        base = 64 * j
                        for hh in range(j, 16, 2):
                            qc, kv = hh // 2, hh // 4
                            b.op("pe", "matmul", out=pL[:, 0:nl], lhsT=qTb[base:base + 64, qc, :],
                                 rhs=kTf[base:base + 64, kv, lo * 128:(hi + 1) * 128], start=True, stop=True)
                            b.op("pe", "matmul", out=pC[:, 0:512], lhsT=qTb[base:base + 64, qc, :],
                                 rhs=kTc[base:base + 64, kv, :], start=True, stop=True)
                            yield
                            b.op("dve", "tensor_tensor", out=slc[:, 0:nl], in0=pL[:, 0:nl], in1=bm, op=ALU.add)
                            b.op("dve", "reduce_max", out=smm[:, 0:1], in_=slc[:, 0:nl], axis=AX.X)
                            b.op("dve", "reduce_max", out=smm[:, 1:2], in_=pC[:, 0:512], axis=AX.X)
                            yield
                            b.op("dve", "tensor_tensor", out=smm[:, 0:1], in0=smm[:, 0:1], in1=smm[:, 1:2], op=ALU.max)
                            b.op("dve", "tensor_scalar", out=smm[:, 2:3], in0=smm[:, 0:1], scalar1=0.125,
                                 scalar2=sink_bc[:, hh:hh + 1], op0=ALU.mult, op1=ALU.max)
                            b.op("dve", "tensor_scalar", out=smm[:, 2:3], in0=smm[:, 2:3], scalar1=-1.0, scalar2=None, op0=ALU.mult)
                            yield
                            b.op("act", "activation", out=Pl[:, 0:nl], in_=slc[:, 0:nl], func=AF.Exp, scale=0.125,
                                 bias=smm[:, 2:3], accum_out=smm[:, 3:4])
                            b.op("act", "activation", out=Pc[:, :], in_=pC[:, 0:512], func=AF.Exp, scale=0.125,
                                 bias=smm[:, 2:3], accum_out=smm[:, 4:5])
                            b.op("act", "activation", out=smm[:, 5:6], in_=sink_bc[:, hh:hh + 1], func=AF.Exp, bias=smm[:, 2:3], scale=1.0)
                            yield
                            b.op("dve", "tensor_tensor", out=smm[:, 6:7], in0=smm[:, 3:4], in1=smm[:, 4:5], op=ALU.add)
                            b.op("dve", "tensor_tensor", out=smm[:, 6:7], in0=smm[:, 6:7], in1=smm[:, 5:6], op=ALU.add)
                            b.op("dve", "reciprocal", out=smm[:, 7:8], in_=smm[:, 6:7])
                            srcs = [Pl[:, jq * 128:(jq + 1) * 128] for jq in range(nlb)] + [Pc[:, jq * 128:(jq + 1) * 128] for jq in range(4)]
                            nsrc = len(srcs)
                            for j0 in range(0, nsrc, 2):
                                nn_ = min(2, nsrc - j0)
                                for q_ in range(nn_):
                                    b.op("pe", "transpose", out=pT[:, q_, :], in_=srcs[j0 + q_], identity=ident_bf[:, :])
                                if (j0 // 2) % 2 == 0:
                                    b.op("dve", "tensor_copy", out=PTl[:, j0:j0 + nn_, :], in_=pT[:, 0:nn_, :])
                                else:
                                    b.op("act", "activation", out=PTl[:, j0:j0 + nn_, :], in_=pT[:, 0:nn_, :], func=AF.Copy)
                                yield
                            for jq in range(nsrc):
                                rv = vtf[:, lo + jq, kv * 64:(kv + 1) * 64] if jq < nlb else vtc[:, jq - nlb, kv * 64:(kv + 1) * 64]
                                b.op("pe", "matmul", accum=(jq > 0), out=pO[:, 0:64], lhsT=PTl[:, jq, :], rhs=rv,
                                     start=(jq == 0), stop=(jq == nsrc - 1))
                            yield
                            b.op("dve", "tensor_scalar", out=otk[:, hh * 64:(hh + 1) * 64], in0=pO[:, 0:64], scalar1=smm[:, 7:8],
                                 scalar2=None, op0=ALU.mult)
                            yield
                    run_interleaved([heads(0), heads(1)])
                    for c in range(KC):
                        b.op("pe", "transpose", out=psTb[c % 2][:, 0, :], in_=otk[:, c * 128:(c + 1) * 128], identity=ident_bf[:, :])
                        b.op("act", "activation", out=oT[:, c, tb * 128:(tb + 1) * 128], in_=psTb[c % 2][:, 0, :], func=AF.Copy)
                proj_out_featmajor(a_w_o, oT)
                post_norm_residual(0, 1)
                lat_ffn_sub(0, 2)
                lat_store(zres, tl)


        def s5_lat():
            P = s5_setup(npow=9, pfx="l5p_")
            LT = 512
            sbf = lambda n_, sh, dt_=F32: b.sb("l5_" + n_, sh, dt_)
            TT = lambda o, a_, b_, op: b.op("dve", "tensor_tensor", out=o, in0=a_, in1=b_, op=op)
            h0r, h0i = sbf("h0r", [128, 64]), sbf("h0i", [128, 64])
            load_rows_T(h0r[:, :], st5_re, 64)
            load_rows_T(h0i[:, :], st5_im, 64)
            cr, ci = sbf("cr", [128, 64]), sbf("ci", [128, 64])
            t1, t2, t3 = P["t1"][:, :], P["t2"][:, :], P["t3"][:, :]
            TT(t1, P["fr"][:, :], P["fr"][:, :], ALU.mult)
            TT(t2, P["fi"][:, :], P["fi"][:, :], ALU.mult)
            TT(t1, t1, t2, ALU.add)
            b.op("dve", "reciprocal", out=t3, in_=t1)
            TT(t1, h0r[:, :], P["fr"][:, :], ALU.mult)
            TT(t2, h0i[:, :], P["fi"][:, :], ALU.mult)
            TT(t1, t1, t2, ALU.add)
            TT(cr[:, :], t1, t3, ALU.mult)
            TT(t1, h0i[:, :], P["fr"][:, :], ALU.mult)
            TT(t2, h0r[:, :], P["fi"][:, :], ALU.mult)
            TT(t1, t1, t2, ALU.subtract)
            TT(ci[:, :], t1, t3, ALU.mult)
            nat = {}
            for t4 in range(4):
                for nm in ("bre", "bim", "cre", "cim"):
                    tl_ = sbf(f"n_{nm}{t4}", [128, 128])
                    b.raw("dve", lambda tl_=tl_: nc.vector.memset(tl_[:, :], 0.0), writes=[tl_[:, :]])
                    nat[(nm, t4)] = tl_
            wB = [sbf(f"wB{ri}", [128, 128], BF16) for ri in range(2)]
            wC = [sbf(f"wC{ri}", [128, 128], BF16) for ri in range(2)]
            c32 = [sbf(f"c32{i}", [128, 128]) for i in range(2)]
            cm = [sbf(f"cm{i}", [128, 128]) for i in range(4)]
            X = (sbf("xr", [128, LT]), sbf("xi", [128, LT]))
            xrb, xib = sbf("xrb", [128, LT], BF16), sbf("xib", [128, LT], BF16)
            tm = [sbf(f"tm{i}", [128, LT]) for i in range(6)]
            sc = sbf("sc", [128, 8])
            rotb = [sbf(f"rotb{i}", [128, 1536]) for i in range(2)]
            ones512 = sbf("ones512", [128, LT])
            b.raw("dve", lambda: nc.vector.memset(ones512[:, :], 1.0), writes=[ones512[:, :]])
            pwu = [(P["cs"], P["sn"])]
            for k in range(1, 9):
                pc_, ps_ = pwu[-1]
                nc_ = sbf(f"uc{k}", [128, 64]); ns_ = sbf(f"us{k}", [128, 64])
                TT(t1, pc_[:, :], pc_[:, :], ALU.mult)
                TT(t2, ps_[:, :], ps_[:, :], ALU.mult)
                TT(nc_[:, :], t1, t2, ALU.subtract)
                TT(t1, pc_[:, :], ps_[:, :], ALU.mult)
                b.op("dve", "tensor_scalar", out=ns_[:, :], in0=t1, scalar1=2.0, scalar2=None, op0=ALU.mult)
                pwu.append((nc_, ns_))
            TSm = lambda o, i_, sc_: b.op("dve", "tensor_scalar", out=o, in0=i_, scalar1=sc_, scalar2=None, op0=ALU.mult)
            for tau in range(64):
                Er, Ei = X[0], X[1]
                b.op("dve", "tensor_copy", out=Er[:, 0:1], in_=P["cs"][:, tau:tau + 1])
                b.op("dve", "tensor_copy", out=Ei[:, 0:1], in_=P["sn"][:, tau:tau + 1])
                for k in range(9):
                    n = 1 << k
                    pc_, ps_ = pwu[k][0][:, tau:tau + 1], pwu[k][1][:, tau:tau + 1]
                    TSm(tm[0][:, 0:n], Er[:, 0:n], pc_)
                    TSm(tm[1][:, 0:n], Ei[:, 0:n], ps_)
                    TSm(tm[2][:, 0:n], Ei[:, 0:n], pc_)
                    TSm(tm[3][:, 0:n], Er[:, 0:n], ps_)
                    TT(Er[:, n:2 * n], tm[0][:, 0:n], tm[1][:, 0:n], ALU.subtract)
                    TT(Ei[:, n:2 * n], tm[2][:, 0:n], tm[3][:, 0:n], ALU.add)
                b.op("act", "activation", out=tm[4][:, :], in_=ones512[:, :], func=AF.Identity, scale=P["mag"][:, tau:tau + 1])
                b.dma("sp", rot[tau, :, 0:512], Er[:, :])
                b.dma("sp", rot[tau, :, 512:1024], Ei[:, :])
                b.dma("sp", rot[tau, :, 1024:1536], tm[4][:, :])
            rit = [0]

            def tile_pass(d, tl):
                for t in range(32):
                    tau = d * 32 + t
                    ch, t4 = t // 4, t % 4
                    for g2 in range(2):
                        g = 2 * t + g2
                        cb = (2 * t4 + g2) * 16
                        b.dma("sp", nat[("bre", t4)][g2 * 64:(g2 + 1) * 64, cb:cb + 16], s5_b_re[d, g, :, :])
                        b.dma("sp", nat[("bim", t4)][g2 * 64:(g2 + 1) * 64, cb:cb + 16], s5_b_im[d, g, :, :])
                        b.dma("sp", nat[("cre", t4)][cb:cb + 16, g2 * 64:(g2 + 1) * 64], s5_c_re[d, g, :, :])
                        b.dma("sp", nat[("cim", t4)][cb:cb + 16, g2 * 64:(g2 + 1) * 64], s5_c_im[d, g, :, :])
                    for ri, nm in enumerate(("bre", "bim")):
                        b.op("pe", "transpose", out=psS[:, 0:128], in_=nat[(nm, t4)][:, :], identity=ident[:, :])
                        b.op("act", "activation", out=wB[ri][:, :], in_=psS[:, 0:128], func=AF.Copy)
                    for ri, nm in enumerate(("cre", "cim")):
                        b.op("pe", "transpose", out=psS[:, 0:128], in_=nat[(nm, t4)][:, :], identity=ident[:, :])
                        b.op("act", "activation", out=c32[ri][:, :], in_=psS[:, 0:128], func=AF.Copy)
                    fr_, fi_ = P["fr"][:, tau:tau + 1], P["fi"][:, tau:tau + 1]
                    TS = lambda o, i_, sc_: b.op("dve", "tensor_scalar", out=o, in0=i_, scalar1=sc_, scalar2=None, op0=ALU.mult)
                    TS(cm[0][:, :], c32[0][:, :], fr_)
                    TS(cm[1][:, :], c32[1][:, :], fi_)
                    TT(wC[0][:, :], cm[0][:, :], cm[1][:, :], ALU.subtract)
                    TS(cm[2][:, :], c32[0][:, :], fi_)
                    TS(cm[3][:, :], c32[1][:, :], fr_)
                    TT(cm[2][:, :], cm[2][:, :], cm[3][:, :], ALU.add)
                    b.op("dve", "tensor_scalar", out=wC[1][:, :], in0=cm[2][:, :], scalar1=-1.0, scalar2=None, op0=ALU.mult)
                    tb = rotb[rit[0] % 2]
                    rit[0] += 1
                    b.dma("sp", tb[:, :], rot[tau, :, :])
                    cT, sT, rT = tb[:, 0:512], tb[:, 512:1024], tb[:, 1024:1536]
                    for ri in range(2):
                        b.op("pe", "matmul", out=psG[ri][:, 0:LT], lhsT=wB[ri][:, :], rhs=h[:, ch, :], start=True, stop=True)
                    e1 = LT - 1 if d == 0 else 0
                    cr_, ci_ = cr[:, tau:tau + 1], ci[:, tau:tau + 1]
                    rv = (lambda A: A[:, 0:LT]) if d == 0 else (lambda A: A[:, LT - 1::-1])
                    brv, biv = rv(psG[0]), rv(psG[1])
                    TT(tm[0][:, :], brv, cT, ALU.mult)
                    TT(tm[1][:, :], biv, sT, ALU.mult)
                    TT(tm[2][:, :], biv, cT, ALU.mult)
                    TT(tm[3][:, :], brv, sT, ALU.mult)
                    TT(X[0][:, :], tm[0][:, :], tm[1][:, :], ALU.add)
                    TT(X[1][:, :], tm[2][:, :], tm[3][:, :], ALU.subtract)
                    b.op("dve", "tensor_tensor_scan", out=tm[4][:, :], data0=rT, data1=X[0][:, :], initial=cr_, op0=ALU.mult, op1=ALU.add)
                    b.op("dve", "tensor_tensor_scan", out=tm[5][:, :], data0=rT, data1=X[1][:, :], initial=ci_, op0=ALU.mult, op1=ALU.add)
                    PT_ = lambda o, a_, b_, op: b.op("pool", "tensor_tensor", out=o, in0=a_, in1=b_, op=op)
                    PT_(tm[0][:, :], tm[4][:, :], cT, ALU.mult)
                    PT_(tm[1][:, :], tm[5][:, :], sT, ALU.mult)
                    PT_(tm[2][:, :], tm[5][:, :], cT, ALU.mult)
                    PT_(tm[3][:, :], tm[4][:, :], sT, ALU.mult)
                    TT(rv(X[0]), tm[0][:, :], tm[1][:, :], ALU.subtract)
                    TT(rv(X[1]), tm[2][:, :], tm[3][:, :], ALU.add)
                    b.op("dve", "tensor_copy", out=cr_, in_=X[0][:, e1:e1 + 1])
                    b.op("dve", "tensor_copy", out=ci_, in_=X[1][:, e1:e1 + 1])
                    b.op("act", "activation", out=xrb[:, :], in_=X[0][:, :], func=AF.Copy)
                    b.op("act", "activation", out=xib[:, :], in_=X[1][:, :], func=AF.Copy)
                    py = psY[t % 2]
                    b.op("pe", "matmul", out=py[:, 0:LT], lhsT=wC[0][:, :], rhs=xrb[:, :], start=True, stop=False)
                    b.op("pe", "matmul", accum=True, out=py[:, 0:LT], lhsT=wC[1][:, :], rhs=xib[:, :], start=False, stop=True)
                    if t4 == 0:
                        b.op("dve", "tensor_copy", out=y[:, ch, :], in_=py[:, 0:LT])
                    else:
                        TT(y[:, ch, :], y[:, ch, :], py[:, 0:LT], ALU.add)

            def tail():
                g_bf = act
                for c in range(KC):
                    t_a, t_b = tmpA[0], tmpA[1]
                    b.op("dve", "tensor_scalar", out=t_a[:, :], in0=h[:, c, :], scalar1=P["dskip"][:, c:c + 1], scalar2=None, op0=ALU.mult)
                    TT(y[:, c, :], y[:, c, :], t_a[:, :], ALU.add)
                    b.op("act", "activation", out=t_a[:, :], in_=y[:, c, :], func=AF.Square)
                    b.op("dve", "tensor_scalar", out=t_a[:, :], in0=t_a[:, :], scalar1=0.044715, scalar2=1.0, op0=ALU.mult, op1=ALU.add)
                    TT(t_a[:, :], t_a[:, :], y[:, c, :], ALU.mult)
                    b.op("act", "activation", out=t_b[:, :], in_=t_a[:, :], func=AF.Tanh, scale=0.7978845608028654)
                    b.op("dve", "tensor_scalar", out=t_b[:, :], in0=t_b[:, :], scalar1=1.0, scalar2=0.5, op0=ALU.add, op1=ALU.mult)
                    TT(g_bf[:, c, :], t_b[:, :], y[:, c, :], ALU.mult)
                for o4 in range(0, KC, 2):
                    sl = wslot()
                    v = sl[:, 0:KC * 512].rearrange("p (k n) -> p k n", k=KC)
                    b.dma("pool", v[:, :, 0:256], s5_w_glu[:, o4 * 128:(o4 + 2) * 128].rearrange("(k p) n -> p k n", p=128))
                    b.dma("pool", v[:, :, 256:512], s5_w_glu[:, D + o4 * 128:D + (o4 + 2) * 128].rearrange("(k p) n -> p k n", p=128))
                    for oo in range(2):
                        o = o4 + oo
                        pa, pg = psG[o % 2], psU[o % 2]
                        for k in range(KC):
                            b.op("pe", "matmul", accum=(k > 0), out=pa[:, 0:T], lhsT=v[:, k, oo * 128:(oo + 1) * 128],
                                 rhs=g_bf[:, k, :], start=(k == 0), stop=(k == KC - 1))
                        for k in range(KC):
                            b.op("pe", "matmul", accum=(k > 0), out=pg[:, 0:T], lhsT=v[:, k, 256 + oo * 128:256 + (oo + 1) * 128],
                                 rhs=g_bf[:, k, :], start=(k == 0), stop=(k == KC - 1))
                        tq = tmpA[o % 2]
                        b.op("act", "activation", out=tq[:, :], in_=pg[:, 0:T], func=AF.Sigmoid)
                        TT(y[:, o, :], tq[:, :], pa[:, 0:T], ALU.mult)

            for tl in range(NTL):
                lat_load(zres, tl)
                lat_ffn_sub(1, 0)
                lat_store(zres, tl)
                pre_norm(1, 1)
                tile_pass(0, tl)
                b.dma("sp", ysc[:, tl * 512:(tl + 1) * 512].rearrange("(c p) t -> p c t", p=128), y[:])
            for tl in range(NTL - 1, -1, -1):
                lat_load(zres, tl)
                pre_norm(1, 1)
                tile_pass(1, tl)
                for c in range(KC):
                    b.dma("sp", tmpA[c % 2][:, :], ysc[c * 128:(c + 1) * 128, tl * 512:(tl + 1) * 512])
                    TT(y[:, c, :], y[:, c, :], tmpA[c % 2][:, :], ALU.add)
                tail()
                post_norm_residual(1, 1)
                lat_ffn_sub(1, 2)
                lat_store(zres, tl)


        def na_lat():
            sbf = lambda n_, sh, dt_=F32: b.sb("ln_" + n_, sh, dt_)
            TT = lambda o, a_, b_, op: b.op("dve", "tensor_tensor", out=o, in0=a_, in1=b_, op=op)
            NR = LSEQ // 64
            kTg = sbf("kT", [128, LSEQ], BF16); vtg = sbf("vt", [64, NR, 128], BF16)
            kTc = sbf("kTc", [128, 512], BF16); vtc = sbf("vtc", [128, 4, 128], BF16)
            Bh = sbf("Bh", [64, 2, 15, 64]); cmask = sbf("cmask", [64, 64]); b.dma("sp", cmask[:], na_cmask)
            c32t = sbf("c32", [128, 128]); ckb = sbf("ckb", [128, 128], BF16)
            qTg = sbf("qT", [128, 512], BF16)
            slcs = [sbf(f"sl{j}", [64, 512]) for j in range(2)]
            Pls = [sbf(f"Pl{j}", [64, 512], BF16) for j in range(2)]; Pcs = [sbf(f"Pc{j}", [64, 512], BF16) for j in range(2)]
            PTls = [sbf(f"PTl{j}", [128, 12, 64], BF16) for j in range(2)]; otk = sbf("otk", [64, 128], BF16)
            smms = [sbf(f"sm{j}", [64, 8]) for j in range(2)]; woT = sbf("woT", [128, 512], BF16)
            stg = [sbf(f"stg{i}", [128, 512], BF16) for i in range(2)]
            vstg = sbf("vstg", [64, 1024], BF16)
            for tl in range(NTL):
                lat_load(zres, tl)
                lat_ffn_sub(2, 0)
                lat_store(zres, tl)
                pre_norm(2, 1)
                for part, dst in ((0, qsc), (1, ksc)):
                    for hf in range(2):
                        sl_ = wslot()
                        v = sl_[:, 0:KC * 512].rearrange("p (k n) -> p k n", k=KC)
                        b.dma("pool", v, na_w_qkv[:, part * D + hf * 512:part * D + (hf + 1) * 512].rearrange("(k p) n -> p k n", p=128))
                        for mm in range(4):
                            c = hf * 4 + mm
                            pq = psG[mm % 2]
                            for k in range(KC):
                                b.op("pe", "matmul", accum=(k > 0), out=pq[:, 0:512], lhsT=v[:, k, mm * 128:(mm + 1) * 128],
                                     rhs=h[:, k, :], start=(k == 0), stop=(k == KC - 1))
                            b.op("act", "activation", out=stg[mm % 2][:, :], in_=pq[:, 0:512], func=AF.Copy)
                            b.dma("sp", dst[c, :, tl * 512:(tl + 1) * 512], stg[mm % 2][:, :])
                wv = []
                for hf in range(2):
                    sl_ = wslot()
                    v = sl_[:, 0:KC * 512].rearrange("p (k n) -> p k n", k=KC)
                    b.dma("pool", v, na_w_qkv[:, 2 * D + hf * 512:2 * D + (hf + 1) * 512].rearrange("(k p) n -> p k n", p=128))
                    wv.append(v)
                for rr in range(8):
                    for hf in range(2):
                        pq = psU[hf]
                        for k in range(KC):
                            b.op("pe", "matmul", accum=(k > 0), out=pq[0:64, 0:512], lhsT=h[:, k, rr * 64:(rr + 1) * 64],
                                 rhs=wv[hf][:, k, :], start=(k == 0), stop=(k == KC - 1))
                        b.op("dve" if hf == 0 else "act", "tensor_copy" if hf == 0 else "activation",
                             out=vstg[:, hf * 512:(hf + 1) * 512], in_=pq[0:64, 0:512], **({} if hf == 0 else {"func": AF.Copy}))
                    b.dma("sp", vsc[:, :, tl * 8 + rr, :].rearrange("g p d -> p g d"), vstg[:, :].rearrange("p (g d) -> p g d", g=8))
            for gi in range(8):
                for j in range(2):
                    b.dma("sp", Bh[:, j, :, :].rearrange("p a k -> p (a k)"), na_bias[2 * gi + j, :, :])
                    for dr in range(15):
                        TT(Bh[:, j, dr, :], Bh[:, j, dr, :], cmask[:, :], ALU.add)
                    b.op("dve", "tensor_scalar", out=Bh[:, j, :, :].rearrange("p a k -> p (a k)"),
                         in0=Bh[:, j, :, :].rearrange("p a k -> p (a k)"), scalar1=8.0, scalar2=None, op0=ALU.mult)
                for cb in range(4):
                    b.dma("sp", c32t[:, :], cache_nk[cb * 128:(cb + 1) * 128, gi * 128:(gi + 1) * 128])
                    b.op("dve", "tensor_copy", out=ckb[:, :], in_=c32t[:, :])
                    b.op("pe", "transpose", out=psTb[cb % 2][:, 0, :], in_=ckb[:, :], identity=ident_bf[:, :])
                    b.op("act", "activation", out=kTc[:, cb * 128:(cb + 1) * 128], in_=psTb[cb % 2][:, 0, :], func=AF.Copy)
                    b.dma("pool", vtc[:, cb, :], cache_nv[cb * 128:(cb + 1) * 128, gi * 128:(gi + 1) * 128])
                b.dma("sp", kTg[:, :], ksc[gi, :, :])
                b.dma("sp", vtg[:, :, :], vsc[gi, :, :, :])
                it = 0
                for tl in range(NTL):
                    if gi == 7:
                        lat_load(zres, tl)
                    b.dma("sp", qTg[:, :], qsc[gi, :, tl * 512:(tl + 1) * 512])
                    for rr in range(8):
                        r = tl * 8 + rr
                        kr0 = min(max(r - 4, 0), NR - 8)
                        dr0 = kr0 - r + 7
                        def head(j, rr=rr, kr0=kr0, dr0=dr0):
                            base = 64 * j
                            slc, Pl, Pc, PTl, smm = slcs[j], Pls[j], Pcs[j], PTls[j], smms[j]
                            pL, pC, pO, pT = psG[j], psU[j], psY[j], psTc[j]
                            lq = qTg[base:base + 64, rr * 64:(rr + 1) * 64]
                            b.op("pe", "matmul", out=pL[0:64, 0:512], lhsT=lq, rhs=kTg[base:base + 64, kr0 * 64:kr0 * 64 + 512],
                                 start=True, stop=True)
                            b.op("pe", "matmul", out=pC[0:64, 0:512], lhsT=lq, rhs=kTc[base:base + 64, :], start=True, stop=True)
                            yield
                            TT(slc[:, :], pL[0:64, 0:512], Bh[:, j, dr0:dr0 + 8, :].rearrange("p a k -> p (a k)"), ALU.add)
                            b.op("dve", "reduce_max", out=smm[:, 0:1], in_=slc[:, :], axis=AX.X)
                            b.op("dve", "reduce_max", out=smm[:, 1:2], in_=pC[0:64, 0:512], axis=AX.X)
                            yield
                            TT(smm[:, 0:1], smm[:, 0:1], smm[:, 1:2], ALU.max)
                            b.op("dve", "tensor_scalar", out=smm[:, 2:3], in0=smm[:, 0:1], scalar1=-0.125, scalar2=None, op0=ALU.mult)
                            yield
                            b.op("act", "activation", out=Pl[:, :], in_=slc[:, :], func=AF.Exp, scale=0.125, bias=smm[:, 2:3],
                                 accum_out=smm[:, 3:4])
                            b.op("act", "activation", out=Pc[:, :], in_=pC[0:64, 0:512], func=AF.Exp, scale=0.125,
                                 bias=smm[:, 2:3], accum_out=smm[:, 4:5])
                            yield
                            TT(smm[:, 5:6], smm[:, 3:4], smm[:, 4:5], ALU.add)
                            b.op("dve", "reciprocal", out=smm[:, 6:7], in_=smm[:, 5:6])
                            for jj in range(0, 8, 2):
                                for q_ in range(2):
                                    b.op("pe", "transpose", out=pT[0:64, q_, 0:64], in_=Pl[:, (jj + q_) * 64:(jj + q_ + 1) * 64],
                                         identity=ident_bf[0:64, 0:64])
                                b.op("dve" if (jj // 2) % 2 == 0 else "act", "tensor_copy" if (jj // 2) % 2 == 0 else "activation",
                                     out=PTl[0:64, jj:jj + 2, :], in_=pT[0:64, :, 0:64], **({} if (jj // 2) % 2 == 0 else {"func": AF.Copy}))
                                yield
                            for jj in range(0, 4, 2):
                                for q_ in range(2):
                                    b.op("pe", "transpose", out=pT[:, q_, 0:64], in_=Pc[:, (jj + q_) * 128:(jj + q_ + 1) * 128],
                                         identity=ident_bf[0:64, 0:64])
                                b.op("dve" if (jj // 2) % 2 == 0 else "act", "tensor_copy" if (jj // 2) % 2 == 0 else "activation",
                                     out=PTl[:, 8 + jj:10 + jj, :], in_=pT[:, :, 0:64], **({} if (jj // 2) % 2 == 0 else {"func": AF.Copy}))
                                yield
                            for jj in range(8):
                                b.op("pe", "matmul", accum=(jj > 0), out=pO[0:64, 0:64], lhsT=PTl[0:64, jj, :],
                                     rhs=vtg[:, kr0 + jj, base:base + 64], start=(jj == 0), stop=False)
                            for jj in range(4):
                                b.op("pe", "matmul", accum=True, out=pO[0:64, 0:64], lhsT=PTl[:, 8 + jj, :],
                                     rhs=vtc[:, jj, base:base + 64], start=False, stop=(jj == 3))
                            yield
                            b.op("dve", "tensor_scalar", out=otk[:, base:base + 64], in0=pO[0:64, 0:64], scalar1=smm[:, 6:7],
                                 scalar2=None, op0=ALU.mult)
                        run_interleaved([head(0), head(1)])
                        b.op("pe", "transpose", out=psTb[rr % 2][:, 0, 0:64], in_=otk[:, :], identity=ident_bf[0:64, 0:64])
                        b.op("act", "activation", out=oT[:, 0, rr * 64:(rr + 1) * 64], in_=psTb[rr % 2][:, 0, 0:64], func=AF.Copy)
                    for o4 in range(2):
                        b.dma("pool", woT[:, :], na_w_o[gi * 128:(gi + 1) * 128, o4 * 512:(o4 + 1) * 512])
                        for oo in range(4):
                            o = o4 * 4 + oo
                            py = psY[oo % 2]
                            b.op("pe", "matmul", out=py[:, 0:512], lhsT=woT[:, oo * 128:(oo + 1) * 128], rhs=oT[:, 0, :],
                                 start=True, stop=True)
                            if gi == 0:
                                b.op("act", "activation", out=y[:, o, :], in_=py[:, 0:512], func=AF.Copy)
                            else:
                                b.dma("sp", tmpA[oo % 2][:, :], ysc[o * 128:(o + 1) * 128, tl * 512:(tl + 1) * 512])
                                TT(y[:, o, :], tmpA[oo % 2][:, :], py[:, 0:512], ALU.add)
                    if gi < 7:
                        b.dma("sp", ysc[:, tl * 512:(tl + 1) * 512].rearrange("(c p) t -> p c t", p=128), y[:])
                    else:
                        post_norm_residual(2, 1)
                        lat_ffn_sub(2, 2)
                        lat_store(zres, tl)


        def dn_lat(final_dst):
            sbf = lambda n_, sh, dt_=F32: b.sb("ld_" + n_, sh, dt_)
            TT = lambda o, a_, b_, op: b.op("dve", "tensor_tensor", out=o, in0=a_, in1=b_, op=op)
            NCH = LSEQ // 64
            msk = sbf("msk", [64, 6, 64]); b.dma("sp", msk[:], dn_mask)
            cw = sbf("cw", [128, 80]); load_rows_T(cw[:, :], dn_conv_w, 80)
            ones_f = sbf("ones", [64, 128]); b.raw("dve", lambda: nc.vector.memset(ones_f[:, :], 1.0), writes=[ones_f[:, :]])
            wba = sbf("wba", [128, 2, KC, 16], BF16)
            for d in range(2):
                b.dma("pool", wba[:, d, :, :], dn_w_ba[d].rearrange("(k p) n -> p k n", p=128))
            alog = sbf("alog", [64, 16]); b.dma("sp", alog[:], dn_a_log.broadcast_to([64, 16]))
            dtb = sbf("dtb", [64, 16]); b.dma("sp", dtb[:], dn_dt_bias.broadcast_to([64, 16]))
            outg = sbf("outg", [64, 128]); b.dma("sp", outg[:], dn_out_g.broadcast_to([64, 128]))
            nexpa = sbf("nexpa", [64, 16])
            b.op("act", "activation", out=nexpa[:, :], in_=alog[:, :], func=AF.Exp)
            b.op("dve", "tensor_scalar", out=nexpa[:, :], in0=nexpa[:, :], scalar1=-1.0, scalar2=None, op0=ALU.mult)
            zt = sbf("zt", [128, 2]); b.raw("dve", lambda: nc.vector.memset(zt[:, :], 0.0), writes=[zt[:, :]])
            for mm in range(16):
                b.dma("sp", pjsc[mm, :, 0:2], zt[:, :])
                b.dma("sp", pjsc[mm, :, LSEQ + 2:LSEQ + 4], zt[:, :])
            zstg = [act[:, 2 + i, :] for i in range(2)]
            graw = sbf("graw", [64, 2, NCH, 16])
            for tl in range(NTL):
                lat_load(zres, tl)
                lat_ffn_sub(3, 0)
                lat_store(zres, tl)
                pre_norm(3, 1)
                for m0 in range(0, 24, 4):
                    sl_ = wslot()
                    v = sl_[:, 0:KC * 512].rearrange("p (k n) -> p k n", k=KC)
                    b.dma("pool", v, dn_w_in[:, m0 * 128:(m0 + 4) * 128].rearrange("(k p) n -> p k n", p=128))
                    for mm in range(4):
                        m = m0 + mm
                        pq = psG[mm % 2]
                        for k in range(KC):
                            b.op("pe", "matmul", accum=(k > 0), out=pq[:, 0:512], lhsT=v[:, k, mm * 128:(mm + 1) * 128],
                                 rhs=h[:, k, :], start=(k == 0), stop=(k == KC - 1))
                        if m >= 16:
                            b.op("act", "activation", out=zstg[mm % 2], in_=pq[:, 0:512], func=AF.Silu)
                            b.dma("sp", zsc[m - 16, :, tl * 512:(tl + 1) * 512], zstg[mm % 2])
                        else:
                            b.op("act", "activation", out=tmpA[mm % 2][:, :], in_=pq[:, 0:512], func=AF.Copy)
                            b.dma("sp", pjsc[m, :, 2 + tl * 512:2 + (tl + 1) * 512], tmpA[mm % 2][:, :])
                for blk in range(8):
                    n = tl * 8 + blk
                    for d in range(2):
                        pq = psU[d]
                        for k in range(KC):
                            b.op("pe", "matmul", accum=(k > 0), out=pq[0:64, 0:16], lhsT=h[:, k, blk * 64:(blk + 1) * 64],
                                 rhs=wba[:, d, k, :], start=(k == 0), stop=(k == KC - 1))
                        b.op("dve", "tensor_copy", out=graw[:, d, n, :], in_=pq[0:64, 0:16])
            cin = sbf("cin", [128, 516])
            qn = sbf("qn", [128, LSEQ], BF16); kn = sbf("kn", [128, LSEQ], BF16)
            kt_tok = sbf("kt", [64, NCH, 128], BF16); vt_tok = sbf("vt", [64, NCH, 128], BF16)
            zs = sbf("zs", [128, LSEQ], BF16); vtmp = act[:, 1, :]
            oTf = oT[:, :, :].rearrange("p c t -> p (c t)")
            beta = sbf("beta", [64, 2, NCH]); xa = sbf("xa", [64, 2, NCH])
            xe = sbf("xe", [64, 2, NCH]); gg = sbf("gg", [64, 2, NCH]); gc = sbf("gc", [64, 2, NCH]); eg = sbf("eg", [64, 2, NCH])
            def mkbufs(sx):
                f64 = lambda n_: sbf(n_ + sx, [64, 64])
                U = {}
                U["dg"], U["decS"], U["decT"], U["PTt"] = f64("dg"), f64("decS"), f64("decT"), f64("PTt")
                U["Ms"] = [(f64("Ma"), f64("MTa")), (f64("Mb"), f64("MTb"))]
                U["gcr"] = sbf("gcr" + sx, [128, 64]); U["TTb"] = sbf("TTb" + sx, [64, 64], BF16); U["aT"] = sbf("aT" + sx, [64, 64], BF16)
                U["vb"] = sbf("vb" + sx, [64, 128], BF16); U["kbg"] = sbf("kbg" + sx, [64, 128], BF16); U["kg"] = sbf("kg" + sx, [64, 128], BF16)
                U["wT"] = sbf("wT" + sx, [128, 64], BF16); U["u_sb"] = sbf("u" + sx, [64, 128]); U["vnew"] = sbf("vnew" + sx, [64, 128])
                U["vnew_b"] = sbf("vnewb" + sx, [64, 128], BF16)
                U["o1"] = sbf("o1" + sx, [64, 128]); U["sc1"] = sbf("sc1" + sx, [64, 4]); U["egl"] = sbf("egl" + sx, [128, 1])
                U["S"] = sbf("S" + sx, [128, 128]); U["Sb"] = sbf("Sb" + sx, [128, 128], BF16)
                return U
            UB = [mkbufs("_a"), mkbufs("_b")]
            og = [sbf("og0", [64, 4, 128]), sbf("og1", [64, 4, 128])]
            ss = sbf("ss", [64, 2]); onb = sbf("onb", [64, 128], BF16); junk = sbf("junk", [64, 128])
            woT = sbf("woT", [128, 512], BF16)
            id64 = ident[0:64, 0:64]
            fl = lambda A: A[:, :, :].rearrange("p a b -> p (a b)")
            for hv in range(8):
                hq = hv // 2
                wcols = [hq * 128, 512 + hq * 128, 1024 + hv * 128, 2048 + hv * 128]
                cwc = [hq, 4 + hq, 8 + hv]
                pjc = [hq, 4 + hq, 8 + hv]
                b.dma("sp", zs[:, :], zsc[hv, :, :])
                for d in range(2):
                    b.op("act", "activation", out=beta[:, d, :], in_=graw[:, d, :, hv], func=AF.Sigmoid)
                    b.op("dve", "tensor_scalar", out=xa[:, d, :], in0=graw[:, d, :, 8 + hv], scalar1=dtb[:, d * 8 + hv:d * 8 + hv + 1],
                         scalar2=None, op0=ALU.add)
                b.op("act", "activation", out=fl(xe), in_=fl(xa), func=AF.Abs)
                b.op("act", "activation", out=fl(xe), in_=fl(xe), func=AF.Exp, scale=-1.0)
                b.op("act", "activation", out=fl(xe), in_=fl(xe), func=AF.Ln, bias=1.0, scale=1.0)
                b.op("dve", "tensor_scalar", out=fl(xa), in0=fl(xa), scalar1=0.0, scalar2=None, op0=ALU.max)
                TT(fl(xa), fl(xa), fl(xe), ALU.add)
                for d in range(2):
                    b.op("dve", "tensor_scalar", out=gg[:, d, :], in0=xa[:, d, :], scalar1=nexpa[:, d * 8 + hv:d * 8 + hv + 1],
                         scalar2=None, op0=ALU.mult)
                    b.op("pe", "matmul", out=psS[0:64, 0:NCH], lhsT=msk[:, d, :], rhs=gg[:, d, :], start=True, stop=True)
                    b.op("dve", "tensor_copy", out=gc[:, d, :], in_=psS[0:64, 0:NCH])
                b.op("act", "activation", out=fl(eg), in_=fl(gc), func=AF.Exp)
                for tl in range(NTL):
                    for mm in range(3):
                        b.dma("sp", cin[:, :], pjsc[pjc[mm], :, tl * 512:tl * 512 + 516])
                        for j in range(5):
                            dst = tmpA[0] if j == 0 else tmpA[1]
                            b.op("act", "activation", out=dst[:, :], in_=cin[:, j:j + 512], func=AF.Identity,
                                 scale=cw[:, j * 16 + cwc[mm]:j * 16 + cwc[mm] + 1])
                            if j > 0:
                                TT(tmpA[0][:, :], tmpA[0][:, :], tmpA[1][:, :], ALU.add)
                        if mm < 2:
                            b.op("act", "activation", out=tmpA[1][:, :], in_=tmpA[0][:, :], func=AF.Silu)
                            b.op("act", "activation", out=act[:, 0, :], in_=tmpA[1][:, :], func=AF.Square)
                            b.op("pe", "matmul", out=psS[:, 0:512], lhsT=ones_bf[:, :], rhs=act[:, 0, :], start=True, stop=True)
                            b.op("act", "activation", out=rstd[:, :], in_=psS[:, 0:512], func=AF.Sqrt, scale=1.0, bias=EPS)
                            b.op("dve", "reciprocal", out=rstd[:, :], in_=rstd[:, :])
                            TT(tmpA[1][:, :], tmpA[1][:, :], rstd[:, :], ALU.mult)
                            if mm == 0:
                                b.op("act", "activation", out=qn[:, tl * 512:(tl + 1) * 512], in_=tmpA[1][:, :], func=AF.Identity,
                                     scale=128 ** -0.5)
                            else:
                                b.op("act", "activation", out=kn[:, tl * 512:(tl + 1) * 512], in_=tmpA[1][:, :], func=AF.Identity, scale=1.0)
                                for blk in range(8):
                                    n = tl * 8 + blk
                                    b.op("pe", "transpose", out=psTb[blk % 2][0:64, 0, :], in_=kn[:, n * 64:(n + 1) * 64], identity=ident_bf[:, :])
                                    b.op("dve", "tensor_copy", out=kt_tok[:, n, :], in_=psTb[blk % 2][0:64, 0, :])
                        else:
                            b.op("act", "activation", out=vtmp, in_=tmpA[0][:, :], func=AF.Silu)
                            for blk in range(8):
                                n = tl * 8 + blk
                                b.op("pe", "transpose", out=psTb[blk % 2][0:64, 0, :], in_=act[:, 1, blk * 64:(blk + 1) * 64], identity=ident_bf[:, :])
                                b.op("dve", "tensor_copy", out=vt_tok[:, n, :], in_=psTb[blk % 2][0:64, 0, :])
                def chain(d):
                    U = UB[d]
                    dg, decS, decT, PTt, Ms, gcr, TTb, aT = U["dg"], U["decS"], U["decT"], U["PTt"], U["Ms"], U["gcr"], U["TTb"], U["aT"]
                    vb, kbg, kg, wT, u_sb, vnew, vnew_b = U["vb"], U["kbg"], U["kg"], U["wT"], U["u_sb"], U["vnew"], U["vnew_b"]
                    o1, sc1, egl, S, Sb = U["o1"], U["sc1"], U["egl"], U["S"], U["Sb"]
                    pG, pU, pY = psG[d], psU[d], psY[d]
                    b.dma("sp", S[:, :], st_dn[d * 8 + hv, :, :])
                    b.op("act", "activation", out=Sb[:, :], in_=S[:, :], func=AF.Copy)
                    yield
                    last = 63 if d == 0 else 0
                    for n in (range(NCH) if d == 0 else range(NCH - 1, -1, -1)):
                        tok0 = n * 64
                        gc_, be_, eg_ = gc[:, d, n:n + 1], beta[:, d, n:n + 1], eg[:, d, n:n + 1]
                        kf, qf = kn[:, tok0:tok0 + 64], qn[:, tok0:tok0 + 64]
                        kt, vt = kt_tok[:, n, :], vt_tok[:, n, :]
                        b.op("dve", "tensor_scalar", out=dg[:, :], in0=id64, scalar1=gc_, scalar2=None, op0=ALU.mult)
                        b.op("pe", "matmul", out=psS[:, 0:64], lhsT=ones_f[:, :], rhs=dg[:, :], start=True, stop=True)
                        b.op("act", "activation", out=gcr[:, :], in_=psS[:, 0:64], func=AF.Copy)
                        yield
                        b.op("dve", "tensor_scalar", out=decS[:, :], in0=gcr[0:64, :], scalar1=-1.0, scalar2=gc_, op0=ALU.mult, op1=ALU.add)
                        b.op("dve", "tensor_scalar", out=decS[:, :], in0=decS[:, :], scalar1=0.0, scalar2=None, op0=ALU.min)
                        b.op("act", "activation", out=decS[:, :], in_=decS[:, :], func=AF.Exp)
                        TT(decS[:, :], decS[:, :], msk[:, 4 + d, :], ALU.mult)
                        yield
                        b.op("dve", "tensor_scalar", out=decT[:, :], in0=gcr[0:64, :], scalar1=gc_, scalar2=None, op0=ALU.subtract)
                        b.op("dve", "tensor_scalar", out=decT[:, :], in0=decT[:, :], scalar1=0.0, scalar2=None, op0=ALU.min)
                        b.op("act", "activation", out=decT[:, :], in_=decT[:, :], func=AF.Exp)
                        TT(decT[:, :], decT[:, :], msk[:, d, :], ALU.mult)
                        yield
                        M, MT = Ms[0]
                        b.op("pe", "matmul", out=pG[0:64, 0:64], lhsT=kf, rhs=kf, start=True, stop=True)
                        b.op("dve", "tensor_scalar", out=M[:, :], in0=pG[0:64, 0:64], scalar1=be_, scalar2=-1.0, op0=ALU.mult, op1=ALU.mult)
                        TT(M[:, :], M[:, :], decS[:, :], ALU.mult)
                        yield
                        b.op("pe", "transpose", out=pU[0:64, 0:64], in_=M[:, :], identity=id64)
                        b.op("act", "activation", out=MT[:, :], in_=pU[0:64, 0:64], func=AF.Copy)
                        TT(PTt[:, :], MT[:, :], id64, ALU.add)
                        yield
                        cur = 0
                        for k in range(1, 6):
                            M, MT = Ms[cur]
                            Mn, MTn = Ms[1 - cur]
                            b.op("pe", "matmul", out=pG[0:64, 0:64], lhsT=MT[:, :], rhs=M[:, :], start=True, stop=True)
                            b.op("act", "activation", out=Mn[:, :], in_=pG[0:64, 0:64], func=AF.Copy)
                            if k < 5:
                                b.op("pe", "matmul", out=pU[0:64, 0:64], lhsT=M[:, :], rhs=MT[:, :], start=True, stop=True)
                                b.op("dve", "tensor_copy", out=MTn[:, :], in_=pU[0:64, 0:64])
                            yield
                            b.op("pe", "matmul", out=pY[0:64, 0:64], lhsT=Mn[:, :], rhs=PTt[:, :], start=True, stop=True)
                            TT(PTt[:, :], PTt[:, :], pY[0:64, 0:64], ALU.add)
                            yield
                            cur = 1 - cur
                        b.op("act", "activation", out=TTb[:, :], in_=PTt[:, :], func=AF.Copy)
                        TT(sc1[:, 0:1], be_, eg_, ALU.mult)
                        b.op("dve", "tensor_scalar", out=vb[:, :], in0=vt, scalar1=be_, scalar2=None, op0=ALU.mult)
                        b.op("dve", "tensor_scalar", out=kbg[:, :], in0=kt, scalar1=sc1[:, 0:1], scalar2=None, op0=ALU.mult)
                        yield
                        b.op("pe", "matmul", out=pG[0:64, 0:128], lhsT=TTb[:, :], rhs=vb[:, :], start=True, stop=True)
                        b.op("act", "activation", out=u_sb[:, :], in_=pG[0:64, 0:128], func=AF.Copy)
                        b.op("pe", "matmul", out=pU[:, 0:64], lhsT=kbg[:, :], rhs=TTb[:, :], start=True, stop=True)
                        b.op("dve", "tensor_copy", out=wT[:, :], in_=pU[:, 0:64])
                        yield
                        b.op("pe", "matmul", out=pY[0:64, 0:128], lhsT=wT[:, :], rhs=Sb[:, :], start=True, stop=True)
                        TT(vnew[:, :], u_sb[:, :], pY[0:64, 0:128], ALU.subtract)
                        b.op("act", "activation", out=vnew_b[:, :], in_=vnew[:, :], func=AF.Copy)
                        yield
                        b.op("pe", "matmul", out=pG[0:64, 0:128], lhsT=qf, rhs=Sb[:, :], start=True, stop=True)
                        b.op("dve", "tensor_scalar", out=o1[:, :], in0=pG[0:64, 0:128], scalar1=eg_, scalar2=None, op0=ALU.mult)
                        b.op("pe", "matmul", out=pU[0:64, 0:64], lhsT=kf, rhs=qf, start=True, stop=True)
                        TT(aT[:, :], pU[0:64, 0:64], decT[:, :], ALU.mult)
                        yield
                        b.op("pe", "matmul", out=pY[0:64, 0:128], lhsT=aT[:, :], rhs=vnew_b[:, :], start=True, stop=True)
                        TT(o1[:, :], o1[:, :], pY[0:64, 0:128], ALU.add)
                        b.dma("sp", osc[d, :, n, :], o1[:, :])
                        yield
                        b.op("act", "activation", out=sc1[:, 1:2], in_=gc_, func=AF.Exp, scale=-1.0, bias=gcr[0:64, last:last + 1])
                        b.op("dve", "tensor_scalar", out=kg[:, :], in0=kt, scalar1=sc1[:, 1:2], scalar2=None, op0=ALU.mult)
                        b.op("act", "activation", out=egl[:, :], in_=gcr[:, last:last + 1], func=AF.Exp)
                        yield
                        b.op("pe", "matmul", out=pG[:, 0:128], lhsT=kg[:, :], rhs=vnew_b[:, :], start=True, stop=True)
                        b.op("dve", "tensor_scalar", out=S[:, :], in0=S[:, :], scalar1=egl[:, 0:1], scalar2=None, op0=ALU.mult)
                        TT(S[:, :], S[:, :], pG[:, 0:128], ALU.add)
                        b.op("act", "activation", out=Sb[:, :], in_=S[:, :], func=AF.Copy)
                        yield

                gens = [chain(0), chain(1)]
                alive = [True, True]
                while any(alive):
                    for gi_ in range(2):
                        if alive[gi_]:
                            try:
                                next(gens[gi_])
                            except StopIteration:
                                alive[gi_] = False
                for n0 in range(0, NCH, 4):
                    for d in range(2):
                        b.dma("sp", og[d][:, :, :], osc[d, :, n0:n0 + 4, :])
                    for nn in range(4):
                        n = n0 + nn
                        tok0 = n * 64
                        o1 = UB[0]["o1"]
                        TT(o1[:, :], og[0][:, nn, :], og[1][:, nn, :], ALU.add)
                        b.op("act", "activation", out=junk[:, :], in_=o1[:, :], func=AF.Square, accum_out=ss[:, 0:1])
                        b.op("act", "activation", out=ss[:, 1:2], in_=ss[:, 0:1], func=AF.Sqrt, scale=1.0 / 128, bias=EPS)
                        b.op("dve", "reciprocal", out=ss[:, 1:2], in_=ss[:, 1:2])
                        b.op("dve", "tensor_scalar", out=junk[:, :], in0=o1[:, :], scalar1=ss[:, 1:2], scalar2=None, op0=ALU.mult)
                        TT(onb[:, :], junk[:, :], outg[:, :], ALU.mult)
                        b.op("pe", "transpose", out=psTb[n % 2][:, 0, 0:64], in_=onb[:, :], identity=ident_bf[0:64, 0:64])
                        TT(oTf[:, tok0:tok0 + 64], psTb[n % 2][:, 0, 0:64], zs[:, tok0:tok0 + 64], ALU.mult)
                for tl in range(NTL):
                    if hv == 7:
                        lat_load(zres, tl)
                    for o in range(KC):
                        if o % 4 == 0:
                            b.dma("pool", woT[:, :], dn_w_o[hv * 128:(hv + 1) * 128, (o // 4) * 512:(o // 4 + 1) * 512])
                        py = psY[o % 2]
                        b.op("pe", "matmul", out=py[:, 0:512], lhsT=woT[:, (o % 4) * 128:(o % 4 + 1) * 128], rhs=oTf[:, tl * 512:(tl + 1) * 512],
                             start=True, stop=True)
                        if hv == 0:
                            b.op("act", "activation", out=y[:, o, :], in_=py[:, 0:512], func=AF.Copy)
                        else:
                            b.dma("sp", tmpA[o % 2][:, :], ysc[o * 128:(o + 1) * 128, tl * 512:(tl + 1) * 512])
                            TT(y[:, o, :], tmpA[o % 2][:, :], py[:, 0:512], ALU.add)
                    if hv < 7:
                        b.dma("sp", ysc[:, tl * 512:(tl + 1) * 512].rearrange("(c p) t -> p c t", p=128), y[:])
                    else:
                        post_norm_residual(3, 1)
                        lat_ffn_sub(3, 2)
                        lat_store(final_dst, tl)

        CS = 0 if CTX_SKIP else STAGE
        def dbg(ap, n, col0=0):
            b.dma("sp", dbg_out[:, col0:col0 + n], ap)

        if CS >= 1:
            ada_layer(0)
            if DEBUG:
                dbg(mods[0][:, :], 72)
                dbg(Acoef[:, 0:24], 24, 72)
                dbg(Gcoef[:, 0:24], 24, 96)
        if CS >= 2:
            pre_norm(0, 0)
            if DEBUG:
                dbg(rstd[:, :], 512, 512)
        if CS >= 3:
            ffn(ffn_w_gu[0, 0], ffn_w_d[0, 0], key=0)
            if DEBUG:
                dbg(y[:, 0, :], 512, 1024)
        if CS >= 4:
            post_norm_residual(0, 0)
        if CS >= 5:
            pre_norm(0, 1)
            proj_tokmajor(a_w_qkv, 1024, 256, k_out)
            proj_tokmajor(a_w_qkv, 1280, 256, v_out, vdst_col0=0)
        if CS >= 6:
            proj_featmajor(qT, 0, a_w_qkv, 0, 8, h)
            proj_featmajor(kT, 0, a_w_qkv, 1024, 4, h, dup64=True)
            hm = {hh: (hh // 2, 64 * (hh % 2), hh // 4, (hh // 4) * 64) for hh in range(16)}
            if SUB >= 2:
                attn_ctx(16, hm, True, 0.125)
            if DEBUG and SUB >= 2:
                b.op("dve", "tensor_copy", out=tmpA[0][:, :], in_=oT[:, 0, :])
                dbg(tmpA[0][:, :], 512, 1536)
            if SUB >= 1:
                proj_out_featmajor(a_w_o, oT)
                post_norm_residual(0, 1)
        if CS >= 7:
            pre_norm(0, 2)
            ffn(ffn_w_gu[0, 1], ffn_w_d[0, 1], key=1)
            post_norm_residual(0, 2)
        if CS >= 8:
            ada_layer(1)
            pre_norm(1, 0)
            ffn(ffn_w_gu[1, 0], ffn_w_d[1, 0], key=2)
            post_norm_residual(1, 0)
        if CS >= 9:
            pre_norm(1, 1)
            s5_es = ExitStack()
            b.es = s5_es
            S5P = s5_setup()
            s5_mixer(S5P)
            b.es = es
            if DEBUG:
                dbg(y[:, 0, :], 512, 2048)
            post_norm_residual(1, 1)
        if CS >= 10:
            pre_norm(1, 2)
            ffn(ffn_w_gu[1, 1], ffn_w_d[1, 1], key=3)
            post_norm_residual(1, 2)
        if CS >= 11:
            ada_layer(2)
            pre_norm(2, 0)
            ffn(ffn_w_gu[2, 0], ffn_w_d[2, 0], key=4)
            post_norm_residual(2, 0)
            pre_norm(2, 1)
            proj_tokmajor(na_w_qkv, 1024, 1024, nak_out)
            proj_tokmajor(na_w_qkv, 2048, 1024, nav_out, vdst_col0=0)
            proj_featmajor(qT, 0, na_w_qkv, 0, 8, h)
            proj_featmajor(kT, 0, na_w_qkv, 1024, 8, h)
            hm2 = {hh: (hh // 2, 64 * (hh % 2), hh // 2, hh * 64) for hh in range(16)}
            attn_ctx(16, hm2, None, 0.125)
            proj_out_featmajor(na_w_o, oT)
            post_norm_residual(2, 1)
            pre_norm(2, 2)
            ffn(ffn_w_gu[2, 1], ffn_w_d[2, 1], key=5)
            post_norm_residual(2, 2)
        if CS >= 12:
            ada_layer(3)
            pre_norm(3, 0)
            ffn(ffn_w_gu[3, 0], ffn_w_d[3, 0], key=6)
            post_norm_residual(3, 0)
        if CS >= 13:
            pre_norm(3, 1)
            b.fence()
            s5_es.close()
            att_es.close()
            dn_es = ExitStack()
            b.es = dn_es
            dn_mixer()
            b.es = es
            if DEBUG:
                dbg(y[:, 0, :], 512, 2560)
            post_norm_residual(3, 1)
            pre_norm(3, 2)
            ffn(ffn_w_gu[3, 1], ffn_w_d[3, 1], key=7)
            post_norm_residual(3, 2)
        b.dma("sp", yT_out.rearrange("(c p) t -> p c t", p=128), x[:])
        if STAGE >= 20:
            b.fence()
            if CTX_SKIP:
                att_es.close()
            else:
                dn_es.close()
            load_rows_T(cc[:, :], c_lat, 8)
            b.op("act", "activation", out=scond_lat[:, :, 0], in_=cc[:, :], func=AF.Silu)
            lat_es = ExitStack()
            b.es = lat_es
            ada_layer(0, scond_lat)
            a_lat(xT_lat)
            b.es = es
            b.fence()
            lat_es.close()
            if STAGE >= 21:
                lat_es = ExitStack()
                b.es = lat_es
                ada_layer(1, scond_lat)
                s5_lat()
                b.es = es
                b.fence()
                lat_es.close()
            if STAGE >= 22:
                lat_es = ExitStack()
                b.es = lat_es
                ada_layer(2, scond_lat)
                na_lat()
                b.es = es
                b.fence()
                lat_es.close()
            if STAGE >= 23:
                lat_es = ExitStack()
                b.es = lat_es
                ada_layer(3, scond_lat)
                dn_lat(zres)
                b.es = es
                b.fence()
                lat_es.close()
        if STAGE >= 20:
            for tl in range(NTL):
                lat_load(zres, tl)
                lat_store(ysamp_out, tl)
        else:
            b.raw("dve", lambda: nc.vector.memset(tmpA[0][:, :], 0.0), writes=[tmpA[0][:, :]])
            for tl in range(NTL):
                for c in range(KC):
                    b.dma("sp", ysamp_out[c * 128:(c + 1) * 128, tl * 512:(tl + 1) * 512], tmpA[0][:, :])
        if STAGE < 13:
            for j in range(32):
                b.dma("sp", dn_out[j, :, :], tmpA[0][:, 0:128])

        b.finish()
        if STAGE >= 20:
            pass
        elif STAGE >= 13:
            dn_es.close()
        else:
            if CS >= 9:
                s5_es.close()
            att_es.close()
    return nc


_PROG = None


def _dn_masks():
    i = np.arange(64)
    bef0 = (i[:, None] <= i[None, :]).astype(np.float32)
    bef1 = (i[:, None] >= i[None, :]).astype(np.float32)
    eye = np.eye(64, dtype=np.float32)
    m = np.stack([bef0, bef1, bef0.T, bef1.T, bef0.T - eye, bef1.T - eye], 1)
    return np.ascontiguousarray(m, dtype=np.float32)


def _rope_tables(L):
    n = 16
    inv = (np.float32(10000.0) ** (-np.arange(n, dtype=np.float32) / np.float32(n))).astype(np.float32)
    t = np.arange(L)
    ang_r = (t // 64).astype(np.float32)[:, None] * inv[None, :]
    ang_c = (t % 64).astype(np.float32)[:, None] * inv[None, :]
    return np.ascontiguousarray(np.concatenate([np.cos(ang_r), np.cos(ang_c), np.sin(ang_r), np.sin(ang_c)], 1), dtype=np.float32)


def _na_bias_gather(rpb):
    q = np.arange(64)[:, None]
    k = np.arange(64)[None, :]
    dc = np.clip(k - q, -15, 15) + 15
    g = rpb[:, :, dc]
    return np.ascontiguousarray(g.transpose(0, 2, 1, 3).reshape(16, 64, 960), dtype=np.float32)


def _na_colmask():
    col = np.arange(64)
    cs = np.clip(col - 8, 0, 48)
    ok = (col[None, :] >= cs[:, None]) & (col[None, :] < cs[:, None] + 16)
    return np.where(ok, 0.0, -30000.0).astype(np.float32)


def _band_mask():
    q = np.arange(128)[:, None]
    j = np.arange(384)[None, :] - 128
    return np.where(np.abs(j - q) <= 128, 0.0, -30000.0).astype(np.float32)


def prep_core(inputs, r, lseq=None):
    f = lambda a: np.ascontiguousarray(np.asarray(a, dtype=np.float32))
    L = LSEQ if lseq is None else lseq
    bsel = r % 2
    return {
        "xT_ctx": np.ascontiguousarray(f(inputs["x_prompt"])[2 * r:2 * r + 2].reshape(TCTX, D).T),
        "xT_lat": np.ascontiguousarray(f(inputs["x_sample"])[bsel, :L].T),
        "c_lat": f(inputs["c"])[bsel].reshape(8, 128),
        "cache_ak": f(inputs["cache_attn_k"])[bsel, 0].reshape(512, 256),
        "cache_av": f(inputs["cache_attn_v"])[bsel, 0].reshape(512, 256),
        "rope_cs": _rope_tables(L),
        "st5_re": f(inputs["state_s5_re"])[bsel, 0].reshape(64, 128),
        "st_dn": f(inputs["state_dn"])[bsel, 0].reshape(16, 128, 128),
        "cache_nk": f(inputs["cache_na_k"])[bsel, 0].reshape(512, D),
        "cache_nv": f(inputs["cache_na_v"])[bsel, 0].reshape(512, D),
        "na_bias": _na_bias_gather(f(inputs["na_rpb"])[0]),
        "na_cmask": _na_colmask(),
        "st5_im": f(inputs["state_s5_im"])[bsel, 0].reshape(64, 128),
        "band_mask": _band_mask(),
    }


def prep_shared(inputs, nl=4):
    f = lambda a: np.ascontiguousarray(np.asarray(a, dtype=np.float32))
    return {
        "ident": np.eye(128, dtype=np.float32),
        "c_ctx": f(inputs["c_ctx"]).reshape(8, 128),
        "norm_g": f(inputs["norm_g"]).reshape(192, 128),
        "w_ada": f(inputs["w_ada"][0:nl]),
        "b_ada": f(inputs["b_ada"]).reshape(288, 128),
        "ffn_w_gu": f(inputs["ffn_w_gu"][0:nl]),
        "ffn_w_d": f(inputs["ffn_w_d"][0:nl]),
        "a_w_qkv": f(inputs["a_w_qkv"])[0],
        "a_w_o": f(inputs["a_w_o"])[0],
        "a_sink": f(inputs["a_sink"]).reshape(1, 16),
        "s5_lam_re": f(inputs["s5_lam_re"]).reshape(64, 128),
        "s5_lam_im": f(inputs["s5_lam_im"]).reshape(64, 128),
        "s5_logdt": f(np.repeat(np.asarray(inputs["s5_log_dt"], np.float32).reshape(2, 32, 2), 64, axis=-1)).reshape(64, 128),
        "s5_b_re": f(inputs["s5_b_re"])[0],
        "s5_b_im": f(inputs["s5_b_im"])[0],
        "s5_c_re": f(inputs["s5_c_re"])[0],
        "s5_c_im": f(inputs["s5_c_im"])[0],
        "s5_d": f(inputs["s5_d"]).reshape(8, 128),
        "s5_w_glu": f(inputs["s5_w_glu"])[0],
        "na_w_qkv": f(inputs["na_w_qkv"])[0],
        "na_w_o": f(inputs["na_w_o"])[0],
        "dn_w_in": f(inputs["dn_w_in"])[0],
        "dn_conv_w": f(inputs["dn_conv_w"]).reshape(5, 16, 128).reshape(80, 128),
        "dn_w_ba": f(inputs["dn_w_ba"])[0],
        "dn_a_log": f(inputs["dn_a_log"]).reshape(1, 16),
        "dn_dt_bias": f(inputs["dn_dt_bias"]).reshape(1, 16),
        "dn_out_g": f(inputs["dn_out_g"]).reshape(1, 128),
        "dn_w_o": f(inputs["dn_w_o"])[0],
        "dn_mask": _dn_masks(),
    }


def kernel(**inputs):
    global _PROG
    f = lambda a: np.ascontiguousarray(np.asarray(a, dtype=np.float32))
    x_prompt = f(inputs["x_prompt"])
    if _PROG is None:
        _PROG = build_program()
    nc = _PROG
    shared = prep_shared(inputs)
    in_maps = []
    for r in range(NCORES):
        m = dict(shared)
        m.update(prep_core(inputs, r))
        in_maps.append(m)
    res = run_bass_kernel_spmd(nc, in_maps, core_ids=list(range(NCORES)))
    R = res.results
    y_prompt = np.stack([R[r]["out_yT_ctx"].T.reshape(2, SEQ, D) for r in range(NCORES)]).reshape(16, SEQ, D)
    new_k = np.concatenate([R[r]["out_attn_k"].reshape(2, 1, SEQ, 4, 64) for r in range(NCORES)], 0)
    new_v = np.concatenate([R[r]["out_attn_v"].reshape(2, 1, SEQ, 4, 64) for r in range(NCORES)], 0)
    s5re = np.concatenate([R[r]["out_s5_re"].reshape(2, 1, 2, 64, 64) for r in range(NCORES)], 0)
    s5im = np.concatenate([R[r]["out_s5_im"].reshape(2, 1, 2, 64, 64) for r in range(NCORES)], 0)
    nak = np.concatenate([R[r]["out_na_k"].reshape(2, 1, SEQ, 16, 64) for r in range(NCORES)], 0)
    nav = np.concatenate([R[r]["out_na_v"].reshape(2, 1, SEQ, 16, 64) for r in range(NCORES)], 0)
    ysamp = np.stack([R[r]["out_y_sample"].T for r in range(2)], 0)
    dn = np.concatenate([R[r]["out_dn"].reshape(2, 1, 2, 8, 128, 128) for r in range(NCORES)], 0)
    c32 = lambda a: np.ascontiguousarray(a, dtype=np.float32)
    return (c32(y_prompt), c32(ysamp), c32(new_k), c32(new_v), c32(s5re), c32(s5im), c32(nak), c32(nav), c32(dn))
```

```python
import numpy as np
from contextlib import ExitStack
import concourse.bass as bass
import concourse.mybir as mybir
from concourse.bass_utils import run_bass_kernel_spmd

F32 = mybir.dt.float32
BF16 = mybir.dt.bfloat16
AF = mybir.ActivationFunctionType
ALU = mybir.AluOpType
AX = mybir.AxisListType

D = 1024
KC = 8
DFF = 2816
FC = 22
NCORES = 8
TCTX = 512
SEQ = 256
EPS = 1e-6
WSLOT = 6144
NWS = 3
STAGE = 99
LSEQ = 4096
CTX_SKIP = False
SUB = 99
DEBUG = False


class Eng:
    def __init__(self, name, h, sem):
        self.name, self.h, self.sem = name, h, sem
        self.count = 0
        self.waited = {}


class Builder:
    def __init__(self, nc, es):
        self.nc, self.es = nc, es
        self.es_sem = es
        self.sems = []
        self.engs = {}
        for nm, h in [("pe", nc.tensor), ("act", nc.scalar), ("dve", nc.vector),
                      ("pool", nc.gpsimd), ("sp", nc.sync)]:
            sem = es.enter_context(nc.semaphore("sem_" + nm))
            self.sems.append(sem)
            e = Eng(nm, h, sem)
            e.key = len(self.sems) - 1
            self.engs[nm] = e
        self.track = {}
        self.dsem = {}
        self.out_events = []

    def sb(self, name, shape, dt):
        return self.es.enter_context(self.nc.sbuf_tensor(name, list(shape), dt))

    def ps(self, name, shape, dt=F32):
        return self.es.enter_context(self.nc.psum_tensor(name, list(shape), dt))

    def _nm(self, a):
        return a if isinstance(a, str) else a.tensor.name

    def _deps(self, reads, writes, skip_own_waw=None, own=None):
        deps = set()
        for r in reads:
            nm = self._nm(r)
            st = self.track.get(nm)
            if st and st[0]:
                deps.add(st[0])
            if st and nm.startswith("ps"):
                for ev in st[1]:
                    if ev[0] != own:
                        deps.add(ev)
        for w in writes:
            st = self.track.get(self._nm(w))
            if st:
                if st[0] and not (skip_own_waw is not None and st[0][0] == skip_own_waw):
                    deps.add(st[0])
                for ev in st[1]:
                    deps.add(ev)
        return deps

    def _wait(self, e, deps):
        best = {}
        for (k, v) in deps:
            if best.get(k, 0) < v:
                best[k] = v
        for k, v in best.items():
            if e.waited.get(k, 0) < v:
                e.h.wait_ge(self.sems[k], v)
                e.waited[k] = v

    def _commit(self, ev, reads, writes, accum=False):
        for r in reads:
            st = self.track.setdefault(self._nm(r), [None, []])
            st[1].append(ev)
        for w in writes:
            nm = self._nm(w)
            if accum and nm in self.track:
                self.track[nm][0] = ev
            else:
                self.track[nm] = [ev, []]

    def op(self, eng, fname, accum=False, extra_reads=(), extra_writes=(), **kw):
        e = self.engs[eng]
        reads, writes = list(extra_reads), list(extra_writes)
        for k, v in kw.items():
            if isinstance(v, bass.AP):
                if k in ("out", "accum_out"):
                    writes.append(v)
                else:
                    reads.append(v)
        deps = self._deps(reads, writes, skip_own_waw=(e.key if accum else None), own=e.key)
        self._wait(e, deps)
        inst = getattr(e.h, fname)(**kw)
        e.count += 1
        inst.then_inc(e.sem, 1)
        ev = (e.key, e.count)
        self._commit(ev, reads, writes, accum=accum)
        return ev

    def raw(self, eng, fn, reads=(), writes=()):
        e = self.engs[eng]
        self._wait(e, self._deps(list(reads), list(writes), own=e.key))
        inst = fn()
        e.count += 1
        inst.then_inc(e.sem, 1)
        ev = (e.key, e.count)
        self._commit(ev, list(reads), list(writes))
        return ev

    def dma(self, q, out, in_, **kw):
        e = self.engs[q]
        deps = self._deps([in_], [out])
        self._wait(e, deps)
        nm = self._nm(out)
        if nm not in self.dsem:
            sem = self.es_sem.enter_context(self.nc.semaphore("dsem_" + nm))
            self.sems.append(sem)
            self.dsem[nm] = [len(self.sems) - 1, 0]
        ds = self.dsem[nm]
        e.h.dma_start(out=out, in_=in_, **kw).then_inc(self.sems[ds[0]], 16)
        ds[1] += 16
        ev = (ds[0], ds[1])
        self._commit(ev, [in_], [out])
        if out.tensor.name.startswith("out_"):
            self.out_events.append(ev)
        return ev

    def fence(self):
        evs = {(x.key, x.count) for x in self.engs.values() if x.count > 0}
        evs |= {(k, c) for (k, c) in self.dsem.values() if c > 0}
        for e in self.engs.values():
            self._wait(e, {ev for ev in evs if ev[0] != e.key})

    def finish(self):
        e = self.engs["sp"]
        self._wait(e, set(self.out_events))
        self._wait(e, {(x.key, x.count) for x in self.engs.values() if x.count > 0 and x.name != "sp"})


def build_program(nl=4):
    nc = bass.Bass("TRN2", target_bir_lowering=False)

    def din(name, shape):
        return nc.dram_tensor(name, list(shape), F32, kind="ExternalInput").ap()

    def dout(name, shape):
        return nc.dram_tensor("out_" + name, list(shape), F32, kind="ExternalOutput").ap()

    T = TCTX
    xT_in = din("xT_ctx", [D, T])
    ident_in = din("ident", [128, 128])
    c_ctx = din("c_ctx", [8, 128])
    norm_g = din("norm_g", [192, 128])
    w_ada = din("w_ada", [nl, D, 9 * D])
    b_ada = din("b_ada", [288, 128])
    ffn_w_gu = din("ffn_w_gu", [nl, 2, D, 2 * DFF])
    ffn_w_d = din("ffn_w_d", [nl, 2, DFF, D])
    a_w_qkv = din("a_w_qkv", [D, 1536])
    a_w_o = din("a_w_o", [D, D])
    a_sink = din("a_sink", [1, 16])
    s5_lam_re = din("s5_lam_re", [64, 128])
    s5_lam_im = din("s5_lam_im", [64, 128])
    s5_logdt = din("s5_logdt", [64, 128])
    s5_b_re = din("s5_b_re", [2, 64, 64, 16])
    s5_b_im = din("s5_b_im", [2, 64, 64, 16])
    s5_c_re = din("s5_c_re", [2, 64, 16, 64])
    s5_c_im = din("s5_c_im", [2, 64, 16, 64])
    s5_d = din("s5_d", [8, 128])
    s5_w_glu = din("s5_w_glu", [D, 2 * D])
    na_w_qkv = din("na_w_qkv", [D, 3 * D])
    na_w_o = din("na_w_o", [D, D])
    dn_w_in = din("dn_w_in", [D, 3 * D])
    dn_conv_w = din("dn_conv_w", [80, 128])
    dn_w_ba = din("dn_w_ba", [2, D, 16])
    dn_a_log = din("dn_a_log", [1, 16])
    dn_dt_bias = din("dn_dt_bias", [1, 16])
    dn_out_g = din("dn_out_g", [1, 128])
    dn_w_o = din("dn_w_o", [D, D])
    dn_mask = din("dn_mask", [64, 6, 64])
    xT_lat = din("xT_lat", [D, LSEQ])
    c_lat = din("c_lat", [8, 128])
    cache_ak = din("cache_ak", [512, 256])
    cache_av = din("cache_av", [512, 256])
    rope_cs = din("rope_cs", [LSEQ, 64])
    band_mask = din("band_mask", [128, 384])
    zres = nc.dram_tensor("zres", [D, LSEQ], F32, kind="Internal").ap()
    wsc = nc.dram_tensor("wsc", [8, 15, 128, WSLOT], BF16, kind="Internal").ap()
    ysc = nc.dram_tensor("ysc", [D, LSEQ], F32, kind="Internal").ap()
    rot = nc.dram_tensor("rot", [64, 128, 1536], F32, kind="Internal").ap()
    cache_nk = din("cache_nk", [512, D])
    cache_nv = din("cache_nv", [512, D])
    na_bias = din("na_bias", [16, 64, 960])
    na_cmask = din("na_cmask", [64, 64])
    st_dn = din("st_dn", [16, 128, 128])
    pjsc = nc.dram_tensor("pjsc", [16, 128, LSEQ + 4], F32, kind="Internal").ap()
    zsc = nc.dram_tensor("zsc", [8, 128, LSEQ], BF16, kind="Internal").ap()
    qsc = nc.dram_tensor("qsc", [8, 128, LSEQ], BF16, kind="Internal").ap()
    ksc = nc.dram_tensor("ksc", [8, 128, LSEQ], BF16, kind="Internal").ap()
    vsc = nc.dram_tensor("vsc", [8, 64, LSEQ // 64, 128], BF16, kind="Internal").ap()
    osc = nc.dram_tensor("osc", [2, 64, LSEQ // 64, 128], F32, kind="Internal").ap()
    st5_re = din("st5_re", [64, 128])
    st5_im = din("st5_im", [64, 128])

    yT_out = dout("yT_ctx", [D, T])
    k_out = dout("attn_k", [T, 256])
    v_out = dout("attn_v", [T, 256])
    nak_out = dout("na_k", [T, D])
    nav_out = dout("na_v", [T, D])
    ysamp_out = dout("y_sample", [D, LSEQ])
    dn_out = dout("dn", [32, 128, 128])
    s5re_out = dout("s5_re", [128, 128])
    s5im_out = dout("s5_im", [128, 128])
    dbg_out = dout("dbg", [128, 4096]) if DEBUG else None

    with ExitStack() as es:
        b = Builder(nc, es)
        x = b.sb("x", [128, KC, T], F32)
        h = b.sb("h", [128, KC, T], BF16)
        y = b.sb("y", [128, KC, T], F32)
        act = b.sb("act", [128, FC, T], BF16)
        sq = act
        rstd = b.sb("rstd", [128, T], F32)
        tmpA = [b.sb(f"tmpA{i}", [128, T], F32) for i in range(2)]
        ident = b.sb("ident_sb", [128, 128], F32)
        ident_bf = b.sb("ident_bf", [128, 128], BF16)
        ones_bf = b.sb("ones_bf", [128, 128], BF16)
        wslots = [b.sb(f"wslot{i}", [128, WSLOT], BF16) for i in range(NWS)]
        ng = b.sb("ng", [128, 192], F32)
        bada = b.sb("bada", [128, 288], F32)
        scond = b.sb("scond", [128, KC, 1], BF16)
        scond_lat = b.sb("scond_lat", [128, KC, 1], BF16)
        cc = b.sb("cc", [128, KC], F32)
        mods = [b.sb(f"mods{i}", [128, 72], F32) for i in range(4)]
        Acoef = b.sb("Acoef", [128, 4 * 3 * KC], F32)
        Gcoef = b.sb("Gcoef", [128, 4 * 3 * KC], F32)
        rows_tmp = b.sb("rows_tmp", [96, 128], F32)
        oT = b.sb("oT", [128, KC, T], BF16)
        sink_bc = b.sb("sink_bc", [128, 16], F32)
        st_m = [b.sb(f"st_m{i}", [128, 8], F32) for i in range(2)]
        att_es = ExitStack()
        b.es = att_es
        qT = b.sb("qT", [128, 8, T], BF16)
        kT = b.sb("kT", [128, 8, T], BF16)
        vtok = b.sb("vtok", [128, 4, 1024], BF16)
        kv32 = [b.sb(f"kv32_{i}", [128, 512], F32) for i in range(2)]
        Pm = [b.sb(f"Pm{i}", [128, 256], BF16) for i in range(2)]
        PT = [b.sb(f"PT{i}", [128, 2, 128], BF16) for i in range(2)]
        otok = b.sb("otok", [128, 1024], BF16)
        b.es = es
        psG = [b.ps(f"psG{i}", [128, 512]) for i in range(2)]
        psU = [b.ps(f"psU{i}", [128, 512]) for i in range(2)]
        psY = [b.ps(f"psY{i}", [128, 512]) for i in range(2)]
        psS = b.ps("psS", [128, 512])
        psM = psS
        psT_all = b.ps("psT", [128, 1024], BF16)
        psTb = [psT_all[:, i * 256:(i + 1) * 256].rearrange("p (a n) -> p a n", a=2) for i in range(2)]
        psTc = [psTb[0], psS[:, :].bitcast(BF16)[:, 0:256].rearrange("p (a n) -> p a n", a=2)]

        def run_interleaved(gens):
            alive = [True] * len(gens)
            while any(alive):
                for gi_ in range(len(gens)):
                    if alive[gi_]:
                        try:
                            next(gens[gi_])
                        except StopIteration:
                            alive[gi_] = False

        wctr = [0]

        def wslot():
            s = wslots[wctr[0] % NWS]
            wctr[0] += 1
            return s

        b.dma("sp", ident[:], ident_in)
        b.op("dve", "tensor_copy", out=ident_bf[:], in_=ident[:])
        b.raw("dve", lambda: nc.vector.memset(ones_bf[:], 1.0), writes=[ones_bf[:]])

        def load_rows_T(dst_ap, src_rows_ap, nrows):
            b.dma("sp", rows_tmp[0:nrows, :], src_rows_ap)
            b.op("pe", "transpose", out=psM[:, 0:nrows], in_=rows_tmp[0:nrows, :], identity=ident[0:nrows, 0:nrows])
            b.op("dve", "tensor_copy", out=dst_ap, in_=psM[:, 0:nrows])

        load_rows_T(ng[:, 0:96], norm_g[0:96, :], 96)
        load_rows_T(ng[:, 96:192], norm_g[96:192, :], 96)
        for i in range(3):
            load_rows_T(bada[:, 96 * i:96 * (i + 1)], b_ada[96 * i:96 * (i + 1), :], 96)
        load_rows_T(cc[:, :], c_ctx, 8)
        b.op("act", "activation", out=scond[:, :, 0], in_=cc[:, :], func=AF.Silu)
        b.dma("sp", sink_bc[:], a_sink.broadcast_to([128, 16]))

        b.dma("sp", x[:], xT_in.rearrange("(c p) t -> p c t", p=128))

        def ada_layer(i, sc=None):
            sc = scond if sc is None else sc
            for pc in range(18):
                sl = wslot()
                v = sl[:, 0:KC * 512].rearrange("p (k n) -> p k n", k=KC)
                b.dma("pool", v, w_ada[i, :, pc * 512:(pc + 1) * 512].rearrange("(k p) n -> p k n", p=128))
                for cj in range(4):
                    j = pc * 4 + cj
                    for k in range(KC):
                        b.op("pe", "matmul", accum=(k > 0), out=psM[:, j:j + 1],
                             lhsT=v[:, k, cj * 128:(cj + 1) * 128], rhs=sc[:, k, :],
                             start=(k == 0), stop=(k == KC - 1))
            b.op("dve", "tensor_tensor", out=mods[i][:, :], in0=psM[:, 0:72], in1=bada[:, 72 * i:72 * (i + 1)],
                 op=ALU.add)
            for s in range(3):
                w = 1.0 if s == 1 else 0.5
                col = (i * 3 + s) * KC
                gpre = ng[:, (i * 6 + 2 * s) * KC:(i * 6 + 2 * s + 1) * KC]
                gpost = ng[:, (i * 6 + 2 * s + 1) * KC:(i * 6 + 2 * s + 2) * KC]
                scale_ = mods[i][:, (3 * s + 1) * KC:(3 * s + 2) * KC]
                gate_ = mods[i][:, (3 * s + 2) * KC:(3 * s + 3) * KC]
                b.op("dve", "scalar_tensor_tensor", out=Acoef[:, col:col + KC], in0=scale_, scalar=1.0, in1=gpre,
                     op0=ALU.add, op1=ALU.mult)
                b.op("dve", "scalar_tensor_tensor", out=Gcoef[:, col:col + KC], in0=gate_, scalar=w, in1=gpost,
                     op0=ALU.mult, op1=ALU.mult)

        def rms_stats(src):
            for c in range(KC):
                b.op("act", "activation", out=sq[:, c, :], in_=src[:, c, :], func=AF.Square)
            for c in range(KC):
                b.op("pe", "matmul", accum=(c > 0), out=psS[:, 0:T], lhsT=ones_bf[:, :], rhs=sq[:, c, :],
                     start=(c == 0), stop=(c == KC - 1))
            b.op("act", "activation", out=rstd[:, :], in_=psS[:, 0:T], func=AF.Sqrt, scale=1.0 / D, bias=EPS)
            b.op("dve", "reciprocal", out=rstd[:, :], in_=rstd[:, :])

        def pre_norm(i, s):
            rms_stats(x)
            col = (i * 3 + s) * KC
            for c in range(KC):
                t = tmpA[c % 2]
                b.op("dve", "tensor_tensor", out=t[:, :], in0=x[:, c, :], in1=rstd[:, :], op=ALU.mult)
                b.op("act", "activation", out=h[:, c, :], in_=t[:, :], func=AF.Identity,
                     scale=Acoef[:, col + c:col + c + 1], bias=mods[i][:, 3 * s * KC + c:3 * s * KC + c + 1])

        def post_norm_residual(i, s):
            rms_stats(y)
            col = (i * 3 + s) * KC
            for c in range(KC):
                t = tmpA[c % 2]
                b.op("dve", "tensor_tensor", out=t[:, :], in0=y[:, c, :], in1=rstd[:, :], op=ALU.mult)
                b.op("dve", "tensor_scalar", out=t[:, :], in0=t[:, :], scalar1=Gcoef[:, col + c:col + c + 1],
                     scalar2=None, op0=ALU.mult)
                b.op("dve", "tensor_tensor", out=x[:, c, :], in0=t[:, :], in1=x[:, c, :], op=ALU.add)

        def ffn(wgu, wd, key=None, consume=False):
            for jp in range(FC // 2):
                sl = wslot()
                v = sl[:, 0:KC * 512].rearrange("p (k n) -> p k n", k=KC)
                if consume:
                    b.dma("pool", sl[:, 0:KC * 512], wsc[key, jp, :, 0:KC * 512])
                else:
                    b.dma("pool", v[:, :, 0:256], wgu[:, jp * 256:(jp + 1) * 256].rearrange("(k p) n -> p k n", p=128))
                    b.dma("pool", v[:, :, 256:512],
                          wgu[:, DFF + jp * 256:DFF + (jp + 1) * 256].rearrange("(k p) n -> p k n", p=128))
                    if key is not None:
                        b.dma("sp", wsc[key, jp, :, 0:KC * 512], sl[:, 0:KC * 512])
                for jj in range(2):
                    j = jp * 2 + jj
                    pg, pu = psG[j % 2], psU[j % 2]
                    for k in range(KC):
                        b.op("pe", "matmul", accum=(k > 0), out=pg[:, 0:T], lhsT=v[:, k, jj * 128:(jj + 1) * 128],
                             rhs=h[:, k, :], start=(k == 0), stop=(k == KC - 1))
                    for k in range(KC):
                        b.op("pe", "matmul", accum=(k > 0), out=pu[:, 0:T],
                             lhsT=v[:, k, 256 + jj * 128:256 + (jj + 1) * 128],
                             rhs=h[:, k, :], start=(k == 0), stop=(k == KC - 1))
                    t = tmpA[j % 2]
                    b.op("act", "activation", out=t[:, :], in_=pg[:, 0:T], func=AF.Silu)
                    b.op("dve", "tensor_tensor", out=act[:, j, :], in0=t[:, :], in1=pu[:, 0:T], op=ALU.mult)
            for op_ in range(4):
                sl = wslot()
                v = sl[:, 0:FC * 256].rearrange("p (k n) -> p k n", k=FC)
                if consume:
                    b.dma("pool", sl[:, 0:FC * 256], wsc[key, 11 + op_, :, 0:FC * 256])
                else:
                    b.dma("pool", v, wd[:, op_ * 256:(op_ + 1) * 256].rearrange("(k p) n -> p k n", p=128))
                    if key is not None:
                        b.dma("sp", wsc[key, 11 + op_, :, 0:FC * 256], sl[:, 0:FC * 256])
                for oo in range(2):
                    o = op_ * 2 + oo
                    py = psY[o % 2]
                    for j in range(FC):
                        b.op("pe", "matmul", accum=(j > 0), out=py[:, 0:T], lhsT=v[:, j, oo * 128:(oo + 1) * 128],
                             rhs=act[:, j, :], start=(j == 0), stop=(j == FC - 1))
                    b.op("act", "activation", out=y[:, o, :], in_=py[:, 0:T], func=AF.Copy)

        def proj_featmajor(dst, dst_chunk0, wsrc, col0, nchunks, src_act, dup64=False):
            for g0 in range(0, nchunks, 4):
                ng_ = min(4, nchunks - g0)
                sl = wslot()
                v = sl[:, 0:KC * 512].rearrange("p (k n) -> p k n", k=KC)
                if dup64:
                    sl0 = wslot()
                    v0 = sl0[:, 0:KC * 256].rearrange("p (k n) -> p k n", k=KC)
                    b.dma("pool", v0[:, :, 0:ng_ * 64],
                          wsrc[:, col0 + g0 * 64:col0 + (g0 + ng_) * 64].rearrange("(k p) n -> p k n", p=128))
                    v4 = sl[:, 0:KC * 512].rearrange("p (k m a d) -> p k m a d", k=KC, m=4, a=2)
                    for a in range(2):
                        b.op("dve", "tensor_copy", out=v4[:, :, 0:ng_, a, :],
                             in_=v0[:, :, 0:ng_ * 64].rearrange("p k (m d) -> p k m d", d=64))
                else:
                    b.dma("pool", v[:, :, 0:ng_ * 128],
                          wsrc[:, col0 + g0 * 128:col0 + (g0 + ng_) * 128].rearrange("(k p) n -> p k n", p=128))
                for m in range(ng_):
                    pq = psG[m % 2]
                    for k in range(KC):
                        b.op("pe", "matmul", accum=(k > 0), out=pq[:, 0:T], lhsT=v[:, k, m * 128:(m + 1) * 128],
                             rhs=src_act[:, k, :], start=(k == 0), stop=(k == KC - 1))
                    b.op("act", "activation", out=dst[:, dst_chunk0 + g0 + m, :], in_=pq[:, 0:T], func=AF.Copy)

        def proj_out_featmajor(wsrc, src_act):
            for g0 in range(0, KC, 4):
                sl = wslot()
                v = sl[:, 0:KC * 512].rearrange("p (k n) -> p k n", k=KC)
                b.dma("pool", v, wsrc[:, g0 * 128:(g0 + 4) * 128].rearrange("(k p) n -> p k n", p=128))
                for m in range(4):
                    py = psY[m % 2]
                    for k in range(KC):
                        b.op("pe", "matmul", accum=(k > 0), out=py[:, 0:T], lhsT=v[:, k, m * 128:(m + 1) * 128],
                             rhs=src_act[:, k, :], start=(k == 0), stop=(k == KC - 1))
                    b.op("act", "activation", out=y[:, g0 + m, :], in_=py[:, 0:T], func=AF.Copy)

        def proj_tokmajor(wsrc, col0, ncols, out_dram, vdst_col0=None):
            for c0 in range(0, ncols, 512):
                n = min(512, ncols - c0)
                sl = wslot()
                v = sl[:, 0:KC * 512].rearrange("p (k n) -> p k n", k=KC)
                b.dma("pool", v[:, :, 0:n], wsrc[:, col0 + c0:col0 + c0 + n].rearrange("(k p) n -> p k n", p=128))
                for tb in range(T // 128):
                    pq = psU[tb % 2]
                    for k in range(KC):
                        b.op("pe", "matmul", accum=(k > 0), out=pq[:, 0:n], lhsT=h[:, k, tb * 128:(tb + 1) * 128],
                             rhs=v[:, k, 0:n], start=(k == 0), stop=(k == KC - 1))
                    t32 = kv32[tb % 2]
                    b.op("dve", "tensor_copy", out=t32[:, 0:n], in_=pq[:, 0:n])
                    if out_dram is not None:
                        b.dma("sp", out_dram[tb * 128:(tb + 1) * 128, c0:c0 + n], t32[:, 0:n])
                    if vdst_col0 is not None:
                        b.op("act", "activation", out=vtok[:, tb, vdst_col0 + c0:vdst_col0 + c0 + n], in_=t32[:, 0:n],
                             func=AF.Copy)

        def attn_ctx(nheads, head_map, sink_cols, scale):
            nseq = T // SEQ
            nqb = SEQ // 128
            it = 0
            for s in range(nseq):
                for qb in range(nqb):
                    q0 = s * SEQ + qb * 128
                    for hh in range(nheads):
                        qc, base, kc, vc0 = head_map[hh]
                        pS = psG[it % 2]
                        b.op("pe", "matmul", out=pS[:, 0:SEQ], lhsT=qT[base:base + 64, qc, q0:q0 + 128],
                             rhs=kT[base:base + 64, kc, s * SEQ:(s + 1) * SEQ], start=True, stop=True)
                        sm = st_m[it % 2]
                        b.op("dve", "reduce_max", out=sm[:, 0:1], in_=pS[:, 0:SEQ], axis=AX.X)
                        if sink_cols is not None:
                            b.op("dve", "tensor_scalar", out=sm[:, 1:2], in0=sm[:, 0:1], scalar1=scale,
                                 scalar2=sink_bc[:, hh:hh + 1], op0=ALU.mult, op1=ALU.max)
                            b.op("dve", "tensor_scalar", out=sm[:, 1:2], in0=sm[:, 1:2], scalar1=-1.0, scalar2=None,
                                 op0=ALU.mult)
                        else:
                            b.op("dve", "tensor_scalar", out=sm[:, 1:2], in0=sm[:, 0:1], scalar1=-scale, scalar2=None,
                                 op0=ALU.mult)
                        P = Pm[it % 2]
                        b.op("act", "activation", out=P[:, 0:SEQ], in_=pS[:, 0:SEQ], func=AF.Exp, scale=scale,
                             bias=sm[:, 1:2], accum_out=sm[:, 2:3])
                        if sink_cols is not None:
                            b.op("act", "activation", out=sm[:, 3:4], in_=sink_bc[:, hh:hh + 1], func=AF.Exp,
                                 bias=sm[:, 1:2], scale=1.0)
                            b.op("dve", "tensor_tensor", out=sm[:, 4:5], in0=sm[:, 2:3], in1=sm[:, 3:4], op=ALU.add)
                            b.op("dve", "reciprocal", out=sm[:, 5:6], in_=sm[:, 4:5])
                        else:
                            b.op("dve", "reciprocal", out=sm[:, 5:6], in_=sm[:, 2:3])
                        pt_sb = PT[it % 2]
                        if SUB < 3:
                            it += 1
                            continue
                        for kb in range(nqb):
                            b.op("pe", "transpose", out=psTb[it % 2][:, kb, :], in_=P[:, kb * 128:(kb + 1) * 128],
                                 identity=ident_bf[:, :])
                        b.op("dve", "tensor_copy", out=pt_sb[:, :, :], in_=psTb[it % 2][:, :, :])
                        if SUB < 4:
                            it += 1
                            continue
                        pO = psY[it % 2]
                        for kb in range(nqb):
                            b.op("pe", "matmul", accum=(kb > 0), out=pO[:, 0:64], lhsT=pt_sb[:, kb, :],
                                 rhs=vtok[:, s * nqb + kb, vc0:vc0 + 64], start=(kb == 0), stop=(kb == nqb - 1))
                        b.op("dve", "tensor_scalar", out=otok[:, hh * 64:(hh + 1) * 64], in0=pO[:, 0:64],
                             scalar1=sm[:, 5:6], scalar2=None, op0=ALU.mult)
                        it += 1
                    for c in range(KC):
                        b.op("pe", "transpose", out=psTb[c % 2][:, 0, :], in_=otok[:, c * 128:(c + 1) * 128],
                             identity=ident_bf[:, :])
                        b.op("act", "activation", out=oT[:, c, q0:q0 + 128], in_=psTb[c % 2][:, 0, :], func=AF.Copy)


        def s5_setup(npow=8, pfx="s5p_"):
            P = {}
            def t64(name):
                P[name] = b.sb(pfx + name, [128, 64], F32)
                return P[name]
            for nm, src in (("lamre", s5_lam_re), ("lamim", s5_lam_im), ("logdt", s5_logdt)):
                t64(nm)
                load_rows_T(P[nm][:, :], src, 64)
            for nm in ("dt", "lr", "li", "mag", "sn", "cs", "ar", "ai", "fr", "fi", "t1", "t2", "t3", "kk"):
                t64(nm)
            dsk = b.sb(pfx + "dskip", [128, KC], F32)
            load_rows_T(dsk[:, :], s5_d, 8)
            P["dskip"] = dsk
            b.op("act", "activation", out=P["dt"][:, :], in_=P["logdt"][:, :], func=AF.Exp)
            b.op("dve", "tensor_tensor", out=P["lr"][:, :], in0=P["lamre"][:, :], in1=P["dt"][:, :], op=ALU.mult)
            b.op("dve", "tensor_tensor", out=P["li"][:, :], in0=P["lamim"][:, :], in1=P["dt"][:, :], op=ALU.mult)
            b.op("act", "activation", out=P["mag"][:, :], in_=P["lr"][:, :], func=AF.Exp)

            def sin_of(dst, ang, shift):
                b.op("dve", "tensor_scalar", out=P["t1"][:, :], in0=ang, scalar1=float(shift), scalar2=None, op0=ALU.add)
                b.raw("dve", lambda: nc.vector.memset(P["kk"][:, :], 0.0), writes=[P["kk"][:, :]])
                for j in range(1, 5):
                    b.op("dve", "tensor_scalar", out=P["t2"][:, :], in0=P["t1"][:, :],
                         scalar1=float((2 * j - 1) * np.pi), scalar2=None, op0=ALU.is_gt)
                    b.op("dve", "tensor_tensor", out=P["kk"][:, :], in0=P["kk"][:, :], in1=P["t2"][:, :], op=ALU.add)
                b.op("dve", "tensor_scalar", out=P["kk"][:, :], in0=P["kk"][:, :], scalar1=float(-2 * np.pi),
                     scalar2=None, op0=ALU.mult)
                b.op("dve", "tensor_tensor", out=P["t1"][:, :], in0=P["t1"][:, :], in1=P["kk"][:, :], op=ALU.add)
                b.op("act", "activation", out=dst, in_=P["t1"][:, :], func=AF.Sin)

            sin_of(P["sn"][:, :], P["li"][:, :], 0.0)
            sin_of(P["cs"][:, :], P["li"][:, :], np.pi / 2)
            b.op("dve", "tensor_tensor", out=P["ar"][:, :], in0=P["mag"][:, :], in1=P["cs"][:, :], op=ALU.mult)
            b.op("dve", "tensor_tensor", out=P["ai"][:, :], in0=P["mag"][:, :], in1=P["sn"][:, :], op=ALU.mult)
            TT = lambda o, a_, b_, op: b.op("dve", "tensor_tensor", out=o, in0=a_, in1=b_, op=op)
            t1, t2, t3 = P["t1"][:, :], P["t2"][:, :], P["t3"][:, :]
            TT(t1, P["lamre"][:, :], P["lamre"][:, :], ALU.mult)
            TT(t2, P["lamim"][:, :], P["lamim"][:, :], ALU.mult)
            TT(t1, t1, t2, ALU.add)
            b.op("dve", "reciprocal", out=t3, in_=t1)
            b.op("dve", "tensor_scalar", out=P["kk"][:, :], in0=P["ar"][:, :], scalar1=-1.0, scalar2=None, op0=ALU.add)
            TT(t1, P["kk"][:, :], P["lamre"][:, :], ALU.mult)
            TT(t2, P["ai"][:, :], P["lamim"][:, :], ALU.mult)
            TT(t1, t1, t2, ALU.add)
            TT(P["fr"][:, :], t1, t3, ALU.mult)
            TT(t1, P["ai"][:, :], P["lamre"][:, :], ALU.mult)
            TT(t2, P["kk"][:, :], P["lamim"][:, :], ALU.mult)
            TT(t1, t1, t2, ALU.subtract)
            TT(P["fi"][:, :], t1, t3, ALU.mult)
            pw = [(P["ar"], P["ai"])]
            for k in range(1, npow):
                pr, pi_ = pw[-1]
                nr = b.sb(f"{pfx}pr{k}", [128, 64], F32)
                ni = b.sb(f"{pfx}pi{k}", [128, 64], F32)
                TT(t1, pr[:, :], pr[:, :], ALU.mult)
                TT(t2, pi_[:, :], pi_[:, :], ALU.mult)
                TT(nr[:, :], t1, t2, ALU.subtract)
                TT(t1, pr[:, :], pi_[:, :], ALU.mult)
                b.op("dve", "tensor_scalar", out=ni[:, :], in0=t1, scalar1=2.0, scalar2=None, op0=ALU.mult)
                pw.append((nr, ni))
            P["pw"] = pw
            return P

        def s5_mixer(P):
            nat = {}
            for t4 in range(4):
                for nm in ("bre", "bim", "cre", "cim"):
                    tl = b.sb(f"s5n_{nm}{t4}", [128, 128], F32)
                    b.raw("dve", lambda tl=tl: nc.vector.memset(tl[:, :], 0.0), writes=[tl[:, :]])
                    nat[(nm, t4)] = tl
            wB = [[b.sb(f"s5_wB{ri}{i}", [128, 128], BF16) for i in range(2)] for ri in range(2)]
            wC = [[b.sb(f"s5_wC{ri}{i}", [128, 128], BF16) for i in range(2)] for ri in range(2)]
            c32 = [b.sb(f"s5_c32{i}", [128, 128], F32) for i in range(2)]
            cm = [b.sb(f"s5_cm{i}", [128, 128], F32) for i in range(4)]
            xr = [b.sb(f"s5_xr{i}", [128, 2, SEQ], F32) for i in range(1)] * 2
            xi = [b.sb(f"s5_xi{i}", [128, 2, SEQ], F32) for i in range(1)] * 2
            xrb = [b.sb(f"s5_xrb{i}", [128, T], BF16) for i in range(1)] * 2
            xib = [b.sb(f"s5_xib{i}", [128, T], BF16) for i in range(1)] * 2
            tm = [b.sb(f"s5_tm{i}", [128, 2, SEQ], F32) for i in range(4)]
            fin_r = b.sb("s5_finr", [128, 64, 2], F32)
            fin_i = b.sb("s5_fini", [128, 64, 2], F32)
            fin_o = b.sb("s5_fino", [128, 128], F32)
            it = 0
            for d in range(2):
                for t in range(32):
                    tau = d * 32 + t
                    ch, t4 = t // 4, t % 4
                    pp = it % 2
                    for g2 in range(2):
                        g = 2 * t + g2
                        cb = (2 * t4 + g2) * 16
                        b.dma("sp", nat[("bre", t4)][g2 * 64:(g2 + 1) * 64, cb:cb + 16], s5_b_re[d, g, :, :])
                        b.dma("sp", nat[("bim", t4)][g2 * 64:(g2 + 1) * 64, cb:cb + 16], s5_b_im[d, g, :, :])
                        b.dma("sp", nat[("cre", t4)][cb:cb + 16, g2 * 64:(g2 + 1) * 64], s5_c_re[d, g, :, :])
                        b.dma("sp", nat[("cim", t4)][cb:cb + 16, g2 * 64:(g2 + 1) * 64], s5_c_im[d, g, :, :])
                    for ri, nm in enumerate(("bre", "bim")):
                        b.op("pe", "transpose", out=psS[:, 0:128], in_=nat[(nm, t4)][:, :], identity=ident[:, :])
                        b.op("act", "activation", out=wB[ri][pp][:, :], in_=psS[:, 0:128], func=AF.Copy)
                    for ri, nm in enumerate(("cre", "cim")):
                        b.op("pe", "transpose", out=psS[:, 0:128], in_=nat[(nm, t4)][:, :], identity=ident[:, :])
                        b.op("act", "activation", out=c32[ri][:, :], in_=psS[:, 0:128], func=AF.Copy)
                    fr_, fi_ = P["fr"][:, tau:tau + 1], P["fi"][:, tau:tau + 1]
                    TS = lambda o, i_, sc: b.op("dve", "tensor_scalar", out=o, in0=i_, scalar1=sc, scalar2=None, op0=ALU.mult)
                    TS(cm[0][:, :], c32[0][:, :], fr_)
                    TS(cm[1][:, :], c32[1][:, :], fi_)
                    b.op("dve", "tensor_tensor", out=wC[0][pp][:, :], in0=cm[0][:, :], in1=cm[1][:, :], op=ALU.subtract)
                    TS(cm[2][:, :], c32[0][:, :], fi_)
                    TS(cm[3][:, :], c32[1][:, :], fr_)
                    b.op("dve", "tensor_tensor", out=cm[2][:, :], in0=cm[2][:, :], in1=cm[3][:, :], op=ALU.add)
                    b.op("dve", "tensor_scalar", out=wC[1][pp][:, :], in0=cm[2][:, :], scalar1=-1.0, scalar2=None, op0=ALU.mult)
                    X = (xr[pp], xi[pp])
                    for ri in range(2):
                        pq = psG[ri]
                        b.op("pe", "matmul", out=pq[:, 0:T], lhsT=wB[ri][pp][:, :], rhs=h[:, ch, :], start=True, stop=True)
                        b.op("act", "activation", out=X[ri][:, :, :].rearrange("p a n -> p (a n)"), in_=pq[:, 0:T], func=AF.Copy)
                    for k in range(8):
                        sh = 1 << k
                        pr, pi_ = P["pw"][k]
                        sr_, si_ = pr[:, tau:tau + 1], pi_[:, tau:tau + 1]
                        if d == 0:
                            src = lambda A: A[:, :, 0:SEQ - sh]
                            dst = lambda A: A[:, :, sh:SEQ]
                        else:
                            src = lambda A: A[:, :, sh:SEQ]
                            dst = lambda A: A[:, :, 0:SEQ - sh]
                        tt = [tm[j] for j in range(4)]
                        n = SEQ - sh
                        MUL = lambda o, i_, sc: b.op("act", "activation", out=o, in_=i_, func=AF.Identity, scale=sc)
                        MUL(tt[0][:, :, 0:n], src(X[0]), sr_)
                        MUL(tt[1][:, :, 0:n], src(X[1]), si_)
                        MUL(tt[2][:, :, 0:n], src(X[1]), sr_)
                        MUL(tt[3][:, :, 0:n], src(X[0]), si_)
                        b.op("dve", "tensor_tensor", out=dst(X[0]), in0=dst(X[0]), in1=tt[0][:, :, 0:n], op=ALU.add)
                        b.op("dve", "tensor_tensor", out=dst(X[0]), in0=dst(X[0]), in1=tt[1][:, :, 0:n], op=ALU.subtract)
                        b.op("dve", "tensor_tensor", out=dst(X[1]), in0=dst(X[1]), in1=tt[2][:, :, 0:n], op=ALU.add)
                        b.op("dve", "tensor_tensor", out=dst(X[1]), in0=dst(X[1]), in1=tt[3][:, :, 0:n], op=ALU.add)
                    e_ = SEQ - 1 if d == 0 else 0
                    b.op("dve", "tensor_copy", out=fin_r[:, tau, :], in_=X[0][:, :, e_])
                    b.op("dve", "tensor_copy", out=fin_i[:, tau, :], in_=X[1][:, :, e_])
                    b.op("act", "activation", out=xrb[pp][:, :], in_=X[0][:, :, :].rearrange("p a n -> p (a n)"), func=AF.Copy)
                    b.op("act", "activation", out=xib[pp][:, :], in_=X[1][:, :, :].rearrange("p a n -> p (a n)"), func=AF.Copy)
                    py = psY[it % 2]
                    b.op("pe", "matmul", out=py[:, 0:T], lhsT=wC[0][pp][:, :], rhs=xrb[pp][:, :], start=True, stop=False)
                    b.op("pe", "matmul", accum=True, out=py[:, 0:T], lhsT=wC[1][pp][:, :], rhs=xib[pp][:, :], start=False, stop=True)
                    if d == 0 and t4 == 0:
                        b.op("dve", "tensor_copy", out=y[:, ch, :], in_=py[:, 0:T])
                    else:
                        b.op("dve", "tensor_tensor", out=y[:, ch, :], in0=y[:, ch, :], in1=py[:, 0:T], op=ALU.add)
                    it += 1
            FR = P["fr"][:, :].unsqueeze(2).to_broadcast([128, 64, 2]) if False else None
            for sq_ in range(2):
                a_, b_ = tm[0][:, 0, 0:64], tm[1][:, 0, 0:64]
                TT = lambda o, x_, y_, op: b.op("dve", "tensor_tensor", out=o, in0=x_, in1=y_, op=op)
                TT(a_, fin_r[:, :, sq_], P["fr"][:, :], ALU.mult)
                TT(b_, fin_i[:, :, sq_], P["fi"][:, :], ALU.mult)
                TT(tm[2][:, 0, sq_ * 64:(sq_ + 1) * 64], a_, b_, ALU.subtract)
                TT(a_, fin_i[:, :, sq_], P["fr"][:, :], ALU.mult)
                TT(b_, fin_r[:, :, sq_], P["fi"][:, :], ALU.mult)
                TT(tm[3][:, 0, sq_ * 64:(sq_ + 1) * 64], a_, b_, ALU.add)
            for src_t, dst_d in ((tm[2], s5re_out), (tm[3], s5im_out)):
                b.op("pe", "transpose", out=psS[:, 0:128], in_=src_t[:, 0, 0:128], identity=ident[:, :])
                b.op("dve", "tensor_copy", out=fin_o[:, :], in_=psS[:, 0:128])
                b.dma("sp", dst_d, fin_o[:, :])
            g_bf = act
            for c in range(KC):
                t_a, t_b = tmpA[0], tmpA[1]
                b.op("dve", "tensor_scalar", out=t_a[:, :], in0=h[:, c, :], scalar1=P["dskip"][:, c:c + 1], scalar2=None, op0=ALU.mult)
                b.op("dve", "tensor_tensor", out=y[:, c, :], in0=y[:, c, :], in1=t_a[:, :], op=ALU.add)
                b.op("act", "activation", out=t_a[:, :], in_=y[:, c, :], func=AF.Square)
                b.op("dve", "tensor_scalar", out=t_a[:, :], in0=t_a[:, :], scalar1=0.044715, scalar2=1.0, op0=ALU.mult, op1=ALU.add)
                b.op("dve", "tensor_tensor", out=t_a[:, :], in0=t_a[:, :], in1=y[:, c, :], op=ALU.mult)
                b.op("act", "activation", out=t_b[:, :], in_=t_a[:, :], func=AF.Tanh, scale=0.7978845608028654)
                b.op("dve", "tensor_scalar", out=t_b[:, :], in0=t_b[:, :], scalar1=1.0, scalar2=0.5, op0=ALU.add, op1=ALU.mult)
                b.op("dve", "tensor_tensor", out=g_bf[:, c, :], in0=t_b[:, :], in1=y[:, c, :], op=ALU.mult)
            for o4 in range(0, KC, 2):
                sl = wslot()
                v = sl[:, 0:KC * 512].rearrange("p (k n) -> p k n", k=KC)
                b.dma("pool", v[:, :, 0:256], s5_w_glu[:, o4 * 128:(o4 + 2) * 128].rearrange("(k p) n -> p k n", p=128))
                b.dma("pool", v[:, :, 256:512], s5_w_glu[:, D + o4 * 128:D + (o4 + 2) * 128].rearrange("(k p) n -> p k n", p=128))
                for oo in range(2):
                    o = o4 + oo
                    pa, pg = psG[o % 2], psU[o % 2]
                    for k in range(KC):
                        b.op("pe", "matmul", accum=(k > 0), out=pa[:, 0:T], lhsT=v[:, k, oo * 128:(oo + 1) * 128],
                             rhs=g_bf[:, k, :], start=(k == 0), stop=(k == KC - 1))
                    for k in range(KC):
                        b.op("pe", "matmul", accum=(k > 0), out=pg[:, 0:T], lhsT=v[:, k, 256 + oo * 128:256 + (oo + 1) * 128],
                             rhs=g_bf[:, k, :], start=(k == 0), stop=(k == KC - 1))
                    tq = tmpA[o % 2]
                    b.op("act", "activation", out=tq[:, :], in_=pg[:, 0:T], func=AF.Sigmoid)
                    b.op("dve", "tensor_tensor", out=y[:, o, :], in0=tq[:, :], in1=pa[:, 0:T], op=ALU.mult)


        def dn_mixer():
            sbf = lambda n_, sh, dt_=F32: b.sb("dn_" + n_, sh, dt_)
            msk = sbf("msk", [64, 6, 64]); b.dma("sp", msk[:], dn_mask)
            cw = sbf("cw", [128, 80]); load_rows_T(cw[:, :], dn_conv_w, 80)
            ones_f = sbf("ones", [64, 128]); b.raw("dve", lambda: nc.vector.memset(ones_f[:, :], 1.0), writes=[ones_f[:, :]])
            wba = sbf("wba", [128, 2, KC, 16], BF16)
            for d in range(2):
                b.dma("pool", wba[:, d, :, :], dn_w_ba[d].rearrange("(k p) n -> p k n", p=128))
            alog = sbf("alog", [64, 16]); b.dma("sp", alog[:], dn_a_log.broadcast_to([64, 16]))
            dtb = sbf("dtb", [64, 16]); b.dma("sp", dtb[:], dn_dt_bias.broadcast_to([64, 16]))
            outg = sbf("outg", [64, 128]); b.dma("sp", outg[:], dn_out_g.broadcast_to([64, 128]))
            nexpa = sbf("nexpa", [64, 16])
            b.op("act", "activation", out=nexpa[:, :], in_=alog[:, :], func=AF.Exp)
            b.op("dve", "tensor_scalar", out=nexpa[:, :], in0=nexpa[:, :], scalar1=-1.0, scalar2=None, op0=ALU.mult)
            cin = sbf("cin", [128, 2, SEQ + 4]); b.raw("dve", lambda: nc.vector.memset(cin[:, :, :], 0.0), writes=[cin[:, :, :]])
            qn = sbf("qn", [128, 4, T], BF16); kn = sbf("kn", [128, 4, T], BF16)
            kt_tok = sbf("kt", [64, 8, 4, 128], BF16); vt_tok = sbf("vt", [64, 8, 8, 128], BF16)
            zs = sbf("zs", [128, 8, T], BF16); vtmp = sbf("vtmp", [128, T], BF16)
            acc3 = tmpA[0][:, :].rearrange("p (a n) -> p a n", a=2)
            tm3 = tmpA[1][:, :].rearrange("p (a n) -> p a n", a=2)
            for m0 in range(0, 24, 4):
                sl = wslot()
                v = sl[:, 0:KC * 512].rearrange("p (k n) -> p k n", k=KC)
                b.dma("pool", v, dn_w_in[:, m0 * 128:(m0 + 4) * 128].rearrange("(k p) n -> p k n", p=128))
                for mm in range(4):
                    m = m0 + mm
                    pq = psG[mm % 2]
                    for k in range(KC):
                        b.op("pe", "matmul", accum=(k > 0), out=pq[:, 0:T], lhsT=v[:, k, mm * 128:(mm + 1) * 128],
                             rhs=h[:, k, :], start=(k == 0), stop=(k == KC - 1))
                    if m >= 16:
                        b.op("act", "activation", out=zs[:, m - 16, :], in_=pq[:, 0:T], func=AF.Silu)
                        continue
                    b.op("act", "activation", out=cin[:, :, 2:2 + SEQ], in_=pq[:, 0:T].rearrange("p (a n) -> p a n", a=2),
                         func=AF.Copy)
                    for j in range(5):
                        dst = acc3 if j == 0 else tm3
                        b.op("act", "activation", out=dst, in_=cin[:, :, j:j + SEQ], func=AF.Identity,
                             scale=cw[:, j * 16 + m:j * 16 + m + 1])
                        if j > 0:
                            b.op("dve", "tensor_tensor", out=acc3, in0=acc3, in1=tm3, op=ALU.add)
                    if m < 8:
                        b.op("act", "activation", out=tmpA[1][:, :], in_=tmpA[0][:, :], func=AF.Silu)
                        b.op("act", "activation", out=act[:, 0, :], in_=tmpA[1][:, :], func=AF.Square)
                        b.op("pe", "matmul", out=psS[:, 0:T], lhsT=ones_bf[:, :], rhs=act[:, 0, :], start=True, stop=True)
                        b.op("act", "activation", out=rstd[:, :], in_=psS[:, 0:T], func=AF.Sqrt, scale=1.0, bias=EPS)
                        b.op("dve", "reciprocal", out=rstd[:, :], in_=rstd[:, :])
                        b.op("dve", "tensor_tensor", out=tmpA[1][:, :], in0=tmpA[1][:, :], in1=rstd[:, :], op=ALU.mult)
                        if m < 4:
                            b.op("act", "activation", out=qn[:, m, :], in_=tmpA[1][:, :], func=AF.Identity, scale=128 ** -0.5)
                        else:
                            b.op("act", "activation", out=kn[:, m - 4, :], in_=tmpA[1][:, :], func=AF.Identity, scale=1.0)
                            for blk in range(8):
                                b.op("pe", "transpose", out=psTb[blk % 2][0:64, 0, :], in_=kn[:, m - 4, blk * 64:(blk + 1) * 64],
                                     identity=ident_bf[:, :])
                                b.op("dve", "tensor_copy", out=kt_tok[:, blk, m - 4, :], in_=psTb[blk % 2][0:64, 0, :])
                    else:
                        b.op("act", "activation", out=vtmp[:, :], in_=tmpA[0][:, :], func=AF.Silu)
                        for blk in range(8):
                            b.op("pe", "transpose", out=psTb[blk % 2][0:64, 0, :], in_=vtmp[:, blk * 64:(blk + 1) * 64],
                                 identity=ident_bf[:, :])
                            b.op("dve", "tensor_copy", out=vt_tok[:, blk, m - 8, :], in_=psTb[blk % 2][0:64, 0, :])
            gts = sbf("gts", [64, 2, 8, 16]); beta = sbf("beta", [64, 2, 8, 8]); xa = sbf("xa", [64, 2, 8, 8])
            xe = sbf("xe", [64, 2, 8, 8]); gg = sbf("gg", [64, 2, 8, 8]); gc = sbf("gc", [64, 2, 8, 8]); eg = sbf("eg", [64, 2, 8, 8])
            for d in range(2):
                for blk in range(8):
                    pq = psU[blk % 2]
                    for k in range(KC):
                        b.op("pe", "matmul", accum=(k > 0), out=pq[0:64, 0:16], lhsT=h[:, k, blk * 64:(blk + 1) * 64],
                             rhs=wba[:, d, k, :], start=(k == 0), stop=(k == KC - 1))
                    b.op("dve", "tensor_copy", out=gts[:, d, blk, :], in_=pq[0:64, 0:16])
                    b.op("act", "activation", out=beta[:, d, blk, :], in_=gts[:, d, blk, 0:8], func=AF.Sigmoid)
                    b.op("dve", "tensor_tensor", out=xa[:, d, blk, :], in0=gts[:, d, blk, 8:16], in1=dtb[:, d * 8:(d + 1) * 8], op=ALU.add)
            fl = lambda A: A[:, :, :, :].rearrange("p a b c -> p (a b c)")
            b.op("act", "activation", out=fl(xe), in_=fl(xa), func=AF.Abs)
            b.op("act", "activation", out=fl(xe), in_=fl(xe), func=AF.Exp, scale=-1.0)
            b.op("act", "activation", out=fl(xe), in_=fl(xe), func=AF.Ln, bias=1.0, scale=1.0)
            b.op("dve", "tensor_scalar", out=fl(xa), in0=fl(xa), scalar1=0.0, scalar2=None, op0=ALU.max)
            b.op("dve", "tensor_tensor", out=fl(xa), in0=fl(xa), in1=fl(xe), op=ALU.add)
            for d in range(2):
                for blk in range(8):
                    b.op("dve", "tensor_tensor", out=gg[:, d, blk, :], in0=xa[:, d, blk, :], in1=nexpa[:, d * 8:(d + 1) * 8], op=ALU.mult)
                b.op("pe", "matmul", out=psS[0:64, 0:64], lhsT=msk[:, d, :], rhs=gg[:, d, :, :].rearrange("p b c -> p (b c)"),
                     start=True, stop=True)
                b.op("dve", "tensor_copy", out=gc[:, d, :, :].rearrange("p b c -> p (b c)"), in_=psS[0:64, 0:64])
            b.op("act", "activation", out=fl(eg), in_=fl(gc), func=AF.Exp)
            f64 = lambda n_: sbf(n_, [64, 64])
            dg, decS, decT, PTt = f64("dg"), f64("decS"), f64("decT"), f64("PTt")
            Ms = [(f64("Ma"), f64("MTa")), (f64("Mb"), f64("MTb"))]
            gcr = sbf("gcr", [128, 64]); TTb = sbf("TTb", [64, 64], BF16); aT = sbf("aT", [64, 64], BF16)
            vb = sbf("vb", [64, 128], BF16); kbg = sbf("kbg", [64, 128], BF16); kg = sbf("kg", [64, 128], BF16)
            wT = sbf("wT", [128, 64], BF16); u_sb = sbf("u", [64, 128]); vnew = sbf("vnew", [64, 128]); vnew_b = sbf("vnewb", [64, 128], BF16)
            o1 = sbf("o1", [64, 128]); sc1 = sbf("sc1", [64, 4]); egl = sbf("egl", [128, 1])
            S = sbf("S", [128, 128]); Sb = sbf("Sb", [128, 128], BF16)
            o_acc = sbf("oacc", [64, 4, 128]); ss = sbf("ss", [64, 4]); onb = sbf("onb", [64, 128], BF16); junk = sbf("junk", [64, 128])
            id64 = ident[0:64, 0:64]
            for s_ in range(2):
                for hv in range(8):
                    hq = hv // 2
                    for d in range(2):
                        b.raw("dve", lambda: nc.vector.memset(S[:, :], 0.0), writes=[S[:, :]])
                        b.raw("dve", lambda: nc.vector.memset(Sb[:, :], 0.0), writes=[Sb[:, :]])
                        last = 63 if d == 0 else 0
                        for n in (range(4) if d == 0 else range(3, -1, -1)):
                            blk = s_ * 4 + n
                            tok0 = blk * 64
                            gc_, be_, eg_ = gc[:, d, blk, hv:hv + 1], beta[:, d, blk, hv:hv + 1], eg[:, d, blk, hv:hv + 1]
                            kf, qf = kn[:, hq, tok0:tok0 + 64], qn[:, hq, tok0:tok0 + 64]
                            kt, vt = kt_tok[:, blk, hq, :], vt_tok[:, blk, hv, :]
                            b.op("dve", "tensor_scalar", out=dg[:, :], in0=id64, scalar1=gc_, scalar2=None, op0=ALU.mult)
                            b.op("pe", "matmul", out=psS[:, 0:64], lhsT=ones_f[:, :], rhs=dg[:, :], start=True, stop=True)
                            b.op("act", "activation", out=gcr[:, :], in_=psS[:, 0:64], func=AF.Copy)
                            b.op("dve", "tensor_scalar", out=decS[:, :], in0=gcr[0:64, :], scalar1=-1.0, scalar2=gc_, op0=ALU.mult, op1=ALU.add)
                            b.op("dve", "tensor_scalar", out=decS[:, :], in0=decS[:, :], scalar1=0.0, scalar2=None, op0=ALU.min)
                            b.op("act", "activation", out=decS[:, :], in_=decS[:, :], func=AF.Exp)
                            b.op("dve", "tensor_tensor", out=decS[:, :], in0=decS[:, :], in1=msk[:, 4 + d, :], op=ALU.mult)
                            b.op("dve", "tensor_scalar", out=decT[:, :], in0=gcr[0:64, :], scalar1=gc_, scalar2=None, op0=ALU.subtract)
                            b.op("dve", "tensor_scalar", out=decT[:, :], in0=decT[:, :], scalar1=0.0, scalar2=None, op0=ALU.min)
                            b.op("act", "activation", out=decT[:, :], in_=decT[:, :], func=AF.Exp)
                            b.op("dve", "tensor_tensor", out=decT[:, :], in0=decT[:, :], in1=msk[:, d, :], op=ALU.mult)
                            M, MT = Ms[0]
                            b.op("pe", "matmul", out=psG[0][0:64, 0:64], lhsT=kf, rhs=kf, start=True, stop=True)
                            b.op("dve", "tensor_scalar", out=M[:, :], in0=psG[0][0:64, 0:64], scalar1=be_, scalar2=-1.0, op0=ALU.mult, op1=ALU.mult)
                            b.op("dve", "tensor_tensor", out=M[:, :], in0=M[:, :], in1=decS[:, :], op=ALU.mult)
                            b.op("pe", "transpose", out=psG[1][0:64, 0:64], in_=M[:, :], identity=id64)
                            b.op("act", "activation", out=MT[:, :], in_=psG[1][0:64, 0:64], func=AF.Copy)
                            b.op("dve", "tensor_tensor", out=PTt[:, :], in0=MT[:, :], in1=id64, op=ALU.add)
                            cur = 0
                            for k in range(1, 6):
                                M, MT = Ms[cur]
                                Mn, MTn = Ms[1 - cur]
                                b.op("pe", "matmul", out=psU[0][0:64, 0:64], lhsT=MT[:, :], rhs=M[:, :], start=True, stop=True)
                                b.op("act", "activation", out=Mn[:, :], in_=psU[0][0:64, 0:64], func=AF.Copy)
                                if k < 5:
                                    b.op("pe", "matmul", out=psU[1][0:64, 0:64], lhsT=M[:, :], rhs=MT[:, :], start=True, stop=True)
                                    b.op("dve", "tensor_copy", out=MTn[:, :], in_=psU[1][0:64, 0:64])
                                b.op("pe", "matmul", out=psY[0][0:64, 0:64], lhsT=Mn[:, :], rhs=PTt[:, :], start=True, stop=True)
                                b.op("dve", "tensor_tensor", out=PTt[:, :], in0=PTt[:, :], in1=psY[0][0:64, 0:64], op=ALU.add)
                                cur = 1 - cur
                            b.op("act", "activation", out=TTb[:, :], in_=PTt[:, :], func=AF.Copy)
                            b.op("dve", "tensor_tensor", out=sc1[:, 0:1], in0=be_, in1=eg_, op=ALU.mult)
                            b.op("dve", "tensor_scalar", out=vb[:, :], in0=vt, scalar1=be_, scalar2=None, op0=ALU.mult)
                            b.op("dve", "tensor_scalar", out=kbg[:, :], in0=kt, scalar1=sc1[:, 0:1], scalar2=None, op0=ALU.mult)
                            b.op("pe", "matmul", out=psG[0][0:64, 0:128], lhsT=TTb[:, :], rhs=vb[:, :], start=True, stop=True)
                            b.op("act", "activation", out=u_sb[:, :], in_=psG[0][0:64, 0:128], func=AF.Copy)
                            b.op("pe", "matmul", out=psG[1][:, 0:64], lhsT=kbg[:, :], rhs=TTb[:, :], start=True, stop=True)
                            b.op("dve", "tensor_copy", out=wT[:, :], in_=psG[1][:, 0:64])
                            b.op("pe", "matmul", out=psU[0][0:64, 0:128], lhsT=wT[:, :], rhs=Sb[:, :], start=True, stop=True)
                            b.op("dve", "tensor_tensor", out=vnew[:, :], in0=u_sb[:, :], in1=psU[0][0:64, 0:128], op=ALU.subtract)
                            b.op("act", "activation", out=vnew_b[:, :], in_=vnew[:, :], func=AF.Copy)
                            b.op("pe", "matmul", out=psU[1][0:64, 0:128], lhsT=qf, rhs=Sb[:, :], start=True, stop=True)
                            b.op("dve", "tensor_scalar", out=o1[:, :], in0=psU[1][0:64, 0:128], scalar1=eg_, scalar2=None, op0=ALU.mult)
                            b.op("pe", "matmul", out=psY[0][0:64, 0:64], lhsT=kf, rhs=qf, start=True, stop=True)
                            b.op("dve", "tensor_tensor", out=aT[:, :], in0=psY[0][0:64, 0:64], in1=decT[:, :], op=ALU.mult)
                            b.op("pe", "matmul", out=psY[1][0:64, 0:128], lhsT=aT[:, :], rhs=vnew_b[:, :], start=True, stop=True)
                            b.op("dve", "tensor_tensor", out=o1[:, :], in0=o1[:, :], in1=psY[1][0:64, 0:128], op=ALU.add)
                            if d == 0:
                                b.op("dve", "tensor_copy", out=o_acc[:, n, :], in_=o1[:, :])
                            else:
                                b.op("dve", "tensor_tensor", out=o_acc[:, n, :], in0=o_acc[:, n, :], in1=o1[:, :], op=ALU.add)
                            b.op("act", "activation", out=sc1[:, 1:2], in_=gc_, func=AF.Exp, scale=-1.0, bias=gcr[0:64, last:last + 1])
                            b.op("dve", "tensor_scalar", out=kg[:, :], in0=kt, scalar1=sc1[:, 1:2], scalar2=None, op0=ALU.mult)
                            b.op("act", "activation", out=egl[:, :], in_=gcr[:, last:last + 1], func=AF.Exp)
                            b.op("pe", "matmul", out=psG[0][:, 0:128], lhsT=kg[:, :], rhs=vnew_b[:, :], start=True, stop=True)
                            b.op("dve", "tensor_scalar", out=S[:, :], in0=S[:, :], scalar1=egl[:, 0:1], scalar2=None, op0=ALU.mult)
                            b.op("dve", "tensor_tensor", out=S[:, :], in0=S[:, :], in1=psG[0][:, 0:128], op=ALU.add)
                            b.op("act", "activation", out=Sb[:, :], in_=S[:, :], func=AF.Copy)
                        b.dma("sp", dn_out[(s_ * 2 + d) * 8 + hv, :, :], S[:, :])
                    for n in range(4):
                        b.op("act", "activation", out=junk[:, :], in_=o_acc[:, n, :], func=AF.Square, accum_out=ss[:, n:n + 1])
                    b.op("act", "activation", out=ss[:, :], in_=ss[:, :], func=AF.Sqrt, scale=1.0 / 128, bias=EPS)
                    b.op("dve", "reciprocal", out=ss[:, :], in_=ss[:, :])
                    for n in range(4):
                        tok0 = (s_ * 4 + n) * 64
                        b.op("dve", "tensor_scalar", out=junk[:, :], in0=o_acc[:, n, :], scalar1=ss[:, n:n + 1], scalar2=None, op0=ALU.mult)
                        b.op("dve", "tensor_tensor", out=onb[:, :], in0=junk[:, :], in1=outg[:, :], op=ALU.mult)
                        b.op("pe", "transpose", out=psTb[n % 2][:, 0, 0:64], in_=onb[:, :], identity=ident_bf[0:64, 0:64])
                        b.op("dve", "tensor_tensor", out=oT[:, hv, tok0:tok0 + 64], in0=psTb[n % 2][:, 0, 0:64],
                             in1=zs[:, hv, tok0:tok0 + 64], op=ALU.mult)
            proj_out_featmajor(dn_w_o, oT)


        NTL = LSEQ // 512
        NBL = LSEQ // 128

        def lat_load(src, tile):
            b.dma("sp", x[:], src[:, tile * 512:(tile + 1) * 512].rearrange("(c p) t -> p c t", p=128))

        def lat_store(dst, tile):
            b.dma("sp", dst[:, tile * 512:(tile + 1) * 512].rearrange("(c p) t -> p c t", p=128), x[:])

        def lat_ffn_sub(i, s):
            pre_norm(i, s)
            ffn(ffn_w_gu[i, 0 if s == 0 else 1], ffn_w_d[i, 0 if s == 0 else 1], key=i * 2 + (0 if s == 0 else 1),
                consume=not CTX_SKIP)
            post_norm_residual(i, s)

        def a_lat(src0):
            sbf = lambda n_, sh, dt_=F32: b.sb("la_" + n_, sh, dt_)
            kTf = sbf("kT", [128, 4, LSEQ], BF16); vtf = sbf("vt", [128, NBL, 256], BF16)
            kTc = sbf("kTc", [128, 4, 512], BF16); vtc = sbf("vtc", [128, 4, 256], BF16)
            bmask = sbf("bm", [128, 384]); b.dma("sp", bmask[:], band_mask)
            cs = [sbf(f"cs{i}", [128, 64]) for i in range(2)]
            kdup = sbf("kdup", [128, 4, 2, 64], BF16)
            qr = sbf("qr", [128, 1024], BF16); qTb = sbf("qTb", [128, 8, 128], BF16)
            slcs = [sbf(f"sl{j}", [128, 384]) for j in range(2)]; Pls = [sbf(f"Pl{j}", [128, 384], BF16) for j in range(2)]
            Pcs = [sbf(f"Pc{j}", [128, 512], BF16) for j in range(2)]
            PTls = [sbf(f"PTl{j}", [128, 8, 128], BF16) for j in range(2)]; otk = sbf("otk", [128, 1024], BF16)
            smms = [sbf(f"smx{j}", [128, 12]) for j in range(2)]
            rt = [sbf(f"rt{i}", [128, 2, 16]) for i in range(4)]
            c32t = sbf("c32", [128, 256])

            def rope(src, nh, dst_fn, cst):
                cosv = cst[:, 0:32].rearrange("p (a f) -> p a f", a=2)
                sinv = cst[:, 32:64].rearrange("p (a f) -> p a f", a=2)
                for hh in range(nh):
                    s4 = src[:, hh * 64:(hh + 1) * 64].rearrange("p (a b f) -> p a b f", a=2, b=2)
                    x1, x2 = s4[:, :, 0, :], s4[:, :, 1, :]
                    d4 = dst_fn(hh).rearrange("p (a b f) -> p a b f", a=2, b=2)
                    TT = lambda o, a_, b_, op: b.op("dve", "tensor_tensor", out=o, in0=a_, in1=b_, op=op)
                    TT(rt[0][:, :, :], x1, cosv, ALU.mult)
                    TT(rt[1][:, :, :], x2, sinv, ALU.mult)
                    TT(d4[:, :, 0, :], rt[0][:, :, :], rt[1][:, :, :], ALU.subtract)
                    TT(rt[2][:, :, :], x2, cosv, ALU.mult)
                    TT(rt[3][:, :, :], x1, sinv, ALU.mult)
                    TT(d4[:, :, 1, :], rt[2][:, :, :], rt[3][:, :, :], ALU.add)

            def k_to_featmajor(dstT, col0):
                b.op("dve", "tensor_copy", out=kdup[:, :, 1, :], in_=kdup[:, :, 0, :])
                for kv in range(4):
                    b.op("pe", "transpose", out=psTb[kv % 2][:, 0, :], in_=kdup[:, kv, :, :].rearrange("p a d -> p (a d)"),
                         identity=ident_bf[:, :])
                    b.op("act", "activation", out=dstT[:, kv, col0:col0 + 128], in_=psTb[kv % 2][:, 0, :], func=AF.Copy)

            for cb in range(4):
                b.dma("sp", c32t[:, :], cache_ak[cb * 128:(cb + 1) * 128, :])
                b.op("dve", "tensor_copy", out=kdup[:, :, 0, :], in_=c32t[:, :].rearrange("p (k d) -> p k d", k=4))
                k_to_featmajor(kTc, cb * 128)
                b.dma("pool", vtc[:, cb, :], cache_av[cb * 128:(cb + 1) * 128, :])
            for tl in range(NTL):
                lat_load(src0, tl)
                lat_ffn_sub(0, 0)
                lat_store(zres, tl)
                pre_norm(0, 1)
                sl_ = wslot()
                v = sl_[:, 0:KC * 512].rearrange("p (k n) -> p k n", k=KC)
                b.dma("pool", v, a_w_qkv[:, 1024:1536].rearrange("(k p) n -> p k n", p=128))
                for tb in range(4):
                    blk = tl * 4 + tb
                    pq = psU[tb % 2]
                    for k in range(KC):
                        b.op("pe", "matmul", accum=(k > 0), out=pq[:, 0:512], lhsT=h[:, k, tb * 128:(tb + 1) * 128],
                             rhs=v[:, k, :], start=(k == 0), stop=(k == KC - 1))
                    b.dma("sp", cs[tb % 2][:, :], rope_cs[blk * 128:(blk + 1) * 128, :])
                    rope(pq[:, 0:256], 4, lambda hh: kdup[:, hh, 0, :], cs[tb % 2])
                    k_to_featmajor(kTf, blk * 128)
                    b.op("act", "activation", out=vtf[:, blk, :], in_=pq[:, 256:512], func=AF.Copy)
            it = 0
            for tl in range(NTL):
                lat_load(zres, tl)
                pre_norm(0, 1)
                wq = []
                for half in range(2):
                    sl_ = wslot()
                    v = sl_[:, 0:KC * 512].rearrange("p (k n) -> p k n", k=KC)
                    b.dma("pool", v, a_w_qkv[:, half * 512:(half + 1) * 512].rearrange("(k p) n -> p k n", p=128))
                    wq.append(v)
                for tb in range(4):
                    blk = tl * 4 + tb
                    b.dma("sp", cs[tb % 2][:, :], rope_cs[blk * 128:(blk + 1) * 128, :])
                    for half in range(2):
                        pq = psG[half]
                        for k in range(KC):
                            b.op("pe", "matmul", accum=(k > 0), out=pq[:, 0:512], lhsT=h[:, k, tb * 128:(tb + 1) * 128],
                                 rhs=wq[half][:, k, :], start=(k == 0), stop=(k == KC - 1))
                        rope(pq[:, 0:512], 8, lambda hh, half=half: qr[:, half * 512 + hh * 64:half * 512 + (hh + 1) * 64], cs[tb % 2])
                    for c in range(KC):
                        b.op("pe", "transpose", out=psTb[c % 2][:, 0, :], in_=qr[:, c * 128:(c + 1) * 128], identity=ident_bf[:, :])
                        b.op("act", "activation", out=qTb[:, c, :], in_=psTb[c % 2][:, 0, :], func=AF.Copy)
                    lo, hi = max(blk - 1, 0), min(blk + 1, NBL - 1)
                    nlb = hi - lo + 1
                    nl = nlb * 128
                    bm = bmask[:, 128:128 + nl] if blk == 0 else bmask[:, 0:nl]
                    def heads(j, lo=lo, hi=hi, nlb=nlb, nl=nl, bm=bm):
                        slc, Pl, Pc, PTl, smm = slcs[j], Pls[j], Pcs[j], PTls[j], smms[j]
                        pL, pC, pO, pT = psG[j], psU[j], psY[j], psTc[j]
                        base = 64 * j
                        for hh in range(j, 16, 2):
                            qc, kv = hh // 2, hh // 4
                            b.op("pe", "matmul", out=pL[:, 0:nl], lhsT=qTb[base:base + 64, qc, :],
                                 rhs=kTf[base:base + 64, kv, lo * 128:(hi + 1) * 128], start=True, stop=True)
                            b.op("pe", "matmul", out=pC[:, 0:512], lhsT=qTb[base:base + 64, qc, :],
                                 rhs=kTc[base:base + 64, kv, :], start=True, stop=True)
                            yield
                            b.op("dve", "tensor_tensor", out=slc[:, 0:nl], in0=pL[:, 0:nl], in1=bm, op=ALU.add)
                            b.op("dve", "reduce_max", out=smm[:, 0:1], in_=slc[:, 0:nl], axis=AX.X)
                            b.op("dve", "reduce_max", out=smm[:, 1:2], in_=pC[:, 0:512], axis=AX.X)
                            yield
                            b.op("dve", "tensor_tensor", out=smm[:, 0:1], in0=smm[:, 0:1], in1=smm[:, 1:2], op=ALU.max)
                            b.op("dve", "tensor_scalar", out=smm[:, 2:3], in0=smm[:, 0:1], scalar1=0.125,
                                 scalar2=sink_bc[:, hh:hh + 1], op0=ALU.mult, op1=ALU.max)
                            b.op("dve", "tensor_scalar", out=smm[:, 2:3], in0=smm[:, 2:3], scalar1=-1.0, scalar2=None, op0=ALU.mult)
                            yield
                            b.op("act", "activation", out=Pl[:, 0:nl], in_=slc[:, 0:nl], func=AF.Exp, scale=0.125,
                                 bias=smm[:, 2:3], accum_out=smm[:, 3:4])
                            b.op("act", "activation", out=Pc[:, :], in_=pC[:, 0:512], func=AF.Exp, scale=0.125,
                                 bias=smm[:, 2:3], accum_out=smm[:, 4:5])
                            b.op("act", "activation", out=smm[:, 5:6], in_=sink_bc[:, hh:hh + 1], func=AF.Exp, bias=smm[:, 2:3], scale=1.0)
                            yield
                            b.op("dve", "tensor_tensor", out=smm[:, 6:7], in0=smm[:, 3:4], in1=smm[:, 4:5], op=ALU.add)
                            b.op("dve", "tensor_tensor", out=smm[:, 6:7], in0=smm[:, 6:7], in1=smm[:, 5:6], op=ALU.add)
                            b.op("dve", "reciprocal", out=smm[:, 7:8], in_=smm[:, 6:7])
                            srcs = [Pl[:, jq * 128:(jq + 1) * 128] for jq in range(nlb)] + [Pc[:, jq * 128:(jq + 1) * 128] for jq in range(4)]
                            nsrc = len(srcs)
                            for j0 in range(0, nsrc, 2):
                                nn_ = min(2, nsrc - j0)
                                for q_ in range(nn_):
                                    b.op("pe", "transpose", out=pT[:, q_, :], in_=srcs[j0 + q_], identity=ident_bf[:, :])
                                if (j0 // 2) % 2 == 0:
                                    b.op("dve", "tensor_copy", out=PTl[:, j0:j0 + nn_, :], in_=pT[:, 0:nn_, :])
                                else:
                                    b.op("act", "activation", out=PTl[:, j0:j0 + nn_, :], in_=pT[:, 0:nn_, :], func=AF.Copy)
                                yield
                            for jq in range(nsrc):
                                rv = vtf[:, lo + jq, kv * 64:(kv + 1) * 64] if jq < nlb else vtc[:, jq - nlb, kv * 64:(kv + 1) * 64]
                                b.op("pe", "matmul", accum=(jq > 0), out=pO[:, 0:64], lhsT=PTl[:, jq, :], rhs=rv,
                                     start=(jq == 0), stop=(jq == nsrc - 1))
                            yield
                            b.op("dve", "tensor_scalar", out=otk[:, hh * 64:(hh + 1) * 64], in0=pO[:, 0:64], scalar1=smm[:, 7:8],
                                 scalar2=None, op0=ALU.mult)
                            yield
                    run_interleaved([heads(0), heads(1)])
                    for c in range(KC):
                        b.op("pe", "transpose", out=psTb[c % 2][:, 0, :], in_=otk[:, c * 128:(c + 1) * 128], identity=ident_bf[:, :])
                        b.op("act", "activation", out=oT[:, c, tb * 128:(tb + 1) * 128], in_=psTb[c % 2][:, 0, :], func=AF.Copy)
                proj_out_featmajor(a_w_o, oT)
                post_norm_residual(0, 1)
                lat_ffn_sub(0, 2)
                lat_store(zres, tl)


        def s5_lat():
            P = s5_setup(npow=9, pfx="l5p_")
            LT = 512
            sbf = lambda n_, sh, dt_=F32: b.sb("l5_" + n_, sh, dt_)
            TT = lambda o, a_, b_, op: b.op("dve", "tensor_tensor", out=o, in0=a_, in1=b_, op=op)
            h0r, h0i = sbf("h0r", [128, 64]), sbf("h0i", [128, 64])
            load_rows_T(h0r[:, :], st5_re, 64)
            load_rows_T(h0i[:, :], st5_im, 64)
            cr, ci = sbf("cr", [128, 64]), sbf("ci", [128, 64])
            t1, t2, t3 = P["t1"][:, :], P["t2"][:, :], P["t3"][:, :]
            TT(t1, P["fr"][:, :], P["fr"][:, :], ALU.mult)
            TT(t2, P["fi"][:, :], P["fi"][:, :], ALU.mult)
            TT(t1, t1, t2, ALU.add)
            b.op("dve", "reciprocal", out=t3, in_=t1)
            TT(t1, h0r[:, :], P["fr"][:, :], ALU.mult)
            TT(t2, h0i[:, :], P["fi"][:, :], ALU.mult)
            TT(t1, t1, t2, ALU.add)
            TT(cr[:, :], t1, t3, ALU.mult)
            TT(t1, h0i[:, :], P["fr"][:, :], ALU.mult)
            TT(t2, h0r[:, :], P["fi"][:, :], ALU.mult)
            TT(t1, t1, t2, ALU.subtract)
            TT(ci[:, :], t1, t3, ALU.mult)
            nat = {}
            for t4 in range(4):
                for nm in ("bre", "bim", "cre", "cim"):
                    tl_ = sbf(f"n_{nm}{t4}", [128, 128])
                    b.raw("dve", lambda tl_=tl_: nc.vector.memset(tl_[:, :], 0.0), writes=[tl_[:, :]])
                    nat[(nm, t4)] = tl_
            wB = [sbf(f"wB{ri}", [128, 128], BF16) for ri in range(2)]
            wC = [sbf(f"wC{ri}", [128, 128], BF16) for ri in range(2)]
            c32 = [sbf(f"c32{i}", [128, 128]) for i in range(2)]
            cm = [sbf(f"cm{i}", [128, 128]) for i in range(4)]
            X = (sbf("xr", [128, LT]), sbf("xi", [128, LT]))
            xrb, xib = sbf("xrb", [128, LT], BF16), sbf("xib", [128, LT], BF16)
            tm = [sbf(f"tm{i}", [128, LT]) for i in range(6)]
            sc = sbf("sc", [128, 8])
            rotb = [sbf(f"rotb{i}", [128, 1536]) for i in range(2)]
            ones512 = sbf("ones512", [128, LT])
            b.raw("dve", lambda: nc.vector.memset(ones512[:, :], 1.0), writes=[ones512[:, :]])
            pwu = [(P["cs"], P["sn"])]
            for k in range(1, 9):
                pc_, ps_ = pwu[-1]
                nc_ = sbf(f"uc{k}", [128, 64]); ns_ = sbf(f"us{k}", [128, 64])
                TT(t1, pc_[:, :], pc_[:, :], ALU.mult)
                TT(t2, ps_[:, :], ps_[:, :], ALU.mult)
                TT(nc_[:, :], t1, t2, ALU.subtract)
                TT(t1, pc_[:, :], ps_[:, :], ALU.mult)
                b.op("dve", "tensor_scalar", out=ns_[:, :], in0=t1, scalar1=2.0, scalar2=None, op0=ALU.mult)
                pwu.append((nc_, ns_))
            TSm = lambda o, i_, sc_: b.op("dve", "tensor_scalar", out=o, in0=i_, scalar1=sc_, scalar2=None, op0=ALU.mult)
            for tau in range(64):
                Er, Ei = X[0], X[1]
                b.op("dve", "tensor_copy", out=Er[:, 0:1], in_=P["cs"][:, tau:tau + 1])
                b.op("dve", "tensor_copy", out=Ei[:, 0:1], in_=P["sn"][:, tau:tau + 1])
                for k in range(9):
                    n = 1 << k
                    pc_, ps_ = pwu[k][0][:, tau:tau + 1], pwu[k][1][:, tau:tau + 1]
                    TSm(tm[0][:, 0:n], Er[:, 0:n], pc_)
                    TSm(tm[1][:, 0:n], Ei[:, 0:n], ps_)
                    TSm(tm[2][:, 0:n], Ei[:, 0:n], pc_)
                    TSm(tm[3][:, 0:n], Er[:, 0:n], ps_)
                    TT(Er[:, n:2 * n], tm[0][:, 0:n], tm[1][:, 0:n], ALU.subtract)
                    TT(Ei[:, n:2 * n], tm[2][:, 0:n], tm[3][:, 0:n], ALU.add)
                b.op("act", "activation", out=tm[4][:, :], in_=ones512[:, :], func=AF.Identity, scale=P["mag"][:, tau:tau + 1])
                b.dma("sp", rot[tau, :, 0:512], Er[:, :])
                b.dma("sp", rot[tau, :, 512:1024], Ei[:, :])
                b.dma("sp", rot[tau, :, 1024:1536], tm[4][:, :])
            rit = [0]

            def tile_pass(d, tl):
                for t in range(32):
                    tau = d * 32 + t
                    ch, t4 = t // 4, t % 4
                    for g2 in range(2):
                        g = 2 * t + g2
                        cb = (2 * t4 + g2) * 16
                        b.dma("sp", nat[("bre", t4)][g2 * 64:(g2 + 1) * 64, cb:cb + 16], s5_b_re[d, g, :, :])
                        b.dma("sp", nat[("bim", t4)][g2 * 64:(g2 + 1) * 64, cb:cb + 16], s5_b_im[d, g, :, :])
                        b.dma("sp", nat[("cre", t4)][cb:cb + 16, g2 * 64:(g2 + 1) * 64], s5_c_re[d, g, :, :])
                        b.dma("sp", nat[("cim", t4)][cb:cb + 16, g2 * 64:(g2 + 1) * 64], s5_c_im[d, g, :, :])
                    for ri, nm in enumerate(("bre", "bim")):
                        b.op("pe", "transpose", out=psS[:, 0:128], in_=nat[(nm, t4)][:, :], identity=ident[:, :])
                        b.op("act", "activation", out=wB[ri][:, :], in_=psS[:, 0:128], func=AF.Copy)
                    for ri, nm in enumerate(("cre", "cim")):
                        b.op("pe", "transpose", out=psS[:, 0:128], in_=nat[(nm, t4)][:, :], identity=ident[:, :])
                        b.op("act", "activation", out=c32[ri][:, :], in_=psS[:, 0:128], func=AF.Copy)
                    fr_, fi_ = P["fr"][:, tau:tau + 1], P["fi"][:, tau:tau + 1]
                    TS = lambda o, i_, sc_: b.op("dve", "tensor_scalar", out=o, in0=i_, scalar1=sc_, scalar2=None, op0=ALU.mult)
                    TS(cm[0][:, :], c32[0][:, :], fr_)
                    TS(cm[1][:, :], c32[1][:, :], fi_)
                    TT(wC[0][:, :], cm[0][:, :], cm[1][:, :], ALU.subtract)
                    TS(cm[2][:, :], c32[0][:, :], fi_)
                    TS(cm[3][:, :], c32[1][:, :], fr_)
                    TT(cm[2][:, :], cm[2][:, :], cm[3][:, :], ALU.add)
                    b.op("dve", "tensor_scalar", out=wC[1][:, :], in0=cm[2][:, :], scalar1=-1.0, scalar2=None, op0=ALU.mult)
                    tb = rotb[rit[0] % 2]
                    rit[0] += 1
                    b.dma("sp", tb[:, :], rot[tau, :, :])
                    cT, sT, rT = tb[:, 0:512], tb[:, 512:1024], tb[:, 1024:1536]
                    for ri in range(2):
                        b.op("pe", "matmul", out=psG[ri][:, 0:LT], lhsT=wB[ri][:, :], rhs=h[:, ch, :], start=True, stop=True)
                    e1 = LT - 1 if d == 0 else 0
                    cr_, ci_ = cr[:, tau:tau + 1], ci[:, tau:tau + 1]
                    rv = (lambda A: A[:, 0:LT]) if d == 0 else (lambda A: A[:, LT - 1::-1])
                    brv, biv = rv(psG[0]), rv(psG[1])
                    TT(tm[0][:, :], brv, cT, ALU.mult)
                    TT(tm[1][:, :], biv, sT, ALU.mult)
                    TT(tm[2][:, :], biv, cT, ALU.mult)
                    TT(tm[3][:, :], brv, sT, ALU.mult)
                    TT(X[0][:, :], tm[0][:, :], tm[1][:, :], ALU.add)
                    b.op("pool", "tensor_tensor", out=X[1][:, :], in0=tm[2][:, :], in1=tm[3][:, :], op=ALU.subtract)
                    b.op("dve", "tensor_tensor_scan", out=tm[4][:, :], data0=rT, data1=X[0][:, :], initial=cr_, op0=ALU.mult, op1=ALU.add)
                    b.op("dve", "tensor_tensor_scan", out=tm[5][:, :], data0=rT, data1=X[1][:, :], initial=ci_, op0=ALU.mult, op1=ALU.add)
                    PT_ = lambda o, a_, b_, op: b.op("pool", "tensor_tensor", out=o, in0=a_, in1=b_, op=op)
                    PT_(tm[0][:, :], tm[4][:, :], cT, ALU.mult)
                    PT_(tm[1][:, :], tm[5][:, :], sT, ALU.mult)
                    PT_(tm[2][:, :], tm[5][:, :], cT, ALU.mult)
                    PT_(tm[3][:, :], tm[4][:, :], sT, ALU.mult)
                    TT(rv(X[0]), tm[0][:, :], tm[1][:, :], ALU.subtract)
                    TT(rv(X[1]), tm[2][:, :], tm[3][:, :], ALU.add)
                    b.op("dve", "tensor_copy", out=cr_, in_=X[0][:, e1:e1 + 1])
                    b.op("dve", "tensor_copy", out=ci_, in_=X[1][:, e1:e1 + 1])
                    b.op("act", "activation", out=xrb[:, :], in_=X[0][:, :], func=AF.Copy)
                    b.op("act", "activation", out=xib[:, :], in_=X[1][:, :], func=AF.Copy)
                    py = psY[t % 2]
                    b.op("pe", "matmul", out=py[:, 0:LT], lhsT=wC[0][:, :], rhs=xrb[:, :], start=True, stop=False)
                    b.op("pe", "matmul", accum=True, out=py[:, 0:LT], lhsT=wC[1][:, :], rhs=xib[:, :], start=False, stop=True)
                    if t4 == 0:
                        b.op("dve", "tensor_copy", out=y[:, ch, :], in_=py[:, 0:LT])
                    else:
                        TT(y[:, ch, :], y[:, ch, :], py[:, 0:LT], ALU.add)

            def tail():
                g_bf = act
                for c in range(KC):
                    t_a, t_b = tmpA[0], tmpA[1]
                    b.op("dve", "tensor_scalar", out=t_a[:, :], in0=h[:, c, :], scalar1=P["dskip"][:, c:c + 1], scalar2=None, op0=ALU.mult)
                    TT(y[:, c, :], y[:, c, :], t_a[:, :], ALU.add)
                    b.op("act", "activation", out=t_a[:, :], in_=y[:, c, :], func=AF.Square)
                    b.op("dve", "tensor_scalar", out=t_a[:, :], in0=t_a[:, :], scalar1=0.044715, scalar2=1.0, op0=ALU.mult, op1=ALU.add)
                    TT(t_a[:, :], t_a[:, :], y[:, c, :], ALU.mult)
                    b.op("act", "activation", out=t_b[:, :], in_=t_a[:, :], func=AF.Tanh, scale=0.7978845608028654)
                    b.op("dve", "tensor_scalar", out=t_b[:, :], in0=t_b[:, :], scalar1=1.0, scalar2=0.5, op0=ALU.add, op1=ALU.mult)
                    TT(g_bf[:, c, :], t_b[:, :], y[:, c, :], ALU.mult)
                for o4 in range(0, KC, 2):
                    sl = wslot()
                    v = sl[:, 0:KC * 512].rearrange("p (k n) -> p k n", k=KC)
                    b.dma("pool", v[:, :, 0:256], s5_w_glu[:, o4 * 128:(o4 + 2) * 128].rearrange("(k p) n -> p k n", p=128))
                    b.dma("pool", v[:, :, 256:512], s5_w_glu[:, D + o4 * 128:D + (o4 + 2) * 128].rearrange("(k p) n -> p k n", p=128))
                    for oo in range(2):
                        o = o4 + oo
                        pa, pg = psG[o % 2], psU[o % 2]
                        for k in range(KC):
                            b.op("pe", "matmul", accum=(k > 0), out=pa[:, 0:T], lhsT=v[:, k, oo * 128:(oo + 1) * 128],
                                 rhs=g_bf[:, k, :], start=(k == 0), stop=(k == KC - 1))
                        for k in range(KC):
                            b.op("pe", "matmul", accum=(k > 0), out=pg[:, 0:T], lhsT=v[:, k, 256 + oo * 128:256 + (oo + 1) * 128],
                                 rhs=g_bf[:, k, :], start=(k == 0), stop=(k == KC - 1))
                        tq = tmpA[o % 2]
                        b.op("act", "activation", out=tq[:, :], in_=pg[:, 0:T], func=AF.Sigmoid)
                        TT(y[:, o, :], tq[:, :], pa[:, 0:T], ALU.mult)

            for tl in range(NTL):
                lat_load(zres, tl)
                lat_ffn_sub(1, 0)
                lat_store(zres, tl)
                pre_norm(1, 1)
                tile_pass(0, tl)
                b.dma("sp", ysc[:, tl * 512:(tl + 1) * 512].rearrange("(c p) t -> p c t", p=128), y[:])
            for tl in range(NTL - 1, -1, -1):
                lat_load(zres, tl)
                pre_norm(1, 1)
                tile_pass(1, tl)
                for c in range(KC):
                    b.dma("sp", tmpA[c % 2][:, :], ysc[c * 128:(c + 1) * 128, tl * 512:(tl + 1) * 512])
                    TT(y[:, c, :], y[:, c, :], tmpA[c % 2][:, :], ALU.add)
                tail()
                post_norm_residual(1, 1)
                lat_ffn_sub(1, 2)
                lat_store(zres, tl)


        def na_lat():
            sbf = lambda n_, sh, dt_=F32: b.sb("ln_" + n_, sh, dt_)
            TT = lambda o, a_, b_, op: b.op("dve", "tensor_tensor", out=o, in0=a_, in1=b_, op=op)
            NR = LSEQ // 64
            kTg = sbf("kT", [128, LSEQ], BF16); vtg = sbf("vt", [64, NR, 128], BF16)
            kTc = sbf("kTc", [128, 512], BF16); vtc = sbf("vtc", [128, 4, 128], BF16)
            Bh = sbf("Bh", [64, 2, 15, 64]); cmask = sbf("cmask", [64, 64]); b.dma("sp", cmask[:], na_cmask)
            c32t = sbf("c32", [128, 128]); ckb = sbf("ckb", [128, 128], BF16)
            qTg = sbf("qT", [128, 512], BF16)
            slcs = [sbf(f"sl{j}", [64, 512]) for j in range(2)]
            Pls = [sbf(f"Pl{j}", [64, 512], BF16) for j in range(2)]; Pcs = [sbf(f"Pc{j}", [64, 512], BF16) for j in range(2)]
            PTls = [sbf(f"PTl{j}", [128, 12, 64], BF16) for j in range(2)]; otk = sbf("otk", [64, 128], BF16)
            smms = [sbf(f"sm{j}", [64, 8]) for j in range(2)]; woT = sbf("woT", [128, 512], BF16)
            stg = [sbf(f"stg{i}", [128, 512], BF16) for i in range(2)]
            vstg = sbf("vstg", [64, 1024], BF16)
            for tl in range(NTL):
                lat_load(zres, tl)
                lat_ffn_sub(2, 0)
                lat_store(zres, tl)
                pre_norm(2, 1)
                for part, dst in ((0, qsc), (1, ksc)):
                    for hf in range(2):
                        sl_ = wslot()
                        v = sl_[:, 0:KC * 512].rearrange("p (k n) -> p k n", k=KC)
                        b.dma("pool", v, na_w_qkv[:, part * D + hf * 512:part * D + (hf + 1) * 512].rearrange("(k p) n -> p k n", p=128))
                        for mm in range(4):
                            c = hf * 4 + mm
                            pq = psG[mm % 2]
                            for k in range(KC):
                                b.op("pe", "matmul", accum=(k > 0), out=pq[:, 0:512], lhsT=v[:, k, mm * 128:(mm + 1) * 128],
                                     rhs=h[:, k, :], start=(k == 0), stop=(k == KC - 1))
                            b.op("act", "activation", out=stg[mm % 2][:, :], in_=pq[:, 0:512], func=AF.Copy)
                            b.dma("sp", dst[c, :, tl * 512:(tl + 1) * 512], stg[mm % 2][:, :])
                wv = []
                for hf in range(2):
                    sl_ = wslot()
                    v = sl_[:, 0:KC * 512].rearrange("p (k n) -> p k n", k=KC)
                    b.dma("pool", v, na_w_qkv[:, 2 * D + hf * 512:2 * D + (hf + 1) * 512].rearrange("(k p) n -> p k n", p=128))
                    wv.append(v)
                for rr in range(8):
                    for hf in range(2):
                        pq = psU[hf]
                        for k in range(KC):
                            b.op("pe", "matmul", accum=(k > 0), out=pq[0:64, 0:512], lhsT=h[:, k, rr * 64:(rr + 1) * 64],
                                 rhs=wv[hf][:, k, :], start=(k == 0), stop=(k == KC - 1))
                        b.op("dve" if hf == 0 else "act", "tensor_copy" if hf == 0 else "activation",
                             out=vstg[:, hf * 512:(hf + 1) * 512], in_=pq[0:64, 0:512], **({} if hf == 0 else {"func": AF.Copy}))
                    b.dma("sp", vsc[:, :, tl * 8 + rr, :].rearrange("g p d -> p g d"), vstg[:, :].rearrange("p (g d) -> p g d", g=8))
            for gi in range(8):
                for j in range(2):
                    b.dma("sp", Bh[:, j, :, :].rearrange("p a k -> p (a k)"), na_bias[2 * gi + j, :, :])
                    for dr in range(15):
                        TT(Bh[:, j, dr, :], Bh[:, j, dr, :], cmask[:, :], ALU.add)
                    b.op("dve", "tensor_scalar", out=Bh[:, j, :, :].rearrange("p a k -> p (a k)"),
                         in0=Bh[:, j, :, :].rearrange("p a k -> p (a k)"), scalar1=8.0, scalar2=None, op0=ALU.mult)
                for cb in range(4):
                    b.dma("sp", c32t[:, :], cache_nk[cb * 128:(cb + 1) * 128, gi * 128:(gi + 1) * 128])
                    b.op("dve", "tensor_copy", out=ckb[:, :], in_=c32t[:, :])
                    b.op("pe", "transpose", out=psTb[cb % 2][:, 0, :], in_=ckb[:, :], identity=ident_bf[:, :])
                    b.op("act", "activation", out=kTc[:, cb * 128:(cb + 1) * 128], in_=psTb[cb % 2][:, 0, :], func=AF.Copy)
                    b.dma("pool", vtc[:, cb, :], cache_nv[cb * 128:(cb + 1) * 128, gi * 128:(gi + 1) * 128])
                b.dma("sp", kTg[:, :], ksc[gi, :, :])
                b.dma("sp", vtg[:, :, :], vsc[gi, :, :, :])
                it = 0
                for tl in range(NTL):
                    if gi == 7:
                        lat_load(zres, tl)
                    b.dma("sp", qTg[:, :], qsc[gi, :, tl * 512:(tl + 1) * 512])
                    for rr in range(8):
                        r = tl * 8 + rr
                        kr0 = min(max(r - 4, 0), NR - 8)
                        dr0 = kr0 - r + 7
                        def head(j, rr=rr, kr0=kr0, dr0=dr0):
                            base = 64 * j
                            slc, Pl, Pc, PTl, smm = slcs[j], Pls[j], Pcs[j], PTls[j], smms[j]
                            pL, pC, pO, pT = psG[j], psU[j], psY[j], psTc[j]
                            lq = qTg[base:base + 64, rr * 64:(rr + 1) * 64]
                            b.op("pe", "matmul", out=pL[0:64, 0:512], lhsT=lq, rhs=kTg[base:base + 64, kr0 * 64:kr0 * 64 + 512],
                                 start=True, stop=True)
                            b.op("pe", "matmul", out=pC[0:64, 0:512], lhsT=lq, rhs=kTc[base:base + 64, :], start=True, stop=True)
                            yield
                            TT(slc[:, :], pL[0:64, 0:512], Bh[:, j, dr0:dr0 + 8, :].rearrange("p a k -> p (a k)"), ALU.add)
                            b.op("dve", "reduce_max", out=smm[:, 0:1], in_=slc[:, :], axis=AX.X)
                            b.op("dve", "reduce_max", out=smm[:, 1:2], in_=pC[0:64, 0:512], axis=AX.X)
                            yield
                            TT(smm[:, 0:1], smm[:, 0:1], smm[:, 1:2], ALU.max)
                            b.op("dve", "tensor_scalar", out=smm[:, 2:3], in0=smm[:, 0:1], scalar1=-0.125, scalar2=None, op0=ALU.mult)
                            yield
                            b.op("act", "activation", out=Pl[:, :], in_=slc[:, :], func=AF.Exp, scale=0.125, bias=smm[:, 2:3],
                                 accum_out=smm[:, 3:4])
                            b.op("act", "activation", out=Pc[:, :], in_=pC[0:64, 0:512], func=AF.Exp, scale=0.125,
                                 bias=smm[:, 2:3], accum_out=smm[:, 4:5])
                            yield
                            TT(smm[:, 5:6], smm[:, 3:4], smm[:, 4:5], ALU.add)
                            b.op("dve", "reciprocal", out=smm[:, 6:7], in_=smm[:, 5:6])
                            for jj in range(0, 8, 2):
                                for q_ in range(2):
                                    b.op("pe", "transpose", out=pT[0:64, q_, 0:64], in_=Pl[:, (jj + q_) * 64:(jj + q_ + 1) * 64],
                                         identity=ident_bf[0:64, 0:64])
                                b.op("dve" if (jj // 2) % 2 == 0 else "act", "tensor_copy" if (jj // 2) % 2 == 0 else "activation",
                                     out=PTl[0:64, jj:jj + 2, :], in_=pT[0:64, :, 0:64], **({} if (jj // 2) % 2 == 0 else {"func": AF.Copy}))
                                yield
                            for jj in range(0, 4, 2):
                                for q_ in range(2):
                                    b.op("pe", "transpose", out=pT[:, q_, 0:64], in_=Pc[:, (jj + q_) * 128:(jj + q_ + 1) * 128],
                                         identity=ident_bf[0:64, 0:64])
                                b.op("dve" if (jj // 2) % 2 == 0 else "act", "tensor_copy" if (jj // 2) % 2 == 0 else "activation",
                                     out=PTl[:, 8 + jj:10 + jj, :], in_=pT[:, :, 0:64], **({} if (jj // 2) % 2 == 0 else {"func": AF.Copy}))
                                yield
                            for jj in range(8):
                                b.op("pe", "matmul", accum=(jj > 0), out=pO[0:64, 0:64], lhsT=PTl[0:64, jj, :],
                                     rhs=vtg[:, kr0 + jj, base:base + 64], start=(jj == 0), stop=False)
                            for jj in range(4):
                                b.op("pe", "matmul", accum=True, out=pO[0:64, 0:64], lhsT=PTl[:, 8 + jj, :],
                                     rhs=vtc[:, jj, base:base + 64], start=False, stop=(jj == 3))
                            yield
                            b.op("dve", "tensor_scalar", out=otk[:, base:base + 64], in0=pO[0:64, 0:64], scalar1=smm[:, 6:7],
                                 scalar2=None, op0=ALU.mult)
                        run_interleaved([head(0), head(1)])
                        b.op("pe", "transpose", out=psTb[rr % 2][:, 0, 0:64], in_=otk[:, :], identity=ident_bf[0:64, 0:64])
                        b.op("act", "activation", out=oT[:, 0, rr * 64:(rr + 1) * 64], in_=psTb[rr % 2][:, 0, 0:64], func=AF.Copy)
                    for o4 in range(2):
                        b.dma("pool", woT[:, :], na_w_o[gi * 128:(gi + 1) * 128, o4 * 512:(o4 + 1) * 512])
                        for oo in range(4):
                            o = o4 * 4 + oo
                            py = psY[oo % 2]
                            b.op("pe", "matmul", out=py[:, 0:512], lhsT=woT[:, oo * 128:(oo + 1) * 128], rhs=oT[:, 0, :],
                                 start=True, stop=True)
                            if gi == 0:
                                b.op("act", "activation", out=y[:, o, :], in_=py[:, 0:512], func=AF.Copy)
                            else:
                                b.dma("sp", tmpA[oo % 2][:, :], ysc[o * 128:(o + 1) * 128, tl * 512:(tl + 1) * 512])
                                TT(y[:, o, :], tmpA[oo % 2][:, :], py[:, 0:512], ALU.add)
                    if gi < 7:
                        b.dma("sp", ysc[:, tl * 512:(tl + 1) * 512].rearrange("(c p) t -> p c t", p=128), y[:])
                    else:
                        post_norm_residual(2, 1)
                        lat_ffn_sub(2, 2)
                        lat_store(zres, tl)


        def dn_lat(final_dst):
            sbf = lambda n_, sh, dt_=F32: b.sb("ld_" + n_, sh, dt_)
            TT = lambda o, a_, b_, op: b.op("dve", "tensor_tensor", out=o, in0=a_, in1=b_, op=op)
            NCH = LSEQ // 64
            msk = sbf("msk", [64, 6, 64]); b.dma("sp", msk[:], dn_mask)
            cw = sbf("cw", [128, 80]); load_rows_T(cw[:, :], dn_conv_w, 80)
            ones_f = sbf("ones", [64, 128]); b.raw("dve", lambda: nc.vector.memset(ones_f[:, :], 1.0), writes=[ones_f[:, :]])
            wba = sbf("wba", [128, 2, KC, 16], BF16)
            for d in range(2):
                b.dma("pool", wba[:, d, :, :], dn_w_ba[d].rearrange("(k p) n -> p k n", p=128))
            alog = sbf("alog", [64, 16]); b.dma("sp", alog[:], dn_a_log.broadcast_to([64, 16]))
            dtb = sbf("dtb", [64, 16]); b.dma("sp", dtb[:], dn_dt_bias.broadcast_to([64, 16]))
            outg = sbf("outg", [64, 128]); b.dma("sp", outg[:], dn_out_g.broadcast_to([64, 128]))
            nexpa = sbf("nexpa", [64, 16])
            b.op("act", "activation", out=nexpa[:, :], in_=alog[:, :], func=AF.Exp)
            b.op("dve", "tensor_scalar", out=nexpa[:, :], in0=nexpa[:, :], scalar1=-1.0, scalar2=None, op0=ALU.mult)
            zt = sbf("zt", [128, 2]); b.raw("dve", lambda: nc.vector.memset(zt[:, :], 0.0), writes=[zt[:, :]])
            for mm in range(16):
                b.dma("sp", pjsc[mm, :, 0:2], zt[:, :])
                b.dma("sp", pjsc[mm, :, LSEQ + 2:LSEQ + 4], zt[:, :])
            zstg = [act[:, 2 + i, :] for i in range(2)]
            graw = sbf("graw", [64, 2, NCH, 16])
            for tl in range(NTL):
                lat_load(zres, tl)
                lat_ffn_sub(3, 0)
                lat_store(zres, tl)
                pre_norm(3, 1)
                for m0 in range(0, 24, 4):
                    sl_ = wslot()
                    v = sl_[:, 0:KC * 512].rearrange("p (k n) -> p k n", k=KC)
                    b.dma("pool", v, dn_w_in[:, m0 * 128:(m0 + 4) * 128].rearrange("(k p) n -> p k n", p=128))
                    for mm in range(4):
                        m = m0 + mm
                        pq = psG[mm % 2]
                        for k in range(KC):
                            b.op("pe", "matmul", accum=(k > 0), out=pq[:, 0:512], lhsT=v[:, k, mm * 128:(mm + 1) * 128],
                                 rhs=h[:, k, :], start=(k == 0), stop=(k == KC - 1))
                        if m >= 16:
                            b.op("act", "activation", out=zstg[mm % 2], in_=pq[:, 0:512], func=AF.Silu)
                            b.dma("sp", zsc[m - 16, :, tl * 512:(tl + 1) * 512], zstg[mm % 2])
                        else:
                            b.op("act", "activation", out=tmpA[mm % 2][:, :], in_=pq[:, 0:512], func=AF.Copy)
                            b.dma("sp", pjsc[m, :, 2 + tl * 512:2 + (tl + 1) * 512], tmpA[mm % 2][:, :])
                for blk in range(8):
                    n = tl * 8 + blk
                    for d in range(2):
                        pq = psU[d]
                        for k in range(KC):
                            b.op("pe", "matmul", accum=(k > 0), out=pq[0:64, 0:16], lhsT=h[:, k, blk * 64:(blk + 1) * 64],
                                 rhs=wba[:, d, k, :], start=(k == 0), stop=(k == KC - 1))
                        b.op("dve", "tensor_copy", out=graw[:, d, n, :], in_=pq[0:64, 0:16])
            cin = sbf("cin", [128, 516])
            qn = sbf("qn", [128, LSEQ], BF16); kn = sbf("kn", [128, LSEQ], BF16)
            kt_tok = sbf("kt", [64, NCH, 128], BF16); vt_tok = sbf("vt", [64, NCH, 128], BF16)
            zs = sbf("zs", [128, LSEQ], BF16); vtmp = act[:, 1, :]
            oTf = oT[:, :, :].rearrange("p c t -> p (c t)")
            beta = sbf("beta", [64, 2, NCH]); xa = sbf("xa", [64, 2, NCH])
            xe = sbf("xe", [64, 2, NCH]); gg = sbf("gg", [64, 2, NCH]); gc = sbf("gc", [64, 2, NCH]); eg = sbf("eg", [64, 2, NCH])
            def mkbufs(sx):
                f64 = lambda n_: sbf(n_ + sx, [64, 64])
                U = {}
                U["dg"], U["decS"], U["decT"], U["PTt"] = f64("dg"), f64("decS"), f64("decT"), f64("PTt")
                U["Ms"] = [(f64("Ma"), f64("MTa")), (f64("Mb"), f64("MTb"))]
                U["gcr"] = sbf("gcr" + sx, [128, 64]); U["TTb"] = sbf("TTb" + sx, [64, 64], BF16); U["aT"] = sbf("aT" + sx, [64, 64], BF16)
                U["vb"] = sbf("vb" + sx, [64, 128], BF16); U["kbg"] = sbf("kbg" + sx, [64, 128], BF16); U["kg"] = sbf("kg" + sx, [64, 128], BF16)
                U["wT"] = sbf("wT" + sx, [128, 64], BF16); U["u_sb"] = sbf("u" + sx, [64, 128]); U["vnew"] = sbf("vnew" + sx, [64, 128])
                U["vnew_b"] = sbf("vnewb" + sx, [64, 128], BF16)
                U["o1"] = sbf("o1" + sx, [64, 128]); U["sc1"] = sbf("sc1" + sx, [64, 4]); U["egl"] = sbf("egl" + sx, [128, 1])
                U["S"] = sbf("S" + sx, [128, 128]); U["Sb"] = sbf("Sb" + sx, [128, 128], BF16)
                return U
            UB = [mkbufs("_a"), mkbufs("_b")]
            og = [sbf("og0", [64, 4, 128]), sbf("og1", [64, 4, 128])]
            ss = sbf("ss", [64, 2]); onb = sbf("onb", [64, 128], BF16); junk = sbf("junk", [64, 128])
            woT = sbf("woT", [128, 512], BF16)
            id64 = ident[0:64, 0:64]
            fl = lambda A: A[:, :, :].rearrange("p a b -> p (a b)")
            for hv in range(8):
                hq = hv // 2
                wcols = [hq * 128, 512 + hq * 128, 1024 + hv * 128, 2048 + hv * 128]
                cwc = [hq, 4 + hq, 8 + hv]
                pjc = [hq, 4 + hq, 8 + hv]
                b.dma("sp", zs[:, :], zsc[hv, :, :])
                for d in range(2):
                    b.op("act", "activation", out=beta[:, d, :], in_=graw[:, d, :, hv], func=AF.Sigmoid)
                    b.op("dve", "tensor_scalar", out=xa[:, d, :], in0=graw[:, d, :, 8 + hv], scalar1=dtb[:, d * 8 + hv:d * 8 + hv + 1],
                         scalar2=None, op0=ALU.add)
                b.op("act", "activation", out=fl(xe), in_=fl(xa), func=AF.Abs)
                b.op("act", "activation", out=fl(xe), in_=fl(xe), func=AF.Exp, scale=-1.0)
                b.op("act", "activation", out=fl(xe), in_=fl(xe), func=AF.Ln, bias=1.0, scale=1.0)
                b.op("dve", "tensor_scalar", out=fl(xa), in0=fl(xa), scalar1=0.0, scalar2=None, op0=ALU.max)
                TT(fl(xa), fl(xa), fl(xe), ALU.add)
                for d in range(2):
                    b.op("dve", "tensor_scalar", out=gg[:, d, :], in0=xa[:, d, :], scalar1=nexpa[:, d * 8 + hv:d * 8 + hv + 1],
                         scalar2=None, op0=ALU.mult)
                    b.op("pe", "matmul", out=psS[0:64, 0:NCH], lhsT=msk[:, d, :], rhs=gg[:, d, :], start=True, stop=True)
                    b.op("dve", "tensor_copy", out=gc[:, d, :], in_=psS[0:64, 0:NCH])
                b.op("act", "activation", out=fl(eg), in_=fl(gc), func=AF.Exp)
                for tl in range(NTL):
                    for mm in (range(3) if hv % 2 == 0 else range(2, 3)):
                        b.dma("sp", cin[:, :], pjsc[pjc[mm], :, tl * 512:tl * 512 + 516])
                        for j in range(5):
                            dst = tmpA[0] if j == 0 else tmpA[1]
                            b.op("act", "activation", out=dst[:, :], in_=cin[:, j:j + 512], func=AF.Identity,
                                 scale=cw[:, j * 16 + cwc[mm]:j * 16 + cwc[mm] + 1])
                            if j > 0:
                                TT(tmpA[0][:, :], tmpA[0][:, :], tmpA[1][:, :], ALU.add)
                        if mm < 2:
                            b.op("act", "activation", out=tmpA[1][:, :], in_=tmpA[0][:, :], func=AF.Silu)
                            b.op("act", "activation", out=act[:, 0, :], in_=tmpA[1][:, :], func=AF.Square)
                            b.op("pe", "matmul", out=psS[:, 0:512], lhsT=ones_bf[:, :], rhs=act[:, 0, :], start=True, stop=True)
                            b.op("act", "activation", out=rstd[:, :], in_=psS[:, 0:512], func=AF.Sqrt, scale=1.0, bias=EPS)
                            b.op("dve", "reciprocal", out=rstd[:, :], in_=rstd[:, :])
                            TT(tmpA[1][:, :], tmpA[1][:, :], rstd[:, :], ALU.mult)
                            if mm == 0:
                                b.op("act", "activation", out=qn[:, tl * 512:(tl + 1) * 512], in_=tmpA[1][:, :], func=AF.Identity,
                                     scale=128 ** -0.5)
                            else:
                                b.op("act", "activation", out=kn[:, tl * 512:(tl + 1) * 512], in_=tmpA[1][:, :], func=AF.Identity, scale=1.0)
                                for blk in range(8):
                                    n = tl * 8 + blk
                                    b.op("pe", "transpose", out=psTb[blk % 2][0:64, 0, :], in_=kn[:, n * 64:(n + 1) * 64], identity=ident_bf[:, :])
                                    b.op("dve", "tensor_copy", out=kt_tok[:, n, :], in_=psTb[blk % 2][0:64, 0, :])
                        else:
                            b.op("act", "activation", out=vtmp, in_=tmpA[0][:, :], func=AF.Silu)
                            for blk in range(8):
                                n = tl * 8 + blk
                                b.op("pe", "transpose", out=psTb[blk % 2][0:64, 0, :], in_=act[:, 1, blk * 64:(blk + 1) * 64], identity=ident_bf[:, :])
                                b.op("dve", "tensor_copy", out=vt_tok[:, n, :], in_=psTb[blk % 2][0:64, 0, :])
                def chain(d):
                    U = UB[d]
                    dg, decS, decT, PTt, Ms, gcr, TTb, aT = U["dg"], U["decS"], U["decT"], U["PTt"], U["Ms"], U["gcr"], U["TTb"], U["aT"]
                    vb, kbg, kg, wT, u_sb, vnew, vnew_b = U["vb"], U["kbg"], U["kg"], U["wT"], U["u_sb"], U["vnew"], U["vnew_b"]
                    o1, sc1, egl, S, Sb = U["o1"], U["sc1"], U["egl"], U["S"], U["Sb"]
                    pG, pU, pY = psG[d], psU[d], psY[d]
                    b.dma("sp", S[:, :], st_dn[d * 8 + hv, :, :])
                    b.op("act", "activation", out=Sb[:, :], in_=S[:, :], func=AF.Copy)
                    yield
                    last = 63 if d == 0 else 0
                    for n in (range(NCH) if d == 0 else range(NCH - 1, -1, -1)):
                        tok0 = n * 64
                        gc_, be_, eg_ = gc[:, d, n:n + 1], beta[:, d, n:n + 1], eg[:, d, n:n + 1]
                        kf, qf = kn[:, tok0:tok0 + 64], qn[:, tok0:tok0 + 64]
                        kt, vt = kt_tok[:, n, :], vt_tok[:, n, :]
                        b.op("dve", "tensor_scalar", out=dg[:, :], in0=id64, scalar1=gc_, scalar2=None, op0=ALU.mult)
                        b.op("pe", "matmul", out=psS[:, 0:64], lhsT=ones_f[:, :], rhs=dg[:, :], start=True, stop=True)
                        b.op("act", "activation", out=gcr[:, :], in_=psS[:, 0:64], func=AF.Copy)
                        yield
                        b.op("dve", "tensor_scalar", out=decS[:, :], in0=gcr[0:64, :], scalar1=-1.0, scalar2=gc_, op0=ALU.mult, op1=ALU.add)
                        b.op("dve", "tensor_scalar", out=decS[:, :], in0=decS[:, :], scalar1=0.0, scalar2=None, op0=ALU.min)
                        b.op("act", "activation", out=decS[:, :], in_=decS[:, :], func=AF.Exp)
                        TT(decS[:, :], decS[:, :], msk[:, 4 + d, :], ALU.mult)
                        yield
                        b.op("dve", "tensor_scalar", out=decT[:, :], in0=gcr[0:64, :], scalar1=gc_, scalar2=None, op0=ALU.subtract)
                        b.op("dve", "tensor_scalar", out=decT[:, :], in0=decT[:, :], scalar1=0.0, scalar2=None, op0=ALU.min)
                        b.op("act", "activation", out=decT[:, :], in_=decT[:, :], func=AF.Exp)
                        TT(decT[:, :], decT[:, :], msk[:, d, :], ALU.mult)
                        yield
                        M, MT = Ms[0]
                        b.op("pe", "matmul", out=pG[0:64, 0:64], lhsT=kf, rhs=kf, start=True, stop=True)
                        b.op("dve", "tensor_scalar", out=M[:, :], in0=pG[0:64, 0:64], scalar1=be_, scalar2=-1.0, op0=ALU.mult, op1=ALU.mult)
                        TT(M[:, :], M[:, :], decS[:, :], ALU.mult)
                        yield
                        b.op("pe", "transpose", out=pU[0:64, 0:64], in_=M[:, :], identity=id64)
                        b.op("act", "activation", out=MT[:, :], in_=pU[0:64, 0:64], func=AF.Copy)
                        TT(PTt[:, :], MT[:, :], id64, ALU.add)
                        yield
                        cur = 0
                        for k in range(1, 6):
                            M, MT = Ms[cur]
                            Mn, MTn = Ms[1 - cur]
                            b.op("pe", "matmul", out=pG[0:64, 0:64], lhsT=MT[:, :], rhs=M[:, :], start=True, stop=True)
                            b.op("act", "activation", out=Mn[:, :], in_=pG[0:64, 0:64], func=AF.Copy)
                            if k < 5:
                                b.op("pe", "matmul", out=pU[0:64, 0:64], lhsT=M[:, :], rhs=MT[:, :], start=True, stop=True)
                                b.op("dve", "tensor_copy", out=MTn[:, :], in_=pU[0:64, 0:64])
                            yield
                            b.op("pe", "matmul", out=pY[0:64, 0:64], lhsT=Mn[:, :], rhs=PTt[:, :], start=True, stop=True)
                            TT(PTt[:, :], PTt[:, :], pY[0:64, 0:64], ALU.add)
                            yield
                            cur = 1 - cur
                        b.op("act", "activation", out=TTb[:, :], in_=PTt[:, :], func=AF.Copy)
                        TT(sc1[:, 0:1], be_, eg_, ALU.mult)
                        b.op("dve", "tensor_scalar", out=vb[:, :], in0=vt, scalar1=be_, scalar2=None, op0=ALU.mult)
                        b.op("dve", "tensor_scalar", out=kbg[:, :], in0=kt, scalar1=sc1[:, 0:1], scalar2=None, op0=ALU.mult)
                        yield
                        b.op("pe", "matmul", out=pG[0:64, 0:128], lhsT=TTb[:, :], rhs=vb[:, :], start=True, stop=True)
                        b.op("act", "activation", out=u_sb[:, :], in_=pG[0:64, 0:128], func=AF.Copy)
                        b.op("pe", "matmul", out=pU[:, 0:64], lhsT=kbg[:, :], rhs=TTb[:, :], start=True, stop=True)
                        b.op("dve", "tensor_copy", out=wT[:, :], in_=pU[:, 0:64])
                        yield
                        b.op("pe", "matmul", out=pY[0:64, 0:128], lhsT=wT[:, :], rhs=Sb[:, :], start=True, stop=True)
                        TT(vnew[:, :], u_sb[:, :], pY[0:64, 0:128], ALU.subtract)
                        b.op("act", "activation", out=vnew_b[:, :], in_=vnew[:, :], func=AF.Copy)
                        yield
                        b.op("pe", "matmul", out=pG[0:64, 0:128], lhsT=qf, rhs=Sb[:, :], start=True, stop=True)
                        b.op("dve", "tensor_scalar", out=o1[:, :], in0=pG[0:64, 0:128], scalar1=eg_, scalar2=None, op0=ALU.mult)
                        b.op("pe", "matmul", out=pU[0:64, 0:64], lhsT=kf, rhs=qf, start=True, stop=True)
                        TT(aT[:, :], pU[0:64, 0:64], decT[:, :], ALU.mult)
                        yield
                        b.op("pe", "matmul", out=pY[0:64, 0:128], lhsT=aT[:, :], rhs=vnew_b[:, :], start=True, stop=True)
                        TT(o1[:, :], o1[:, :], pY[0:64, 0:128], ALU.add)
                        b.dma("sp", osc[d, :, n, :], o1[:, :])
                        yield
                        b.op("act", "activation", out=sc1[:, 1:2], in_=gc_, func=AF.Exp, scale=-1.0, bias=gcr[0:64, last:last + 1])
                        b.op("dve", "tensor_scalar", out=kg[:, :], in0=kt, scalar1=sc1[:, 1:2], scalar2=None, op0=ALU.mult)
                        b.op("act", "activation", out=egl[:, :], in_=gcr[:, last:last + 1], func=AF.Exp)
                        yield
                        b.op("pe", "matmul", out=pG[:, 0:128], lhsT=kg[:, :], rhs=vnew_b[:, :], start=True, stop=True)
                        b.op("dve", "tensor_scalar", out=S[:, :], in0=S[:, :], scalar1=egl[:, 0:1], scalar2=None, op0=ALU.mult)
                        TT(S[:, :], S[:, :], pG[:, 0:128], ALU.add)
                        b.op("act", "activation", out=Sb[:, :], in_=S[:, :], func=AF.Copy)
                        yield

                gens = [chain(0), chain(1)]
                alive = [True, True]
                while any(alive):
                    for gi_ in range(2):
                        if alive[gi_]:
                            try:
                                next(gens[gi_])
                            except StopIteration:
                                alive[gi_] = False
                for n0 in range(0, NCH, 4):
                    for d in range(2):
                        b.dma("sp", og[d][:, :, :], osc[d, :, n0:n0 + 4, :])
                    for nn in range(4):
                        n = n0 + nn
                        tok0 = n * 64
                        o1 = UB[0]["o1"]
                        TT(o1[:, :], og[0][:, nn, :], og[1][:, nn, :], ALU.add)
                        b.op("act", "activation", out=junk[:, :], in_=o1[:, :], func=AF.Square, accum_out=ss[:, 0:1])
                        b.op("act", "activation", out=ss[:, 1:2], in_=ss[:, 0:1], func=AF.Sqrt, scale=1.0 / 128, bias=EPS)
                        b.op("dve", "reciprocal", out=ss[:, 1:2], in_=ss[:, 1:2])
                        b.op("dve", "tensor_scalar", out=junk[:, :], in0=o1[:, :], scalar1=ss[:, 1:2], scalar2=None, op0=ALU.mult)
                        TT(onb[:, :], junk[:, :], outg[:, :], ALU.mult)
                        b.op("pe", "transpose", out=psTb[n % 2][:, 0, 0:64], in_=onb[:, :], identity=ident_bf[0:64, 0:64])
                        TT(oTf[:, tok0:tok0 + 64], psTb[n % 2][:, 0, 0:64], zs[:, tok0:tok0 + 64], ALU.mult)
                for tl in range(NTL):
                    if hv == 7:
                        lat_load(zres, tl)
                    for o in range(KC):
                        if o % 4 == 0:
                            b.dma("pool", woT[:, :], dn_w_o[hv * 128:(hv + 1) * 128, (o // 4) * 512:(o // 4 + 1) * 512])
                        py = psY[o % 2]
                        b.op("pe", "matmul", out=py[:, 0:512], lhsT=woT[:, (o % 4) * 128:(o % 4 + 1) * 128], rhs=oTf[:, tl * 512:(tl + 1) * 512],
                             start=True, stop=True)
                        if hv == 0:
                            b.op("act", "activation", out=y[:, o, :], in_=py[:, 0:512], func=AF.Copy)
                        else:
                            b.dma("sp", tmpA[o % 2][:, :], ysc[o * 128:(o + 1) * 128, tl * 512:(tl + 1) * 512])
                            TT(y[:, o, :], tmpA[o % 2][:, :], py[:, 0:512], ALU.add)
                    if hv < 7:
                        b.dma("sp", ysc[:, tl * 512:(tl + 1) * 512].rearrange("(c p) t -> p c t", p=128), y[:])
                    else:
                        post_norm_residual(3, 1)
                        lat_ffn_sub(3, 2)
                        lat_store(final_dst, tl)

        CS = 0 if CTX_SKIP else STAGE
        def dbg(ap, n, col0=0):
            b.dma("sp", dbg_out[:, col0:col0 + n], ap)

        if CS >= 1:
            ada_layer(0)
            if DEBUG:
                dbg(mods[0][:, :], 72)
                dbg(Acoef[:, 0:24], 24, 72)
                dbg(Gcoef[:, 0:24], 24, 96)
        if CS >= 2:
            pre_norm(0, 0)
            if DEBUG:
                dbg(rstd[:, :], 512, 512)
        if CS >= 3:
            ffn(ffn_w_gu[0, 0], ffn_w_d[0, 0], key=0)
            if DEBUG:
                dbg(y[:, 0, :], 512, 1024)
        if CS >= 4:
            post_norm_residual(0, 0)
        if CS >= 5:
            pre_norm(0, 1)
            proj_tokmajor(a_w_qkv, 1024, 256, k_out)
            proj_tokmajor(a_w_qkv, 1280, 256, v_out, vdst_col0=0)
        if CS >= 6:
            proj_featmajor(qT, 0, a_w_qkv, 0, 8, h)
            proj_featmajor(kT, 0, a_w_qkv, 1024, 4, h, dup64=True)
            hm = {hh: (hh // 2, 64 * (hh % 2), hh // 4, (hh // 4) * 64) for hh in range(16)}
            if SUB >= 2:
                attn_ctx(16, hm, True, 0.125)
            if DEBUG and SUB >= 2:
                b.op("dve", "tensor_copy", out=tmpA[0][:, :], in_=oT[:, 0, :])
                dbg(tmpA[0][:, :], 512, 1536)
            if SUB >= 1:
                proj_out_featmajor(a_w_o, oT)
                post_norm_residual(0, 1)
        if CS >= 7:
            pre_norm(0, 2)
            ffn(ffn_w_gu[0, 1], ffn_w_d[0, 1], key=1)
            post_norm_residual(0, 2)
        if CS >= 8:
            ada_layer(1)
            pre_norm(1, 0)
            ffn(ffn_w_gu[1, 0], ffn_w_d[1, 0], key=2)
            post_norm_residual(1, 0)
        if CS >= 9:
            pre_norm(1, 1)
            s5_es = ExitStack()
            b.es = s5_es
            S5P = s5_setup()
            s5_mixer(S5P)
            b.es = es
            if DEBUG:
                dbg(y[:, 0, :], 512, 2048)
            post_norm_residual(1, 1)
        if CS >= 10:
            pre_norm(1, 2)
            ffn(ffn_w_gu[1, 1], ffn_w_d[1, 1], key=3)
            post_norm_residual(1, 2)
        if CS >= 11:
            ada_layer(2)
            pre_norm(2, 0)
            ffn(ffn_w_gu[2, 0], ffn_w_d[2, 0], key=4)
            post_norm_residual(2, 0)
            pre_norm(2, 1)
            proj_tokmajor(na_w_qkv, 1024, 1024, nak_out)
            proj_tokmajor(na_w_qkv, 2048, 1024, nav_out, vdst_col0=0)
            proj_featmajor(qT, 0, na_w_qkv, 0, 8, h)
            proj_featmajor(kT, 0, na_w_qkv, 1024, 8, h)
            hm2 = {hh: (hh // 2, 64 * (hh % 2), hh // 2, hh * 64) for hh in range(16)}
            attn_ctx(16, hm2, None, 0.125)
            proj_out_featmajor(na_w_o, oT)
            post_norm_residual(2, 1)
            pre_norm(2, 2)
            ffn(ffn_w_gu[2, 1], ffn_w_d[2, 1], key=5)
            post_norm_residual(2, 2)
        if CS >= 12:
            ada_layer(3)
            pre_norm(3, 0)
            ffn(ffn_w_gu[3, 0], ffn_w_d[3, 0], key=6)
            post_norm_residual(3, 0)
        if CS >= 13:
            pre_norm(3, 1)
            b.fence()
            s5_es.close()
            att_es.close()
            dn_es = ExitStack()
            b.es = dn_es
            dn_mixer()
            b.es = es
            if DEBUG:
                dbg(y[:, 0, :], 512, 2560)
            post_norm_residual(3, 1)
            pre_norm(3, 2)
            ffn(ffn_w_gu[3, 1], ffn_w_d[3, 1], key=7)
            post_norm_residual(3, 2)
        b.dma("sp", yT_out.rearrange("(c p) t -> p c t", p=128), x[:])
        if STAGE >= 20:
            b.fence()
            if CTX_SKIP:
                att_es.close()
            else:
                dn_es.close()
            load_rows_T(cc[:, :], c_lat, 8)
            b.op("act", "activation", out=scond_lat[:, :, 0], in_=cc[:, :], func=AF.Silu)
            lat_es = ExitStack()
            b.es = lat_es
            ada_layer(0, scond_lat)
            a_lat(xT_lat)
            b.es = es
            b.fence()
            lat_es.close()
            if STAGE >= 21:
                lat_es = ExitStack()
                b.es = lat_es
                ada_layer(1, scond_lat)
                s5_lat()
                b.es = es
                b.fence()
                lat_es.close()
            if STAGE >= 22:
                lat_es = ExitStack()
                b.es = lat_es
                ada_layer(2, scond_lat)
                na_lat()
                b.es = es
                b.fence()
                lat_es.close()
            if STAGE >= 23:
                lat_es = ExitStack()
                b.es = lat_es
                ada_layer(3, scond_lat)
                dn_lat(zres)
                b.es = es
                b.fence()
                lat_es.close()
        if STAGE >= 20:
            for tl in range(NTL):
                lat_load(zres, tl)
                lat_store(ysamp_out, tl)
        else:
            b.raw("dve", lambda: nc.vector.memset(tmpA[0][:, :], 0.0), writes=[tmpA[0][:, :]])
            for tl in range(NTL):
                for c in range(KC):
                    b.dma("sp", ysamp_out[c * 128:(c + 1) * 128, tl * 512:(tl + 1) * 512], tmpA[0][:, :])
        if STAGE < 13:
            for j in range(32):
                b.dma("sp", dn_out[j, :, :], tmpA[0][:, 0:128])

        b.finish()
        if STAGE >= 20:
            pass
        elif STAGE >= 13:
            dn_es.close()
        else:
            if CS >= 9:
                s5_es.close()
            att_es.close()
    return nc


_PROG = None


def _dn_masks():
    i = np.arange(64)
    bef0 = (i[:, None] <= i[None, :]).astype(np.float32)
    bef1 = (i[:, None] >= i[None, :]).astype(np.float32)
    eye = np.eye(64, dtype=np.float32)
    m = np.stack([bef0, bef1, bef0.T, bef1.T, bef0.T - eye, bef1.T - eye], 1)
    return np.ascontiguousarray(m, dtype=np.float32)


def _rope_tables(L):
    n = 16
    inv = (np.float32(10000.0) ** (-np.arange(n, dtype=np.float32) / np.float32(n))).astype(np.float32)
    t = np.arange(L)
    ang_r = (t // 64).astype(np.float32)[:, None] * inv[None, :]
    ang_c = (t % 64).astype(np.float32)[:, None] * inv[None, :]
    return np.ascontiguousarray(np.concatenate([np.cos(ang_r), np.cos(ang_c), np.sin(ang_r), np.sin(ang_c)], 1), dtype=np.float32)


def _na_bias_gather(rpb):
    q = np.arange(64)[:, None]
    k = np.arange(64)[None, :]
    dc = np.clip(k - q, -15, 15) + 15
    g = rpb[:, :, dc]
    return np.ascontiguousarray(g.transpose(0, 2, 1, 3).reshape(16, 64, 960), dtype=np.float32)


def _na_colmask():
    col = np.arange(64)
    cs = np.clip(col - 8, 0, 48)
    ok = (col[None, :] >= cs[:, None]) & (col[None, :] < cs[:, None] + 16)
    return np.where(ok, 0.0, -30000.0).astype(np.float32)


def _band_mask():
    q = np.arange(128)[:, None]
    j = np.arange(384)[None, :] - 128
    return np.where(np.abs(j - q) <= 128, 0.0, -30000.0).astype(np.float32)


def prep_core(inputs, r, lseq=None):
    f = lambda a: np.ascontiguousarray(np.asarray(a, dtype=np.float32))
    L = LSEQ if lseq is None else lseq
    bsel = r % 2
    return {
        "xT_ctx": np.ascontiguousarray(f(inputs["x_prompt"])[2 * r:2 * r + 2].reshape(TCTX, D).T),
        "xT_lat": np.ascontiguousarray(f(inputs["x_sample"])[bsel, :L].T),
        "c_lat": f(inputs["c"])[bsel].reshape(8, 128),
        "cache_ak": f(inputs["cache_attn_k"])[bsel, 0].reshape(512, 256),
        "cache_av": f(inputs["cache_attn_v"])[bsel, 0].reshape(512, 256),
        "rope_cs": _rope_tables(L),
        "st5_re": f(inputs["state_s5_re"])[bsel, 0].reshape(64, 128),
        "st_dn": f(inputs["state_dn"])[bsel, 0].reshape(16, 128, 128),
        "cache_nk": f(inputs["cache_na_k"])[bsel, 0].reshape(512, D),
        "cache_nv": f(inputs["cache_na_v"])[bsel, 0].reshape(512, D),
        "na_bias": _na_bias_gather(f(inputs["na_rpb"])[0]),
        "na_cmask": _na_colmask(),
        "st5_im": f(inputs["state_s5_im"])[bsel, 0].reshape(64, 128),
        "band_mask": _band_mask(),
    }


def prep_shared(inputs, nl=4):
    f = lambda a: np.ascontiguousarray(np.asarray(a, dtype=np.float32))
    return {
        "ident": np.eye(128, dtype=np.float32),
        "c_ctx": f(inputs["c_ctx"]).reshape(8, 128),
        "norm_g": f(inputs["norm_g"]).reshape(192, 128),
        "w_ada": f(inputs["w_ada"][0:nl]),
        "b_ada": f(inputs["b_ada"]).reshape(288, 128),
        "ffn_w_gu": f(inputs["ffn_w_gu"][0:nl]),
        "ffn_w_d": f(inputs["ffn_w_d"][0:nl]),
        "a_w_qkv": f(inputs["a_w_qkv"])[0],
        "a_w_o": f(inputs["a_w_o"])[0],
        "a_sink": f(inputs["a_sink"]).reshape(1, 16),
        "s5_lam_re": f(inputs["s5_lam_re"]).reshape(64, 128),
        "s5_lam_im": f(inputs["s5_lam_im"]).reshape(64, 128),
        "s5_logdt": f(np.repeat(np.asarray(inputs["s5_log_dt"], np.float32).reshape(2, 32, 2), 64, axis=-1)).reshape(64, 128),
        "s5_b_re": f(inputs["s5_b_re"])[0],
        "s5_b_im": f(inputs["s5_b_im"])[0],
        "s5_c_re": f(inputs["s5_c_re"])[0],
        "s5_c_im": f(inputs["s5_c_im"])[0],
        "s5_d": f(inputs["s5_d"]).reshape(8, 128),
        "s5_w_glu": f(inputs["s5_w_glu"])[0],
        "na_w_qkv": f(inputs["na_w_qkv"])[0],
        "na_w_o": f(inputs["na_w_o"])[0],
        "dn_w_in": f(inputs["dn_w_in"])[0],
        "dn_conv_w": f(inputs["dn_conv_w"]).reshape(5, 16, 128).reshape(80, 128),
        "dn_w_ba": f(inputs["dn_w_ba"])[0],
        "dn_a_log": f(inputs["dn_a_log"]).reshape(1, 16),
        "dn_dt_bias": f(inputs["dn_dt_bias"]).reshape(1, 16),
        "dn_out_g": f(inputs["dn_out_g"]).reshape(1, 128),
        "dn_w_o": f(inputs["dn_w_o"])[0],
        "dn_mask": _dn_masks(),
    }


def kernel(**inputs):
    global _PROG
    f = lambda a: np.ascontiguousarray(np.asarray(a, dtype=np.float32))
    x_prompt = f(inputs["x_prompt"])
    if _PROG is None:
        _PROG = build_program()
    nc = _PROG
    shared = prep_shared(inputs)
    in_maps = []
    for r in range(NCORES):
        m = dict(shared)
        m.update(prep_core(inputs, r))
        in_maps.append(m)
    res = run_bass_kernel_spmd(nc, in_maps, core_ids=list(range(NCORES)))
    R = res.results
    y_prompt = np.stack([R[r]["out_yT_ctx"].T.reshape(2, SEQ, D) for r in range(NCORES)]).reshape(16, SEQ, D)
    new_k = np.concatenate([R[r]["out_attn_k"].reshape(2, 1, SEQ, 4, 64) for r in range(NCORES)], 0)
    new_v = np.concatenate([R[r]["out_attn_v"].reshape(2, 1, SEQ, 4, 64) for r in range(NCORES)], 0)
    s5re = np.concatenate([R[r]["out_s5_re"].reshape(2, 1, 2, 64, 64) for r in range(NCORES)], 0)
    s5im = np.concatenate([R[r]["out_s5_im"].reshape(2, 1, 2, 64, 64) for r in range(NCORES)], 0)
    nak = np.concatenate([R[r]["out_na_k"].reshape(2, 1, SEQ, 16, 64) for r in range(NCORES)], 0)
    nav = np.concatenate([R[r]["out_na_v"].reshape(2, 1, SEQ, 16, 64) for r in range(NCORES)], 0)
    ysamp = np.stack([R[r]["out_y_sample"].T for r in range(2)], 0)
    dn = np.concatenate([R[r]["out_dn"].reshape(2, 1, 2, 8, 128, 128) for r in range(NCORES)], 0)
    c32 = lambda a: np.ascontiguousarray(a, dtype=np.float32)
    return (c32(y_prompt), c32(ysamp), c32(new_k), c32(new_v), c32(s5re), c32(s5im), c32(nak), c32(nav), c32(dn))
```

```python
import numpy as np
from contextlib import ExitStack
import concourse.bass as bass
import concourse.mybir as mybir
from concourse.bass_utils import run_bass_kernel_spmd

F32 = mybir.dt.float32
BF16 = mybir.dt.bfloat16
AF = mybir.ActivationFunctionType
ALU = mybir.AluOpType
AX = mybir.AxisListType

D = 1024
KC = 8
DFF = 2816
FC = 22
NCORES = 8
TCTX = 512
SEQ = 256
EPS = 1e-6
WSLOT = 6144
NWS = 3
STAGE = 99
LSEQ = 4096
CTX_SKIP = False
SUB = 99
DEBUG = False


class Eng:
    def __init__(self, name, h, sem):
        self.name, self.h, self.sem = name, h, sem
        self.count = 0
        self.waited = {}


class Builder:
    def __init__(self, nc, es):
        self.nc, self.es = nc, es
        self.es_sem = es
        self.sems = []
        self.engs = {}
        for nm, h in [("pe", nc.tensor), ("act", nc.scalar), ("dve", nc.vector),
                      ("pool", nc.gpsimd), ("sp", nc.sync)]:
            sem = es.enter_context(nc.semaphore("sem_" + nm))
            self.sems.append(sem)
            e = Eng(nm, h, sem)
            e.key = len(self.sems) - 1
            self.engs[nm] = e
        self.track = {}
        self.dsem = {}
        self.out_events = []

    def sb(self, name, shape, dt):
        return self.es.enter_context(self.nc.sbuf_tensor(name, list(shape), dt))

    def ps(self, name, shape, dt=F32):
        return self.es.enter_context(self.nc.psum_tensor(name, list(shape), dt))

    def _nm(self, a):
        return a if isinstance(a, str) else a.tensor.name

    def _deps(self, reads, writes, skip_own_waw=None, own=None):
        deps = set()
        for r in reads:
            nm = self._nm(r)
            st = self.track.get(nm)
            if st and st[0]:
                deps.add(st[0])
            if st and nm.startswith("ps"):
                for ev in st[1]:
                    if ev[0] != own:
                        deps.add(ev)
        for w in writes:
            st = self.track.get(self._nm(w))
            if st:
                if st[0] and not (skip_own_waw is not None and st[0][0] == skip_own_waw):
                    deps.add(st[0])
                for ev in st[1]:
                    deps.add(ev)
        return deps

    def _wait(self, e, deps):
        best = {}
        for (k, v) in deps:
            if best.get(k, 0) < v:
                best[k] = v
        for k, v in best.items():
            if e.waited.get(k, 0) < v:
                e.h.wait_ge(self.sems[k], v)
                e.waited[k] = v

    def _commit(self, ev, reads, writes, accum=False):
        for r in reads:
            st = self.track.setdefault(self._nm(r), [None, []])
            st[1].append(ev)
        for w in writes:
            nm = self._nm(w)
            if accum and nm in self.track:
                self.track[nm][0] = ev
            else:
                self.track[nm] = [ev, []]

    def op(self, eng, fname, accum=False, extra_reads=(), extra_writes=(), **kw):
        e = self.engs[eng]
        reads, writes = list(extra_reads), list(extra_writes)
        for k, v in kw.items():
            if isinstance(v, bass.AP):
                if k in ("out", "accum_out"):
                    writes.append(v)
                else:
                    reads.append(v)
        deps = self._deps(reads, writes, skip_own_waw=(e.key if accum else None), own=e.key)
        self._wait(e, deps)
        inst = getattr(e.h, fname)(**kw)
        e.count += 1
        inst.then_inc(e.sem, 1)
        ev = (e.key, e.count)
        self._commit(ev, reads, writes, accum=accum)
        return ev

    def raw(self, eng, fn, reads=(), writes=()):
        e = self.engs[eng]
        self._wait(e, self._deps(list(reads), list(writes), own=e.key))
        inst = fn()
        e.count += 1
        inst.then_inc(e.sem, 1)
        ev = (e.key, e.count)
        self._commit(ev, list(reads), list(writes))
        return ev

    def dma(self, q, out, in_, **kw):
        e = self.engs[q]
        deps = self._deps([in_], [out])
        self._wait(e, deps)
        nm = self._nm(out)
        if nm not in self.dsem:
            sem = self.es_sem.enter_context(self.nc.semaphore("dsem_" + nm))
            self.sems.append(sem)
            self.dsem[nm] = [len(self.sems) - 1, 0]
        ds = self.dsem[nm]
        e.h.dma_start(out=out, in_=in_, **kw).then_inc(self.sems[ds[0]], 16)
        ds[1] += 16
        ev = (ds[0], ds[1])
        self._commit(ev, [in_], [out])
        if out.tensor.name.startswith("out_"):
            self.out_events.append(ev)
        return ev

    def fence(self):
        evs = {(x.key, x.count) for x in self.engs.values() if x.count > 0}
        evs |= {(k, c) for (k, c) in self.dsem.values() if c > 0}
        for e in self.engs.values():
            self._wait(e, {ev for ev in evs if ev[0] != e.key})

    def finish(self):
        e = self.engs["sp"]
        self._wait(e, set(self.out_events))
        self._wait(e, {(x.key, x.count) for x in self.engs.values() if x.count > 0 and x.name != "sp"})


def build_program(nl=4):
    nc = bass.Bass("TRN2", target_bir_lowering=False)

    def din(name, shape):
        return nc.dram_tensor(name, list(shape), F32, kind="ExternalInput").ap()

    def dout(name, shape):
        return nc.dram_tensor("out_" + name, list(shape), F32, kind="ExternalOutput").ap()

    T = TCTX
    xT_in = din("xT_ctx", [D, T])
    ident_in = din("ident", [128, 128])
    c_ctx = din("c_ctx", [8, 128])
    norm_g = din("norm_g", [192, 128])
    w_ada = din("w_ada", [nl, D, 9 * D])
    b_ada = din("b_ada", [288, 128])
    ffn_w_gu = din("ffn_w_gu", [nl, 2, D, 2 * DFF])
    ffn_w_d = din("ffn_w_d", [nl, 2, DFF, D])
    a_w_qkv = din("a_w_qkv", [D, 1536])
    a_w_o = din("a_w_o", [D, D])
    a_sink = din("a_sink", [1, 16])
    s5_lam_re = din("s5_lam_re", [64, 128])
    s5_lam_im = din("s5_lam_im", [64, 128])
    s5_logdt = din("s5_logdt", [64, 128])
    s5_b_re = din("s5_b_re", [2, 64, 64, 16])
    s5_b_im = din("s5_b_im", [2, 64, 64, 16])
    s5_c_re = din("s5_c_re", [2, 64, 16, 64])
    s5_c_im = din("s5_c_im", [2, 64, 16, 64])
    s5_d = din("s5_d", [8, 128])
    s5_w_glu = din("s5_w_glu", [D, 2 * D])
    na_w_qkv = din("na_w_qkv", [D, 3 * D])
    na_w_o = din("na_w_o", [D, D])
    dn_w_in = din("dn_w_in", [D, 3 * D])
    dn_conv_w = din("dn_conv_w", [80, 128])
    dn_w_ba = din("dn_w_ba", [2, D, 16])
    dn_a_log = din("dn_a_log", [1, 16])
    dn_dt_bias = din("dn_dt_bias", [1, 16])
    dn_out_g = din("dn_out_g", [1, 128])
    dn_w_o = din("dn_w_o", [D, D])
    dn_mask = din("dn_mask", [64, 6, 64])
    xT_lat = din("xT_lat", [D, LSEQ])
    c_lat = din("c_lat", [8, 128])
    cache_ak = din("cache_ak", [512, 256])
    cache_av = din("cache_av", [512, 256])
    rope_cs = din("rope_cs", [LSEQ, 64])
    band_mask = din("band_mask", [128, 384])
    zres = nc.dram_tensor("zres", [D, LSEQ], F32, kind="Internal").ap()
    wsc = nc.dram_tensor("wsc", [8, 15, 128, WSLOT], BF16, kind="Internal").ap()
    ysc = nc.dram_tensor("ysc", [D, LSEQ], F32, kind="Internal").ap()
    rot = nc.dram_tensor("rot", [64, 128, 1536], F32, kind="Internal").ap()
    w5sc = nc.dram_tensor("w5sc", [64, 128, 512], BF16, kind="Internal").ap()
    cache_nk = din("cache_nk", [512, D])
    cache_nv = din("cache_nv", [512, D])
    na_bias = din("na_bias", [16, 64, 960])
    na_cmask = din("na_cmask", [64, 64])
    st_dn = din("st_dn", [16, 128, 128])
    pjsc = nc.dram_tensor("pjsc", [16, 128, LSEQ + 4], F32, kind="Internal").ap()
    zsc = nc.dram_tensor("zsc", [8, 128, LSEQ], BF16, kind="Internal").ap()
    qsc = nc.dram_tensor("qsc", [8, 128, LSEQ], BF16, kind="Internal").ap()
    ksc = nc.dram_tensor("ksc", [8, 128, LSEQ], BF16, kind="Internal").ap()
    vsc = nc.dram_tensor("vsc", [8, 64, LSEQ // 64, 128], BF16, kind="Internal").ap()
    osc = nc.dram_tensor("osc", [2, 64, LSEQ // 64, 128], F32, kind="Internal").ap()
    st5_re = din("st5_re", [64, 128])
    st5_im = din("st5_im", [64, 128])

    yT_out = dout("yT_ctx", [D, T])
    k_out = dout("attn_k", [T, 256])
    v_out = dout("attn_v", [T, 256])
    nak_out = dout("na_k", [T, D])
    nav_out = dout("na_v", [T, D])
    ysamp_out = dout("y_sample", [D, LSEQ])
    dn_out = dout("dn", [32, 128, 128])
    s5re_out = dout("s5_re", [128, 128])
    s5im_out = dout("s5_im", [128, 128])
    dbg_out = dout("dbg", [128, 4096]) if DEBUG else None

    with ExitStack() as es:
        b = Builder(nc, es)
        x = b.sb("x", [128, KC, T], F32)
        h = b.sb("h", [128, KC, T], BF16)
        y = b.sb("y", [128, KC, T], F32)
        act = b.sb("act", [128, FC, T], BF16)
        sq = act
        rstd = b.sb("rstd", [128, T], F32)
        tmpA = [b.sb(f"tmpA{i}", [128, T], F32) for i in range(2)]
        ident = b.sb("ident_sb", [128, 128], F32)
        ident_bf = b.sb("ident_bf", [128, 128], BF16)
        ones_bf = b.sb("ones_bf", [128, 128], BF16)
        wslots = [b.sb(f"wslot{i}", [128, WSLOT], BF16) for i in range(NWS)]
        ng = b.sb("ng", [128, 192], F32)
        bada = b.sb("bada", [128, 288], F32)
        scond = b.sb("scond", [128, KC, 1], BF16)
        scond_lat = b.sb("scond_lat", [128, KC, 1], BF16)
        cc = b.sb("cc", [128, KC], F32)
        mods = [b.sb(f"mods{i}", [128, 72], F32) for i in range(4)]
        Acoef = b.sb("Acoef", [128, 4 * 3 * KC], F32)
        Gcoef = b.sb("Gcoef", [128, 4 * 3 * KC], F32)
        rows_tmp = b.sb("rows_tmp", [96, 128], F32)
        oT = b.sb("oT", [128, KC, T], BF16)
        sink_bc = b.sb("sink_bc", [128, 16], F32)
        st_m = [b.sb(f"st_m{i}", [128, 8], F32) for i in range(2)]
        att_es = ExitStack()
        b.es = att_es
        qT = b.sb("qT", [128, 8, T], BF16)
        kT = b.sb("kT", [128, 8, T], BF16)
        vtok = b.sb("vtok", [128, 4, 1024], BF16)
        kv32 = [b.sb(f"kv32_{i}", [128, 512], F32) for i in range(2)]
        Pm = [b.sb(f"Pm{i}", [128, 256], BF16) for i in range(2)]
        PT = [b.sb(f"PT{i}", [128, 2, 128], BF16) for i in range(2)]
        otok = b.sb("otok", [128, 1024], BF16)
        b.es = es
        psG = [b.ps(f"psG{i}", [128, 512]) for i in range(2)]
        psU = [b.ps(f"psU{i}", [128, 512]) for i in range(2)]
        psY = [b.ps(f"psY{i}", [128, 512]) for i in range(2)]
        psS = b.ps("psS", [128, 512])
        psM = psS
        psT_all = b.ps("psT", [128, 1024], BF16)
        psTb = [psT_all[:, i * 256:(i + 1) * 256].rearrange("p (a n) -> p a n", a=2) for i in range(2)]
        psTc = [psTb[0], psS[:, :].bitcast(BF16)[:, 0:256].rearrange("p (a n) -> p a n", a=2)]

        def run_interleaved(gens):
            alive = [True] * len(gens)
            while any(alive):
                for gi_ in range(len(gens)):
                    if alive[gi_]:
                        try:
                            next(gens[gi_])
                        except StopIteration:
                            alive[gi_] = False

        wctr = [0]

        def wslot():
            s = wslots[wctr[0] % NWS]
            wctr[0] += 1
            return s

        b.dma("sp", ident[:], ident_in)
        b.op("dve", "tensor_copy", out=ident_bf[:], in_=ident[:])
        b.raw("dve", lambda: nc.vector.memset(ones_bf[:], 1.0), writes=[ones_bf[:]])

        def load_rows_T(dst_ap, src_rows_ap, nrows):
            b.dma("sp", rows_tmp[0:nrows, :], src_rows_ap)
            b.op("pe", "transpose", out=psM[:, 0:nrows], in_=rows_tmp[0:nrows, :], identity=ident[0:nrows, 0:nrows])
            b.op("dve", "tensor_copy", out=dst_ap, in_=psM[:, 0:nrows])

        load_rows_T(ng[:, 0:96], norm_g[0:96, :], 96)
        load_rows_T(ng[:, 96:192], norm_g[96:192, :], 96)
        for i in range(3):
            load_rows_T(bada[:, 96 * i:96 * (i + 1)], b_ada[96 * i:96 * (i + 1), :], 96)
        load_rows_T(cc[:, :], c_ctx, 8)
        b.op("act", "activation", out=scond[:, :, 0], in_=cc[:, :], func=AF.Silu)
        b.dma("sp", sink_bc[:], a_sink.broadcast_to([128, 16]))

        b.dma("sp", x[:], xT_in.rearrange("(c p) t -> p c t", p=128))

        def ada_layer(i, sc=None):
            sc = scond if sc is None else sc
            for pc in range(18):
                sl = wslot()
                v = sl[:, 0:KC * 512].rearrange("p (k n) -> p k n", k=KC)
                b.dma("pool", v, w_ada[i, :, pc * 512:(pc + 1) * 512].rearrange("(k p) n -> p k n", p=128))
                for cj in range(4):
                    j = pc * 4 + cj
                    for k in range(KC):
                        b.op("pe", "matmul", accum=(k > 0), out=psM[:, j:j + 1],
                             lhsT=v[:, k, cj * 128:(cj + 1) * 128], rhs=sc[:, k, :],
                             start=(k == 0), stop=(k == KC - 1))
            b.op("dve", "tensor_tensor", out=mods[i][:, :], in0=psM[:, 0:72], in1=bada[:, 72 * i:72 * (i + 1)],
                 op=ALU.add)
            for s in range(3):
                w = 1.0 if s == 1 else 0.5
                col = (i * 3 + s) * KC
                gpre = ng[:, (i * 6 + 2 * s) * KC:(i * 6 + 2 * s + 1) * KC]
                gpost = ng[:, (i * 6 + 2 * s + 1) * KC:(i * 6 + 2 * s + 2) * KC]
                scale_ = mods[i][:, (3 * s + 1) * KC:(3 * s + 2) * KC]
                gate_ = mods[i][:, (3 * s + 2) * KC:(3 * s + 3) * KC]
                b.op("dve", "scalar_tensor_tensor", out=Acoef[:, col:col + KC], in0=scale_, scalar=1.0, in1=gpre,
                     op0=ALU.add, op1=ALU.mult)
                b.op("dve", "scalar_tensor_tensor", out=Gcoef[:, col:col + KC], in0=gate_, scalar=w, in1=gpost,
                     op0=ALU.mult, op1=ALU.mult)

        def rms_stats(src):
            for c in range(KC):
                b.op("act", "activation", out=sq[:, c, :], in_=src[:, c, :], func=AF.Square)
            for c in range(KC):
                b.op("pe", "matmul", accum=(c > 0), out=psS[:, 0:T], lhsT=ones_bf[:, :], rhs=sq[:, c, :],
                     start=(c == 0), stop=(c == KC - 1))
            b.op("act", "activation", out=rstd[:, :], in_=psS[:, 0:T], func=AF.Sqrt, scale=1.0 / D, bias=EPS)
            b.op("dve", "reciprocal", out=rstd[:, :], in_=rstd[:, :])

        def pre_norm(i, s):
            rms_stats(x)
            col = (i * 3 + s) * KC
            for c in range(KC):
                t = tmpA[c % 2]
                b.op("dve", "tensor_tensor", out=t[:, :], in0=x[:, c, :], in1=rstd[:, :], op=ALU.mult)
                b.op("act", "activation", out=h[:, c, :], in_=t[:, :], func=AF.Identity,
                     scale=Acoef[:, col + c:col + c + 1], bias=mods[i][:, 3 * s * KC + c:3 * s * KC + c + 1])

        def post_norm_residual(i, s):
            rms_stats(y)
            col = (i * 3 + s) * KC
            for c in range(KC):
                t = tmpA[c % 2]
                b.op("dve", "tensor_tensor", out=t[:, :], in0=y[:, c, :], in1=rstd[:, :], op=ALU.mult)
                b.op("dve", "tensor_scalar", out=t[:, :], in0=t[:, :], scalar1=Gcoef[:, col + c:col + c + 1],
                     scalar2=None, op0=ALU.mult)
                b.op("dve", "tensor_tensor", out=x[:, c, :], in0=t[:, :], in1=x[:, c, :], op=ALU.add)

        def ffn(wgu, wd, key=None, consume=False):
            for jp in range(FC // 2):
                sl = wslot()
                v = sl[:, 0:KC * 512].rearrange("p (k n) -> p k n", k=KC)
                if consume:
                    b.dma("pool", sl[:, 0:KC * 512], wsc[key, jp, :, 0:KC * 512])
                else:
                    b.dma("pool", v[:, :, 0:256], wgu[:, jp * 256:(jp + 1) * 256].rearrange("(k p) n -> p k n", p=128))
                    b.dma("pool", v[:, :, 256:512],
                          wgu[:, DFF + jp * 256:DFF + (jp + 1) * 256].rearrange("(k p) n -> p k n", p=128))
                    if key is not None:
                        b.dma("sp", wsc[key, jp, :, 0:KC * 512], sl[:, 0:KC * 512])
                for jj in range(2):
                    j = jp * 2 + jj
                    pg, pu = psG[j % 2], psU[j % 2]
                    for k in range(KC):
                        b.op("pe", "matmul", accum=(k > 0), out=pg[:, 0:T], lhsT=v[:, k, jj * 128:(jj + 1) * 128],
                             rhs=h[:, k, :], start=(k == 0), stop=(k == KC - 1))
                    for k in range(KC):
                        b.op("pe", "matmul", accum=(k > 0), out=pu[:, 0:T],
                             lhsT=v[:, k, 256 + jj * 128:256 + (jj + 1) * 128],
                             rhs=h[:, k, :], start=(k == 0), stop=(k == KC - 1))
                    t = tmpA[j % 2]
                    b.op("act", "activation", out=t[:, :], in_=pg[:, 0:T], func=AF.Silu)
                    b.op("dve", "tensor_tensor", out=act[:, j, :], in0=t[:, :], in1=pu[:, 0:T], op=ALU.mult)
            for op_ in range(4):
                sl = wslot()
                v = sl[:, 0:FC * 256].rearrange("p (k n) -> p k n", k=FC)
                if consume:
                    b.dma("pool", sl[:, 0:FC * 256], wsc[key, 11 + op_, :, 0:FC * 256])
                else:
                    b.dma("pool", v, wd[:, op_ * 256:(op_ + 1) * 256].rearrange("(k p) n -> p k n", p=128))
                    if key is not None:
                        b.dma("sp", wsc[key, 11 + op_, :, 0:FC * 256], sl[:, 0:FC * 256])
                for oo in range(2):
                    o = op_ * 2 + oo
                    py = psY[o % 2]
                    for j in range(FC):
                        b.op("pe", "matmul", accum=(j > 0), out=py[:, 0:T], lhsT=v[:, j, oo * 128:(oo + 1) * 128],
                             rhs=act[:, j, :], start=(j == 0), stop=(j == FC - 1))
                    b.op("act", "activation", out=y[:, o, :], in_=py[:, 0:T], func=AF.Copy)

        def proj_featmajor(dst, dst_chunk0, wsrc, col0, nchunks, src_act, dup64=False):
            for g0 in range(0, nchunks, 4):
                ng_ = min(4, nchunks - g0)
                sl = wslot()
                v = sl[:, 0:KC * 512].rearrange("p (k n) -> p k n", k=KC)
                if dup64:
                    sl0 = wslot()
                    v0 = sl0[:, 0:KC * 256].rearrange("p (k n) -> p k n", k=KC)
                    b.dma("pool", v0[:, :, 0:ng_ * 64],
                          wsrc[:, col0 + g0 * 64:col0 + (g0 + ng_) * 64].rearrange("(k p) n -> p k n", p=128))
                    v4 = sl[:, 0:KC * 512].rearrange("p (k m a d) -> p k m a d", k=KC, m=4, a=2)
                    for a in range(2):
                        b.op("dve", "tensor_copy", out=v4[:, :, 0:ng_, a, :],
                             in_=v0[:, :, 0:ng_ * 64].rearrange("p k (m d) -> p k m d", d=64))
                else:
                    b.dma("pool", v[:, :, 0:ng_ * 128],
                          wsrc[:, col0 + g0 * 128:col0 + (g0 + ng_) * 128].rearrange("(k p) n -> p k n", p=128))
                for m in range(ng_):
                    pq = psG[m % 2]
                    for k in range(KC):
                        b.op("pe", "matmul", accum=(k > 0), out=pq[:, 0:T], lhsT=v[:, k, m * 128:(m + 1) * 128],
                             rhs=src_act[:, k, :], start=(k == 0), stop=(k == KC - 1))
                    b.op("act", "activation", out=dst[:, dst_chunk0 + g0 + m, :], in_=pq[:, 0:T], func=AF.Copy)

        def proj_out_featmajor(wsrc, src_act):
            for g0 in range(0, KC, 4):
                sl = wslot()
                v = sl[:, 0:KC * 512].rearrange("p (k n) -> p k n", k=KC)
                b.dma("pool", v, wsrc[:, g0 * 128:(g0 + 4) * 128].rearrange("(k p) n -> p k n", p=128))
                for m in range(4):
                    py = psY[m % 2]
                    for k in range(KC):
                        b.op("pe", "matmul", accum=(k > 0), out=py[:, 0:T], lhsT=v[:, k, m * 128:(m + 1) * 128],
                             rhs=src_act[:, k, :], start=(k == 0), stop=(k == KC - 1))
                    b.op("act", "activation", out=y[:, g0 + m, :], in_=py[:, 0:T], func=AF.Copy)

        def proj_tokmajor(wsrc, col0, ncols, out_dram, vdst_col0=None):
            for c0 in range(0, ncols, 512):
                n = min(512, ncols - c0)
                sl = wslot()
                v = sl[:, 0:KC * 512].rearrange("p (k n) -> p k n", k=KC)
                b.dma("pool", v[:, :, 0:n], wsrc[:, col0 + c0:col0 + c0 + n].rearrange("(k p) n -> p k n", p=128))
                for tb in range(T // 128):
                    pq = psU[tb % 2]
                    for k in range(KC):
                        b.op("pe", "matmul", accum=(k > 0), out=pq[:, 0:n], lhsT=h[:, k, tb * 128:(tb + 1) * 128],
                             rhs=v[:, k, 0:n], start=(k == 0), stop=(k == KC - 1))
                    t32 = kv32[tb % 2]
                    b.op("dve", "tensor_copy", out=t32[:, 0:n], in_=pq[:, 0:n])
                    if out_dram is not None:
                        b.dma("sp", out_dram[tb * 128:(tb + 1) * 128, c0:c0 + n], t32[:, 0:n])
                    if vdst_col0 is not None:
                        b.op("act", "activation", out=vtok[:, tb, vdst_col0 + c0:vdst_col0 + c0 + n], in_=t32[:, 0:n],
                             func=AF.Copy)

        def attn_ctx(nheads, head_map, sink_cols, scale):
            nseq = T // SEQ
            nqb = SEQ // 128
            it = 0
            for s in range(nseq):
                for qb in range(nqb):
                    q0 = s * SEQ + qb * 128
                    for hh in range(nheads):
                        qc, base, kc, vc0 = head_map[hh]
                        pS = psG[it % 2]
                        b.op("pe", "matmul", out=pS[:, 0:SEQ], lhsT=qT[base:base + 64, qc, q0:q0 + 128],
                             rhs=kT[base:base + 64, kc, s * SEQ:(s + 1) * SEQ], start=True, stop=True)
                        sm = st_m[it % 2]
                        b.op("dve", "reduce_max", out=sm[:, 0:1], in_=pS[:, 0:SEQ], axis=AX.X)
                        if sink_cols is not None:
                            b.op("dve", "tensor_scalar", out=sm[:, 1:2], in0=sm[:, 0:1], scalar1=scale,
                                 scalar2=sink_bc[:, hh:hh + 1], op0=ALU.mult, op1=ALU.max)
                            b.op("dve", "tensor_scalar", out=sm[:, 1:2], in0=sm[:, 1:2], scalar1=-1.0, scalar2=None,
                                 op0=ALU.mult)
                        else:
                            b.op("dve", "tensor_scalar", out=sm[:, 1:2], in0=sm[:, 0:1], scalar1=-scale, scalar2=None,
                                 op0=ALU.mult)
                        P = Pm[it % 2]
                        b.op("act", "activation", out=P[:, 0:SEQ], in_=pS[:, 0:SEQ], func=AF.Exp, scale=scale,
                             bias=sm[:, 1:2], accum_out=sm[:, 2:3])
                        if sink_cols is not None:
                            b.op("act", "activation", out=sm[:, 3:4], in_=sink_bc[:, hh:hh + 1], func=AF.Exp,
                                 bias=sm[:, 1:2], scale=1.0)
                            b.op("dve", "tensor_tensor", out=sm[:, 4:5], in0=sm[:, 2:3], in1=sm[:, 3:4], op=ALU.add)
                            b.op("dve", "reciprocal", out=sm[:, 5:6], in_=sm[:, 4:5])
                        else:
                            b.op("dve", "reciprocal", out=sm[:, 5:6], in_=sm[:, 2:3])
                        pt_sb = PT[it % 2]
                        if SUB < 3:
                            it += 1
                            continue
                        for kb in range(nqb):
                            b.op("pe", "transpose", out=psTb[it % 2][:, kb, :], in_=P[:, kb * 128:(kb + 1) * 128],
                                 identity=ident_bf[:, :])
                        b.op("dve", "tensor_copy", out=pt_sb[:, :, :], in_=psTb[it % 2][:, :, :])
                        if SUB < 4:
                            it += 1
                            continue
                        pO = psY[it % 2]
                        for kb in range(nqb):
                            b.op("pe", "matmul", accum=(kb > 0), out=pO[:, 0:64], lhsT=pt_sb[:, kb, :],
                                 rhs=vtok[:, s * nqb + kb, vc0:vc0 + 64], start=(kb == 0), stop=(kb == nqb - 1))
                        b.op("dve", "tensor_scalar", out=otok[:, hh * 64:(hh + 1) * 64], in0=pO[:, 0:64],
                             scalar1=sm[:, 5:6], scalar2=None, op0=ALU.mult)
                        it += 1
                    for c in range(KC):
                        b.op("pe", "transpose", out=psTb[c % 2][:, 0, :], in_=otok[:, c * 128:(c + 1) * 128],
                             identity=ident_bf[:, :])
                        b.op("act", "activation", out=oT[:, c, q0:q0 + 128], in_=psTb[c % 2][:, 0, :], func=AF.Copy)


        def s5_setup(npow=8, pfx="s5p_"):
            P = {}
            def t64(name):
                P[name] = b.sb(pfx + name, [128, 64], F32)
                return P[name]
            for nm, src in (("lamre", s5_lam_re), ("lamim", s5_lam_im), ("logdt", s5_logdt)):
                t64(nm)
                load_rows_T(P[nm][:, :], src, 64)
            for nm in ("dt", "lr", "li", "mag", "sn", "cs", "ar", "ai", "fr", "fi", "t1", "t2", "t3", "kk"):
                t64(nm)
            dsk = b.sb(pfx + "dskip", [128, KC], F32)
            load_rows_T(dsk[:, :], s5_d, 8)
            P["dskip"] = dsk
            b.op("act", "activation", out=P["dt"][:, :], in_=P["logdt"][:, :], func=AF.Exp)
            b.op("dve", "tensor_tensor", out=P["lr"][:, :], in0=P["lamre"][:, :], in1=P["dt"][:, :], op=ALU.mult)
            b.op("dve", "tensor_tensor", out=P["li"][:, :], in0=P["lamim"][:, :], in1=P["dt"][:, :], op=ALU.mult)
            b.op("act", "activation", out=P["mag"][:, :], in_=P["lr"][:, :], func=AF.Exp)

            def sin_of(dst, ang, shift):
                b.op("dve", "tensor_scalar", out=P["t1"][:, :], in0=ang, scalar1=float(shift), scalar2=None, op0=ALU.add)
                b.raw("dve", lambda: nc.vector.memset(P["kk"][:, :], 0.0), writes=[P["kk"][:, :]])
                for j in range(1, 5):
                    b.op("dve", "tensor_scalar", out=P["t2"][:, :], in0=P["t1"][:, :],
                         scalar1=float((2 * j - 1) * np.pi), scalar2=None, op0=ALU.is_gt)
                    b.op("dve", "tensor_tensor", out=P["kk"][:, :], in0=P["kk"][:, :], in1=P["t2"][:, :], op=ALU.add)
                b.op("dve", "tensor_scalar", out=P["kk"][:, :], in0=P["kk"][:, :], scalar1=float(-2 * np.pi),
                     scalar2=None, op0=ALU.mult)
                b.op("dve", "tensor_tensor", out=P["t1"][:, :], in0=P["t1"][:, :], in1=P["kk"][:, :], op=ALU.add)
                b.op("act", "activation", out=dst, in_=P["t1"][:, :], func=AF.Sin)

            sin_of(P["sn"][:, :], P["li"][:, :], 0.0)
            sin_of(P["cs"][:, :], P["li"][:, :], np.pi / 2)
            b.op("dve", "tensor_tensor", out=P["ar"][:, :], in0=P["mag"][:, :], in1=P["cs"][:, :], op=ALU.mult)
            b.op("dve", "tensor_tensor", out=P["ai"][:, :], in0=P["mag"][:, :], in1=P["sn"][:, :], op=ALU.mult)
            TT = lambda o, a_, b_, op: b.op("dve", "tensor_tensor", out=o, in0=a_, in1=b_, op=op)
            t1, t2, t3 = P["t1"][:, :], P["t2"][:, :], P["t3"][:, :]
            TT(t1, P["lamre"][:, :], P["lamre"][:, :], ALU.mult)
            TT(t2, P["lamim"][:, :], P["lamim"][:, :], ALU.mult)
            TT(t1, t1, t2, ALU.add)
            b.op("dve", "reciprocal", out=t3, in_=t1)
            b.op("dve", "tensor_scalar", out=P["kk"][:, :], in0=P["ar"][:, :], scalar1=-1.0, scalar2=None, op0=ALU.add)
            TT(t1, P["kk"][:, :], P["lamre"][:, :], ALU.mult)
            TT(t2, P["ai"][:, :], P["lamim"][:, :], ALU.mult)
            TT(t1, t1, t2, ALU.add)
            TT(P["fr"][:, :], t1, t3, ALU.mult)
            TT(t1, P["ai"][:, :], P["lamre"][:, :], ALU.mult)
            TT(t2, P["kk"][:, :], P["lamim"][:, :], ALU.mult)
            TT(t1, t1, t2, ALU.subtract)
            TT(P["fi"][:, :], t1, t3, ALU.mult)
            pw = [(P["ar"], P["ai"])]
            for k in range(1, npow):
                pr, pi_ = pw[-1]
                nr = b.sb(f"{pfx}pr{k}", [128, 64], F32)
                ni = b.sb(f"{pfx}pi{k}", [128, 64], F32)
                TT(t1, pr[:, :], pr[:, :], ALU.mult)
                TT(t2, pi_[:, :], pi_[:, :], ALU.mult)
                TT(nr[:, :], t1, t2, ALU.subtract)
                TT(t1, pr[:, :], pi_[:, :], ALU.mult)
                b.op("dve", "tensor_scalar", out=ni[:, :], in0=t1, scalar1=2.0, scalar2=None, op0=ALU.mult)
                pw.append((nr, ni))
            P["pw"] = pw
            return P

        def s5_mixer(P):
            nat = {}
            for t4 in range(4):
                for nm in ("bre", "bim", "cre", "cim"):
                    tl = b.sb(f"s5n_{nm}{t4}", [128, 128], F32)
                    b.raw("dve", lambda tl=tl: nc.vector.memset(tl[:, :], 0.0), writes=[tl[:, :]])
                    nat[(nm, t4)] = tl
            wB = [[b.sb(f"s5_wB{ri}{i}", [128, 128], BF16) for i in range(2)] for ri in range(2)]
            wC = [[b.sb(f"s5_wC{ri}{i}", [128, 128], BF16) for i in range(2)] for ri in range(2)]
            c32 = [b.sb(f"s5_c32{i}", [128, 128], F32) for i in range(2)]
            cm = [b.sb(f"s5_cm{i}", [128, 128], F32) for i in range(4)]
            xr = [b.sb(f"s5_xr{i}", [128, 2, SEQ], F32) for i in range(1)] * 2
            xi = [b.sb(f"s5_xi{i}", [128, 2, SEQ], F32) for i in range(1)] * 2
            xrb = [b.sb(f"s5_xrb{i}", [128, T], BF16) for i in range(1)] * 2
            xib = [b.sb(f"s5_xib{i}", [128, T], BF16) for i in range(1)] * 2
            tm = [b.sb(f"s5_tm{i}", [128, 2, SEQ], F32) for i in range(4)]
            fin_r = b.sb("s5_finr", [128, 64, 2], F32)
            fin_i = b.sb("s5_fini", [128, 64, 2], F32)
            fin_o = b.sb("s5_fino", [128, 128], F32)
            it = 0
            for d in range(2):
                for t in range(32):
                    tau = d * 32 + t
                    ch, t4 = t // 4, t % 4
                    pp = it % 2
                    for g2 in range(2):
                        g = 2 * t + g2
                        cb = (2 * t4 + g2) * 16
                        b.dma("sp", nat[("bre", t4)][g2 * 64:(g2 + 1) * 64, cb:cb + 16], s5_b_re[d, g, :, :])
                        b.dma("sp", nat[("bim", t4)][g2 * 64:(g2 + 1) * 64, cb:cb + 16], s5_b_im[d, g, :, :])
                        b.dma("sp", nat[("cre", t4)][cb:cb + 16, g2 * 64:(g2 + 1) * 64], s5_c_re[d, g, :, :])
                        b.dma("sp", nat[("cim", t4)][cb:cb + 16, g2 * 64:(g2 + 1) * 64], s5_c_im[d, g, :, :])
                    for ri, nm in enumerate(("bre", "bim")):
                        b.op("pe", "transpose", out=psS[:, 0:128], in_=nat[(nm, t4)][:, :], identity=ident[:, :])
                        b.op("act", "activation", out=wB[ri][pp][:, :], in_=psS[:, 0:128], func=AF.Copy)
                    for ri, nm in enumerate(("cre", "cim")):
                        b.op("pe", "transpose", out=psS[:, 0:128], in_=nat[(nm, t4)][:, :], identity=ident[:, :])
                        b.op("act", "activation", out=c32[ri][:, :], in_=psS[:, 0:128], func=AF.Copy)
                    fr_, fi_ = P["fr"][:, tau:tau + 1], P["fi"][:, tau:tau + 1]
                    TS = lambda o, i_, sc: b.op("dve", "tensor_scalar", out=o, in0=i_, scalar1=sc, scalar2=None, op0=ALU.mult)
                    TS(cm[0][:, :], c32[0][:, :], fr_)
                    TS(cm[1][:, :], c32[1][:, :], fi_)
                    b.op("dve", "tensor_tensor", out=wC[0][pp][:, :], in0=cm[0][:, :], in1=cm[1][:, :], op=ALU.subtract)
                    TS(cm[2][:, :], c32[0][:, :], fi_)
                    TS(cm[3][:, :], c32[1][:, :], fr_)
                    b.op("dve", "tensor_tensor", out=cm[2][:, :], in0=cm[2][:, :], in1=cm[3][:, :], op=ALU.add)
                    b.op("dve", "tensor_scalar", out=wC[1][pp][:, :], in0=cm[2][:, :], scalar1=-1.0, scalar2=None, op0=ALU.mult)
                    X = (xr[pp], xi[pp])
                    for ri in range(2):
                        pq = psG[ri]
                        b.op("pe", "matmul", out=pq[:, 0:T], lhsT=wB[ri][pp][:, :], rhs=h[:, ch, :], start=True, stop=True)
                        b.op("act", "activation", out=X[ri][:, :, :].rearrange("p a n -> p (a n)"), in_=pq[:, 0:T], func=AF.Copy)
                    for k in range(8):
                        sh = 1 << k
                        pr, pi_ = P["pw"][k]
                        sr_, si_ = pr[:, tau:tau + 1], pi_[:, tau:tau + 1]
                        if d == 0:
                            src = lambda A: A[:, :, 0:SEQ - sh]
                            dst = lambda A: A[:, :, sh:SEQ]
                        else:
                            src = lambda A: A[:, :, sh:SEQ]
                            dst = lambda A: A[:, :, 0:SEQ - sh]
                        tt = [tm[j] for j in range(4)]
                        n = SEQ - sh
                        MUL = lambda o, i_, sc: b.op("act", "activation", out=o, in_=i_, func=AF.Identity, scale=sc)
                        MUL(tt[0][:, :, 0:n], src(X[0]), sr_)
                        MUL(tt[1][:, :, 0:n], src(X[1]), si_)
                        MUL(tt[2][:, :, 0:n], src(X[1]), sr_)
                        MUL(tt[3][:, :, 0:n], src(X[0]), si_)
                        b.op("dve", "tensor_tensor", out=dst(X[0]), in0=dst(X[0]), in1=tt[0][:, :, 0:n], op=ALU.add)
                        b.op("dve", "tensor_tensor", out=dst(X[0]), in0=dst(X[0]), in1=tt[1][:, :, 0:n], op=ALU.subtract)
                        b.op("dve", "tensor_tensor", out=dst(X[1]), in0=dst(X[1]), in1=tt[2][:, :, 0:n], op=ALU.add)
                        b.op("dve", "tensor_tensor", out=dst(X[1]), in0=dst(X[1]), in1=tt[3][:, :, 0:n], op=ALU.add)
                    e_ = SEQ - 1 if d == 0 else 0
                    b.op("dve", "tensor_copy", out=fin_r[:, tau, :], in_=X[0][:, :, e_])
                    b.op("dve", "tensor_copy", out=fin_i[:, tau, :], in_=X[1][:, :, e_])
                    b.op("act", "activation", out=xrb[pp][:, :], in_=X[0][:, :, :].rearrange("p a n -> p (a n)"), func=AF.Copy)
                    b.op("act", "activation", out=xib[pp][:, :], in_=X[1][:, :, :].rearrange("p a n -> p (a n)"), func=AF.Copy)
                    py = psY[it % 2]
                    b.op("pe", "matmul", out=py[:, 0:T], lhsT=wC[0][pp][:, :], rhs=xrb[pp][:, :], start=True, stop=False)
                    b.op("pe", "matmul", accum=True, out=py[:, 0:T], lhsT=wC[1][pp][:, :], rhs=xib[pp][:, :], start=False, stop=True)
                    if d == 0 and t4 == 0:
                        b.op("dve", "tensor_copy", out=y[:, ch, :], in_=py[:, 0:T])
                    else:
                        b.op("dve", "tensor_tensor", out=y[:, ch, :], in0=y[:, ch, :], in1=py[:, 0:T], op=ALU.add)
                    it += 1
            FR = P["fr"][:, :].unsqueeze(2).to_broadcast([128, 64, 2]) if False else None
            for sq_ in range(2):
                a_, b_ = tm[0][:, 0, 0:64], tm[1][:, 0, 0:64]
                TT = lambda o, x_, y_, op: b.op("dve", "tensor_tensor", out=o, in0=x_, in1=y_, op=op)
                TT(a_, fin_r[:, :, sq_], P["fr"][:, :], ALU.mult)
                TT(b_, fin_i[:, :, sq_], P["fi"][:, :], ALU.mult)
                TT(tm[2][:, 0, sq_ * 64:(sq_ + 1) * 64], a_, b_, ALU.subtract)
                TT(a_, fin_i[:, :, sq_], P["fr"][:, :], ALU.mult)
                TT(b_, fin_r[:, :, sq_], P["fi"][:, :], ALU.mult)
                TT(tm[3][:, 0, sq_ * 64:(sq_ + 1) * 64], a_, b_, ALU.add)
            for src_t, dst_d in ((tm[2], s5re_out), (tm[3], s5im_out)):
                b.op("pe", "transpose", out=psS[:, 0:128], in_=src_t[:, 0, 0:128], identity=ident[:, :])
                b.op("dve", "tensor_copy", out=fin_o[:, :], in_=psS[:, 0:128])
                b.dma("sp", dst_d, fin_o[:, :])
            g_bf = act
            for c in range(KC):
                t_a, t_b = tmpA[0], tmpA[1]
                b.op("dve", "tensor_scalar", out=t_a[:, :], in0=h[:, c, :], scalar1=P["dskip"][:, c:c + 1], scalar2=None, op0=ALU.mult)
                b.op("dve", "tensor_tensor", out=y[:, c, :], in0=y[:, c, :], in1=t_a[:, :], op=ALU.add)
                b.op("act", "activation", out=t_a[:, :], in_=y[:, c, :], func=AF.Square)
                b.op("dve", "tensor_scalar", out=t_a[:, :], in0=t_a[:, :], scalar1=0.044715, scalar2=1.0, op0=ALU.mult, op1=ALU.add)
                b.op("dve", "tensor_tensor", out=t_a[:, :], in0=t_a[:, :], in1=y[:, c, :], op=ALU.mult)
                b.op("act", "activation", out=t_b[:, :], in_=t_a[:, :], func=AF.Tanh, scale=0.7978845608028654)
                b.op("dve", "tensor_scalar", out=t_b[:, :], in0=t_b[:, :], scalar1=1.0, scalar2=0.5, op0=ALU.add, op1=ALU.mult)
                b.op("dve", "tensor_tensor", out=g_bf[:, c, :], in0=t_b[:, :], in1=y[:, c, :], op=ALU.mult)
            for o4 in range(0, KC, 2):
                sl = wslot()
                v = sl[:, 0:KC * 512].rearrange("p (k n) -> p k n", k=KC)
                b.dma("pool", v[:, :, 0:256], s5_w_glu[:, o4 * 128:(o4 + 2) * 128].rearrange("(k p) n -> p k n", p=128))
                b.dma("pool", v[:, :, 256:512], s5_w_glu[:, D + o4 * 128:D + (o4 + 2) * 128].rearrange("(k p) n -> p k n", p=128))
                for oo in range(2):
                    o = o4 + oo
                    pa, pg = psG[o % 2], psU[o % 2]
                    for k in range(KC):
                        b.op("pe", "matmul", accum=(k > 0), out=pa[:, 0:T], lhsT=v[:, k, oo * 128:(oo + 1) * 128],
                             rhs=g_bf[:, k, :], start=(k == 0), stop=(k == KC - 1))
                    for k in range(KC):
                        b.op("pe", "matmul", accum=(k > 0), out=pg[:, 0:T], lhsT=v[:, k, 256 + oo * 128:256 + (oo + 1) * 128],
                             rhs=g_bf[:, k, :], start=(k == 0), stop=(k == KC - 1))
                    tq = tmpA[o % 2]
                    b.op("act", "activation", out=tq[:, :], in_=pg[:, 0:T], func=AF.Sigmoid)
                    b.op("dve", "tensor_tensor", out=y[:, o, :], in0=tq[:, :], in1=pa[:, 0:T], op=ALU.mult)


        def dn_mixer():
            sbf = lambda n_, sh, dt_=F32: b.sb("dn_" + n_, sh, dt_)
            msk = sbf("msk", [64, 6, 64]); b.dma("sp", msk[:], dn_mask)
            cw = sbf("cw", [128, 80]); load_rows_T(cw[:, :], dn_conv_w, 80)
            ones_f = sbf("ones", [64, 128]); b.raw("dve", lambda: nc.vector.memset(ones_f[:, :], 1.0), writes=[ones_f[:, :]])
            wba = sbf("wba", [128, 2, KC, 16], BF16)
            for d in range(2):
                b.dma("pool", wba[:, d, :, :], dn_w_ba[d].rearrange("(k p) n -> p k n", p=128))
            alog = sbf("alog", [64, 16]); b.dma("sp", alog[:], dn_a_log.broadcast_to([64, 16]))
            dtb = sbf("dtb", [64, 16]); b.dma("sp", dtb[:], dn_dt_bias.broadcast_to([64, 16]))
            outg = sbf("outg", [64, 128]); b.dma("sp", outg[:], dn_out_g.broadcast_to([64, 128]))
            nexpa = sbf("nexpa", [64, 16])
            b.op("act", "activation", out=nexpa[:, :], in_=alog[:, :], func=AF.Exp)
            b.op("dve", "tensor_scalar", out=nexpa[:, :], in0=nexpa[:, :], scalar1=-1.0, scalar2=None, op0=ALU.mult)
            cin = sbf("cin", [128, 2, SEQ + 4]); b.raw("dve", lambda: nc.vector.memset(cin[:, :, :], 0.0), writes=[cin[:, :, :]])
            qn = sbf("qn", [128, 4, T], BF16); kn = sbf("kn", [128, 4, T], BF16)
            kt_tok = sbf("kt", [64, 8, 4, 128], BF16); vt_tok = sbf("vt", [64, 8, 8, 128], BF16)
            zs = sbf("zs", [128, 8, T], BF16); vtmp = sbf("vtmp", [128, T], BF16)
            acc3 = tmpA[0][:, :].rearrange("p (a n) -> p a n", a=2)
            tm3 = tmpA[1][:, :].rearrange("p (a n) -> p a n", a=2)
            for m0 in range(0, 24, 4):
                sl = wslot()
                v = sl[:, 0:KC * 512].rearrange("p (k n) -> p k n", k=KC)
                b.dma("pool", v, dn_w_in[:, m0 * 128:(m0 + 4) * 128].rearrange("(k p) n -> p k n", p=128))
                for mm in range(4):
                    m = m0 + mm
                    pq = psG[mm % 2]
                    for k in range(KC):
                        b.op("pe", "matmul", accum=(k > 0), out=pq[:, 0:T], lhsT=v[:, k, mm * 128:(mm + 1) * 128],
                             rhs=h[:, k, :], start=(k == 0), stop=(k == KC - 1))
                    if m >= 16:
                        b.op("act", "activation", out=zs[:, m - 16, :], in_=pq[:, 0:T], func=AF.Silu)
                        continue
                    b.op("act", "activation", out=cin[:, :, 2:2 + SEQ], in_=pq[:, 0:T].rearrange("p (a n) -> p a n", a=2),
                         func=AF.Copy)
                    for j in range(5):
                        dst = acc3 if j == 0 else tm3
                        b.op("act", "activation", out=dst, in_=cin[:, :, j:j + SEQ], func=AF.Identity,
                             scale=cw[:, j * 16 + m:j * 16 + m + 1])
                        if j > 0:
                            b.op("dve", "tensor_tensor", out=acc3, in0=acc3, in1=tm3, op=ALU.add)
                    if m < 8:
                        b.op("act", "activation", out=tmpA[1][:, :], in_=tmpA[0][:, :], func=AF.Silu)
                        b.op("act", "activation", out=act[:, 0, :], in_=tmpA[1][:, :], func=AF.Square)
                        b.op("pe", "matmul", out=psS[:, 0:T], lhsT=ones_bf[:, :], rhs=act[:, 0, :], start=True, stop=True)
                        b.op("act", "activation", out=rstd[:, :], in_=psS[:, 0:T], func=AF.Sqrt, scale=1.0, bias=EPS)
                        b.op("dve", "reciprocal", out=rstd[:, :], in_=rstd[:, :])
                        b.op("dve", "tensor_tensor", out=tmpA[1][:, :], in0=tmpA[1][:, :], in1=rstd[:, :], op=ALU.mult)
                        if m < 4:
                            b.op("act", "activation", out=qn[:, m, :], in_=tmpA[1][:, :], func=AF.Identity, scale=128 ** -0.5)
                        else:
                            b.op("act", "activation", out=kn[:, m - 4, :], in_=tmpA[1][:, :], func=AF.Identity, scale=1.0)
                            for blk in range(8):
                                b.op("pe", "transpose", out=psTb[blk % 2][0:64, 0, :], in_=kn[:, m - 4, blk * 64:(blk + 1) * 64],
                                     identity=ident_bf[:, :])
                                b.op("dve", "tensor_copy", out=kt_tok[:, blk, m - 4, :], in_=psTb[blk % 2][0:64, 0, :])
                    else:
                        b.op("act", "activation", out=vtmp[:, :], in_=tmpA[0][:, :], func=AF.Silu)
                        for blk in range(8):
                            b.op("pe", "transpose", out=psTb[blk % 2][0:64, 0, :], in_=vtmp[:, blk * 64:(blk + 1) * 64],
                                 identity=ident_bf[:, :])
                            b.op("dve", "tensor_copy", out=vt_tok[:, blk, m - 8, :], in_=psTb[blk % 2][0:64, 0, :])
            gts = sbf("gts", [64, 2, 8, 16]); beta = sbf("beta", [64, 2, 8, 8]); xa = sbf("xa", [64, 2, 8, 8])
            xe = sbf("xe", [64, 2, 8, 8]); gg = sbf("gg", [64, 2, 8, 8]); gc = sbf("gc", [64, 2, 8, 8]); eg = sbf("eg", [64, 2, 8, 8])
            for d in range(2):
                for blk in range(8):
                    pq = psU[blk % 2]
                    for k in range(KC):
                        b.op("pe", "matmul", accum=(k > 0), out=pq[0:64, 0:16], lhsT=h[:, k, blk * 64:(blk + 1) * 64],
                             rhs=wba[:, d, k, :], start=(k == 0), stop=(k == KC - 1))
                    b.op("dve", "tensor_copy", out=gts[:, d, blk, :], in_=pq[0:64, 0:16])
                    b.op("act", "activation", out=beta[:, d, blk, :], in_=gts[:, d, blk, 0:8], func=AF.Sigmoid)
                    b.op("dve", "tensor_tensor", out=xa[:, d, blk, :], in0=gts[:, d, blk, 8:16], in1=dtb[:, d * 8:(d + 1) * 8], op=ALU.add)
            fl = lambda A: A[:, :, :, :].rearrange("p a b c -> p (a b c)")
            b.op("act", "activation", out=fl(xe), in_=fl(xa), func=AF.Abs)
            b.op("act", "activation", out=fl(xe), in_=fl(xe), func=AF.Exp, scale=-1.0)
            b.op("act", "activation", out=fl(xe), in_=fl(xe), func=AF.Ln, bias=1.0, scale=1.0)
            b.op("dve", "tensor_scalar", out=fl(xa), in0=fl(xa), scalar1=0.0, scalar2=None, op0=ALU.max)
            b.op("dve", "tensor_tensor", out=fl(xa), in0=fl(xa), in1=fl(xe), op=ALU.add)
            for d in range(2):
                for blk in range(8):
                    b.op("dve", "tensor_tensor", out=gg[:, d, blk, :], in0=xa[:, d, blk, :], in1=nexpa[:, d * 8:(d + 1) * 8], op=ALU.mult)
                b.op("pe", "matmul", out=psS[0:64, 0:64], lhsT=msk[:, d, :], rhs=gg[:, d, :, :].rearrange("p b c -> p (b c)"),
                     start=True, stop=True)
                b.op("dve", "tensor_copy", out=gc[:, d, :, :].rearrange("p b c -> p (b c)"), in_=psS[0:64, 0:64])
            b.op("act", "activation", out=fl(eg), in_=fl(gc), func=AF.Exp)
            f64 = lambda n_: sbf(n_, [64, 64])
            dg, decS, decT, PTt = f64("dg"), f64("decS"), f64("decT"), f64("PTt")
            Ms = [(f64("Ma"), f64("MTa")), (f64("Mb"), f64("MTb"))]
            gcr = sbf("gcr", [128, 64]); TTb = sbf("TTb", [64, 64], BF16); aT = sbf("aT", [64, 64], BF16)
            vb = sbf("vb", [64, 128], BF16); kbg = sbf("kbg", [64, 128], BF16); kg = sbf("kg", [64, 128], BF16)
            wT = sbf("wT", [128, 64], BF16); u_sb = sbf("u", [64, 128]); vnew = sbf("vnew", [64, 128]); vnew_b = sbf("vnewb", [64, 128], BF16)
            o1 = sbf("o1", [64, 128]); sc1 = sbf("sc1", [64, 4]); egl = sbf("egl", [128, 1])
            S = sbf("S", [128, 128]); Sb = sbf("Sb", [128, 128], BF16)
            o_acc = sbf("oacc", [64, 4, 128]); ss = sbf("ss", [64, 4]); onb = sbf("onb", [64, 128], BF16); junk = sbf("junk", [64, 128])
            id64 = ident[0:64, 0:64]
            for s_ in range(2):
                for hv in range(8):
                    hq = hv // 2
                    for d in range(2):
                        b.raw("dve", lambda: nc.vector.memset(S[:, :], 0.0), writes=[S[:, :]])
                        b.raw("dve", lambda: nc.vector.memset(Sb[:, :], 0.0), writes=[Sb[:, :]])
                        last = 63 if d == 0 else 0
                        for n in (range(4) if d == 0 else range(3, -1, -1)):
                            blk = s_ * 4 + n
                            tok0 = blk * 64
                            gc_, be_, eg_ = gc[:, d, blk, hv:hv + 1], beta[:, d, blk, hv:hv + 1], eg[:, d, blk, hv:hv + 1]
                            kf, qf = kn[:, hq, tok0:tok0 + 64], qn[:, hq, tok0:tok0 + 64]
                            kt, vt = kt_tok[:, blk, hq, :], vt_tok[:, blk, hv, :]
                            b.op("dve", "tensor_scalar", out=dg[:, :], in0=id64, scalar1=gc_, scalar2=None, op0=ALU.mult)
                            b.op("pe", "matmul", out=psS[:, 0:64], lhsT=ones_f[:, :], rhs=dg[:, :], start=True, stop=True)
                            b.op("act", "activation", out=gcr[:, :], in_=psS[:, 0:64], func=AF.Copy)
                            b.op("dve", "tensor_scalar", out=decS[:, :], in0=gcr[0:64, :], scalar1=-1.0, scalar2=gc_, op0=ALU.mult, op1=ALU.add)
                            b.op("dve", "tensor_scalar", out=decS[:, :], in0=decS[:, :], scalar1=0.0, scalar2=None, op0=ALU.min)
                            b.op("act", "activation", out=decS[:, :], in_=decS[:, :], func=AF.Exp)
                            b.op("dve", "tensor_tensor", out=decS[:, :], in0=decS[:, :], in1=msk[:, 4 + d, :], op=ALU.mult)
                            b.op("dve", "tensor_scalar", out=decT[:, :], in0=gcr[0:64, :], scalar1=gc_, scalar2=None, op0=ALU.subtract)
                            b.op("dve", "tensor_scalar", out=decT[:, :], in0=decT[:, :], scalar1=0.0, scalar2=None, op0=ALU.min)
                            b.op("act", "activation", out=decT[:, :], in_=decT[:, :], func=AF.Exp)
                            b.op("dve", "tensor_tensor", out=decT[:, :], in0=decT[:, :], in1=msk[:, d, :], op=ALU.mult)
                            M, MT = Ms[0]
                            b.op("pe", "matmul", out=psG[0][0:64, 0:64], lhsT=kf, rhs=kf, start=True, stop=True)
                            b.op("dve", "tensor_scalar", out=M[:, :], in0=psG[0][0:64, 0:64], scalar1=be_, scalar2=-1.0, op0=ALU.mult, op1=ALU.mult)
                            b.op("dve", "tensor_tensor", out=M[:, :], in0=M[:, :], in1=decS[:, :], op=ALU.mult)
                            b.op("pe", "transpose", out=psG[1][0:64, 0:64], in_=M[:, :], identity=id64)
                            b.op("act", "activation", out=MT[:, :], in_=psG[1][0:64, 0:64], func=AF.Copy)
                            b.op("dve", "tensor_tensor", out=PTt[:, :], in0=MT[:, :], in1=id64, op=ALU.add)
                            cur = 0
                            for k in range(1, 6):
                                M, MT = Ms[cur]
                                Mn, MTn = Ms[1 - cur]
                                b.op("pe", "matmul", out=psU[0][0:64, 0:64], lhsT=MT[:, :], rhs=M[:, :], start=True, stop=True)
                                b.op("act", "activation", out=Mn[:, :], in_=psU[0][0:64, 0:64], func=AF.Copy)
                                if k < 5:
                                    b.op("pe", "matmul", out=psU[1][0:64, 0:64], lhsT=M[:, :], rhs=MT[:, :], start=True, stop=True)
                                    b.op("dve", "tensor_copy", out=MTn[:, :], in_=psU[1][0:64, 0:64])
                                b.op("pe", "matmul", out=psY[0][0:64, 0:64], lhsT=Mn[:, :], rhs=PTt[:, :], start=True, stop=True)
                                b.op("dve", "tensor_tensor", out=PTt[:, :], in0=PTt[:, :], in1=psY[0][0:64, 0:64], op=ALU.add)
                                cur = 1 - cur
                            b.op("act", "activation", out=TTb[:, :], in_=PTt[:, :], func=AF.Copy)
                            b.op("dve", "tensor_tensor", out=sc1[:, 0:1], in0=be_, in1=eg_, op=ALU.mult)
                            b.op("dve", "tensor_scalar", out=vb[:, :], in0=vt, scalar1=be_, scalar2=None, op0=ALU.mult)
                            b.op("dve", "tensor_scalar", out=kbg[:, :], in0=kt, scalar1=sc1[:, 0:1], scalar2=None, op0=ALU.mult)
                            b.op("pe", "matmul", out=psG[0][0:64, 0:128], lhsT=TTb[:, :], rhs=vb[:, :], start=True, stop=True)
                            b.op("act", "activation", out=u_sb[:, :], in_=psG[0][0:64, 0:128], func=AF.Copy)
                            b.op("pe", "matmul", out=psG[1][:, 0:64], lhsT=kbg[:, :], rhs=TTb[:, :], start=True, stop=True)
                            b.op("dve", "tensor_copy", out=wT[:, :], in_=psG[1][:, 0:64])
                            b.op("pe", "matmul", out=psU[0][0:64, 0:128], lhsT=wT[:, :], rhs=Sb[:, :], start=True, stop=True)
                            b.op("dve", "tensor_tensor", out=vnew[:, :], in0=u_sb[:, :], in1=psU[0][0:64, 0:128], op=ALU.subtract)
                            b.op("act", "activation", out=vnew_b[:, :], in_=vnew[:, :], func=AF.Copy)
                            b.op("pe", "matmul", out=psU[1][0:64, 0:128], lhsT=qf, rhs=Sb[:, :], start=True, stop=True)
                            b.op("dve", "tensor_scalar", out=o1[:, :], in0=psU[1][0:64, 0:128], scalar1=eg_, scalar2=None, op0=ALU.mult)
                            b.op("pe", "matmul", out=psY[0][0:64, 0:64], lhsT=kf, rhs=qf, start=True, stop=True)
                            b.op("dve", "tensor_tensor", out=aT[:, :], in0=psY[0][0:64, 0:64], in1=decT[:, :], op=ALU.mult)
                            b.op("pe", "matmul", out=psY[1][0:64, 0:128], lhsT=aT[:, :], rhs=vnew_b[:, :], start=True, stop=True)
                            b.op("dve", "tensor_tensor", out=o1[:, :], in0=o1[:, :], in1=psY[1][0:64, 0:128], op=ALU.add)
                            if d == 0:
                                b.op("dve", "tensor_copy", out=o_acc[:, n, :], in_=o1[:, :])
                            else:
                                b.op("dve", "tensor_tensor", out=o_acc[:, n, :], in0=o_acc[:, n, :], in1=o1[:, :], op=ALU.add)
                            b.op("act", "activation", out=sc1[:, 1:2], in_=gc_, func=AF.Exp, scale=-1.0, bias=gcr[0:64, last:last + 1])
                            b.op("dve", "tensor_scalar", out=kg[:, :], in0=kt, scalar1=sc1[:, 1:2], scalar2=None, op0=ALU.mult)
                            b.op("act", "activation", out=egl[:, :], in_=gcr[:, last:last + 1], func=AF.Exp)
                            b.op("pe", "matmul", out=psG[0][:, 0:128], lhsT=kg[:, :], rhs=vnew_b[:, :], start=True, stop=True)
                            b.op("dve", "tensor_scalar", out=S[:, :], in0=S[:, :], scalar1=egl[:, 0:1], scalar2=None, op0=ALU.mult)
                            b.op("dve", "tensor_tensor", out=S[:, :], in0=S[:, :], in1=psG[0][:, 0:128], op=ALU.add)
                            b.op("act", "activation", out=Sb[:, :], in_=S[:, :], func=AF.Copy)
                        b.dma("sp", dn_out[(s_ * 2 + d) * 8 + hv, :, :], S[:, :])
                    for n in range(4):
                        b.op("act", "activation", out=junk[:, :], in_=o_acc[:, n, :], func=AF.Square, accum_out=ss[:, n:n + 1])
                    b.op("act", "activation", out=ss[:, :], in_=ss[:, :], func=AF.Sqrt, scale=1.0 / 128, bias=EPS)
                    b.op("dve", "reciprocal", out=ss[:, :], in_=ss[:, :])
                    for n in range(4):
                        tok0 = (s_ * 4 + n) * 64
                        b.op("dve", "tensor_scalar", out=junk[:, :], in0=o_acc[:, n, :], scalar1=ss[:, n:n + 1], scalar2=None, op0=ALU.mult)
                        b.op("dve", "tensor_tensor", out=onb[:, :], in0=junk[:, :], in1=outg[:, :], op=ALU.mult)
                        b.op("pe", "transpose", out=psTb[n % 2][:, 0, 0:64], in_=onb[:, :], identity=ident_bf[0:64, 0:64])
                        b.op("dve", "tensor_tensor", out=oT[:, hv, tok0:tok0 + 64], in0=psTb[n % 2][:, 0, 0:64],
                             in1=zs[:, hv, tok0:tok0 + 64], op=ALU.mult)
            proj_out_featmajor(dn_w_o, oT)


        NTL = LSEQ // 512
        NBL = LSEQ // 128

        def lat_load(src, tile):
            b.dma("sp", x[:], src[:, tile * 512:(tile + 1) * 512].rearrange("(c p) t -> p c t", p=128))

        def lat_store(dst, tile):
            b.dma("sp", dst[:, tile * 512:(tile + 1) * 512].rearrange("(c p) t -> p c t", p=128), x[:])

        def lat_ffn_sub(i, s):
            pre_norm(i, s)
            ffn(ffn_w_gu[i, 0 if s == 0 else 1], ffn_w_d[i, 0 if s == 0 else 1], key=i * 2 + (0 if s == 0 else 1),
                consume=not CTX_SKIP)
            post_norm_residual(i, s)

        def a_lat(src0):
            sbf = lambda n_, sh, dt_=F32: b.sb("la_" + n_, sh, dt_)
            kTf = sbf("kT", [128, 4, LSEQ], BF16); vtf = sbf("vt", [128, NBL, 256], BF16)
            kTc = sbf("kTc", [128, 4, 512], BF16); vtc = sbf("vtc", [128, 4, 256], BF16)
            bmask = sbf("bm", [128, 384]); b.dma("sp", bmask[:], band_mask)
            cs = [sbf(f"cs{i}", [128, 64]) for i in range(2)]
            kdup = sbf("kdup", [128, 4, 2, 64], BF16)
            qr = sbf("qr", [128, 1024], BF16); qTb = sbf("qTb", [128, 8, 128], BF16)
            slcs = [sbf(f"sl{j}", [128, 384]) for j in range(2)]; Pls = [sbf(f"Pl{j}", [128, 384], BF16) for j in range(2)]
            Pcs = [sbf(f"Pc{j}", [128, 512], BF16) for j in range(2)]
            PTls = [sbf(f"PTl{j}", [128, 8, 128], BF16) for j in range(2)]; otk = sbf("otk", [128, 1024], BF16)
            smms = [sbf(f"smx{j}", [128, 12]) for j in range(2)]
            rt = [sbf(f"rt{i}", [128, 2, 16]) for i in range(4)]
            c32t = sbf("c32", [128, 256])

            def rope(src, nh, dst_fn, cst):
                cosv = cst[:, 0:32].rearrange("p (a f) -> p a f", a=2)
                sinv = cst[:, 32:64].rearrange("p (a f) -> p a f", a=2)
                for hh in range(nh):
                    s4 = src[:, hh * 64:(hh + 1) * 64].rearrange("p (a b f) -> p a b f", a=2, b=2)
                    x1, x2 = s4[:, :, 0, :], s4[:, :, 1, :]
                    d4 = dst_fn(hh).rearrange("p (a b f) -> p a b f", a=2, b=2)
                    TT = lambda o, a_, b_, op: b.op("dve", "tensor_tensor", out=o, in0=a_, in1=b_, op=op)
                    TT(rt[0][:, :, :], x1, cosv, ALU.mult)
                    TT(rt[1][:, :, :], x2, sinv, ALU.mult)
                    TT(d4[:, :, 0, :], rt[0][:, :, :], rt[1][:, :, :], ALU.subtract)
                    TT(rt[2][:, :, :], x2, cosv, ALU.mult)
                    TT(rt[3][:, :, :], x1, sinv, ALU.mult)
                    TT(d4[:, :, 1, :], rt[2][:, :, :], rt[3][:, :, :], ALU.add)

            def k_to_featmajor(dstT, col0):
                b.op("dve", "tensor_copy", out=kdup[:, :, 1, :], in_=kdup[:, :, 0, :])
                for kv in range(4):
                    b.op("pe", "transpose", out=psTb[kv % 2][:, 0, :], in_=kdup[:, kv, :, :].rearrange("p a d -> p (a d)"),
                         identity=ident_bf[:, :])
                    b.op("act", "activation", out=dstT[:, kv, col0:col0 + 128], in_=psTb[kv % 2][:, 0, :], func=AF.Copy)

            for cb in range(4):
                b.dma("sp", c32t[:, :], cache_ak[cb * 128:(cb + 1) * 128, :])
                b.op("dve", "tensor_copy", out=kdup[:, :, 0, :], in_=c32t[:, :].rearrange("p (k d) -> p k d", k=4))
                k_to_featmajor(kTc, cb * 128)
                b.dma("pool", vtc[:, cb, :], cache_av[cb * 128:(cb + 1) * 128, :])
            for tl in range(NTL):
                lat_load(src0, tl)
                lat_ffn_sub(0, 0)
                lat_store(zres, tl)
                pre_norm(0, 1)
                sl_ = wslot()
                v = sl_[:, 0:KC * 512].rearrange("p (k n) -> p k n", k=KC)
                b.dma("pool", v, a_w_qkv[:, 1024:1536].rearrange("(k p) n -> p k n", p=128))
                for tb in range(4):
                    blk = tl * 4 + tb
                    pq = psU[tb % 2]
                    for k in range(KC):
                        b.op("pe", "matmul", accum=(k > 0), out=pq[:, 0:512], lhsT=h[:, k, tb * 128:(tb + 1) * 128],
                             rhs=v[:, k, :], start=(k == 0), stop=(k == KC - 1))
                    b.dma("sp", cs[tb % 2][:, :], rope_cs[blk * 128:(blk + 1) * 128, :])
                    rope(pq[:, 0:256], 4, lambda hh: kdup[:, hh, 0, :], cs[tb % 2])
                    k_to_featmajor(kTf, blk * 128)
                    b.op("act", "activation", out=vtf[:, blk, :], in_=pq[:, 256:512], func=AF.Copy)
            it = 0
            for tl in range(NTL):
                lat_load(zres, tl)
                pre_norm(0, 1)
                wq = []
                for half in range(2):
                    sl_ = wslot()
                    v = sl_[:, 0:KC * 512].rearrange("p (k n) -> p k n", k=KC)
                    b.dma("pool", v, a_w_qkv[:, half * 512:(half + 1) * 512].rearrange("(k p) n -> p k n", p=128))
                    wq.append(v)
                for tb in range(4):
                    blk = tl * 4 + tb
                    b.dma("sp", cs[tb % 2][:, :], rope_cs[blk * 128:(blk + 1) * 128, :])
                    for half in range(2):
                        pq = psG[half]
                        for k in range(KC):
                            b.op("pe", "matmul", accum=(k > 0), out=pq[:, 0:512], lhsT=h[:, k, tb * 128:(tb + 1) * 128],
                                 rhs=wq[half][:, k, :], start=(k == 0), stop=(k == KC - 1))
                        rope(pq[:, 0:512], 8, lambda hh, half=half: qr[:, half * 512 + hh * 64:half * 512 + (hh + 1) * 64], cs[tb % 2])
                    for c in range(KC):
                        b.op("pe", "transpose", out=psTb[c % 2][:, 0, :], in_=qr[:, c * 128:(c + 1) * 128], identity=ident_bf[:, :])
                        b.op("act", "activation", out=qTb[:, c, :], in_=psTb[c % 2][:, 0, :], func=AF.Copy)
                    lo, hi = max(blk - 1, 0), min(blk + 1, NBL - 1)
                    nlb = hi - lo + 1
                    nl = nlb * 128
                    bm = bmask[:, 128:128 + nl] if blk == 0 else bmask[:, 0:nl]
                    def heads(j, lo=lo, hi=hi, nlb=nlb, nl=nl, bm=bm):
                        slc, Pl, Pc, PTl, smm = slcs[j], Pls[j], Pcs[j], PTls[j], smms[j]
                        pL, pC, pO, pT = psG[j], psU[j], psY[j], psTc[j]
                        base = 64 * j
                        for hh in range(j, 16, 2):
                            qc, kv = hh // 2, hh // 4
                            b.op("pe", "matmul", out=pL[:, 0:nl], lhsT=qTb[base:base + 64, qc, :],
                                 rhs=kTf[base:base + 64, kv, lo * 128:(hi + 1) * 128], start=True, stop=True)
                            b.op("pe", "matmul", out=pC[:, 0:512], lhsT=qTb[base:base + 64, qc, :],
                                 rhs=kTc[base:base + 64, kv, :], start=True, stop=True)
                            yield
                            b.op("dve", "tensor_tensor", out=slc[:, 0:nl], in0=pL[:, 0:nl], in1=bm, op=ALU.add)
                            b.op("dve", "reduce_max", out=smm[:, 0:1], in_=slc[:, 0:nl], axis=AX.X)
                            b.op("dve", "reduce_max", out=smm[:, 1:2], in_=pC[:, 0:512], axis=AX.X)
                            yield
                            b.op("dve", "tensor_tensor", out=smm[:, 0:1], in0=smm[:, 0:1], in1=smm[:, 1:2], op=ALU.max)
                            b.op("dve", "tensor_scalar", out=smm[:, 2:3], in0=smm[:, 0:1], scalar1=0.125,
                                 scalar2=sink_bc[:, hh:hh + 1], op0=ALU.mult, op1=ALU.max)
                            b.op("dve", "tensor_scalar", out=smm[:, 2:3], in0=smm[:, 2:3], scalar1=-1.0, scalar2=None, op0=ALU.mult)
                            yield
                            b.op("act", "activation", out=Pl[:, 0:nl], in_=slc[:, 0:nl], func=AF.Exp, scale=0.125,
                                 bias=smm[:, 2:3], accum_out=smm[:, 3:4])
                            b.op("act", "activation", out=Pc[:, :], in_=pC[:, 0:512], func=AF.Exp, scale=0.125,
                                 bias=smm[:, 2:3], accum_out=smm[:, 4:5])
                            b.op("act", "activation", out=smm[:, 5:6], in_=sink_bc[:, hh:hh + 1], func=AF.Exp, bias=smm[:, 2:3], scale=1.0)
                            yield
                            b.op("dve", "tensor_tensor", out=smm[:, 6:7], in0=smm[:, 3:4], in1=smm[:, 4:5], op=ALU.add)
                            b.op("dve", "tensor_tensor", out=smm[:, 6:7], in0=smm[:, 6:7], in1=smm[:, 5:6], op=ALU.add)
                            b.op("dve", "reciprocal", out=smm[:, 7:8], in_=smm[:, 6:7])
                            srcs = [Pl[:, jq * 128:(jq + 1) * 128] for jq in range(nlb)] + [Pc[:, jq * 128:(jq + 1) * 128] for jq in range(4)]
                            nsrc = len(srcs)
                            for j0 in range(0, nsrc, 2):
                                nn_ = min(2, nsrc - j0)
                                for q_ in range(nn_):
                                    b.op("pe", "transpose", out=pT[:, q_, :], in_=srcs[j0 + q_], identity=ident_bf[:, :])
                                if (j0 // 2) % 2 == 0:
                                    b.op("dve", "tensor_copy", out=PTl[:, j0:j0 + nn_, :], in_=pT[:, 0:nn_, :])
                                else:
                                    b.op("act", "activation", out=PTl[:, j0:j0 + nn_, :], in_=pT[:, 0:nn_, :], func=AF.Copy)
                                yield
                            for jq in range(nsrc):
                                rv = vtf[:, lo + jq, kv * 64:(kv + 1) * 64] if jq < nlb else vtc[:, jq - nlb, kv * 64:(kv + 1) * 64]
                                b.op("pe", "matmul", accum=(jq > 0), out=pO[:, 0:64], lhsT=PTl[:, jq, :], rhs=rv,
                                     start=(jq == 0), stop=(jq == nsrc - 1))
                            yield
                            b.op("dve", "tensor_scalar", out=otk[:, hh * 64:(hh + 1) * 64], in0=pO[:, 0:64], scalar1=smm[:, 7:8],
                                 scalar2=None, op0=ALU.mult)
                            yield
                    run_interleaved([heads(0), heads(1)])
                    for c in range(KC):
                        b.op("pe", "transpose", out=psTb[c % 2][:, 0, :], in_=otk[:, c * 128:(c + 1) * 128], identity=ident_bf[:, :])
                        b.op("act", "activation", out=oT[:, c, tb * 128:(tb + 1) * 128], in_=psTb[c % 2][:, 0, :], func=AF.Copy)
                proj_out_featmajor(a_w_o, oT)
                post_norm_residual(0, 1)
                lat_ffn_sub(0, 2)
                lat_store(zres, tl)


        def s5_lat():
            P = s5_setup(npow=9, pfx="l5p_")
            LT = 512
            sbf = lambda n_, sh, dt_=F32: b.sb("l5_" + n_, sh, dt_)
            TT = lambda o, a_, b_, op: b.op("dve", "tensor_tensor", out=o, in0=a_, in1=b_, op=op)
            h0r, h0i = sbf("h0r", [128, 64]), sbf("h0i", [128, 64])
            load_rows_T(h0r[:, :], st5_re, 64)
            load_rows_T(h0i[:, :], st5_im, 64)
            cr, ci = sbf("cr", [128, 64]), sbf("ci", [128, 64])
            t1, t2, t3 = P["t1"][:, :], P["t2"][:, :], P["t3"][:, :]
            TT(t1, P["fr"][:, :], P["fr"][:, :], ALU.mult)
            TT(t2, P["fi"][:, :], P["fi"][:, :], ALU.mult)
            TT(t1, t1, t2, ALU.add)
            b.op("dve", "reciprocal", out=t3, in_=t1)
            TT(t1, h0r[:, :], P["fr"][:, :], ALU.mult)
            TT(t2, h0i[:, :], P["fi"][:, :], ALU.mult)
            TT(t1, t1, t2, ALU.add)
            TT(cr[:, :], t1, t3, ALU.mult)
            TT(t1, h0i[:, :], P["fr"][:, :], ALU.mult)
            TT(t2, h0r[:, :], P["fi"][:, :], ALU.mult)
            TT(t1, t1, t2, ALU.subtract)
            TT(ci[:, :], t1, t3, ALU.mult)
            nat = {}
            for t4 in range(4):
                for nm in ("bre", "bim", "cre", "cim"):
                    tl_ = sbf(f"n_{nm}{t4}", [128, 128])
                    b.raw("dve", lambda tl_=tl_: nc.vector.memset(tl_[:, :], 0.0), writes=[tl_[:, :]])
                    nat[(nm, t4)] = tl_
            wpack = [sbf(f"wpack{i}", [128, 4, 128], BF16) for i in range(2)]
            wit = [0]
            c32 = [sbf(f"c32{i}", [128, 128]) for i in range(2)]
            cm = [sbf(f"cm{i}", [128, 128]) for i in range(4)]
            X = (sbf("xr", [128, LT]), sbf("xi", [128, LT]))
            xrb, xib = sbf("xrb", [128, LT], BF16), sbf("xib", [128, LT], BF16)
            tm = [sbf(f"tm{i}", [128, LT]) for i in range(6)]
            sc = sbf("sc", [128, 8])
            rotb = [sbf(f"rotb{i}", [128, 1536]) for i in range(2)]
            ones512 = sbf("ones512", [128, LT])
            b.raw("dve", lambda: nc.vector.memset(ones512[:, :], 1.0), writes=[ones512[:, :]])
            pwu = [(P["cs"], P["sn"])]
            for k in range(1, 9):
                pc_, ps_ = pwu[-1]
                nc_ = sbf(f"uc{k}", [128, 64]); ns_ = sbf(f"us{k}", [128, 64])
                TT(t1, pc_[:, :], pc_[:, :], ALU.mult)
                TT(t2, ps_[:, :], ps_[:, :], ALU.mult)
                TT(nc_[:, :], t1, t2, ALU.subtract)
                TT(t1, pc_[:, :], ps_[:, :], ALU.mult)
                b.op("dve", "tensor_scalar", out=ns_[:, :], in0=t1, scalar1=2.0, scalar2=None, op0=ALU.mult)
                pwu.append((nc_, ns_))
            TSm = lambda o, i_, sc_: b.op("dve", "tensor_scalar", out=o, in0=i_, scalar1=sc_, scalar2=None, op0=ALU.mult)
            for tau in range(64):
                Er, Ei = X[0], X[1]
                b.op("dve", "tensor_copy", out=Er[:, 0:1], in_=P["cs"][:, tau:tau + 1])
                b.op("dve", "tensor_copy", out=Ei[:, 0:1], in_=P["sn"][:, tau:tau + 1])
                for k in range(9):
                    n = 1 << k
                    pc_, ps_ = pwu[k][0][:, tau:tau + 1], pwu[k][1][:, tau:tau + 1]
                    TSm(tm[0][:, 0:n], Er[:, 0:n], pc_)
                    TSm(tm[1][:, 0:n], Ei[:, 0:n], ps_)
                    TSm(tm[2][:, 0:n], Ei[:, 0:n], pc_)
                    TSm(tm[3][:, 0:n], Er[:, 0:n], ps_)
                    TT(Er[:, n:2 * n], tm[0][:, 0:n], tm[1][:, 0:n], ALU.subtract)
                    TT(Ei[:, n:2 * n], tm[2][:, 0:n], tm[3][:, 0:n], ALU.add)
                b.op("act", "activation", out=tm[4][:, :], in_=ones512[:, :], func=AF.Identity, scale=P["mag"][:, tau:tau + 1])
                b.dma("sp", rot[tau, :, 0:512], Er[:, :])
                b.dma("sp", rot[tau, :, 512:1024], Ei[:, :])
                b.dma("sp", rot[tau, :, 1024:1536], tm[4][:, :])
            rit = [0]

            def tile_pass(d, tl):
                for t in range(32):
                    tau = d * 32 + t
                    ch, t4 = t // 4, t % 4
                    wp = wpack[wit[0] % 2]
                    wit[0] += 1
                    wB = [wp[:, 0, :], wp[:, 1, :]]
                    wC = [wp[:, 2, :], wp[:, 3, :]]
                    first_tile = (tl == 0) if d == 0 else (tl == NTL - 1)
                    if not first_tile:
                        b.dma("sp", wp[:, :, :].rearrange("p a n -> p (a n)"), w5sc[tau, :, :])
                    else:
                        for g2 in range(2):
                            g = 2 * t + g2
                            cb = (2 * t4 + g2) * 16
                            b.dma("sp", nat[("bre", t4)][g2 * 64:(g2 + 1) * 64, cb:cb + 16], s5_b_re[d, g, :, :])
                            b.dma("sp", nat[("bim", t4)][g2 * 64:(g2 + 1) * 64, cb:cb + 16], s5_b_im[d, g, :, :])
                            b.dma("sp", nat[("cre", t4)][cb:cb + 16, g2 * 64:(g2 + 1) * 64], s5_c_re[d, g, :, :])
                            b.dma("sp", nat[("cim", t4)][cb:cb + 16, g2 * 64:(g2 + 1) * 64], s5_c_im[d, g, :, :])
                        for ri, nm in enumerate(("bre", "bim")):
                            b.op("pe", "transpose", out=psS[:, 0:128], in_=nat[(nm, t4)][:, :], identity=ident[:, :])
                            b.op("act", "activation", out=wB[ri], in_=psS[:, 0:128], func=AF.Copy)
                        for ri, nm in enumerate(("cre", "cim")):
                            b.op("pe", "transpose", out=psS[:, 0:128], in_=nat[(nm, t4)][:, :], identity=ident[:, :])
                            b.op("act", "activation", out=c32[ri][:, :], in_=psS[:, 0:128], func=AF.Copy)
                        fr_, fi_ = P["fr"][:, tau:tau + 1], P["fi"][:, tau:tau + 1]
                        TS = lambda o, i_, sc_: b.op("dve", "tensor_scalar", out=o, in0=i_, scalar1=sc_, scalar2=None, op0=ALU.mult)
                        TS(cm[0][:, :], c32[0][:, :], fr_)
                        TS(cm[1][:, :], c32[1][:, :], fi_)
                        TT(wC[0], cm[0][:, :], cm[1][:, :], ALU.subtract)
                        TS(cm[2][:, :], c32[0][:, :], fi_)
                        TS(cm[3][:, :], c32[1][:, :], fr_)
                        TT(cm[2][:, :], cm[2][:, :], cm[3][:, :], ALU.add)
                        b.op("dve", "tensor_scalar", out=wC[1], in0=cm[2][:, :], scalar1=-1.0, scalar2=None, op0=ALU.mult)
                        b.dma("sp", w5sc[tau, :, :], wp[:, :, :].rearrange("p a n -> p (a n)"))
                    tb = rotb[rit[0] % 2]
                    rit[0] += 1
                    b.dma("sp", tb[:, :], rot[tau, :, :])
                    cT, sT, rT = tb[:, 0:512], tb[:, 512:1024], tb[:, 1024:1536]
                    for ri in range(2):
                        b.op("pe", "matmul", out=psG[ri][:, 0:LT], lhsT=wB[ri], rhs=h[:, ch, :], start=True, stop=True)
                    e1 = LT - 1 if d == 0 else 0
                    cr_, ci_ = cr[:, tau:tau + 1], ci[:, tau:tau + 1]
                    rv = (lambda A: A[:, 0:LT]) if d == 0 else (lambda A: A[:, LT - 1::-1])
                    brv, biv = rv(psG[0]), rv(psG[1])
                    TT(tm[0][:, :], brv, cT, ALU.mult)
                    TT(tm[1][:, :], biv, sT, ALU.mult)
                    TT(tm[2][:, :], biv, cT, ALU.mult)
                    TT(tm[3][:, :], brv, sT, ALU.mult)
                    TT(X[0][:, :], tm[0][:, :], tm[1][:, :], ALU.add)
                    b.op("pool", "tensor_tensor", out=X[1][:, :], in0=tm[2][:, :], in1=tm[3][:, :], op=ALU.subtract)
                    b.op("dve", "tensor_tensor_scan", out=tm[4][:, :], data0=rT, data1=X[0][:, :], initial=cr_, op0=ALU.mult, op1=ALU.add)
                    b.op("dve", "tensor_tensor_scan", out=tm[5][:, :], data0=rT, data1=X[1][:, :], initial=ci_, op0=ALU.mult, op1=ALU.add)
                    PT_ = lambda o, a_, b_, op: b.op("pool", "tensor_tensor", out=o, in0=a_, in1=b_, op=op)
                    PT_(tm[0][:, :], tm[4][:, :], cT, ALU.mult)
                    PT_(tm[1][:, :], tm[5][:, :], sT, ALU.mult)
                    PT_(tm[2][:, :], tm[5][:, :], cT, ALU.mult)
                    PT_(tm[3][:, :], tm[4][:, :], sT, ALU.mult)
                    TT(rv(X[0]), tm[0][:, :], tm[1][:, :], ALU.subtract)
                    TT(rv(X[1]), tm[2][:, :], tm[3][:, :], ALU.add)
                    b.op("dve", "tensor_copy", out=cr_, in_=X[0][:, e1:e1 + 1])
                    b.op("dve", "tensor_copy", out=ci_, in_=X[1][:, e1:e1 + 1])
                    b.op("act", "activation", out=xrb[:, :], in_=X[0][:, :], func=AF.Copy)
                    b.op("act", "activation", out=xib[:, :], in_=X[1][:, :], func=AF.Copy)
                    py = psY[t % 2]
                    b.op("pe", "matmul", out=py[:, 0:LT], lhsT=wC[0], rhs=xrb[:, :], start=True, stop=False)
                    b.op("pe", "matmul", accum=True, out=py[:, 0:LT], lhsT=wC[1], rhs=xib[:, :], start=False, stop=True)
                    if t4 == 0:
                        b.op("dve", "tensor_copy", out=y[:, ch, :], in_=py[:, 0:LT])
                    else:
                        TT(y[:, ch, :], y[:, ch, :], py[:, 0:LT], ALU.add)

            def tail():
                g_bf = act
                for c in range(KC):
                    t_a, t_b = tmpA[0], tmpA[1]
                    b.op("dve", "tensor_scalar", out=t_a[:, :], in0=h[:, c, :], scalar1=P["dskip"][:, c:c + 1], scalar2=None, op0=ALU.mult)
                    TT(y[:, c, :], y[:, c, :], t_a[:, :], ALU.add)
                    b.op("act", "activation", out=t_a[:, :], in_=y[:, c, :], func=AF.Square)
                    b.op("dve", "tensor_scalar", out=t_a[:, :], in0=t_a[:, :], scalar1=0.044715, scalar2=1.0, op0=ALU.mult, op1=ALU.add)
                    TT(t_a[:, :], t_a[:, :], y[:, c, :], ALU.mult)
                    b.op("act", "activation", out=t_b[:, :], in_=t_a[:, :], func=AF.Tanh, scale=0.7978845608028654)
                    b.op("dve", "tensor_scalar", out=t_b[:, :], in0=t_b[:, :], scalar1=1.0, scalar2=0.5, op0=ALU.add, op1=ALU.mult)
                    TT(g_bf[:, c, :], t_b[:, :], y[:, c, :], ALU.mult)
                for o4 in range(0, KC, 2):
                    sl = wslot()
                    v = sl[:, 0:KC * 512].rearrange("p (k n) -> p k n", k=KC)
                    b.dma("pool", v[:, :, 0:256], s5_w_glu[:, o4 * 128:(o4 + 2) * 128].rearrange("(k p) n -> p k n", p=128))
                    b.dma("pool", v[:, :, 256:512], s5_w_glu[:, D + o4 * 128:D + (o4 + 2) * 128].rearrange("(k p) n -> p k n", p=128))
                    for oo in range(2):
                        o = o4 + oo
                        pa, pg = psG[o % 2], psU[o % 2]
                        for k in range(KC):
                            b.op("pe", "matmul", accum=(k > 0), out=pa[:, 0:T], lhsT=v[:, k, oo * 128:(oo + 1) * 128],
                                 rhs=g_bf[:, k, :], start=(k == 0), stop=(k == KC - 1))
                        for k in range(KC):
                            b.op("pe", "matmul", accum=(k > 0), out=pg[:, 0:T], lhsT=v[:, k, 256 + oo * 128:256 + (oo + 1) * 128],
                                 rhs=g_bf[:, k, :], start=(k == 0), stop=(k == KC - 1))
                        tq = tmpA[o % 2]
                        b.op("act", "activation", out=tq[:, :], in_=pg[:, 0:T], func=AF.Sigmoid)
                        TT(y[:, o, :], tq[:, :], pa[:, 0:T], ALU.mult)

            for tl in range(NTL):
                lat_load(zres, tl)
                lat_ffn_sub(1, 0)
                lat_store(zres, tl)
                pre_norm(1, 1)
                tile_pass(0, tl)
                b.dma("sp", ysc[:, tl * 512:(tl + 1) * 512].rearrange("(c p) t -> p c t", p=128), y[:])
            for tl in range(NTL - 1, -1, -1):
                lat_load(zres, tl)
                pre_norm(1, 1)
                tile_pass(1, tl)
                for c in range(KC):
                    b.dma("sp", tmpA[c % 2][:, :], ysc[c * 128:(c + 1) * 128, tl * 512:(tl + 1) * 512])
                    TT(y[:, c, :], y[:, c, :], tmpA[c % 2][:, :], ALU.add)
                tail()
                post_norm_residual(1, 1)
                lat_ffn_sub(1, 2)
                lat_store(zres, tl)


        def na_lat():
            sbf = lambda n_, sh, dt_=F32: b.sb("ln_" + n_, sh, dt_)
            TT = lambda o, a_, b_, op: b.op("dve", "tensor_tensor", out=o, in0=a_, in1=b_, op=op)
            NR = LSEQ // 64
            kTg = sbf("kT", [128, LSEQ], BF16); vtg = sbf("vt", [64, NR, 128], BF16)
            kTc = sbf("kTc", [128, 512], BF16); vtc = sbf("vtc", [128, 4, 128], BF16)
            Bh = sbf("Bh", [64, 2, 15, 64]); cmask = sbf("cmask", [64, 64]); b.dma("sp", cmask[:], na_cmask)
            c32t = sbf("c32", [128, 128]); ckb = sbf("ckb", [128, 128], BF16)
            qTg = sbf("qT", [128, 512], BF16)
            slcs = [sbf(f"sl{j}", [64, 512]) for j in range(2)]
            Pls = [sbf(f"Pl{j}", [64, 512], BF16) for j in range(2)]; Pcs = [sbf(f"Pc{j}", [64, 512], BF16) for j in range(2)]
            PTls = [sbf(f"PTl{j}", [128, 12, 64], BF16) for j in range(2)]; otk = sbf("otk", [64, 128], BF16)
            smms = [sbf(f"sm{j}", [64, 8]) for j in range(2)]; woT = sbf("woT", [128, 512], BF16)
            stg = [sbf(f"stg{i}", [128, 512], BF16) for i in range(2)]
            vstg = sbf("vstg", [64, 1024], BF16)
            for tl in range(NTL):
                lat_load(zres, tl)
                lat_ffn_sub(2, 0)
                lat_store(zres, tl)
                pre_norm(2, 1)
                for part, dst in ((0, qsc), (1, ksc)):
                    for hf in range(2):
                        sl_ = wslot()
                        v = sl_[:, 0:KC * 512].rearrange("p (k n) -> p k n", k=KC)
                        b.dma("pool", v, na_w_qkv[:, part * D + hf * 512:part * D + (hf + 1) * 512].rearrange("(k p) n -> p k n", p=128))
                        for mm in range(4):
                            c = hf * 4 + mm
                            pq = psG[mm % 2]
                            for k in range(KC):
                                b.op("pe", "matmul", accum=(k > 0), out=pq[:, 0:512], lhsT=v[:, k, mm * 128:(mm + 1) * 128],
                                     rhs=h[:, k, :], start=(k == 0), stop=(k == KC - 1))
                            b.op("act", "activation", out=stg[mm % 2][:, :], in_=pq[:, 0:512], func=AF.Copy)
                            b.dma("sp", dst[c, :, tl * 512:(tl + 1) * 512], stg[mm % 2][:, :])
                wv = []
                for hf in range(2):
                    sl_ = wslot()
                    v = sl_[:, 0:KC * 512].rearrange("p (k n) -> p k n", k=KC)
                    b.dma("pool", v, na_w_qkv[:, 2 * D + hf * 512:2 * D + (hf + 1) * 512].rearrange("(k p) n -> p k n", p=128))
                    wv.append(v)
                for rr in range(8):
                    for hf in range(2):
                        pq = psU[hf]
                        for k in range(KC):
                            b.op("pe", "matmul", accum=(k > 0), out=pq[0:64, 0:512], lhsT=h[:, k, rr * 64:(rr + 1) * 64],
                                 rhs=wv[hf][:, k, :], start=(k == 0), stop=(k == KC - 1))
                        b.op("dve" if hf == 0 else "act", "tensor_copy" if hf == 0 else "activation",
                             out=vstg[:, hf * 512:(hf + 1) * 512], in_=pq[0:64, 0:512], **({} if hf == 0 else {"func": AF.Copy}))
                    b.dma("sp", vsc[:, :, tl * 8 + rr, :].rearrange("g p d -> p g d"), vstg[:, :].rearrange("p (g d) -> p g d", g=8))
            for gi in range(8):
                for j in range(2):
                    b.dma("sp", Bh[:, j, :, :].rearrange("p a k -> p (a k)"), na_bias[2 * gi + j, :, :])
                    for dr in range(15):
                        TT(Bh[:, j, dr, :], Bh[:, j, dr, :], cmask[:, :], ALU.add)
                    b.op("dve", "tensor_scalar", out=Bh[:, j, :, :].rearrange("p a k -> p (a k)"),
                         in0=Bh[:, j, :, :].rearrange("p a k -> p (a k)"), scalar1=8.0, scalar2=None, op0=ALU.mult)
                for cb in range(4):
                    b.dma("sp", c32t[:, :], cache_nk[cb * 128:(cb + 1) * 128, gi * 128:(gi + 1) * 128])
                    b.op("dve", "tensor_copy", out=ckb[:, :], in_=c32t[:, :])
                    b.op("pe", "transpose", out=psTb[cb % 2][:, 0, :], in_=ckb[:, :], identity=ident_bf[:, :])
                    b.op("act", "activation", out=kTc[:, cb * 128:(cb + 1) * 128], in_=psTb[cb % 2][:, 0, :], func=AF.Copy)
                    b.dma("pool", vtc[:, cb, :], cache_nv[cb * 128:(cb + 1) * 128, gi * 128:(gi + 1) * 128])
                b.dma("sp", kTg[:, :], ksc[gi, :, :])
                b.dma("sp", vtg[:, :, :], vsc[gi, :, :, :])
                it = 0
                for tl in range(NTL):
                    if gi == 7:
                        lat_load(zres, tl)
                    b.dma("sp", qTg[:, :], qsc[gi, :, tl * 512:(tl + 1) * 512])
                    for rr in range(8):
                        r = tl * 8 + rr
                        kr0 = min(max(r - 4, 0), NR - 8)
                        dr0 = kr0 - r + 7
                        def head(j, rr=rr, kr0=kr0, dr0=dr0):
                            base = 64 * j
                            slc, Pl, Pc, PTl, smm = slcs[j], Pls[j], Pcs[j], PTls[j], smms[j]
                            pL, pC, pO, pT = psG[j], psU[j], psY[j], psTc[j]
                            lq = qTg[base:base + 64, rr * 64:(rr + 1) * 64]
                            b.op("pe", "matmul", out=pL[0:64, 0:512], lhsT=lq, rhs=kTg[base:base + 64, kr0 * 64:kr0 * 64 + 512],
                                 start=True, stop=True)
                            b.op("pe", "matmul", out=pC[0:64, 0:512], lhsT=lq, rhs=kTc[base:base + 64, :], start=True, stop=True)
                            yield
                            TT(slc[:, :], pL[0:64, 0:512], Bh[:, j, dr0:dr0 + 8, :].rearrange("p a k -> p (a k)"), ALU.add)
                            b.op("dve", "reduce_max", out=smm[:, 0:1], in_=slc[:, :], axis=AX.X)
                            b.op("dve", "reduce_max", out=smm[:, 1:2], in_=pC[0:64, 0:512], axis=AX.X)
                            yield
                            TT(smm[:, 0:1], smm[:, 0:1], smm[:, 1:2], ALU.max)
                            b.op("dve", "tensor_scalar", out=smm[:, 2:3], in0=smm[:, 0:1], scalar1=-0.125, scalar2=None, op0=ALU.mult)
                            yield
                            b.op("act", "activation", out=Pl[:, :], in_=slc[:, :], func=AF.Exp, scale=0.125, bias=smm[:, 2:3],
                                 accum_out=smm[:, 3:4])
                            b.op("act", "activation", out=Pc[:, :], in_=pC[0:64, 0:512], func=AF.Exp, scale=0.125,
                                 bias=smm[:, 2:3], accum_out=smm[:, 4:5])
                            yield
                            TT(smm[:, 5:6], smm[:, 3:4], smm[:, 4:5], ALU.add)
                            b.op("dve", "reciprocal", out=smm[:, 6:7], in_=smm[:, 5:6])
                            for jj in range(0, 8, 2):
                                for q_ in range(2):
                                    b.op("pe", "transpose", out=pT[0:64, q_, 0:64], in_=Pl[:, (jj + q_) * 64:(jj + q_ + 1) * 64],
                                         identity=ident_bf[0:64, 0:64])
                                b.op("dve" if (jj // 2) % 2 == 0 else "act", "tensor_copy" if (jj // 2) % 2 == 0 else "activation",
                                     out=PTl[0:64, jj:jj + 2, :], in_=pT[0:64, :, 0:64], **({} if (jj // 2) % 2 == 0 else {"func": AF.Copy}))
                                yield
                            for jj in range(0, 4, 2):
                                for q_ in range(2):
                                    b.op("pe", "transpose", out=pT[:, q_, 0:64], in_=Pc[:, (jj + q_) * 128:(jj + q_ + 1) * 128],
                                         identity=ident_bf[0:64, 0:64])
                                b.op("dve" if (jj // 2) % 2 == 0 else "act", "tensor_copy" if (jj // 2) % 2 == 0 else "activation",
                                     out=PTl[:, 8 + jj:10 + jj, :], in_=pT[:, :, 0:64], **({} if (jj // 2) % 2 == 0 else {"func": AF.Copy}))
                                yield
                            for jj in range(8):
                                b.op("pe", "matmul", accum=(jj > 0), out=pO[0:64, 0:64], lhsT=PTl[0:64, jj, :],
                                     rhs=vtg[:, kr0 + jj, base:base + 64], start=(jj == 0), stop=False)
                            for jj in range(4):
                                b.op("pe", "matmul", accum=True, out=pO[0:64, 0:64], lhsT=PTl[:, 8 + jj, :],
                                     rhs=vtc[:, jj, base:base + 64], start=False, stop=(jj == 3))
                            yield
                            b.op("dve", "tensor_scalar", out=otk[:, base:base + 64], in0=pO[0:64, 0:64], scalar1=smm[:, 6:7],
                                 scalar2=None, op0=ALU.mult)
                        run_interleaved([head(0), head(1)])
                        b.op("pe", "transpose", out=psTb[rr % 2][:, 0, 0:64], in_=otk[:, :], identity=ident_bf[0:64, 0:64])
                        b.op("act", "activation", out=oT[:, 0, rr * 64:(rr + 1) * 64], in_=psTb[rr % 2][:, 0, 0:64], func=AF.Copy)
                    for o4 in range(2):
                        b.dma("pool", woT[:, :], na_w_o[gi * 128:(gi + 1) * 128, o4 * 512:(o4 + 1) * 512])
                        for oo in range(4):
                            o = o4 * 4 + oo
                            py = psY[oo % 2]
                            b.op("pe", "matmul", out=py[:, 0:512], lhsT=woT[:, oo * 128:(oo + 1) * 128], rhs=oT[:, 0, :],
                                 start=True, stop=True)
                            if gi == 0:
                                b.op("act", "activation", out=y[:, o, :], in_=py[:, 0:512], func=AF.Copy)
                            else:
                                b.dma("sp", tmpA[oo % 2][:, :], ysc[o * 128:(o + 1) * 128, tl * 512:(tl + 1) * 512])
                                TT(y[:, o, :], tmpA[oo % 2][:, :], py[:, 0:512], ALU.add)
                    if gi < 7:
                        b.dma("sp", ysc[:, tl * 512:(tl + 1) * 512].rearrange("(c p) t -> p c t", p=128), y[:])
                    else:
                        post_norm_residual(2, 1)
                        lat_ffn_sub(2, 2)
                        lat_store(zres, tl)


        def dn_lat(final_dst):
            sbf = lambda n_, sh, dt_=F32: b.sb("ld_" + n_, sh, dt_)
            TT = lambda o, a_, b_, op: b.op("dve", "tensor_tensor", out=o, in0=a_, in1=b_, op=op)
            NCH = LSEQ // 64
            msk = sbf("msk", [64, 6, 64]); b.dma("sp", msk[:], dn_mask)
            cw = sbf("cw", [128, 80]); load_rows_T(cw[:, :], dn_conv_w, 80)
            ones_f = sbf("ones", [64, 128]); b.raw("dve", lambda: nc.vector.memset(ones_f[:, :], 1.0), writes=[ones_f[:, :]])
            wba = sbf("wba", [128, 2, KC, 16], BF16)
            for d in range(2):
                b.dma("pool", wba[:, d, :, :], dn_w_ba[d].rearrange("(k p) n -> p k n", p=128))
            alog = sbf("alog", [64, 16]); b.dma("sp", alog[:], dn_a_log.broadcast_to([64, 16]))
            dtb = sbf("dtb", [64, 16]); b.dma("sp", dtb[:], dn_dt_bias.broadcast_to([64, 16]))
            outg = sbf("outg", [64, 128]); b.dma("sp", outg[:], dn_out_g.broadcast_to([64, 128]))
            nexpa = sbf("nexpa", [64, 16])
            b.op("act", "activation", out=nexpa[:, :], in_=alog[:, :], func=AF.Exp)
            b.op("dve", "tensor_scalar", out=nexpa[:, :], in0=nexpa[:, :], scalar1=-1.0, scalar2=None, op0=ALU.mult)
            zt = sbf("zt", [128, 2]); b.raw("dve", lambda: nc.vector.memset(zt[:, :], 0.0), writes=[zt[:, :]])
            for mm in range(16):
                b.dma("sp", pjsc[mm, :, 0:2], zt[:, :])
                b.dma("sp", pjsc[mm, :, LSEQ + 2:LSEQ + 4], zt[:, :])
            zstg = [act[:, 2 + i, :] for i in range(2)]
            graw = sbf("graw", [64, 2, NCH, 16])
            for tl in range(NTL):
                lat_load(zres, tl)
                lat_ffn_sub(3, 0)
                lat_store(zres, tl)
                pre_norm(3, 1)
                for m0 in range(0, 24, 4):
                    sl_ = wslot()
                    v = sl_[:, 0:KC * 512].rearrange("p (k n) -> p k n", k=KC)
                    b.dma("pool", v, dn_w_in[:, m0 * 128:(m0 + 4) * 128].rearrange("(k p) n -> p k n", p=128))
                    for mm in range(4):
                        m = m0 + mm
                        pq = psG[mm % 2]
                        for k in range(KC):
                            b.op("pe", "matmul", accum=(k > 0), out=pq[:, 0:512], lhsT=v[:, k, mm * 128:(mm + 1) * 128],
                                 rhs=h[:, k, :], start=(k == 0), stop=(k == KC - 1))
                        if m >= 16:
                            b.op("act", "activation", out=zstg[mm % 2], in_=pq[:, 0:512], func=AF.Silu)
                            b.dma("sp", zsc[m - 16, :, tl * 512:(tl + 1) * 512], zstg[mm % 2])
                        else:
                            b.op("act", "activation", out=tmpA[mm % 2][:, :], in_=pq[:, 0:512], func=AF.Copy)
                            b.dma("sp", pjsc[m, :, 2 + tl * 512:2 + (tl + 1) * 512], tmpA[mm % 2][:, :])
                for blk in range(8):
                    n = tl * 8 + blk
                    for d in range(2):
                        pq = psU[d]
                        for k in range(KC):
                            b.op("pe", "matmul", accum=(k > 0), out=pq[0:64, 0:16], lhsT=h[:, k, blk * 64:(blk + 1) * 64],
                                 rhs=wba[:, d, k, :], start=(k == 0), stop=(k == KC - 1))
                        b.op("dve", "tensor_copy", out=graw[:, d, n, :], in_=pq[0:64, 0:16])
            cin = sbf("cin", [128, 516])
            qn = sbf("qn", [128, LSEQ], BF16); kn = sbf("kn", [128, LSEQ], BF16)
            kt_tok = sbf("kt", [64, NCH, 128], BF16); vt_tok = sbf("vt", [64, NCH, 128], BF16)
            zs = sbf("zs", [128, LSEQ], BF16); vtmp = act[:, 1, :]
            oTf = oT[:, :, :].rearrange("p c t -> p (c t)")
            beta = sbf("beta", [64, 2, NCH]); xa = sbf("xa", [64, 2, NCH])
            xe = sbf("xe", [64, 2, NCH]); gg = sbf("gg", [64, 2, NCH]); gc = sbf("gc", [64, 2, NCH]); eg = sbf("eg", [64, 2, NCH])
            def mkbufs(sx):
                f64 = lambda n_: sbf(n_ + sx, [64, 64])
                U = {}
                U["dg"], U["decS"], U["decT"], U["PTt"] = f64("dg"), f64("decS"), f64("decT"), f64("PTt")
                U["Ms"] = [(f64("Ma"), f64("MTa")), (f64("Mb"), f64("MTb"))]
                U["gcr"] = sbf("gcr" + sx, [128, 64]); U["TTb"] = sbf("TTb" + sx, [64, 64], BF16); U["aT"] = sbf("aT" + sx, [64, 64], BF16)
                U["vb"] = sbf("vb" + sx, [64, 128], BF16); U["kbg"] = sbf("kbg" + sx, [64, 128], BF16); U["kg"] = sbf("kg" + sx, [64, 128], BF16)
                U["wT"] = sbf("wT" + sx, [128, 64], BF16); U["u_sb"] = sbf("u" + sx, [64, 128]); U["vnew"] = sbf("vnew" + sx, [64, 128])
                U["vnew_b"] = sbf("vnewb" + sx, [64, 128], BF16)
                U["o1"] = sbf("o1" + sx, [64, 128]); U["sc1"] = sbf("sc1" + sx, [64, 4]); U["egl"] = sbf("egl" + sx, [128, 1])
                U["S"] = sbf("S" + sx, [128, 128]); U["Sb"] = sbf("Sb" + sx, [128, 128], BF16)
                return U
            UB = [mkbufs("_a"), mkbufs("_b")]
            og = [sbf("og0", [64, 4, 128]), sbf("og1", [64, 4, 128])]
            ss = sbf("ss", [64, 2]); onb = sbf("onb", [64, 128], BF16); junk = sbf("junk", [64, 128])
            woT = sbf("woT", [128, 512], BF16)
            id64 = ident[0:64, 0:64]
            fl = lambda A: A[:, :, :].rearrange("p a b -> p (a b)")
            for hv in range(8):
                hq = hv // 2
                wcols = [hq * 128, 512 + hq * 128, 1024 + hv * 128, 2048 + hv * 128]
                cwc = [hq, 4 + hq, 8 + hv]
                pjc = [hq, 4 + hq, 8 + hv]
                b.dma("sp", zs[:, :], zsc[hv, :, :])
                for d in range(2):
                    b.op("act", "activation", out=beta[:, d, :], in_=graw[:, d, :, hv], func=AF.Sigmoid)
                    b.op("dve", "tensor_scalar", out=xa[:, d, :], in0=graw[:, d, :, 8 + hv], scalar1=dtb[:, d * 8 + hv:d * 8 + hv + 1],
                         scalar2=None, op0=ALU.add)
                b.op("act", "activation", out=fl(xe), in_=fl(xa), func=AF.Abs)
                b.op("act", "activation", out=fl(xe), in_=fl(xe), func=AF.Exp, scale=-1.0)
                b.op("act", "activation", out=fl(xe), in_=fl(xe), func=AF.Ln, bias=1.0, scale=1.0)
                b.op("dve", "tensor_scalar", out=fl(xa), in0=fl(xa), scalar1=0.0, scalar2=None, op0=ALU.max)
                TT(fl(xa), fl(xa), fl(xe), ALU.add)
                for d in range(2):
                    b.op("dve", "tensor_scalar", out=gg[:, d, :], in0=xa[:, d, :], scalar1=nexpa[:, d * 8 + hv:d * 8 + hv + 1],
                         scalar2=None, op0=ALU.mult)
                    b.op("pe", "matmul", out=psS[0:64, 0:NCH], lhsT=msk[:, d, :], rhs=gg[:, d, :], start=True, stop=True)
                    b.op("dve", "tensor_copy", out=gc[:, d, :], in_=psS[0:64, 0:NCH])
                b.op("act", "activation", out=fl(eg), in_=fl(gc), func=AF.Exp)
                for tl in range(NTL):
                    for mm in (range(3) if hv % 2 == 0 else range(2, 3)):
                        b.dma("sp", cin[:, :], pjsc[pjc[mm], :, tl * 512:tl * 512 + 516])
                        for j in range(5):
                            dst = tmpA[0] if j == 0 else tmpA[1]
                            b.op("act", "activation", out=dst[:, :], in_=cin[:, j:j + 512], func=AF.Identity,
                                 scale=cw[:, j * 16 + cwc[mm]:j * 16 + cwc[mm] + 1])
                            if j > 0:
                                TT(tmpA[0][:, :], tmpA[0][:, :], tmpA[1][:, :], ALU.add)
                        if mm < 2:
                            b.op("act", "activation", out=tmpA[1][:, :], in_=tmpA[0][:, :], func=AF.Silu)
                            b.op("act", "activation", out=act[:, 0, :], in_=tmpA[1][:, :], func=AF.Square)
                            b.op("pe", "matmul", out=psS[:, 0:512], lhsT=ones_bf[:, :], rhs=act[:, 0, :], start=True, stop=True)
                            b.op("act", "activation", out=rstd[:, :], in_=psS[:, 0:512], func=AF.Sqrt, scale=1.0, bias=EPS)
                            b.op("dve", "reciprocal", out=rstd[:, :], in_=rstd[:, :])
                            TT(tmpA[1][:, :], tmpA[1][:, :], rstd[:, :], ALU.mult)
                            if mm == 0:
                                b.op("act", "activation", out=qn[:, tl * 512:(tl + 1) * 512], in_=tmpA[1][:, :], func=AF.Identity,
                                     scale=128 ** -0.5)
                            else:
                                b.op("act", "activation", out=kn[:, tl * 512:(tl + 1) * 512], in_=tmpA[1][:, :], func=AF.Identity, scale=1.0)
                                for blk in range(8):
                                    n = tl * 8 + blk
                                    b.op("pe", "transpose", out=psTb[blk % 2][0:64, 0, :], in_=kn[:, n * 64:(n + 1) * 64], identity=ident_bf[:, :])
                                    b.op("dve", "tensor_copy", out=kt_tok[:, n, :], in_=psTb[blk % 2][0:64, 0, :])
                        else:
                            b.op("act", "activation", out=vtmp, in_=tmpA[0][:, :], func=AF.Silu)
                            for blk in range(8):
                                n = tl * 8 + blk
                                b.op("pe", "transpose", out=psTb[blk % 2][0:64, 0, :], in_=act[:, 1, blk * 64:(blk + 1) * 64], identity=ident_bf[:, :])
                                b.op("dve", "tensor_copy", out=vt_tok[:, n, :], in_=psTb[blk % 2][0:64, 0, :])
                def chain(d):
                    U = UB[d]
                    dg, decS, decT, PTt, Ms, gcr, TTb, aT = U["dg"], U["decS"], U["decT"], U["PTt"], U["Ms"], U["gcr"], U["TTb"], U["aT"]
                    vb, kbg, kg, wT, u_sb, vnew, vnew_b = U["vb"], U["kbg"], U["kg"], U["wT"], U["u_sb"], U["vnew"], U["vnew_b"]
                    o1, sc1, egl, S, Sb = U["o1"], U["sc1"], U["egl"], U["S"], U["Sb"]
                    pG, pU, pY = psG[d], psU[d], psY[d]
                    b.dma("sp", S[:, :], st_dn[d * 8 + hv, :, :])
                    b.op("act", "activation", out=Sb[:, :], in_=S[:, :], func=AF.Copy)
                    yield
                    last = 63 if d == 0 else 0
                    for n in (range(NCH) if d == 0 else range(NCH - 1, -1, -1)):
                        tok0 = n * 64
                        gc_, be_, eg_ = gc[:, d, n:n + 1], beta[:, d, n:n + 1], eg[:, d, n:n + 1]
                        kf, qf = kn[:, tok0:tok0 + 64], qn[:, tok0:tok0 + 64]
                        kt, vt = kt_tok[:, n, :], vt_tok[:, n, :]
                        b.op("dve", "tensor_scalar", out=dg[:, :], in0=id64, scalar1=gc_, scalar2=None, op0=ALU.mult)
                        b.op("pe", "matmul", out=psS[:, 0:64], lhsT=ones_f[:, :], rhs=dg[:, :], start=True, stop=True)
                        b.op("act", "activation", out=gcr[:, :], in_=psS[:, 0:64], func=AF.Copy)
                        yield
                        b.op("dve", "tensor_scalar", out=decS[:, :], in0=gcr[0:64, :], scalar1=-1.0, scalar2=gc_, op0=ALU.mult, op1=ALU.add)
                        b.op("dve", "tensor_scalar", out=decS[:, :], in0=decS[:, :], scalar1=0.0, scalar2=None, op0=ALU.min)
                        b.op("act", "activation", out=decS[:, :], in_=decS[:, :], func=AF.Exp)
                        TT(decS[:, :], decS[:, :], msk[:, 4 + d, :], ALU.mult)
                        yield
                        b.op("dve", "tensor_scalar", out=decT[:, :], in0=gcr[0:64, :], scalar1=gc_, scalar2=None, op0=ALU.subtract)
                        b.op("dve", "tensor_scalar", out=decT[:, :], in0=decT[:, :], scalar1=0.0, scalar2=None, op0=ALU.min)
                        b.op("act", "activation", out=decT[:, :], in_=decT[:, :], func=AF.Exp)
                        TT(decT[:, :], decT[:, :], msk[:, d, :], ALU.mult)
                        yield
                        M, MT = Ms[0]
                        b.op("pe", "matmul", out=pG[0:64, 0:64], lhsT=kf, rhs=kf, start=True, stop=True)
                        b.op("dve", "tensor_scalar", out=M[:, :], in0=pG[0:64, 0:64], scalar1=be_, scalar2=-1.0, op0=ALU.mult, op1=ALU.mult)
                        TT(M[:, :], M[:, :], decS[:, :], ALU.mult)
                        yield
                        b.op("pe", "transpose", out=pU[0:64, 0:64], in_=M[:, :], identity=id64)
                        b.op("act", "activation", out=MT[:, :], in_=pU[0:64, 0:64], func=AF.Copy)
                        TT(PTt[:, :], MT[:, :], id64, ALU.add)
                        yield
                        cur = 0
                        for k in range(1, 6):
                            M, MT = Ms[cur]
                            Mn, MTn = Ms[1 - cur]
                            b.op("pe", "matmul", out=pG[0:64, 0:64], lhsT=MT[:, :], rhs=M[:, :], start=True, stop=True)
                            b.op("act", "activation", out=Mn[:, :], in_=pG[0:64, 0:64], func=AF.Copy)
                            if k < 5:
                                b.op("pe", "matmul", out=pU[0:64, 0:64], lhsT=M[:, :], rhs=MT[:, :], start=True, stop=True)
                                b.op("dve", "tensor_copy", out=MTn[:, :], in_=pU[0:64, 0:64])
                            yield
                            b.op("pe", "matmul", out=pY[0:64, 0:64], lhsT=Mn[:, :], rhs=PTt[:, :], start=True, stop=True)
                            TT(PTt[:, :], PTt[:, :], pY[0:64, 0:64], ALU.add)
                            yield
                            cur = 1 - cur
                        b.op("act", "activation", out=TTb[:, :], in_=PTt[:, :], func=AF.Copy)
                        TT(sc1[:, 0:1], be_, eg_, ALU.mult)
                        b.op("dve", "tensor_scalar", out=vb[:, :], in0=vt, scalar1=be_, scalar2=None, op0=ALU.mult)
                        b.op("dve", "tensor_scalar", out=kbg[:, :], in0=kt, scalar1=sc1[:, 0:1], scalar2=None, op0=ALU.mult)
                        yield
                        b.op("pe", "matmul", out=pG[0:64, 0:128], lhsT=TTb[:, :], rhs=vb[:, :], start=True, stop=True)
                        b.op("act", "activation", out=u_sb[:, :], in_=pG[0:64, 0:128], func=AF.Copy)
                        b.op("pe", "matmul", out=pU[:, 0:64], lhsT=kbg[:, :], rhs=TTb[:, :], start=True, stop=True)
                        b.op("dve", "tensor_copy", out=wT[:, :], in_=pU[:, 0:64])
                        yield
                        b.op("pe", "matmul", out=pY[0:64, 0:128], lhsT=wT[:, :], rhs=Sb[:, :], start=True, stop=True)
                        TT(vnew[:, :], u_sb[:, :], pY[0:64, 0:128], ALU.subtract)
                        b.op("act", "activation", out=vnew_b[:, :], in_=vnew[:, :], func=AF.Copy)
                        yield
                        b.op("pe", "matmul", out=pG[0:64, 0:128], lhsT=qf, rhs=Sb[:, :], start=True, stop=True)
                        b.op("dve", "tensor_scalar", out=o1[:, :], in0=pG[0:64, 0:128], scalar1=eg_, scalar2=None, op0=ALU.mult)
                        b.op("pe", "matmul", out=pU[0:64, 0:64], lhsT=kf, rhs=qf, start=True, stop=True)
                        TT(aT[:, :], pU[0:64, 0:64], decT[:, :], ALU.mult)
                        yield
                        b.op("pe", "matmul", out=pY[0:64, 0:128], lhsT=aT[:, :], rhs=vnew_b[:, :], start=True, stop=True)
                        TT(o1[:, :], o1[:, :], pY[0:64, 0:128], ALU.add)
                        b.dma("sp", osc[d, :, n, :], o1[:, :])
                        yield
                        b.op("act", "activation", out=sc1[:, 1:2], in_=gc_, func=AF.Exp, scale=-1.0, bias=gcr[0:64, last:last + 1])
                        b.op("dve", "tensor_scalar", out=kg[:, :], in0=kt, scalar1=sc1[:, 1:2], scalar2=None, op0=ALU.mult)
                        b.op("act", "activation", out=egl[:, :], in_=gcr[:, last:last + 1], func=AF.Exp)
                        yield
                        b.op("pe", "matmul", out=pG[:, 0:128], lhsT=kg[:, :], rhs=vnew_b[:, :], start=True, stop=True)
                        b.op("dve", "tensor_scalar", out=S[:, :], in0=S[:, :], scalar1=egl[:, 0:1], scalar2=None, op0=ALU.mult)
                        TT(S[:, :], S[:, :], pG[:, 0:128], ALU.add)
                        b.op("act", "activation", out=Sb[:, :], in_=S[:, :], func=AF.Copy)
                        yield

                gens = [chain(0), chain(1)]
                alive = [True, True]
                while any(alive):
                    for gi_ in range(2):
                        if alive[gi_]:
                            try:
                                next(gens[gi_])
                            except StopIteration:
                                alive[gi_] = False
                for n0 in range(0, NCH, 4):
                    for d in range(2):
                        b.dma("sp", og[d][:, :, :], osc[d, :, n0:n0 + 4, :])
                    for nn in range(4):
                        n = n0 + nn
                        tok0 = n * 64
                        o1 = UB[0]["o1"]
                        TT(o1[:, :], og[0][:, nn, :], og[1][:, nn, :], ALU.add)
                        b.op("act", "activation", out=junk[:, :], in_=o1[:, :], func=AF.Square, accum_out=ss[:, 0:1])
                        b.op("act", "activation", out=ss[:, 1:2], in_=ss[:, 0:1], func=AF.Sqrt, scale=1.0 / 128, bias=EPS)
                        b.op("dve", "reciprocal", out=ss[:, 1:2], in_=ss[:, 1:2])
                        b.op("dve", "tensor_scalar", out=junk[:, :], in0=o1[:, :], scalar1=ss[:, 1:2], scalar2=None, op0=ALU.mult)
                        TT(onb[:, :], junk[:, :], outg[:, :], ALU.mult)
                        b.op("pe", "transpose", out=psTb[n % 2][:, 0, 0:64], in_=onb[:, :], identity=ident_bf[0:64, 0:64])
                        TT(oTf[:, tok0:tok0 + 64], psTb[n % 2][:, 0, 0:64], zs[:, tok0:tok0 + 64], ALU.mult)
                for tl in range(NTL):
                    if hv == 7:
                        lat_load(zres, tl)
                    for o in range(KC):
                        if o % 4 == 0:
                            b.dma("pool", woT[:, :], dn_w_o[hv * 128:(hv + 1) * 128, (o // 4) * 512:(o // 4 + 1) * 512])
                        py = psY[o % 2]
                        b.op("pe", "matmul", out=py[:, 0:512], lhsT=woT[:, (o % 4) * 128:(o % 4 + 1) * 128], rhs=oTf[:, tl * 512:(tl + 1) * 512],
                             start=True, stop=True)
                        if hv == 0:
                            b.op("act", "activation", out=y[:, o, :], in_=py[:, 0:512], func=AF.Copy)
                        else:
                            b.dma("sp", tmpA[o % 2][:, :], ysc[o * 128:(o + 1) * 128, tl * 512:(tl + 1) * 512])
                            TT(y[:, o, :], tmpA[o % 2][:, :], py[:, 0:512], ALU.add)
                    if hv < 7:
                        b.dma("sp", ysc[:, tl * 512:(tl + 1) * 512].rearrange("(c p) t -> p c t", p=128), y[:])
                    else:
                        post_norm_residual(3, 1)
                        lat_ffn_sub(3, 2)
                        lat_store(final_dst, tl)

        CS = 0 if CTX_SKIP else STAGE
        def dbg(ap, n, col0=0):
            b.dma("sp", dbg_out[:, col0:col0 + n], ap)

        if CS >= 1:
            ada_layer(0)
            if DEBUG:
                dbg(mods[0][:, :], 72)
                dbg(Acoef[:, 0:24], 24, 72)
                dbg(Gcoef[:, 0:24], 24, 96)
        if CS >= 2:
            pre_norm(0, 0)
            if DEBUG:
                dbg(rstd[:, :], 512, 512)
        if CS >= 3:
            ffn(ffn_w_gu[0, 0], ffn_w_d[0, 0], key=0)
            if DEBUG:
                dbg(y[:, 0, :], 512, 1024)
        if CS >= 4:
            post_norm_residual(0, 0)
        if CS >= 5:
            pre_norm(0, 1)
            proj_tokmajor(a_w_qkv, 1024, 256, k_out)
            proj_tokmajor(a_w_qkv, 1280, 256, v_out, vdst_col0=0)
        if CS >= 6:
            proj_featmajor(qT, 0, a_w_qkv, 0, 8, h)
            proj_featmajor(kT, 0, a_w_qkv, 1024, 4, h, dup64=True)
            hm = {hh: (hh // 2, 64 * (hh % 2), hh // 4, (hh // 4) * 64) for hh in range(16)}
            if SUB >= 2:
                attn_ctx(16, hm, True, 0.125)
            if DEBUG and SUB >= 2:
                b.op("dve", "tensor_copy", out=tmpA[0][:, :], in_=oT[:, 0, :])
                dbg(tmpA[0][:, :], 512, 1536)
            if SUB >= 1:
                proj_out_featmajor(a_w_o, oT)
                post_norm_residual(0, 1)
        if CS >= 7:
            pre_norm(0, 2)
            ffn(ffn_w_gu[0, 1], ffn_w_d[0, 1], key=1)
            post_norm_residual(0, 2)
        if CS >= 8:
            ada_layer(1)
            pre_norm(1, 0)
            ffn(ffn_w_gu[1, 0], ffn_w_d[1, 0], key=2)
            post_norm_residual(1, 0)
        if CS >= 9:
            pre_norm(1, 1)
            s5_es = ExitStack()
            b.es = s5_es
            S5P = s5_setup()
            s5_mixer(S5P)
            b.es = es
            if DEBUG:
                dbg(y[:, 0, :], 512, 2048)
            post_norm_residual(1, 1)
        if CS >= 10:
            pre_norm(1, 2)
            ffn(ffn_w_gu[1, 1], ffn_w_d[1, 1], key=3)
            post_norm_residual(1, 2)
        if CS >= 11:
            ada_layer(2)
            pre_norm(2, 0)
            ffn(ffn_w_gu[2, 0], ffn_w_d[2, 0], key=4)
            post_norm_residual(2, 0)
            pre_norm(2, 1)
            proj_tokmajor(na_w_qkv, 1024, 1024, nak_out)
            proj_tokmajor(na_w_qkv, 2048, 1024, nav_out, vdst_col0=0)
            proj_featmajor(qT, 0, na_w_qkv, 0, 8, h)
            proj_featmajor(kT, 0, na_w_qkv, 1024, 8, h)
            hm2 = {hh: (hh // 2, 64 * (hh % 2), hh // 2, hh * 64) for hh in range(16)}
            attn_ctx(16, hm2, None, 0.125)
            proj_out_featmajor(na_w_o, oT)
            post_norm_residual(2, 1)
            pre_norm(2, 2)
            ffn(ffn_w_gu[2, 1], ffn_w_d[2, 1], key=5)
            post_norm_residual(2, 2)
        if CS >= 12:
            ada_layer(3)
            pre_norm(3, 0)
            ffn(ffn_w_gu[3, 0], ffn_w_d[3, 0], key=6)
            post_norm_residual(3, 0)
        if CS >= 13:
            pre_norm(3, 1)
            b.fence()
            s5_es.close()
            att_es.close()
            dn_es = ExitStack()
            b.es = dn_es
            dn_mixer()
            b.es = es
            if DEBUG:
                dbg(y[:, 0, :], 512, 2560)
            post_norm_residual(3, 1)
            pre_norm(3, 2)
            ffn(ffn_w_gu[3, 1], ffn_w_d[3, 1], key=7)
            post_norm_residual(3, 2)
        b.dma("sp", yT_out.rearrange("(c p) t -> p c t", p=128), x[:])
        if STAGE >= 20:
            b.fence()
            if CTX_SKIP:
                att_es.close()
            else:
                dn_es.close()
            load_rows_T(cc[:, :], c_lat, 8)
            b.op("act", "activation", out=scond_lat[:, :, 0], in_=cc[:, :], func=AF.Silu)
            lat_es = ExitStack()
            b.es = lat_es
            ada_layer(0, scond_lat)
            a_lat(xT_lat)
            b.es = es
            b.fence()
            lat_es.close()
            if STAGE >= 21:
                lat_es = ExitStack()
                b.es = lat_es
                ada_layer(1, scond_lat)
                s5_lat()
                b.es = es
                b.fence()
                lat_es.close()
            if STAGE >= 22:
                lat_es = ExitStack()
                b.es = lat_es
                ada_layer(2, scond_lat)
                na_lat()
                b.es = es
                b.fence()
                lat_es.close()
            if STAGE >= 23:
                lat_es = ExitStack()
                b.es = lat_es
                ada_layer(3, scond_lat)
                dn_lat(zres)
                b.es = es
                b.fence()
                lat_es.close()
        if STAGE >= 20:
            for tl in range(NTL):
                lat_load(zres, tl)
                lat_store(ysamp_out, tl)
        else:
            b.raw("dve", lambda: nc.vector.memset(tmpA[0][:, :], 0.0), writes=[tmpA[0][:, :]])
            for tl in range(NTL):
                for c in range(KC):
                    b.dma("sp", ysamp_out[c * 128:(c + 1) * 128, tl * 512:(tl + 1) * 512], tmpA[0][:, :])
        if STAGE < 13:
            for j in range(32):
                b.dma("sp", dn_out[j, :, :], tmpA[0][:, 0:128])

        b.finish()
        if STAGE >= 20:
            pass
        elif STAGE >= 13:
            dn_es.close()
        else:
            if CS >= 9:
                s5_es.close()
            att_es.close()
    return nc


_PROG = None


def _dn_masks():
    i = np.arange(64)
    bef0 = (i[:, None] <= i[None, :]).astype(np.float32)
    bef1 = (i[:, None] >= i[None, :]).astype(np.float32)
    eye = np.eye(64, dtype=np.float32)
    m = np.stack([bef0, bef1, bef0.T, bef1.T, bef0.T - eye, bef1.T - eye], 1)
    return np.ascontiguousarray(m, dtype=np.float32)


def _rope_tables(L):
    n = 16
    inv = (np.float32(10000.0) ** (-np.arange(n, dtype=np.float32) / np.float32(n))).astype(np.float32)
    t = np.arange(L)
    ang_r = (t // 64).astype(np.float32)[:, None] * inv[None, :]
    ang_c = (t % 64).astype(np.float32)[:, None] * inv[None, :]
    return np.ascontiguousarray(np.concatenate([np.cos(ang_r), np.cos(ang_c), np.sin(ang_r), np.sin(ang_c)], 1), dtype=np.float32)


def _na_bias_gather(rpb):
    q = np.arange(64)[:, None]
    k = np.arange(64)[None, :]
    dc = np.clip(k - q, -15, 15) + 15
    g = rpb[:, :, dc]
    return np.ascontiguousarray(g.transpose(0, 2, 1, 3).reshape(16, 64, 960), dtype=np.float32)


def _na_colmask():
    col = np.arange(64)
    cs = np.clip(col - 8, 0, 48)
    ok = (col[None, :] >= cs[:, None]) & (col[None, :] < cs[:, None] + 16)
    return np.where(ok, 0.0, -30000.0).astype(np.float32)


def _band_mask():
    q = np.arange(128)[:, None]
    j = np.arange(384)[None, :] - 128
    return np.where(np.abs(j - q) <= 128, 0.0, -30000.0).astype(np.float32)


def prep_core(inputs, r, lseq=None):
    f = lambda a: np.ascontiguousarray(np.asarray(a, dtype=np.float32))
    L = LSEQ if lseq is None else lseq
    bsel = r % 2
    return {
        "xT_ctx": np.ascontiguousarray(f(inputs["x_prompt"])[2 * r:2 * r + 2].reshape(TCTX, D).T),
        "xT_lat": np.ascontiguousarray(f(inputs["x_sample"])[bsel, :L].T),
        "c_lat": f(inputs["c"])[bsel].reshape(8, 128),
        "cache_ak": f(inputs["cache_attn_k"])[bsel, 0].reshape(512, 256),
        "cache_av": f(inputs["cache_attn_v"])[bsel, 0].reshape(512, 256),
        "rope_cs": _rope_tables(L),
        "st5_re": f(inputs["state_s5_re"])[bsel, 0].reshape(64, 128),
        "st_dn": f(inputs["state_dn"])[bsel, 0].reshape(16, 128, 128),
        "cache_nk": f(inputs["cache_na_k"])[bsel, 0].reshape(512, D),
        "cache_nv": f(inputs["cache_na_v"])[bsel, 0].reshape(512, D),
        "na_bias": _na_bias_gather(f(inputs["na_rpb"])[0]),
        "na_cmask": _na_colmask(),
        "st5_im": f(inputs["state_s5_im"])[bsel, 0].reshape(64, 128),
        "band_mask": _band_mask(),
    }


def prep_shared(inputs, nl=4):
    f = lambda a: np.ascontiguousarray(np.asarray(a, dtype=np.float32))
    return {
        "ident": np.eye(128, dtype=np.float32),
        "c_ctx": f(inputs["c_ctx"]).reshape(8, 128),
        "norm_g": f(inputs["norm_g"]).reshape(192, 128),
        "w_ada": f(inputs["w_ada"][0:nl]),
        "b_ada": f(inputs["b_ada"]).reshape(288, 128),
        "ffn_w_gu": f(inputs["ffn_w_gu"][0:nl]),
        "ffn_w_d": f(inputs["ffn_w_d"][0:nl]),
        "a_w_qkv": f(inputs["a_w_qkv"])[0],
        "a_w_o": f(inputs["a_w_o"])[0],
        "a_sink": f(inputs["a_sink"]).reshape(1, 16),
        "s5_lam_re": f(inputs["s5_lam_re"]).reshape(64, 128),
        "s5_lam_im": f(inputs["s5_lam_im"]).reshape(64, 128),
        "s5_logdt": f(np.repeat(np.asarray(inputs["s5_log_dt"], np.float32).reshape(2, 32, 2), 64, axis=-1)).reshape(64, 128),
        "s5_b_re": f(inputs["s5_b_re"])[0],
        "s5_b_im": f(inputs["s5_b_im"])[0],
        "s5_c_re": f(inputs["s5_c_re"])[0],
        "s5_c_im": f(inputs["s5_c_im"])[0],
        "s5_d": f(inputs["s5_d"]).reshape(8, 128),
        "s5_w_glu": f(inputs["s5_w_glu"])[0],
        "na_w_qkv": f(inputs["na_w_qkv"])[0],
        "na_w_o": f(inputs["na_w_o"])[0],
        "dn_w_in": f(inputs["dn_w_in"])[0],
        "dn_conv_w": f(inputs["dn_conv_w"]).reshape(5, 16, 128).reshape(80, 128),
        "dn_w_ba": f(inputs["dn_w_ba"])[0],
        "dn_a_log": f(inputs["dn_a_log"]).reshape(1, 16),
        "dn_dt_bias": f(inputs["dn_dt_bias"]).reshape(1, 16),
        "dn_out_g": f(inputs["dn_out_g"]).reshape(1, 128),
        "dn_w_o": f(inputs["dn_w_o"])[0],
        "dn_mask": _dn_masks(),
    }


def kernel(**inputs):
    global _PROG
    f = lambda a: np.ascontiguousarray(np.asarray(a, dtype=np.float32))
    x_prompt = f(inputs["x_prompt"])
    if _PROG is None:
        _PROG = build_program()
    nc = _PROG
    shared = prep_shared(inputs)
    in_maps = []
    for r in range(NCORES):
        m = dict(shared)
        m.update(prep_core(inputs, r))
        in_maps.append(m)
    res = run_bass_kernel_spmd(nc, in_maps, core_ids=list(range(NCORES)))
    R = res.results
    y_prompt = np.stack([R[r]["out_yT_ctx"].T.reshape(2, SEQ, D) for r in range(NCORES)]).reshape(16, SEQ, D)
    new_k = np.concatenate([R[r]["out_attn_k"].reshape(2, 1, SEQ, 4, 64) for r in range(NCORES)], 0)
    new_v = np.concatenate([R[r]["out_attn_v"].reshape(2, 1, SEQ, 4, 64) for r in range(NCORES)], 0)
    s5re = np.concatenate([R[r]["out_s5_re"].reshape(2, 1, 2, 64, 64) for r in range(NCORES)], 0)
    s5im = np.concatenate([R[r]["out_s5_im"].reshape(2, 1, 2, 64, 64) for r in range(NCORES)], 0)
    nak = np.concatenate([R[r]["out_na_k"].reshape(2, 1, SEQ, 16, 64) for r in range(NCORES)], 0)
    nav = np.concatenate([R[r]["out_na_v"].reshape(2, 1, SEQ, 16, 64) for r in range(NCORES)], 0)
    ysamp = np.stack([R[r]["out_y_sample"].T for r in range(2)], 0)
    dn = np.concatenate([R[r]["out_dn"].reshape(2, 1, 2, 8, 128, 128) for r in range(NCORES)], 0)
    c32 = lambda a: np.ascontiguousarray(a, dtype=np.float32)
    return (c32(y_prompt), c32(ysamp), c32(new_k), c32(new_v), c32(s5re), c32(s5im), c32(nak), c32(nav), c32(dn))
```
